# Optimizing a Trainium2 kernel written in Bass

```python
import math
import jax, jax.numpy as jnp
from jax import lax
import numpy as np

D_MODEL = 1024
BATCH = 8
SEQ = 2048
DEPTH = 4

A_HEAD_DIM = 64
A_HEADS = D_MODEL // A_HEAD_DIM
D_A = A_HEADS * A_HEAD_DIM
LORA_W = 64
LORA_A = 64
RWKV_GN_EPS = 64e-5
POOL_WINDOWS = (2, 4, 8, 16)
POOL_GROUPS = 4
D_B = 3 * D_MODEL // 4
POOL_CH = D_B // POOL_GROUPS
C_HEAD_DIM = 64
D_C = 3 * D_MODEL // 4
C_HEADS = D_C // C_HEAD_DIM
DILATION_GROUPS = ((128, 1), (512, 4), (2048, 16))
C_HEADS_PER_GROUP = C_HEADS // len(DILATION_GROUPS)
D_C_OUT = C_HEADS_PER_GROUP * C_HEAD_DIM
ROPE_THETA = 10000.0
Q_BLOCK = 128
N_SHIFT = 3 * D_A + 2 * LORA_W + 2 * LORA_A
SPLIT_SIZES = (N_SHIFT, D_A, D_B, D_B, D_C, D_C, D_C, D_C_OUT, D_MODEL, D_MODEL, D_MODEL)
N_IN = sum(SPLIT_SIZES)
DEEPNORM_ALPHA = (2 * DEPTH) ** 0.25
DEEPNORM_BETA = (8 * DEPTH) ** -0.25
LN_EPS = 1e-5

kernel_name = "hybrid_rwkv7_pool_dilated_attn_encoder"


def _split(t, sizes):
    out, start = [], 0
    for s in sizes:
        out.append(t[..., start:start + s])
        start += s
    return out


def _normalize(t, eps):
    t = t.astype(jnp.float32)
    mu = jnp.mean(t, axis=-1, keepdims=True)
    var = jnp.mean(jnp.square(t - mu), axis=-1, keepdims=True)
    return (t - mu) * lax.rsqrt(var + eps)


def _orient(t):
    return jnp.concatenate([t[:1], jnp.flip(t[1:], axis=2)], axis=0)


def rwkv7_branch(u, z, mu, w0, w_up, a0, a_up, k_k, k_a, r_k, gn_g, gn_b):
    f32 = jnp.float32
    u = u.astype(f32)
    B, S, _ = u.shape
    prev = jnp.pad(u, ((0, 0), (1, 0), (0, 0)))[:, :-1]
    nxt = jnp.pad(u, ((0, 0), (0, 1), (0, 0)))[:, 1:]
    u = u + mu[0] * (prev - u) + mu[1] * (nxt - u)
    r, k, v, wd_f, wd_b, ad_f, ad_b = _split(u, (D_A, D_A, D_A, LORA_W, LORA_W, LORA_A, LORA_A))
    wd = jnp.stack([wd_f, wd_b])
    ad = jnp.stack([ad_f, ad_b])
    w_log = -jax.nn.softplus(-(w0[:, None, None, :] + jnp.einsum('zbsl,zld->zbsd', jnp.tanh(wd), w_up))) - 0.5
    decay = jnp.exp(-jnp.exp(w_log))
    a = jax.nn.sigmoid(a0[:, None, None, :] + jnp.einsum('zbsl,zld->zbsd', ad, a_up))
    heads = lambda t: t.reshape(*t.shape[:-1], A_HEADS, A_HEAD_DIM)
    kk = heads(k * k_k)
    kk = kk * lax.rsqrt(jnp.sum(kk * kk, axis=-1, keepdims=True) + 1e-12)
    k_dir = heads(k[None] * (1.0 + (a - 1.0) * k_a))
    a_h = heads(a)
    r_h, v_h = heads(r), heads(v)
    both = lambda t: jnp.broadcast_to(t[None], (2,) + t.shape)
    inputs = (both(r_h), heads(decay), k_dir, both(v_h), both(-kk), kk[None] * a_h)
    xs = tuple(jnp.moveaxis(_orient(t), 2, 0) for t in inputs)

    def step(state, inp):
        r_t, w_t, k_t, v_t, nkk_t, b_t = inp
        sa = jnp.einsum('zbhij,zbhj->zbhi', state, nkk_t)
        state = state * w_t[..., None, :] + sa[..., None] * b_t[..., None, :] + v_t[..., None] * k_t[..., None, :]
        return state, jnp.einsum('zbhij,zbhj->zbhi', state, r_t)

    state0 = jnp.zeros((2, B, A_HEADS, A_HEAD_DIM, A_HEAD_DIM), f32)
    _, ys = lax.scan(step, state0, xs)
    y = _orient(jnp.moveaxis(ys, 0, 2))
    y = y[0] + y[1]
    y = _normalize(y, RWKV_GN_EPS).reshape(B, S, D_A) * gn_g + gn_b
    bonus = jnp.sum(r_h * heads(k) * r_k, axis=-1, keepdims=True) * v_h
    y = y + bonus.reshape(B, S, D_A)
    return y * jax.nn.silu(z.astype(f32))


def pool_branch(p, z, w_g, b_g, scale):
    f32 = jnp.float32
    B, S, _ = p.shape
    pg = p.astype(f32).reshape(B, S, POOL_GROUPS, POOL_CH)
    csum = jnp.pad(jnp.cumsum(pg, axis=1), ((0, 0), (1, 0), (0, 0), (0, 0)))
    half = jnp.array([w // 2 for w in POOL_WINDOWS], dtype=jnp.int32)
    pos = jnp.arange(S, dtype=jnp.int32)[:, None]
    lo = jnp.clip(pos - half[None, :], 0, S - 1)
    hi = jnp.clip(pos + half[None, :], 0, S - 1)
    g_idx = jnp.arange(POOL_GROUPS, dtype=jnp.int32)[None, :]
    window_sum = csum[:, hi + 1, g_idx] - csum[:, lo, g_idx]
    count = (hi - lo + 1).astype(f32)[..., None]
    mixed = window_sum / count - pg
    y = jnp.einsum('bsgc,gcd->bsgd', mixed, w_g).reshape(B, S, D_B) + b_g
    return y * scale * jax.nn.silu(z.astype(f32))


def rope_tables(S):
    inv = jnp.power(ROPE_THETA, -jnp.arange(0, C_HEAD_DIM, 2, dtype=jnp.float32) / C_HEAD_DIM)
    ang = jnp.arange(S, dtype=jnp.float32)[:, None] * inv[None, :]
    ang = jnp.concatenate([ang, ang], axis=-1)
    return jnp.cos(ang), jnp.sin(ang)


def apply_rope(t, cos, sin):
    t1, t2 = jnp.split(t, 2, axis=-1)
    rot = jnp.concatenate([-t2, t1], axis=-1)
    return t * cos[None, :, None, :] + rot * sin[None, :, None, :]


def dilated_band_attention(q, k, v, dilation, half_span):
    f32 = jnp.float32
    B, S, H, Dh = q.shape
    L = S // dilation
    n = half_span // dilation
    qb = math.gcd(Q_BLOCK, L)
    nb = L // qb
    kw = qb + 2 * n
    sub = lambda t: t.reshape(B, L, dilation, H, Dh).transpose(0, 2, 3, 1, 4)
    qs = sub(q).reshape(B, dilation, H, nb, qb, Dh)
    pad = ((0, 0), (0, 0), (0, 0), (n, n), (0, 0))
    ks = jnp.pad(sub(k), pad)
    vs = jnp.pad(sub(v), pad)
    kidx = jnp.arange(nb)[:, None] * qb + jnp.arange(kw)[None, :]
    kb = ks[:, :, :, kidx]
    vb = vs[:, :, :, kidx]
    s = jnp.einsum('bdhnqc,bdhnkc->bdhnqk', qs.astype(f32), kb.astype(f32)) * (Dh ** -0.5)
    rel = jnp.arange(kw)[None, :] - n - jnp.arange(qb)[:, None]
    key_pos = kidx[:, None, :] - n
    valid = (jnp.abs(rel)[None] <= n) & (key_pos >= 0) & (key_pos < L)
    s = jnp.where(valid, s, -jnp.inf)
    m = jnp.max(s, axis=-1, keepdims=True)
    e = jnp.exp(s - m)
    den = jnp.sum(e, axis=-1, keepdims=True)
    o = jnp.einsum('bdhnqk,bdhnkc->bdhnqc', e / den, vb.astype(f32))
    lse = (m + jnp.log(den))[..., 0]
    o = o.reshape(B, dilation, H, L, Dh).transpose(0, 3, 1, 2, 4).reshape(B, S, H, Dh)
    lse = lse.reshape(B, dilation, H, L).transpose(0, 3, 1, 2).reshape(B, S, H)
    return o, lse


def attention_branch(q, k, v, z, cos, sin):
    B, S, _ = q.shape
    hd = lambda t: t.reshape(B, S, C_HEADS, C_HEAD_DIM)
    q = apply_rope(hd(q), cos, sin)
    k = apply_rope(hd(k), cos, sin)
    v = hd(v)
    outs, lses = [], []
    for g, (window, dilation) in enumerate(DILATION_GROUPS):
        sl = slice(g * C_HEADS_PER_GROUP, (g + 1) * C_HEADS_PER_GROUP)
        o, lse = dilated_band_attention(q[:, :, sl], k[:, :, sl], v[:, :, sl], dilation, window // 2)
        outs.append(o)
        lses.append(lse)
    wts = jax.nn.softmax(jnp.stack(lses), axis=0)
    o = jnp.sum(wts[..., None] * jnp.stack(outs), axis=0).reshape(B, S, D_C_OUT)
    return o * jax.nn.silu(z.astype(jnp.float32))


def setup_inputs(seed: int = 0) -> dict:
    key = jax.random.key(seed)
    ks = jax.random.split(key, 24)
    f32 = jnp.float32
    L = DEPTH
    nrm = lambda kk, shape, s: jax.random.normal(kk, shape, f32) * s
    return {
        "x": nrm(ks[0], (BATCH, SEQ, D_MODEL), 1.0),
        "w_in": nrm(ks[1], (L, D_MODEL, N_IN), D_MODEL ** -0.5),
        "b_in": nrm(ks[2], (L, N_IN), 0.02),
        "rwkv_mu": jax.random.uniform(ks[3], (L, 2, N_SHIFT), f32, 0.0, 0.5),
        "rwkv_w0": jax.random.uniform(ks[4], (L, 2, D_A), f32, -6.0, 1.0),
        "rwkv_w_up": nrm(ks[5], (L, 2, LORA_W, D_A), 0.1),
        "rwkv_a0": nrm(ks[6], (L, 2, D_A), 0.5),
        "rwkv_a_up": nrm(ks[7], (L, 2, LORA_A, D_A), 0.1),
        "rwkv_k_k": 0.85 + nrm(ks[8], (L, D_A), 0.05),
        "rwkv_k_a": 1.0 + nrm(ks[9], (L, D_A), 0.05),
        "rwkv_r_k": nrm(ks[10], (L, A_HEADS, A_HEAD_DIM), 0.1),
        "rwkv_gn_g": 1.0 + nrm(ks[11], (L, D_A), 0.05),
        "rwkv_gn_b": nrm(ks[12], (L, D_A), 0.02),
        "pool_w": nrm(ks[13], (L, POOL_GROUPS, POOL_CH, POOL_CH), POOL_CH ** -0.5),
        "pool_b": nrm(ks[14], (L, D_B), 0.02),
        "pool_scale": 1.0 + nrm(ks[15], (L, D_B), 0.05),
        "proj_a": nrm(ks[16], (L, D_A, D_MODEL), DEEPNORM_BETA * D_A ** -0.5),
        "proj_b": nrm(ks[17], (L, D_B, D_MODEL), DEEPNORM_BETA * D_B ** -0.5),
        "proj_c": nrm(ks[18], (L, D_C_OUT, D_MODEL), DEEPNORM_BETA * D_C_OUT ** -0.5),
        "w_out": nrm(ks[19], (L, D_MODEL, D_MODEL), DEEPNORM_BETA * D_MODEL ** -0.5),
        "ln_g": 1.0 + nrm(ks[20], (L, D_MODEL), 0.05),
        "ln_b": nrm(ks[21], (L, D_MODEL), 0.02),
    }


def reference(x, w_in, b_in, rwkv_mu, rwkv_w0, rwkv_w_up, rwkv_a0, rwkv_a_up, rwkv_k_k, rwkv_k_a,
              rwkv_r_k, rwkv_gn_g, rwkv_gn_b, pool_w, pool_b, pool_scale, proj_a, proj_b, proj_c,
              w_out, ln_g, ln_b):
    S = x.shape[1]
    cos, sin = rope_tables(S)
    for l in range(DEPTH):
        h = jnp.einsum('bsd,dn->bsn', x, w_in[l]) + b_in[l]
        (u_a, z_a, p_b, z_b, q_c, k_c, v_c, z_c, g_a, g_b, g_c) = _split(h, SPLIT_SIZES)
        o_a = rwkv7_branch(u_a, z_a, rwkv_mu[l], rwkv_w0[l], rwkv_w_up[l], rwkv_a0[l], rwkv_a_up[l],
                           rwkv_k_k[l], rwkv_k_a[l], rwkv_r_k[l], rwkv_gn_g[l], rwkv_gn_b[l])
        o_b = pool_branch(p_b, z_b, pool_w[l], pool_b[l], pool_scale[l])
        o_c = attention_branch(q_c, k_c, v_c, z_c, cos, sin)
        merged = (jax.nn.sigmoid(g_a) * jnp.einsum('bsc,cd->bsd', o_a, proj_a[l])
                  + jax.nn.sigmoid(g_b) * jnp.einsum('bsc,cd->bsd', o_b, proj_b[l])
                  + jax.nn.sigmoid(g_c) * jnp.einsum('bsc,cd->bsd', o_c, proj_c[l]))
        out = jnp.einsum('bsd,de->bse', merged, w_out[l])
        x = (_normalize(DEEPNORM_ALPHA * x + out, LN_EPS) * ln_g[l] + ln_b[l]).astype(x.dtype)
    return x
```

```python
import math
import os
import numpy as np
import ml_dtypes
import concourse.bass as bass
import concourse.mybir as mybir
from concourse.bass_utils import run_bass_kernel_spmd

F32 = mybir.dt.float32
BF16 = mybir.dt.bfloat16
AF = mybir.ActivationFunctionType
ALU = mybir.AluOpType
AX = mybir.AxisListType

ENGS = ("pe", "act", "dve", "pool", "sp")
DMA_POOL = 16


class _Buf:
    __slots__ = ("last_w", "readers", "dma_readers")

    def __init__(self):
        self.last_w = None
        self.readers = {}
        self.dma_readers = []


class _Op:
    __slots__ = ("eng", "idx", "gid", "fn", "deps", "is_dma", "signal", "sem", "val", "waits",
                 "know", "dma_n", "pre_wait")

    def __init__(self, eng, idx, gid, fn, is_dma):
        self.eng = eng
        self.idx = idx
        self.gid = gid
        self.fn = fn
        self.is_dma = is_dma
        self.deps = []
        self.signal = False
        self.sem = None
        self.val = None
        self.waits = []
        self.know = None
        self.dma_n = None
        self.pre_wait = None


class Prog:
    def __init__(self):
        self.ops = {e: [] for e in ENGS}
        self.all = []
        self.bufs = {}
        self.n_dma = {e: 0 for e in ENGS}

    def _buf(self, name):
        b = self.bufs.get(name)
        if b is None:
            b = _Buf()
            self.bufs[name] = b
        return b

    def add(self, eng, fn, reads=(), writes=(), dma=False):
        op = _Op(eng, len(self.ops[eng]), len(self.all), fn, dma)
        deps = {}
        for r in reads:
            b = self._buf(r)
            if b.last_w is not None:
                deps[b.last_w.gid] = b.last_w
        for w in writes:
            b = self._buf(w)
            if b.last_w is not None:
                deps[b.last_w.gid] = b.last_w
            for d in b.readers.values():
                deps[d.gid] = d
            for d in b.dma_readers:
                deps[d.gid] = d
        op.deps = [deps[k] for k in sorted(deps)]
        for r in reads:
            b = self._buf(r)
            if dma:
                b.dma_readers.append(op)
            else:
                b.readers[eng] = op
        for w in writes:
            b = self._buf(w)
            b.last_w = op
            b.readers = {}
            b.dma_readers = []
        if dma:
            op.dma_n = self.n_dma[eng]
            self.n_dma[eng] += 1
        self.ops[eng].append(op)
        self.all.append(op)
        return op

    def resolve(self):
        know = {e: {f: -1 for f in ENGS} for e in ENGS}
        know_dma = {e: set() for e in ENGS}
        sig_count = {e: 0 for e in ENGS}
        for op in self.all:
            E = op.eng
            kn = know[E]
            for d in op.deps:
                if d.is_dma:
                    if d.gid in know_dma[E]:
                        continue
                    know_dma[E].add(d.gid)
                    op.waits.append(d)
                    for f, v in d.know.items():
                        if v > kn[f]:
                            kn[f] = v
                    continue
                F = d.eng
                if F == E:
                    if E == "pe" or op.idx - d.idx > 2:
                        continue
                    if kn[F] >= d.idx:
                        continue
                elif kn[F] >= d.idx:
                    continue
                d.signal = True
                op.waits.append(d)
                kn[F] = max(kn[F], d.idx)
                for f, v in d.know.items():
                    if f != E and v > kn[f]:
                        kn[f] = v
            snap = dict(kn)
            if not op.is_dma:
                snap[E] = op.idx
            op.know = snap
        for e in ENGS:
            c = 0
            for op in self.ops[e]:
                if op.is_dma:
                    continue
                if op.signal:
                    c += 1
                    op.val = c

    def emit(self, nc):
        self.resolve()
        import contextlib
        with contextlib.ExitStack() as st:
            esem = {e: st.enter_context(nc.semaphore("s_" + e)) for e in ENGS}
            dsem = {e: [st.enter_context(nc.semaphore("d_%s%d" % (e, i))) for i in range(DMA_POOL)]
                    for e in ENGS if self.n_dma[e] > 0}
            block = st.enter_context(nc.Block())

            def wait_for(engine, d):
                if d.is_dma:
                    engine.wait_ge(dsem[d.eng][d.dma_n % DMA_POOL], 16 * (d.dma_n // DMA_POOL + 1))
                else:
                    engine.wait_ge(esem[d.eng], d.val)

            def run(engine, e):
                ops = self.ops[e]
                for op in ops:
                    for d in op.waits:
                        wait_for(engine, d)
                    if op.is_dma:
                        n = op.dma_n
                        if n >= DMA_POOL:
                            engine.wait_ge(dsem[e][n % DMA_POOL], 16 * (n // DMA_POOL))
                        ins = op.fn(engine)
                        ins.then_inc(dsem[e][n % DMA_POOL], 16)
                    else:
                        ins = op.fn(engine)
                        if op.signal:
                            ins.then_inc(esem[e], 1)
                nd = self.n_dma[e]
                for i in range(min(nd, DMA_POOL)):
                    n = nd - 1 - i
                    engine.wait_ge(dsem[e][n % DMA_POOL], 16 * (n // DMA_POOL + 1))

            @block.tensor
            def _(eng):
                run(eng, "pe")

            @block.scalar
            def _(eng):
                run(eng, "act")

            @block.vector
            def _(eng):
                run(eng, "dve")

            @block.gpsimd
            def _(eng):
                run(eng, "pool")

            @block.sync
            def _(eng):
                run(eng, "sp")


S = 2048
D = 1024
NIN = 11520
DEPTH = 4
PADX = 256
XW = S + 2 * PADX
ALPHA = (2 * DEPTH) ** 0.25
CDEC = math.exp(-0.5)
C_BIN = 0
C_MU0 = 90
C_MU1 = 116
C_W0 = 142
C_A0 = 158
C_KK = 174
C_KA = 182
C_RK = 190
C_GG = 198
C_GB = 206
C_PB = 214
C_PS = 220
NCOLS = 226


class Ctx:
    pass


def build(depth=DEPTH, dbg=None):
    nc = bass.Bass("TRN2", target_bir_lowering=False)
    P = Prog()
    dt_in = lambda name, shape: nc.dram_tensor(name, shape, F32, kind="ExternalInput").ap()
    x_in = dt_in("x", [S, D])
    w_in = dt_in("w_in", [DEPTH, D, NIN])
    cols_d = dt_in("cols", [DEPTH, 128, NCOLS])
    vrow_d = dt_in("vrow", [DEPTH, 1, 768])
    wup_d = dt_in("w_up", [DEPTH, 128, 1024])
    aup_d = dt_in("a_up", [DEPTH, 128, 1024])
    poolw_d = dt_in("pool_w", [DEPTH, 768, 768])
    proja_d = dt_in("proj_a", [DEPTH, 1024, 1024])
    projb_d = dt_in("proj_b", [DEPTH, 768, 1024])
    projc_d = dt_in("proj_c", [DEPTH, 256, 1024])
    wout_d = dt_in("w_out", [DEPTH, 1024, 1024])
    lnrow_d = dt_in("lnrow", [DEPTH, 2, 1024])
    cst_d = dt_in("cst", [128, 128 * 3 + 64])
    rope_d = dt_in("rope", [2, 128, S])
    amask_d = dt_in("amask", [4, 128, 512])
    rmask_d = dt_in("rmask", [2, 128, 512 + 512 + 256 + 512])
    pedge_d = dt_in("pedge", [128, 4, 16])
    y_out = nc.dram_tensor("y", [S, D], F32, kind="ExternalOutput").ap()
    xres = nc.dram_tensor("xres", [S, D], F32, kind="Internal").ap()
    dbg_out = None
    if dbg is not None:
        dbg_out = nc.dram_tensor("dbg", [128, 16, S], F32, kind="ExternalOutput").ap()

    import contextlib
    st = contextlib.ExitStack()
    sb = lambda name, shape, dt: st.enter_context(nc.sbuf_tensor(name, shape, dt))
    xT = sb("xT", [128, 8, XW], BF16)
    oT = sb("oT", [128, 16, S], BF16)
    wb = [sb("wb%d" % i, [128, 8, 512], BF16) for i in range(2)]
    ARENA = 16384
    arena = sb("arena", [128, ARENA], F32)
    colsb = sb("colsb", [128, NCOLS], F32)
    c0col = sb("c0col", [128, 26], F32)
    identb = sb("identb", [128, 128], BF16)
    permb = sb("permb", [128, 128], BF16)
    onesb = sb("onesb", [128, 128], BF16)
    eye2 = sb("eye2", [128, 64], BF16)
    bones = sb("bones", [128, 128], F32)
    vrowb = sb("vrowb", [1, 768], BF16)
    selcol = sb("selcol", [128, 24], F32)
    mixf = sb("mixf", [128, S], F32)
    selc_d = dt_in("selc", [128, 24])
    psb = [st.enter_context(nc.psum_tensor("psb%d" % i, [128, 512], F32)) for i in range(8)]

    C = Ctx()
    C.bank_i = 0

    def nextbank():
        i = C.bank_i
        C.bank_i = (i + 1) % 4
        return psb[i], "ps%d" % i

    def af32(off, n):
        return arena[:, off:off + n]

    def abf(off, n):
        return arena[:, off:off + n // 2].bitcast(BF16)

    def o8f32(off, n):
        return oT[:, 8:16, :].rearrange("p a b -> p (a b)").bitcast(F32)[:, off:off + n]

    def o8bf(off, n):
        return oT[:, 8:16, :].rearrange("p a b -> p (a b)")[:, off:off + n]

    add = P.add
    V = "dve"
    A = "act"
    G = "pool"

    def mm(out, lhsT, rhs, start, stop, rd, wr):
        add("pe", lambda e: e.matmul(out, lhsT, rhs, start=start, stop=stop), reads=rd, writes=wr)

    def act(out, in_, func, rd, wr, bias=0.0, scale=1.0):
        add(A, lambda e: e.activation(out=out, in_=in_, func=func, bias=bias, scale=scale), reads=rd, writes=wr)

    def tt(eng, out, in0, in1, op, rd, wr):
        eng = V if eng == G else eng
        add(eng, lambda e: e.tensor_tensor(out=out, in0=in0, in1=in1, op=op), reads=rd, writes=wr)

    def ts(eng, out, in0, s1, s2, op0, op1, rd, wr):
        eng = V if eng == G else eng
        if s2 is None:
            add(eng, lambda e: e.tensor_scalar(out=out, in0=in0, scalar1=s1, scalar2=None, op0=op0), reads=rd, writes=wr)
        else:
            add(eng, lambda e: e.tensor_scalar(out=out, in0=in0, scalar1=s1, scalar2=s2, op0=op0, op1=op1), reads=rd, writes=wr)

    def stt(out, in0, scalar, in1, op0, op1, rd, wr):
        add(V, lambda e: e.scalar_tensor_tensor(out=out, in0=in0, scalar=scalar, in1=in1, op0=op0, op1=op1),
            reads=rd, writes=wr)

    def cp(eng, out, in_, rd, wr):
        eng = V if eng == G else eng
        if eng == A:
            add(eng, lambda e: e.activation(out=out, in_=in_, func=AF.Copy), reads=rd, writes=wr)
        else:
            add(eng, lambda e: e.tensor_copy(out=out, in_=in_), reads=rd, writes=wr)

    def dma(q, out, in_, rd, wr):
        add(q, lambda e: e.dma_start(out=out, in_=in_), reads=rd, writes=wr, dma=True)

    def memset(eng, ap, val, wr):
        add(eng, lambda e: e.memset(ap, val), writes=wr)

    bscr = sb("bscr", [128, 8], F32)

    def barrier():
        names = [n for n in P.bufs.keys() if not n.startswith("ps")] + ["bscr"]
        mm(psb[7][:, 0:8], identb[:, 0:128], identb[:, 0:8], True, True, [], names + ["ps7"])
        act(bscr[:, 0:1], bscr[:, 1:2], AF.Copy, [], names)
        memset(V, bscr[:, 2:3], 0.0, names)
        dma("sp", bscr[0:1, 3:4], cst_d[0:1, 0:1], [], names)
        dma(G, bscr[0:1, 4:5], cst_d[0:1, 0:1], [], names)

    memset(V, bscr[:], 0.0, ["bscr"])
    dma(G, identb[:], cst_d[:, 0:128], [], ["identb"])
    dma("sp", bones[:], cst_d[:, 128:256], [], ["bones"])
    dma(G, permb[:], cst_d[:, 256:384], [], ["permb"])
    dma("sp", selcol[:], selc_d, [], ["selcol"])
    dma(G, eye2[:], cst_d[:, 384:448], [], ["eye2"])
    memset(V, onesb[:], 1.0, ["onesb"])
    memset(V, xT[:, :, 0:PADX], 0.0, ["xT"])
    memset(V, xT[:, :, PADX + S:XW], 0.0, ["xT"])

    C.wres = [None, None]
    C.wlast = 0

    def load_w(key, src3):
        for i in range(2):
            if C.wres[i] == key:
                C.wlast = i
                return wb[i], "wb%d" % i
        i = 1 - C.wlast
        C.wres[i] = key
        C.wlast = i
        kc, ncol = src3.shape[1], src3.shape[2]
        dma(G, wb[i][:, 0:kc, 0:ncol], src3, [], ["wb%d" % i])
        return wb[i], "wb%d" % i

    def win_src(l, col0, ncol):
        return w_in[l].rearrange("(k p) n -> p k n", p=128)[:, :, col0:col0 + ncol]

    def inproj(l, cg, evac, wkey=None):
        blk = cg // 4
        ncol = min(512, NIN - blk * 512)
        w, wn = load_w(("win", l, blk), win_src(l, blk * 512, ncol))
        c0 = (cg % 4) * 128
        for t4 in range(4):
            bank, bn = nextbank()
            for k in range(8):
                mm(bank[:], w[:, k, c0:c0 + 128], xT[:, k, PADX + t4 * 512:PADX + (t4 + 1) * 512],
                   k == 0, k == 7, ["xT", wn], [bn])
            evac(t4, bank, bn)

    def col(ci):
        return colsb[:, ci:ci + 1]

    xstage = sb("xstage", [128, 1024], BF16)

    def store_xT(src_f32, srcname, t16):
        cp(A, xstage[:], src_f32, [srcname], ["xstage"])
        bank, bn = nextbank()
        bb = bank[:].bitcast(BF16)
        for k in range(8):
            add("pe", lambda e, k=k: e.transpose(bb[:, k * 128:(k + 1) * 128], xstage[:, k * 128:(k + 1) * 128], identb[:]),
                reads=["xstage", "identb"], writes=[bn])
        cp(V, xT[:, :, PADX + t16 * 128:PADX + (t16 + 1) * 128], bb.rearrange("p (k t) -> p k t", k=8), [], [bn, "xT"])

    def final_phase(l, last):
        barrier()
        mergedT = abf(0, 8 * S).rearrange("p (k t) -> p k t", k=8)
        sig = abf(8192, 512)
        tmpf = af32(8448, 512)
        lng = af32(9216, 1024)
        lnb = af32(10240, 1024)
        xt_ = [af32(11264, 1024), af32(12288, 1024)]
        yt_ = [af32(13312, 1024), af32(14336, 1024)]
        stat = af32(15360, 8)
        dma("sp", lng, lnrow_d[l, 0:1, :].to_broadcast([128, 1024]), [], ["lng"])
        dma("sp", lnb, lnrow_d[l, 1:2, :].to_broadcast([128, 1024]), [], ["lnb"])
        if STOP == 4:
            return
        branches = [(proja_d, 8, 0, 66), (projb_d, 6, 8, 74), (projc_d, 2, 14, 82)]
        for bi, (pd, kc, o0, g0) in enumerate(branches):
            for eb in range(2):
                for ec in range(eb * 4, eb * 4 + 4):
                    for t4 in range(4):
                        tsl = slice(t4 * 512, (t4 + 1) * 512)
                        pw, pwn = load_w(("proj", l, bi, eb), pd[l].rearrange("(k p) n -> p k n", p=128)[:, :, eb * 512:(eb + 1) * 512])
                        b1, b1n = nextbank()
                        for k in range(kc):
                            mm(b1[:], pw[:, k, (ec % 4) * 128:(ec % 4 + 1) * 128], oT[:, o0 + k, tsl], k == 0, k == kc - 1,
                               ["oT%d" % (o0 + k), pwn], [b1n])
                        gw, gwn = load_w(("gate", l, bi, eb), win_src(l, (g0 + eb * 4) * 128, 512))
                        b2, b2n = nextbank()
                        for k in range(8):
                            mm(b2[:], gw[:, k, (ec % 4) * 128:(ec % 4 + 1) * 128], xT[:, k, PADX + t4 * 512:PADX + (t4 + 1) * 512],
                               k == 0, k == 7, ["xT", gwn], [b2n])
                        act(sig, b2[:], AF.Sigmoid, ["colsb"], [b2n, "sig"], bias=col(C_BIN + g0 + ec))
                        if bi == 0:
                            tt(V, mergedT[:, ec, tsl], b1[:], sig, ALU.mult, ["sig"], [b1n, "mg%d" % ec])
                        else:
                            tt(V, tmpf, b1[:], sig, ALU.mult, ["sig"], [b1n, "tmpf"])
                            tt(G, mergedT[:, ec, tsl], mergedT[:, ec, tsl], tmpf, ALU.add, ["tmpf"], ["mg%d" % ec])
        if STOP == 3:
            return
        wo = []
        for fh in range(2):
            wo.append(load_w(("wout", l, fh), wout_d[l].rearrange("(k p) n -> p k n", p=128)[:, :, fh * 512:(fh + 1) * 512]))
        xsrc = x_in if l == 0 else xres
        dst = y_out if last else xres
        for t16 in range(16):
            xt = xt_[t16 % 2]
            yt = yt_[t16 % 2]
            xn, yn = "xt%d" % (t16 % 2), "yt%d" % (t16 % 2)
            dma("sp", xt, xsrc[t16 * 128:(t16 + 1) * 128, :], ["xres"] if l > 0 else [], [xn])
            for fh in range(2):
                w, wn = wo[fh]
                bank, bn = nextbank()
                for k in range(8):
                    mm(bank[:], mergedT[:, k, t16 * 128:(t16 + 1) * 128], w[:, k, :], k == 0, k == 7,
                       ["mg%d" % k, wn], [bn])
                stt(yt[:, fh * 512:(fh + 1) * 512], xt[:, fh * 512:(fh + 1) * 512], ALPHA, bank[:], ALU.mult, ALU.add,
                    [xn], [bn, yn])
            if STOP == 5:
                dma("sp", dst[t16 * 128:(t16 + 1) * 128, :], yt, [yn], ["xres"])
                continue
            add(V, lambda e, yt=yt: e.tensor_reduce(out=stat[:, 0:1], in_=yt, axis=AX.X, op=ALU.add), reads=[yn], writes=["stat0"])
            ts(V, stat[:, 1:2], stat[:, 0:1], -1.0 / D, None, ALU.mult, None, ["stat0"], ["stat1"])
            ts(V, yt, yt, stat[:, 1:2], None, ALU.add, None, ["stat1"], [yn])
            add(A, lambda e, yt=yt, xt=xt: e.activation(out=xt, in_=yt, func=AF.Square, accum_out=stat[:, 2:3]),
                reads=[yn], writes=[xn, "stat2"])
            act(stat[:, 3:4], stat[:, 2:3], AF.Sqrt, ["stat2"], ["stat3"], bias=1e-5, scale=1.0 / D)
            add(V, lambda e: e.reciprocal(out=stat[:, 4:5], in_=stat[:, 3:4]), reads=["stat3"], writes=["stat4"])
            if STOP == 6:
                dma("sp", dst[t16 * 128:(t16 + 1) * 128, :], yt, [yn], ["xres"])
                continue
            if STOP != 8:
                stt(yt, yt, stat[:, 4:5], lng, ALU.mult, ALU.mult, ["stat4", "lng"], [yn])
            if STOP != 7:
                tt(V, yt, yt, lnb, ALU.add, ["lnb"], [yn])
            dma("sp", dst[t16 * 128:(t16 + 1) * 128, :], yt, [yn], ["xres"])
            if not last:
                store_xT(yt, yn, t16)

    def pool_phase(l):
        barrier()
        W = S + 32
        pbuf = af32(0, W)
        a_ = [af32(2080, W), af32(4160, W)]
        mixed = abf(6240, 6 * S).rearrange("p (c t) -> p c t", c=6)
        wgt = abf(12384, 6 * 768).rearrange("p (a b) -> p a b", a=6)
        sacc = af32(14688, 0) if False else None
        pe_t = af32(14688, 64).rearrange("p (g e) -> p g e", g=4)
        t1 = af32(14752, 512)
        memset(V, pbuf[:, 0:16], 0.0, ["pbuf"])
        memset(V, pbuf[:, W - 16:W], 0.0, ["pbuf"])
        dma("sp", pe_t, pedge_d, [], ["pe_t"])
        dma(G, wgt, poolw_d[l].rearrange("(k p) n -> p k n", p=128), [], ["wgt"])
        for c in range(6):
            inproj(l, 40 + c, lambda t4, bank, bn, c=c: act(oT[:, 8 + c, t4 * 512:(t4 + 1) * 512], bank[:], AF.Silu,
                                                             ["colsb"], [bn, "oT%d" % (8 + c)], bias=col(C_BIN + 40 + c)))
        for c in range(6):
            inproj(l, 34 + c, lambda t4, bank, bn, c=c: act(pbuf[:, 16 + t4 * 512:16 + (t4 + 1) * 512], bank[:], AF.Identity,
                                                             ["colsb"], [bn, "pbuf"], bias=col(C_BIN + 34 + c)))
            gs = sorted(set((2 * c + hf) // 3 for hf in range(2)))
            first = True
            for g in gs:
                h = 1 << g
                kk = g + 1
                src, srcn = pbuf, "pbuf"
                for j in range(kk):
                    sh = 1 << j
                    dstt = a_[j % 2]
                    n = W - (2 << j) + 1
                    tt(V, dstt[:, 0:n], src[:, 0:n], src[:, sh:sh + n], ALU.add, [srcn], ["a%d" % (j % 2)])
                    src, srcn = dstt, "a%d" % (j % 2)
                sfin = a_[kk % 2]
                sn = "a%d" % (kk % 2)
                tt(V, sfin[:, 0:S], src[:, 16 - h:16 - h + S], pbuf[:, 16 + h:16 + h + S], ALU.add, [srcn, "pbuf"], [sn])
                tt(V, sfin[:, 0:8], sfin[:, 0:8], pe_t[:, g, 0:8], ALU.mult, ["pe_t"], [sn])
                tt(V, sfin[:, S - 8:S], sfin[:, S - 8:S], pe_t[:, g, 8:16], ALU.mult, ["pe_t"], [sn])
                selw = selcol[:, c * 4 + g:c * 4 + g + 1]
                if first:
                    stt(mixf[:, :], sfin[:, 0:S], selw, pbuf[:, 16:16 + S], ALU.mult, ALU.subtract, [sn, "pbuf", "selcol"], ["mixf"])
                else:
                    stt(mixf[:, :], sfin[:, 0:S], selw, mixf[:, :], ALU.mult, ALU.add, [sn, "selcol"], ["mixf"])
                first = False
            cp(A, mixed[:, c, :], mixf[:, :], ["mixf"], ["mixed"])
        for oc in range(6):
            ics = [ic for ic in range(6) if any((2 * ic + a) // 3 == (2 * oc + b) // 3 for a in range(2) for b in range(2))]
            for t4 in range(4):
                tsl = slice(t4 * 512, (t4 + 1) * 512)
                bank, bn = nextbank()
                for n_, ic in enumerate(ics):
                    mm(bank[:], wgt[:, ic, oc * 128:(oc + 1) * 128], mixed[:, ic, tsl], n_ == 0, n_ == len(ics) - 1,
                       ["wgt", "mixed"], [bn])
                ts(V, t1, bank[:], col(C_PB + oc), col(C_PS + oc), ALU.add, ALU.mult, ["colsb"], [bn, "t1"])
                tt(V, oT[:, 8 + oc, tsl], t1, oT[:, 8 + oc, tsl], ALU.mult, ["t1"], ["oT%d" % (8 + oc)])

    C.attn = None
    C.rwkv = None
    def attn_phase(l):
        barrier()
        Qr = abf(0, XW)
        Kr = abf(1280, XW)
        qraw = abf(2560, S)
        ropec = af32(3584, S)
        ropes = af32(5632, S)
        t1 = af32(7680, 512)
        t2 = af32(8192, 512)
        accn = af32(8704, S)
        accd = af32(10752, S)
        Vt = abf(12800, 20 * 128).rearrange("p (a b) -> p a b", a=20)
        pT = [abf(14080, 512), abf(14336, 512)]
        msk = abf(14592, 4 * 512).rearrange("p (a b) -> p a b", a=4)
        dma("sp", ropec, rope_d[0], [], ["ropec"])
        dma("sp", ropes, rope_d[1], [], ["ropes"])
        for a_ in range(4):
            dma(G, msk[:, a_, :], amask_d[a_], [], ["msk"])
        for buf, nm in ((Qr, "Qr"), (Kr, "Kr")):
            memset(V, buf[:, 0:PADX], 0.0, [nm])
            memset(V, buf[:, PADX + S:XW], 0.0, [nm])
        cnt = [0]
        SUB = int(os.environ.get("ATT_SUB", "9"))
        if SUB == 1:
            return
        for pp in range(2):
            for g in range(3):
                d = (1, 4, 16)[g]
                for cg, dst, nm in ((46 + 2 * g + pp, Qr, "Qr"), (52 + 2 * g + pp, Kr, "Kr")):
                    inproj(l, cg, lambda t4, bank, bn, cg=cg: act(qraw[:, t4 * 512:(t4 + 1) * 512], bank[:], AF.Identity,
                                                                  ["colsb"], [bn, "qraw"], bias=col(C_BIN + cg)))
                    for t4 in range(4):
                        tsl = slice(t4 * 512, (t4 + 1) * 512)
                        bank, bn = nextbank()
                        mm(bank[:], permb[:], qraw[:, tsl], True, True, ["permb", "qraw"], [bn])
                        tt(V, t1, bank[:], ropes[:, tsl], ALU.mult, ["ropes"], [bn, "t1"])
                        tt(V, t2, qraw[:, tsl], ropec[:, tsl], ALU.mult, ["qraw", "ropec"], ["t2"])
                        tt(V, dst[:, PADX + t4 * 512:PADX + (t4 + 1) * 512], t1, t2, ALU.add, ["t1", "t2"], [nm])
                if SUB == 2:
                    return
                wv, wvn = load_w(("wv", l, g, pp), win_src(l, (58 + 2 * g + pp) * 128, 128))
                if d == 1:
                    tsls = [slice(PADX + 128 * m - 64, PADX + 128 * m + 64) for m in range(17)]
                elif d == 4:
                    tsls = []
                    for r in range(4):
                        for m in range(5):
                            s0 = PADX + r + 4 * (128 * m - 64)
                            tsls.append(slice(s0, s0 + 509, 4))
                else:
                    tsls = [slice(PADX + r, PADX + r + 2033, 16) for r in range(16)]
                for j0 in range(0, len(tsls), 4):
                    grp = tsls[j0:j0 + 4]
                    bank, bn = nextbank()
                    for j, sl in enumerate(grp):
                        for k in range(8):
                            mm(bank[:, j * 128:(j + 1) * 128], xT[:, k, sl], wv[:, k, 0:128], k == 0, False, ["xT", wvn], [bn])
                        mm(bank[:, j * 128:(j + 1) * 128], onesb[0:1, 0:128], vrowb[0:1, (2 * g + pp) * 128:(2 * g + pp + 1) * 128],
                           False, True, ["onesb", "vrowb"], [bn])
                    n = len(grp)
                    cp(A, Vt[:, j0:j0 + n, :], bank[:, 0:n * 128].rearrange("p (a b) -> p a b", a=n), [], [bn, "Vt"])
                if SUB == 3:
                    return
                for sbk in range(4):
                    for j in range(4):
                        if d == 1:
                            m = 4 * sbk + j
                            qsl = slice(PADX + 128 * m, PADX + 128 * m + 128)
                            chunks = [(slice(PADX + 128 * m - 64, PADX + 128 * m + 64), m),
                                      (slice(PADX + 128 * m + 64, PADX + 128 * m + 192), m + 1)]
                            mi = 1 if m == 0 else (2 if m == 15 else 0)
                        elif d == 4:
                            r, m = sbk, j
                            q0 = PADX + r + 512 * m
                            qsl = slice(q0, q0 + 509, 4)
                            k0 = PADX + r + 4 * (128 * m - 64)
                            chunks = [(slice(k0, k0 + 509, 4), r * 5 + m), (slice(k0 + 512, k0 + 512 + 509, 4), r * 5 + m + 1)]
                            mi = 1 if m == 0 else (2 if m == 3 else 0)
                        else:
                            r = 4 * sbk + j
                            qsl = slice(PADX + r, PADX + r + 2033, 16)
                            chunks = [(qsl, r)]
                            mi = 3
                        nch = len(chunks)
                        wd = nch * 128
                        for h in range(2):
                            hp_ = slice(64 * h, 64 * h + 64)
                            si = h + 2 * (cnt[0] % 2)
                            sbank, sbn = psb[si], "ps%d" % si
                            pTb, pTn = pT[h], "pT%d" % h
                            for ci, (ks, vt) in enumerate(chunks):
                                mm(sbank[:, ci * 128:(ci + 1) * 128], Kr[hp_, ks], Qr[hp_, qsl], True, True, ["Kr", "Qr"], [sbn])
                            act(pTb[:, 0:wd], sbank[:, 0:wd], AF.Exp, [], [sbn, pTn], scale=0.125)
                            tt(V, pTb[:, 0:wd], pTb[:, 0:wd], msk[:, mi, 0:wd], ALU.mult, ["msk"], [pTn])
                            for ci, (ks, vt) in enumerate(chunks):
                                mm(psb[6][hp_, j * 128:(j + 1) * 128], Vt[:, vt, 64 * h:64 * h + 64], pTb[:, ci * 128:(ci + 1) * 128],
                                   ci == 0, ci == nch - 1, ["Vt", pTn], ["ps6"])
                            for ci, (ks, vt) in enumerate(chunks):
                                mm(psb[7][hp_, j * 128:(j + 1) * 128], onesb[:, 0:64], pTb[:, ci * 128:(ci + 1) * 128],
                                   ci == 0, ci == nch - 1, ["onesb", pTn], ["ps7"])
                        cnt[0] += 1
                    if d == 1:
                        vn, vd = accn[:, 512 * sbk:512 * sbk + 512], accd[:, 512 * sbk:512 * sbk + 512]
                        bn_, bd_ = psb[6][:, :], psb[7][:, :]
                    elif d == 4:
                        vn, vd = accn[:, sbk:S:4], accd[:, sbk:S:4]
                        bn_, bd_ = psb[6][:, :], psb[7][:, :]
                    else:
                        vn = accn.rearrange("p (i r) -> p r i", r=16)[:, 4 * sbk:4 * sbk + 4, :]
                        vd = accd.rearrange("p (i r) -> p r i", r=16)[:, 4 * sbk:4 * sbk + 4, :]
                        bn_ = psb[6][:, :].rearrange("p (r i) -> p r i", r=4)
                        bd_ = psb[7][:, :].rearrange("p (r i) -> p r i", r=4)
                    if g == 0:
                        cp(V, vn, bn_, [], ["ps6", "accn"])
                        cp(A, vd, bd_, [], ["ps7", "accd"])
                    else:
                        tt(V, vn, vn, bn_, ALU.add, [], ["ps6", "accn"])
                        tt(V, vd, vd, bd_, ALU.add, [], ["ps7", "accd"])
                if os.environ.get("ATT_STOP") == str(g + 1):
                    return
            oc = 14 + pp
            inproj(l, 64 + pp, lambda t4, bank, bn, oc=oc, pp=pp: act(oT[:, oc, t4 * 512:(t4 + 1) * 512], bank[:], AF.Silu,
                                                                        ["colsb"], [bn, "oT%d" % oc], bias=col(C_BIN + 64 + pp)))
            add(V, lambda e: e.reciprocal(out=accd, in_=accd), reads=[], writes=["accd"])
            tt(V, accn, accn, accd, ALU.mult, ["accd"], ["accn"])
            tt(V, oT[:, oc, :], accn, oT[:, oc, :], ALU.mult, ["accn"], ["oT%d" % oc])

    C.attn = attn_phase

    def rwkv_phase(l):
        barrier()
        lw = abf(0, S)
        la = abf(1024, S)
        wup = abf(2048, 1024)
        aup = abf(2560, 1024)
        hbuf = af32(3072, 2050)
        tmpB = af32(3072, S)
        tmpf = af32(5124, S)
        rbf = abf(7172, S)
        kbf = abf(8196, S)
        vbf = abf(9220, S)
        kkbf = abf(10244, S)
        bonus = abf(11268, S)
        ytok = af32(12292, S).rearrange("p (c i) -> p c i", c=32)
        Vtok = abf(14340, S).rearrange("p (c i) -> p c i", c=32)
        reset = af32(15364, 512)
        STf = af32(15876, 64)
        STb = abf(15940, 64)
        Xs = abf(15972, 64)
        Us = abf(16004, 64)
        Wc = af32(16036, 8)
        totc = af32(16044, 8)
        stat = af32(16052, 128)
        ynb = rbf.rearrange("p (c i) -> p c i", c=32)
        tmpf3 = tmpf.rearrange("p (c i) -> p c i", c=32)
        sg = o8f32(0, 512)
        aa = o8f32(512, 512)
        Gc = o8f32(1024, 512)
        tmpG = o8f32(1536, 512)
        E = o8f32(2048, 512)
        bb = o8f32(2560, 512)
        kd = o8f32(3072, 512)
        AR2 = o8bf(2 * 3584, 1024).rearrange("p (c n) -> p c n", c=8)
        AR4 = o8bf(2 * 3584, 1024).rearrange("p (c a j) -> p c a j", c=8, a=2)
        BT = o8bf(2 * 4096, 512)
        KT = o8bf(2 * 4352, 512)
        BHT = o8bf(2 * 4608, 512)
        KHT = o8bf(2 * 4864, 512)
        BHtok = o8bf(2 * 5120, 512).rearrange("p (c j) -> p c j", c=8)
        KHtok = o8bf(2 * 5376, 512).rearrange("p (c j) -> p c j", c=8)
        G1s = o8bf(2 * 5632, 1024).rearrange("p (c n) -> p c n", c=8)
        G2s = o8bf(2 * 6144, 1024).rearrange("p (c n) -> p c n", c=8)
        QP = o8bf(2 * 6656, 1024).rearrange("p (c n) -> p c n", c=8)
        Pn = o8bf(2 * 7168, 512).rearrange("p (c n) -> p c n", c=8)
        m1 = [o8bf(2 * 7424, 512), o8bf(2 * 7680, 512)]
        m2 = [o8bf(2 * 7936, 256), o8bf(2 * 8064, 256)]
        t1f = E

        def c8v(ap):
            return ap.rearrange("p (c j) -> p c j", c=8)

        dma(G, wup, wup_d[l], [], ["wup"])
        dma(G, aup, aup_d[l], [], ["aup"])
        dma("sp", reset, rmask_d[0][:, 1280:1792], [], ["reset"])
        for z in range(2):
            dma(G, m1[z], rmask_d[z][:, 0:512], [], ["m1_%d" % z])
            dma(G, m2[z], rmask_d[z][:, 1024:1280], [], ["m2_%d" % z])
        memset(V, hbuf[:, 0:1], 0.0, ["hbuf"])
        memset(V, hbuf[:, 2049:2050], 0.0, ["hbuf"])

        def shifted(cg, dst, dstname):
            inproj(l, cg, lambda t4, bank, bn: act(hbuf[:, 1 + t4 * 512:1 + (t4 + 1) * 512], bank[:], AF.Identity,
                                                   ["colsb"], [bn, "hbuf"], bias=col(C_BIN + cg)))
            act(tmpf, hbuf[:, 1:2049], AF.Identity, ["hbuf", "c0col"], ["tmpf"], scale=c0col[:, cg:cg + 1])
            stt(tmpf, hbuf[:, 0:2048], col(C_MU0 + cg), tmpf, ALU.mult, ALU.add, ["hbuf", "colsb"], ["tmpf"])
            stt(dst, hbuf[:, 2:2050], col(C_MU1 + cg), tmpf, ALU.mult, ALU.add, ["hbuf", "colsb", "tmpf"], [dstname])

        def pairbank():
            return [nextbank(), nextbank()]

        shifted(24, tmpf, "tmpf")
        act(lw, tmpf, AF.Tanh, ["tmpf"], ["lw"])
        shifted(25, la, "la")

        for hp in range(8):
            shifted(hp, rbf, "rbf")
            shifted(8 + hp, kbf, "kbf")
            shifted(16 + hp, vbf, "vbf")
            inproj(l, 26 + hp, lambda t4, bank, bn, hp=hp: act(oT[:, hp, t4 * 512:(t4 + 1) * 512], bank[:], AF.Silu,
                                                               ["colsb"], [bn, "oT%d" % hp], bias=col(C_BIN + 26 + hp)))
            ts(V, tmpf, kbf, col(C_KK + hp), None, ALU.mult, None, ["kbf", "colsb"], ["tmpf"])
            act(tmpB, tmpf, AF.Square, ["tmpf"], ["hbuf"])
            for t4 in range(4):
                tsl = slice(t4 * 512, (t4 + 1) * 512)
                bank, bn = nextbank()
                mm(bank[:], bones[:], tmpB[:, tsl], True, True, ["bones", "hbuf"], [bn])
                act(tmpB[:, tsl], bank[:], AF.Sqrt, [], [bn, "hbuf"], bias=1e-12)
            add(V, lambda e: e.reciprocal(out=tmpB, in_=tmpB), reads=[], writes=["hbuf"])
            tt(V, kkbf, tmpf, tmpB, ALU.mult, ["tmpf", "hbuf"], ["kkbf"])
            stt(tmpf, rbf, col(C_RK + hp), kbf, ALU.mult, ALU.mult, ["rbf", "kbf", "colsb"], ["tmpf"])
            for t4 in range(4):
                tsl = slice(t4 * 512, (t4 + 1) * 512)
                bank, bn = nextbank()
                mm(bank[:], bones[:], tmpf[:, tsl], True, True, ["bones", "tmpf"], [bn])
                tt(V, bonus[:, tsl], bank[:], vbf[:, tsl], ALU.mult, ["vbf"], [bn, "bonus"])
            for T in range(4):
                pb = pairbank()
                for h in range(2):
                    hs = slice(64 * h, 64 * h + 64)
                    bank, bn = pb[h]
                    for c8 in range(8):
                        c = T * 8 + c8
                        mm(bank[hs, c8 * 64:(c8 + 1) * 64], vbf[hs, c * 64:(c + 1) * 64], identb[hs, 64 * h:64 * h + 64],
                           True, True, ["vbf", "identb"], [bn])
                    cp(A, Vtok[hs, T * 8:(T + 1) * 8, :], c8v(bank[hs, :]), [], [bn, "Vtok"])

            for z in range(2):
                zs = slice(64 * z, 64 * z + 64)
                memset(V, STf, 0.0, ["STf0", "STf1"])
                memset(V, STb, 0.0, ["STb0", "STb1"])
                Tord = range(4) if z == 0 else range(3, -1, -1)
                for T in Tord:
                    tsl = slice(T * 512, (T + 1) * 512)
                    b1, b1n = nextbank()
                    mm(b1[:], wup[zs, hp * 128:(hp + 1) * 128], lw[zs, tsl], True, True, ["wup", "lw"], [b1n])
                    act(sg, b1[:], AF.Sigmoid, ["colsb"], [b1n, "sg"], bias=col(C_W0 + z * 8 + hp))
                    b2, b2n = nextbank()
                    mm(b2[:], aup[zs, hp * 128:(hp + 1) * 128], la[zs, tsl], True, True, ["aup", "la"], [b2n])
                    act(aa, b2[:], AF.Sigmoid, ["colsb"], [b2n, "aa"], bias=col(C_A0 + z * 8 + hp))
                    add(V, lambda e: e.tensor_tensor_scan(out=Gc, data0=reset, data1=sg, initial=0.0, op0=ALU.mult, op1=ALU.add),
                        reads=["reset", "sg"], writes=["Gc"])
                    cp(V, totc, c8v(Gc)[:, :, 63], ["Gc"], ["totc"])
                    totb = totc.unsqueeze(2).to_broadcast([128, 8, 64])
                    if z == 1:
                        tt(V, tmpG, sg, Gc, ALU.subtract, ["sg", "Gc"], ["tmpG"])
                        tt(V, c8v(Gc), c8v(tmpG), totb, ALU.add, ["tmpG", "totc"], ["Gc"])
                    act(E, Gc, AF.Exp, ["Gc"], ["E"], scale=-CDEC)
                    tt(V, AR4[:, :, 1, :], c8v(rbf[:, tsl]), c8v(E), ALU.mult, ["rbf", "E"], ["AR"])
                    tt(V, tmpG, Gc, sg, ALU.subtract, ["Gc", "sg"], ["tmpG"])
                    act(E, tmpG, AF.Exp, ["tmpG"], ["E"], scale=-CDEC)
                    stt(AR4[:, :, 0, :], c8v(kkbf[:, tsl]), -1.0, c8v(E), ALU.mult, ALU.mult, ["kkbf", "E"], ["AR"])
                    ts(V, tmpG, aa, -1.0, col(C_KA + hp), ALU.add, ALU.mult, ["aa", "colsb"], ["tmpG"])
                    stt(kd, tmpG, 1.0, kbf[:, tsl], ALU.add, ALU.mult, ["tmpG", "kbf"], ["kd"])
                    tt(V, bb, kkbf[:, tsl], aa, ALU.mult, ["kkbf", "aa"], ["bb"])
                    act(E, Gc, AF.Exp, ["Gc"], ["E"], scale=CDEC)
                    tt(V, BT, bb, E, ALU.mult, ["bb", "E"], ["BT"])
                    tt(V, KT, kd, E, ALU.mult, ["kd", "E"], ["KT"])
                    tt(V, c8v(tmpG), c8v(Gc), totb, ALU.subtract, ["Gc", "totc"], ["tmpG"])
                    act(E, tmpG, AF.Exp, ["tmpG"], ["E"], scale=CDEC)
                    tt(V, BHT, bb, E, ALU.mult, ["bb", "E"], ["BHT"])
                    tt(V, KHT, kd, E, ALU.mult, ["kd", "E"], ["KHT"])
                    act(Wc, totc, AF.Exp, ["totc"], ["Wc"], scale=-CDEC)
                    for src, srcn, dst, dstn in ((BHT, "BHT", BHtok, "BHtok"), (KHT, "KHT", KHtok, "KHtok")):
                        pb = pairbank()
                        for h in range(2):
                            hs = slice(64 * h, 64 * h + 64)
                            bank, bn = pb[h]
                            for c8 in range(8):
                                mm(bank[hs, c8 * 64:(c8 + 1) * 64], src[hs, c8 * 64:(c8 + 1) * 64], identb[hs, 64 * h:64 * h + 64],
                                   True, True, [srcn, "identb"], [bn])
                            cp(A, dst[hs, :, :], c8v(bank[hs, :]), [], [bn, dstn + str(h)])
                    for hv in range(2):
                        cs = slice(hv * 4, hv * 4 + 4)
                        for h in range(2):
                            hs = slice(64 * h, 64 * h + 64)
                            (g1b, g1n), (g2b, g2n) = pairbank()
                            g3b, g3n = nextbank()
                            for cl in range(4):
                                c8 = hv * 4 + cl
                                mm(g1b[hs, cl * 128:(cl + 1) * 128], BT[hs, c8 * 64:(c8 + 1) * 64], AR2[hs, c8, :], True, True,
                                   ["BT", "AR"], [g1n])
                            for cl in range(4):
                                c8 = hv * 4 + cl
                                mm(g2b[hs, cl * 128:(cl + 1) * 128], KT[hs, c8 * 64:(c8 + 1) * 64], AR2[hs, c8, :], True, True,
                                   ["KT", "AR"], [g2n])
                            for cl in range(4):
                                c8 = hv * 4 + cl
                                mm(g3b[hs, cl * 64:(cl + 1) * 64], AR4[hs, c8, 0, :], BT[hs, c8 * 64:(c8 + 1) * 64], True, True,
                                   ["BT", "AR"], [g3n])
                            hn = str(h)
                            tt(V, G1s[hs, cs, :], g1b[hs, :].rearrange("p (c n) -> p c n", c=4),
                               m1[z][hs, :].rearrange("p (c n) -> p c n", c=4), ALU.mult, ["m1_%d" % z], [g1n, "G1s" + hn])
                            tt(V, G2s[hs, cs, :], g2b[hs, :].rearrange("p (c n) -> p c n", c=4),
                               m1[z][hs, :].rearrange("p (c n) -> p c n", c=4), ALU.mult, ["m1_%d" % z], [g2n, "G2s" + hn])
                            tt(V, Pn[hs, cs, :], g3b[hs, 0:256].rearrange("p (c n) -> p c n", c=4),
                               m2[z][hs, :].rearrange("p (c n) -> p c n", c=4), ALU.mult, ["m2_%d" % z], [g3n, "Pn" + hn])
                            cp(A, QP[hs, cs, 0:64], eye2[hs, :].unsqueeze(1).to_broadcast([64, 4, 64]), ["eye2"], ["QP" + hn])
                            cp(A, QP[hs, cs, 64:128], G1s[hs, cs, 0:64], ["G1s" + hn], ["QP" + hn])
                            for k in range(6):
                                last = (k == 5)
                                wA = 64 if last else 128
                                bA, bAn = nextbank()
                                for cl in range(4):
                                    c8 = hv * 4 + cl
                                    mm(bA[hs, cl * 128:cl * 128 + wA], Pn[hs, c8, :], QP[hs, c8, 0:wA], True, True,
                                       ["Pn" + hn, "QP" + hn], [bAn])
                                if not last:
                                    bB, bBn = nextbank()
                                    for cl in range(4):
                                        c8 = hv * 4 + cl
                                        mm(bB[hs, cl * 64:(cl + 1) * 64], QP[hs, c8, 64:128], Pn[hs, c8, :], True, True,
                                           ["Pn" + hn, "QP" + hn], [bBn])
                                bA3 = bA[hs, :].rearrange("p (c n) -> p c n", c=4)
                                tt(V, QP[hs, cs, 0:64], QP[hs, cs, 0:64], bA3[:, :, 0:64], ALU.add, [], [bAn, "QP" + hn])
                                if not last:
                                    cp(A, QP[hs, cs, 64:128], bA3[:, :, 64:128], [], [bAn, "QP" + hn])
                                    cp(A, Pn[hs, cs, :], bB[hs, 0:256].rearrange("p (c n) -> p c n", c=4), [], [bBn, "Pn" + hn])
                    cord = range(8) if z == 0 else range(7, -1, -1)
                    for c8 in cord:
                        c = T * 8 + c8
                        for h in range(2):
                            hs = slice(64 * h, 64 * h + 64)
                            hn = str(h)
                            cb, cbn = psb[4 + h], "ps%d" % (4 + h)
                            yb, ybn = psb[6 + h], "ps%d" % (6 + h)
                            mm(cb[hs, 0:64], AR4[hs, c8, 0, :], STb[hs, :], True, False, ["AR", "STb" + hn], [cbn])
                            mm(cb[hs, 0:64], G2s[hs, c8, 0:64], Vtok[hs, c, :], False, True, ["G2s" + hn, "Vtok"], [cbn])
                            cp(A, Xs[hs, :], cb[hs, 0:64], [], [cbn, "Xs" + hn])
                            mm(cb[hs, 64:128], QP[hs, c8, 0:64], Xs[hs, :], True, True, ["QP" + hn, "Xs" + hn], [cbn])
                            cp(V, Us[hs, :], cb[hs, 64:128], [], [cbn, "Us" + hn])
                            mm(yb[hs, c8 * 64:(c8 + 1) * 64], AR4[hs, c8, 1, :], STb[hs, :], True, False, ["AR", "STb" + hn], [ybn])
                            mm(yb[hs, c8 * 64:(c8 + 1) * 64], G1s[hs, c8, 64:128], Us[hs, :], False, False,
                               ["G1s" + hn, "Us" + hn], [ybn])
                            mm(yb[hs, c8 * 64:(c8 + 1) * 64], G2s[hs, c8, 64:128], Vtok[hs, c, :], False, True,
                               ["G2s" + hn, "Vtok"], [ybn])
                            mm(cb[hs, 128:192], BHtok[hs, c8, :], Us[hs, :], True, False, ["BHtok" + hn, "Us" + hn], [cbn])
                            mm(cb[hs, 128:192], KHtok[hs, c8, :], Vtok[hs, c, :], False, True, ["KHtok" + hn, "Vtok"], [cbn])
                            stt(STf[hs, :], STf[hs, :], Wc[hs, c8:c8 + 1], cb[hs, 128:192], ALU.mult, ALU.add,
                                ["Wc"], [cbn, "STf" + hn])
                            cp(A, STb[hs, :], STf[hs, :], ["STf" + hn], ["STb" + hn])
                    for h in range(2):
                        hs = slice(64 * h, 64 * h + 64)
                        yb, ybn = psb[6 + h], "ps%d" % (6 + h)
                        if z == 0:
                            cp(V, ytok[hs, T * 8:(T + 1) * 8, :], c8v(yb[hs, :]), [], [ybn, "ytok"])
                        else:
                            tt(V, ytok[hs, T * 8:(T + 1) * 8, :], ytok[hs, T * 8:(T + 1) * 8, :], c8v(yb[hs, :]), ALU.add,
                               [], [ybn, "ytok"])
            add(V, lambda e: e.tensor_reduce(out=stat[:, 0:32], in_=ytok, axis=AX.X, op=ALU.add), reads=["ytok"], writes=["st0"])
            ts(V, stat[:, 32:64], stat[:, 0:32], -1.0 / 64, None, ALU.mult, None, ["st0"], ["st1"])
            tt(V, ytok, ytok, stat[:, 32:64].unsqueeze(2).to_broadcast([128, 32, 64]), ALU.add, ["st1"], ["ytok"])
            act(tmpf3, ytok, AF.Square, ["ytok"], ["tmpf"])
            add(V, lambda e: e.tensor_reduce(out=stat[:, 64:96], in_=tmpf3, axis=AX.X, op=ALU.add), reads=["tmpf"], writes=["st2"])
            act(stat[:, 96:128], stat[:, 64:96], AF.Sqrt, ["st2"], ["st3"], bias=64e-5, scale=1.0 / 64)
            add(V, lambda e: e.reciprocal(out=stat[:, 96:128], in_=stat[:, 96:128]), reads=[], writes=["st3"])
            tt(V, ynb, ytok, stat[:, 96:128].unsqueeze(2).to_broadcast([128, 32, 64]), ALU.mult, ["ytok", "st3"], ["rbf"])
            for T in range(4):
                tsl = slice(T * 512, (T + 1) * 512)
                pb = pairbank()
                for h in range(2):
                    hs = slice(64 * h, 64 * h + 64)
                    bank, bn = pb[h]
                    for c8 in range(8):
                        mm(bank[hs, c8 * 64:(c8 + 1) * 64], ynb[hs, T * 8 + c8, :], identb[hs, 64 * h:64 * h + 64], True, True,
                           ["rbf", "identb"], [bn])
                    act(t1f[hs, :], bank[hs, :], AF.Identity, ["colsb"], [bn, "t1f" + str(h)],
                        bias=colsb[hs, C_GB + hp:C_GB + hp + 1], scale=colsb[hs, C_GG + hp:C_GG + hp + 1])
                tt(V, t1f, t1f, bonus[:, tsl], ALU.add, ["bonus", "t1f0", "t1f1"], ["t1f0", "t1f1"])
                tt(V, oT[:, hp, tsl], t1f, oT[:, hp, tsl], ALU.mult, ["t1f0", "t1f1"], ["oT%d" % hp])

    C.rwkv = rwkv_phase


    dma("sp", colsb[:], cols_d[0], [], ["colsb"])
    xld = [af32(0, 1024), af32(1024, 1024)]
    for t16 in range(16):
        dma("sp", xld[t16 % 2], x_in[t16 * 128:(t16 + 1) * 128, :], [], ["xld%d" % (t16 % 2)])
        store_xT(xld[t16 % 2], "xld%d" % (t16 % 2), t16)
    import os
    STOP = int(os.environ.get("KSTOP", "9"))
    for l in range(depth if STOP > 1 else 0):
        if l > 0:
            dma("sp", colsb[:], cols_d[l], [], ["colsb"])
        dma(G, vrowb[:], vrow_d[l], [], ["vrowb"])
        tt(V, c0col[:], colsb[:, C_MU0:C_MU0 + 26], colsb[:, C_MU1:C_MU1 + 26], ALU.add, ["colsb"], ["c0col"])
        ts(V, c0col[:], c0col[:], -1.0, 1.0, ALU.mult, ALU.add, [], ["c0col"])
        if C.rwkv is not None and "A" in PH:
            C.rwkv(l)
        else:
            for c in range(8):
                memset(V, oT[:, c, :], 0.0, ["oT%d" % c])
        if "B" in PH:
            pool_phase(l)
        else:
            barrier()
            for c in range(8, 14):
                memset(V, oT[:, c, :], 0.0, ["oT%d" % c])
        if C.attn is not None and "C" in PH:
            C.attn(l)
        else:
            barrier()
            for c in range(14, 16):
                memset(V, oT[:, c, :], 0.0, ["oT%d" % c])
        if dbg is not None and l == depth - 1:
            dtmp = af32(0, S)
            for c in range(16):
                cp(V, dtmp, oT[:, c, :], ["oT%d" % c], ["dtmp"])
                dma("sp", dbg_out[:, c, :], dtmp, ["dtmp"], [])
        if STOP > 2:
            final_phase(l, l == depth - 1)
    P.emit(nc)
    st.close()
    return nc


PH = "ABC"


def EXTRA_PHASES(L):
    pass


def host_prep(inp):
    f = np.float32
    g = lambda k: np.asarray(inp[k], dtype=f)
    colv = lambda v: np.ascontiguousarray(v.reshape(-1, 128).T)
    cols = []
    for l in range(DEPTH):
        parts = [colv(g("b_in")[l]), colv(g("rwkv_mu")[l, 0]), colv(g("rwkv_mu")[l, 1]),
                 colv(g("rwkv_w0")[l, 0]), colv(g("rwkv_w0")[l, 1]), colv(g("rwkv_a0")[l, 0]), colv(g("rwkv_a0")[l, 1]),
                 colv(g("rwkv_k_k")[l]), colv(g("rwkv_k_a")[l]), colv(g("rwkv_r_k")[l].reshape(-1)),
                 colv(g("rwkv_gn_g")[l]), colv(g("rwkv_gn_b")[l]), colv(g("pool_b")[l]), colv(g("pool_scale")[l])]
        cols.append(np.concatenate(parts, axis=1))
    cols = np.stack(cols)
    assert cols.shape == (DEPTH, 128, NCOLS), cols.shape
    shared = {
        "w_in": g("w_in"), "cols": cols,
        "vrow": np.ascontiguousarray(g("b_in")[:, None, 7424:8192]),
        "w_up": np.ascontiguousarray(g("rwkv_w_up").reshape(DEPTH, 128, 1024)),
        "a_up": np.ascontiguousarray(g("rwkv_a_up").reshape(DEPTH, 128, 1024)),
        "pool_w": _blockdiag(g("pool_w")), "proj_a": g("proj_a"), "proj_b": g("proj_b"), "proj_c": g("proj_c"),
        "w_out": g("w_out"),
        "lnrow": np.ascontiguousarray(np.stack([g("ln_g"), g("ln_b")], axis=1)),
    }
    shared.update(host_consts())
    return shared


def _blockdiag(pw):
    out = np.zeros((DEPTH, 768, 768), np.float32)
    for gi in range(4):
        out[:, 192 * gi:192 * gi + 192, 192 * gi:192 * gi + 192] = pw[:, gi]
    return out


def host_consts():
    f = np.float32
    ident = np.eye(128, dtype=f)
    bones = np.zeros((128, 128), f)
    bones[:64, :64] = 1
    bones[64:, 64:] = 1
    perm = np.zeros((128, 128), f)
    for m in range(128):
        c = m % 64
        k = m + 32 if c < 32 else m - 32
        perm[k, m] = 1
    eye2 = np.concatenate([np.eye(64, dtype=f), np.eye(64, dtype=f)], 0)
    cst = np.concatenate([ident, bones, perm, eye2], axis=1)
    inv = np.power(f(10000.0), -np.arange(0, 64, 2, dtype=f) / f(64))
    ang = np.arange(S, dtype=f)[:, None] * inv[None, :]
    ang = np.concatenate([ang, ang], axis=-1).astype(f)
    cosT = np.cos(ang).T.astype(f)
    sinT = np.sin(ang).T.astype(f)
    sign = np.where(np.arange(64) < 32, -1.0, 1.0).astype(f)[:, None]
    rope = np.stack([np.concatenate([cosT, cosT], 0), np.concatenate([sinT * sign, sinT * sign], 0)]).astype(f)
    b = np.arange(128)[:, None]
    a = np.arange(128)[None, :]
    mA = (b >= a).astype(f)
    mB = (a >= b).astype(f)
    mAf = mA * (b >= 64)
    mBl = mB * (b < 64)
    m16 = (np.abs(a - b) <= 64).astype(f)
    amask = np.stack([np.concatenate([mA, mB, mA, mB], 1), np.concatenate([mAf, mB, mAf, mB], 1),
                      np.concatenate([mA, mBl, mA, mBl], 1), np.concatenate([m16, m16, m16, m16], 1)]).astype(f)
    s_ = np.arange(64)[:, None]
    t_ = np.arange(64)[None, :]
    rm = []
    for z in range(2):
        if z == 0:
            strict = (s_ < t_)
            incl = (s_ <= t_)
        else:
            strict = (s_ > t_)
            incl = (s_ >= t_)
        m1 = np.concatenate([strict, incl], 1).astype(f)
        m2 = strict.T.astype(f)
        reset = np.ones((64, 512), f)
        reset[:, ::64] = 0
        row = np.concatenate([np.tile(m1, (1, 4)), np.tile(m1, (1, 4)), np.tile(m2, (1, 4)), reset], 1)
        rm.append(np.concatenate([row, row], 0))
    rmask = np.stack(rm).astype(f)
    pedge = np.ones((128, 4, 16), f)
    for gi in range(4):
        h = 1 << gi
        for e in range(8):
            t = e
            cnt = min(t + h, S - 1) - max(t - h, 0) + 1
            pedge[:, gi, e] = (2 * h + 1) / cnt
            t = S - 8 + e
            cnt = min(t + h, S - 1) - max(t - h, 0) + 1
            pedge[:, gi, 8 + e] = (2 * h + 1) / cnt
    selc = np.zeros((128, 24), f)
    for c in range(6):
        for p in range(128):
            gi = (128 * c + p) // 192
            selc[p, c * 4 + gi] = 1.0 / (2 * (1 << gi) + 1)
    return {"selc": selc, "cst": cst, "rope": rope, "amask": amask, "rmask": rmask, "pedge": pedge}


_NC_CACHE = {}


def kernel(**inputs):
    shared = host_prep(inputs)
    x = np.asarray(inputs["x"], dtype=np.float32)
    if "nc" not in _NC_CACHE:
        _NC_CACHE["nc"] = build()
    nc = _NC_CACHE["nc"]
    in_maps = []
    for c in range(8):
        m = dict(shared)
        m["x"] = np.ascontiguousarray(x[c])
        in_maps.append(m)
    res = run_bass_kernel_spmd(nc, in_maps, core_ids=list(range(8)))
    return np.stack([np.asarray(r["y"], dtype=np.float32) for r in res.results], axis=0)
```

```python
import math
import os
import numpy as np
import ml_dtypes
import concourse.bass as bass
import concourse.mybir as mybir
from concourse.bass_utils import run_bass_kernel_spmd

F32 = mybir.dt.float32
BF16 = mybir.dt.bfloat16
AF = mybir.ActivationFunctionType
ALU = mybir.AluOpType
AX = mybir.AxisListType

ENGS = ("pe", "act", "dve", "pool", "sp")
DMA_POOL = 16


class _Buf:
    __slots__ = ("last_w", "readers", "dma_readers")

    def __init__(self):
        self.last_w = None
        self.readers = {}
        self.dma_readers = []


class _Op:
    __slots__ = ("eng", "idx", "gid", "fn", "deps", "is_dma", "signal", "sem", "val", "waits",
                 "know", "dma_n", "pre_wait")

    def __init__(self, eng, idx, gid, fn, is_dma):
        self.eng = eng
        self.idx = idx
        self.gid = gid
        self.fn = fn
        self.is_dma = is_dma
        self.deps = []
        self.signal = False
        self.sem = None
        self.val = None
        self.waits = []
        self.know = None
        self.dma_n = None
        self.pre_wait = None


class Prog:
    def __init__(self):
        self.ops = {e: [] for e in ENGS}
        self.all = []
        self.bufs = {}
        self.n_dma = {e: 0 for e in ENGS}

    def _buf(self, name):
        b = self.bufs.get(name)
        if b is None:
            b = _Buf()
            self.bufs[name] = b
        return b

    def add(self, eng, fn, reads=(), writes=(), dma=False):
        op = _Op(eng, len(self.ops[eng]), len(self.all), fn, dma)
        deps = {}
        for r in reads:
            b = self._buf(r)
            if b.last_w is not None:
                deps[b.last_w.gid] = b.last_w
        for w in writes:
            b = self._buf(w)
            if b.last_w is not None:
                deps[b.last_w.gid] = b.last_w
            for d in b.readers.values():
                deps[d.gid] = d
            for d in b.dma_readers:
                deps[d.gid] = d
        op.deps = [deps[k] for k in sorted(deps)]
        for r in reads:
            b = self._buf(r)
            if dma:
                b.dma_readers.append(op)
            else:
                b.readers[eng] = op
        for w in writes:
            b = self._buf(w)
            b.last_w = op
            b.readers = {}
            b.dma_readers = []
        if dma:
            op.dma_n = self.n_dma[eng]
            self.n_dma[eng] += 1
        self.ops[eng].append(op)
        self.all.append(op)
        return op

    def resolve(self):
        know = {e: {f: -1 for f in ENGS} for e in ENGS}
        know_dma = {e: set() for e in ENGS}
        sig_count = {e: 0 for e in ENGS}
        for op in self.all:
            E = op.eng
            kn = know[E]
            for d in op.deps:
                if d.is_dma:
                    if d.gid in know_dma[E]:
                        continue
                    know_dma[E].add(d.gid)
                    op.waits.append(d)
                    for f, v in d.know.items():
                        if v > kn[f]:
                            kn[f] = v
                    continue
                F = d.eng
                if F == E:
                    if E == "pe" or op.idx - d.idx > 2:
                        continue
                    if kn[F] >= d.idx:
                        continue
                elif kn[F] >= d.idx:
                    continue
                d.signal = True
                op.waits.append(d)
                kn[F] = max(kn[F], d.idx)
                for f, v in d.know.items():
                    if f != E and v > kn[f]:
                        kn[f] = v
            snap = dict(kn)
            if not op.is_dma:
                snap[E] = op.idx
            op.know = snap
        for e in ENGS:
            c = 0
            for op in self.ops[e]:
                if op.is_dma:
                    continue
                if op.signal:
                    c += 1
                    op.val = c

    def emit(self, nc):
        self.resolve()
        import contextlib
        with contextlib.ExitStack() as st:
            esem = {e: st.enter_context(nc.semaphore("s_" + e)) for e in ENGS}
            dsem = {e: [st.enter_context(nc.semaphore("d_%s%d" % (e, i))) for i in range(DMA_POOL)]
                    for e in ENGS if self.n_dma[e] > 0}
            block = st.enter_context(nc.Block())

            def wait_for(engine, d):
                if d.is_dma:
                    engine.wait_ge(dsem[d.eng][d.dma_n % DMA_POOL], 16 * (d.dma_n // DMA_POOL + 1))
                else:
                    engine.wait_ge(esem[d.eng], d.val)

            def run(engine, e):
                ops = self.ops[e]
                for op in ops:
                    for d in op.waits:
                        wait_for(engine, d)
                    if op.is_dma:
                        n = op.dma_n
                        if n >= DMA_POOL:
                            engine.wait_ge(dsem[e][n % DMA_POOL], 16 * (n // DMA_POOL))
                        ins = op.fn(engine)
                        ins.then_inc(dsem[e][n % DMA_POOL], 16)
                    else:
                        ins = op.fn(engine)
                        if op.signal:
                            ins.then_inc(esem[e], 1)
                nd = self.n_dma[e]
                for i in range(min(nd, DMA_POOL)):
                    n = nd - 1 - i
                    engine.wait_ge(dsem[e][n % DMA_POOL], 16 * (n // DMA_POOL + 1))

            @block.tensor
            def _(eng):
                run(eng, "pe")

            @block.scalar
            def _(eng):
                run(eng, "act")

            @block.vector
            def _(eng):
                run(eng, "dve")

            @block.gpsimd
            def _(eng):
                run(eng, "pool")

            @block.sync
            def _(eng):
                run(eng, "sp")


S = 2048
D = 1024
NIN = 11520
DEPTH = 4
PADX = 256
XW = S + 2 * PADX
ALPHA = (2 * DEPTH) ** 0.25
CDEC = math.exp(-0.5)
C_BIN = 0
C_MU0 = 90
C_MU1 = 116
C_W0 = 142
C_A0 = 158
C_KK = 174
C_KA = 182
C_RK = 190
C_GG = 198
C_GB = 206
C_PB = 214
C_PS = 220
NCOLS = 226


class Ctx:
    pass


def build(depth=DEPTH, dbg=None):
    nc = bass.Bass("TRN2", target_bir_lowering=False)
    P = Prog()
    dt_in = lambda name, shape: nc.dram_tensor(name, shape, F32, kind="ExternalInput").ap()
    x_in = dt_in("x", [S, D])
    w_in = dt_in("w_in", [DEPTH, D, NIN])
    cols_d = dt_in("cols", [DEPTH, 128, NCOLS])
    vrow_d = dt_in("vrow", [DEPTH, 1, 768])
    wup_d = dt_in("w_up", [DEPTH, 128, 1024])
    aup_d = dt_in("a_up", [DEPTH, 128, 1024])
    poolw_d = dt_in("pool_w", [DEPTH, 768, 768])
    proja_d = dt_in("proj_a", [DEPTH, 1024, 1024])
    projb_d = dt_in("proj_b", [DEPTH, 768, 1024])
    projc_d = dt_in("proj_c", [DEPTH, 256, 1024])
    wout_d = dt_in("w_out", [DEPTH, 1024, 1024])
    lnrow_d = dt_in("lnrow", [DEPTH, 2, 1024])
    cst_d = dt_in("cst", [128, 128 * 3 + 64])
    rope_d = dt_in("rope", [2, 128, S])
    amask_d = dt_in("amask", [4, 128, 512])
    rmask_d = dt_in("rmask", [2, 128, 512 + 512 + 256 + 512])
    pedge_d = dt_in("pedge", [128, 4, 16])
    y_out = nc.dram_tensor("y", [S, D], F32, kind="ExternalOutput").ap()
    xres = nc.dram_tensor("xres", [S, D], F32, kind="Internal").ap()
    dbg_out = None
    if dbg is not None:
        dbg_out = nc.dram_tensor("dbg", [128, 16, S], F32, kind="ExternalOutput").ap()

    import contextlib
    st = contextlib.ExitStack()
    sb = lambda name, shape, dt: st.enter_context(nc.sbuf_tensor(name, shape, dt))
    xT = sb("xT", [128, 8, XW], BF16)
    oT = sb("oT", [128, 16, S], BF16)
    wb = [sb("wb%d" % i, [128, 8, 512], BF16) for i in range(2)]
    ARENA = 16384
    arena = sb("arena", [128, ARENA], F32)
    colsb = sb("colsb", [128, NCOLS], F32)
    c0col = sb("c0col", [128, 26], F32)
    identb = sb("identb", [128, 128], BF16)
    permb = sb("permb", [128, 128], BF16)
    onesb = sb("onesb", [128, 128], BF16)
    eye2 = sb("eye2", [128, 64], BF16)
    bones = sb("bones", [128, 128], F32)
    vrowb = sb("vrowb", [1, 768], BF16)
    selcol = sb("selcol", [128, 24], F32)
    mixf = sb("mixf", [128, S], F32)
    selc_d = dt_in("selc", [128, 24])
    psb = [st.enter_context(nc.psum_tensor("psb%d" % i, [128, 512], F32)) for i in range(8)]

    C = Ctx()
    C.bank_i = 0

    def nextbank():
        i = C.bank_i
        C.bank_i = (i + 1) % 4
        return psb[i], "ps%d" % i

    def af32(off, n):
        return arena[:, off:off + n]

    def abf(off, n):
        return arena[:, off:off + n // 2].bitcast(BF16)

    def o8f32(off, n):
        return oT[:, 8:16, :].rearrange("p a b -> p (a b)").bitcast(F32)[:, off:off + n]

    def o8bf(off, n):
        return oT[:, 8:16, :].rearrange("p a b -> p (a b)")[:, off:off + n]

    add = P.add
    V = "dve"
    A = "act"
    G = "pool"

    def mm(out, lhsT, rhs, start, stop, rd, wr):
        add("pe", lambda e: e.matmul(out, lhsT, rhs, start=start, stop=stop), reads=rd, writes=wr)

    def act(out, in_, func, rd, wr, bias=0.0, scale=1.0):
        add(A, lambda e: e.activation(out=out, in_=in_, func=func, bias=bias, scale=scale), reads=rd, writes=wr)

    def tt(eng, out, in0, in1, op, rd, wr):
        eng = V if eng == G else eng
        add(eng, lambda e: e.tensor_tensor(out=out, in0=in0, in1=in1, op=op), reads=rd, writes=wr)

    def ts(eng, out, in0, s1, s2, op0, op1, rd, wr):
        eng = V if eng == G else eng
        if s2 is None:
            add(eng, lambda e: e.tensor_scalar(out=out, in0=in0, scalar1=s1, scalar2=None, op0=op0), reads=rd, writes=wr)
        else:
            add(eng, lambda e: e.tensor_scalar(out=out, in0=in0, scalar1=s1, scalar2=s2, op0=op0, op1=op1), reads=rd, writes=wr)

    def stt(out, in0, scalar, in1, op0, op1, rd, wr):
        add(V, lambda e: e.scalar_tensor_tensor(out=out, in0=in0, scalar=scalar, in1=in1, op0=op0, op1=op1),
            reads=rd, writes=wr)

    def cp(eng, out, in_, rd, wr):
        eng = V if eng == G else eng
        if eng == A:
            add(eng, lambda e: e.activation(out=out, in_=in_, func=AF.Copy), reads=rd, writes=wr)
        else:
            add(eng, lambda e: e.tensor_copy(out=out, in_=in_), reads=rd, writes=wr)

    def dma(q, out, in_, rd, wr):
        add(q, lambda e: e.dma_start(out=out, in_=in_), reads=rd, writes=wr, dma=True)

    def memset(eng, ap, val, wr):
        add(eng, lambda e: e.memset(ap, val), writes=wr)

    bscr = sb("bscr", [128, 8], F32)

    def barrier():
        names = [n for n in P.bufs.keys() if not n.startswith("ps")] + ["bscr"]
        mm(psb[7][:, 0:8], identb[:, 0:128], identb[:, 0:8], True, True, [], names + ["ps7"])
        act(bscr[:, 0:1], bscr[:, 1:2], AF.Copy, [], names)
        memset(V, bscr[:, 2:3], 0.0, names)
        dma("sp", bscr[0:1, 3:4], cst_d[0:1, 0:1], [], names)
        dma(G, bscr[0:1, 4:5], cst_d[0:1, 0:1], [], names)

    memset(V, bscr[:], 0.0, ["bscr"])
    dma(G, identb[:], cst_d[:, 0:128], [], ["identb"])
    dma("sp", bones[:], cst_d[:, 128:256], [], ["bones"])
    dma(G, permb[:], cst_d[:, 256:384], [], ["permb"])
    dma("sp", selcol[:], selc_d, [], ["selcol"])
    dma(G, eye2[:], cst_d[:, 384:448], [], ["eye2"])
    memset(V, onesb[:], 1.0, ["onesb"])
    memset(V, xT[:, :, 0:PADX], 0.0, ["xT"])
    memset(V, xT[:, :, PADX + S:XW], 0.0, ["xT"])

    C.wres = [None, None]
    C.wlast = 0

    def load_w(key, src3):
        for i in range(2):
            if C.wres[i] == key:
                C.wlast = i
                return wb[i], "wb%d" % i
        i = 1 - C.wlast
        C.wres[i] = key
        C.wlast = i
        kc, ncol = src3.shape[1], src3.shape[2]
        dma(G, wb[i][:, 0:kc, 0:ncol], src3, [], ["wb%d" % i])
        return wb[i], "wb%d" % i

    def win_src(l, col0, ncol):
        return w_in[l].rearrange("(k p) n -> p k n", p=128)[:, :, col0:col0 + ncol]

    def inproj(l, cg, evac, wkey=None):
        blk = cg // 4
        ncol = min(512, NIN - blk * 512)
        w, wn = load_w(("win", l, blk), win_src(l, blk * 512, ncol))
        c0 = (cg % 4) * 128
        for t4 in range(4):
            bank, bn = nextbank()
            for k in range(8):
                mm(bank[:], w[:, k, c0:c0 + 128], xT[:, k, PADX + t4 * 512:PADX + (t4 + 1) * 512],
                   k == 0, k == 7, ["xT", wn], [bn])
            evac(t4, bank, bn)

    def col(ci):
        return colsb[:, ci:ci + 1]

    xstage = sb("xstage", [128, 1024], BF16)

    def store_xT(src_f32, srcname, t16):
        cp(A, xstage[:], src_f32, [srcname], ["xstage"])
        bank, bn = nextbank()
        bb = bank[:].bitcast(BF16)
        for k in range(8):
            add("pe", lambda e, k=k: e.transpose(bb[:, k * 128:(k + 1) * 128], xstage[:, k * 128:(k + 1) * 128], identb[:]),
                reads=["xstage", "identb"], writes=[bn])
        cp(V, xT[:, :, PADX + t16 * 128:PADX + (t16 + 1) * 128], bb.rearrange("p (k t) -> p k t", k=8), [], [bn, "xT"])

    def final_phase(l, last):
        barrier()
        mergedT = abf(0, 8 * S).rearrange("p (k t) -> p k t", k=8)
        sig = abf(8192, 512)
        tmpf = af32(8448, 512)
        lng = af32(9216, 1024)
        lnb = af32(10240, 1024)
        xt_ = [af32(11264, 1024), af32(12288, 1024)]
        yt_ = [af32(13312, 1024), af32(14336, 1024)]
        stat = af32(15360, 8)
        dma("sp", lng, lnrow_d[l, 0:1, :].to_broadcast([128, 1024]), [], ["lng"])
        dma("sp", lnb, lnrow_d[l, 1:2, :].to_broadcast([128, 1024]), [], ["lnb"])
        if STOP == 4:
            return
        branches = [(proja_d, 8, 0, 66), (projb_d, 6, 8, 74), (projc_d, 2, 14, 82)]
        for bi, (pd, kc, o0, g0) in enumerate(branches):
            for eb in range(2):
                for ec in range(eb * 4, eb * 4 + 4):
                    for t4 in range(4):
                        tsl = slice(t4 * 512, (t4 + 1) * 512)
                        pw, pwn = load_w(("proj", l, bi, eb), pd[l].rearrange("(k p) n -> p k n", p=128)[:, :, eb * 512:(eb + 1) * 512])
                        b1, b1n = nextbank()
                        for k in range(kc):
                            mm(b1[:], pw[:, k, (ec % 4) * 128:(ec % 4 + 1) * 128], oT[:, o0 + k, tsl], k == 0, k == kc - 1,
                               ["oT%d" % (o0 + k), pwn], [b1n])
                        gw, gwn = load_w(("gate", l, bi, eb), win_src(l, (g0 + eb * 4) * 128, 512))
                        b2, b2n = nextbank()
                        for k in range(8):
                            mm(b2[:], gw[:, k, (ec % 4) * 128:(ec % 4 + 1) * 128], xT[:, k, PADX + t4 * 512:PADX + (t4 + 1) * 512],
                               k == 0, k == 7, ["xT", gwn], [b2n])
                        act(sig, b2[:], AF.Sigmoid, ["colsb"], [b2n, "sig"], bias=col(C_BIN + g0 + ec))
                        if bi == 0:
                            tt(V, mergedT[:, ec, tsl], b1[:], sig, ALU.mult, ["sig"], [b1n, "mg%d" % ec])
                        else:
                            tt(V, tmpf, b1[:], sig, ALU.mult, ["sig"], [b1n, "tmpf"])
                            tt(G, mergedT[:, ec, tsl], mergedT[:, ec, tsl], tmpf, ALU.add, ["tmpf"], ["mg%d" % ec])
        if STOP == 3:
            return
        wo = []
        for fh in range(2):
            wo.append(load_w(("wout", l, fh), wout_d[l].rearrange("(k p) n -> p k n", p=128)[:, :, fh * 512:(fh + 1) * 512]))
        xsrc = x_in if l == 0 else xres
        dst = y_out if last else xres
        for t16 in range(16):
            xt = xt_[t16 % 2]
            yt = yt_[t16 % 2]
            xn, yn = "xt%d" % (t16 % 2), "yt%d" % (t16 % 2)
            dma("sp", xt, xsrc[t16 * 128:(t16 + 1) * 128, :], ["xres"] if l > 0 else [], [xn])
            for fh in range(2):
                w, wn = wo[fh]
                bank, bn = nextbank()
                for k in range(8):
                    mm(bank[:], mergedT[:, k, t16 * 128:(t16 + 1) * 128], w[:, k, :], k == 0, k == 7,
                       ["mg%d" % k, wn], [bn])
                stt(yt[:, fh * 512:(fh + 1) * 512], xt[:, fh * 512:(fh + 1) * 512], ALPHA, bank[:], ALU.mult, ALU.add,
                    [xn], [bn, yn])
            if STOP == 5:
                dma("sp", dst[t16 * 128:(t16 + 1) * 128, :], yt, [yn], ["xres"])
                continue
            add(V, lambda e, yt=yt: e.tensor_reduce(out=stat[:, 0:1], in_=yt, axis=AX.X, op=ALU.add), reads=[yn], writes=["stat0"])
            ts(V, stat[:, 1:2], stat[:, 0:1], -1.0 / D, None, ALU.mult, None, ["stat0"], ["stat1"])
            ts(V, yt, yt, stat[:, 1:2], None, ALU.add, None, ["stat1"], [yn])
            add(A, lambda e, yt=yt, xt=xt: e.activation(out=xt, in_=yt, func=AF.Square, accum_out=stat[:, 2:3]),
                reads=[yn], writes=[xn, "stat2"])
            act(stat[:, 3:4], stat[:, 2:3], AF.Sqrt, ["stat2"], ["stat3"], bias=1e-5, scale=1.0 / D)
            add(V, lambda e: e.reciprocal(out=stat[:, 4:5], in_=stat[:, 3:4]), reads=["stat3"], writes=["stat4"])
            if STOP == 6:
                dma("sp", dst[t16 * 128:(t16 + 1) * 128, :], yt, [yn], ["xres"])
                continue
            if STOP != 8:
                stt(yt, yt, stat[:, 4:5], lng, ALU.mult, ALU.mult, ["stat4", "lng"], [yn])
            if STOP != 7:
                tt(V, yt, yt, lnb, ALU.add, ["lnb"], [yn])
            dma("sp", dst[t16 * 128:(t16 + 1) * 128, :], yt, [yn], ["xres"])
            if not last:
                store_xT(yt, yn, t16)

    def pool_phase(l):
        barrier()
        W = S + 32
        pbuf = af32(0, W)
        a_ = [af32(2080, W), af32(4160, W)]
        mixed = abf(6240, 6 * S).rearrange("p (c t) -> p c t", c=6)
        wgt = abf(12384, 6 * 768).rearrange("p (a b) -> p a b", a=6)
        sacc = af32(14688, 0) if False else None
        pe_t = af32(14688, 64).rearrange("p (g e) -> p g e", g=4)
        t1 = af32(14752, 512)
        memset(V, pbuf[:, 0:16], 0.0, ["pbuf"])
        memset(V, pbuf[:, W - 16:W], 0.0, ["pbuf"])
        dma("sp", pe_t, pedge_d, [], ["pe_t"])
        dma(G, wgt, poolw_d[l].rearrange("(k p) n -> p k n", p=128), [], ["wgt"])
        for c in range(6):
            inproj(l, 40 + c, lambda t4, bank, bn, c=c: act(oT[:, 8 + c, t4 * 512:(t4 + 1) * 512], bank[:], AF.Silu,
                                                             ["colsb"], [bn, "oT%d" % (8 + c)], bias=col(C_BIN + 40 + c)))
        for c in range(6):
            inproj(l, 34 + c, lambda t4, bank, bn, c=c: act(pbuf[:, 16 + t4 * 512:16 + (t4 + 1) * 512], bank[:], AF.Identity,
                                                             ["colsb"], [bn, "pbuf"], bias=col(C_BIN + 34 + c)))
            gs = sorted(set((2 * c + hf) // 3 for hf in range(2)))
            first = True
            for g in gs:
                h = 1 << g
                kk = g + 1
                src, srcn = pbuf, "pbuf"
                for j in range(kk):
                    sh = 1 << j
                    dstt = a_[j % 2]
                    n = W - (2 << j) + 1
                    tt(V, dstt[:, 0:n], src[:, 0:n], src[:, sh:sh + n], ALU.add, [srcn], ["a%d" % (j % 2)])
                    src, srcn = dstt, "a%d" % (j % 2)
                sfin = a_[kk % 2]
                sn = "a%d" % (kk % 2)
                tt(V, sfin[:, 0:S], src[:, 16 - h:16 - h + S], pbuf[:, 16 + h:16 + h + S], ALU.add, [srcn, "pbuf"], [sn])
                tt(V, sfin[:, 0:8], sfin[:, 0:8], pe_t[:, g, 0:8], ALU.mult, ["pe_t"], [sn])
                tt(V, sfin[:, S - 8:S], sfin[:, S - 8:S], pe_t[:, g, 8:16], ALU.mult, ["pe_t"], [sn])
                selw = selcol[:, c * 4 + g:c * 4 + g + 1]
                if first:
                    stt(mixf[:, :], sfin[:, 0:S], selw, pbuf[:, 16:16 + S], ALU.mult, ALU.subtract, [sn, "pbuf", "selcol"], ["mixf"])
                else:
                    stt(mixf[:, :], sfin[:, 0:S], selw, mixf[:, :], ALU.mult, ALU.add, [sn, "selcol"], ["mixf"])
                first = False
            cp(A, mixed[:, c, :], mixf[:, :], ["mixf"], ["mixed"])
        for oc in range(6):
            ics = [ic for ic in range(6) if any((2 * ic + a) // 3 == (2 * oc + b) // 3 for a in range(2) for b in range(2))]
            for t4 in range(4):
                tsl = slice(t4 * 512, (t4 + 1) * 512)
                bank, bn = nextbank()
                for n_, ic in enumerate(ics):
                    mm(bank[:], wgt[:, ic, oc * 128:(oc + 1) * 128], mixed[:, ic, tsl], n_ == 0, n_ == len(ics) - 1,
                       ["wgt", "mixed"], [bn])
                ts(V, t1, bank[:], col(C_PB + oc), col(C_PS + oc), ALU.add, ALU.mult, ["colsb"], [bn, "t1"])
                tt(V, oT[:, 8 + oc, tsl], t1, oT[:, 8 + oc, tsl], ALU.mult, ["t1"], ["oT%d" % (8 + oc)])

    C.attn = None
    C.rwkv = None
    def attn_phase(l):
        barrier()
        Qr = abf(0, XW)
        Kr = abf(1280, XW)
        qraw = abf(2560, S)
        ropec = af32(3584, S)
        ropes = af32(5632, S)
        t1 = af32(7680, 512)
        t2 = af32(8192, 512)
        accn = af32(8704, S)
        accd = af32(10752, S)
        Vt = abf(12800, 20 * 128).rearrange("p (a b) -> p a b", a=20)
        pT = [abf(14080, 512), abf(14336, 512)]
        msk = abf(14592, 4 * 512).rearrange("p (a b) -> p a b", a=4)
        dma("sp", ropec, rope_d[0], [], ["ropec"])
        dma("sp", ropes, rope_d[1], [], ["ropes"])
        for a_ in range(4):
            dma(G, msk[:, a_, :], amask_d[a_], [], ["msk"])
        for buf, nm in ((Qr, "Qr"), (Kr, "Kr")):
            memset(V, buf[:, 0:PADX], 0.0, [nm])
            memset(V, buf[:, PADX + S:XW], 0.0, [nm])
        cnt = [0]
        SUB = int(os.environ.get("ATT_SUB", "9"))
        if SUB == 1:
            return
        for pp in range(2):
            for g in range(3):
                d = (1, 4, 16)[g]
                for cg, dst, nm in ((46 + 2 * g + pp, Qr, "Qr"), (52 + 2 * g + pp, Kr, "Kr")):
                    inproj(l, cg, lambda t4, bank, bn, cg=cg: act(qraw[:, t4 * 512:(t4 + 1) * 512], bank[:], AF.Identity,
                                                                  ["colsb"], [bn, "qraw"], bias=col(C_BIN + cg)))
                    for t4 in range(4):
                        tsl = slice(t4 * 512, (t4 + 1) * 512)
                        bank, bn = nextbank()
                        mm(bank[:], permb[:], qraw[:, tsl], True, True, ["permb", "qraw"], [bn])
                        tt(V, t1, bank[:], ropes[:, tsl], ALU.mult, ["ropes"], [bn, "t1"])
                        tt(V, t2, qraw[:, tsl], ropec[:, tsl], ALU.mult, ["qraw", "ropec"], ["t2"])
                        tt(V, dst[:, PADX + t4 * 512:PADX + (t4 + 1) * 512], t1, t2, ALU.add, ["t1", "t2"], [nm])
                if SUB == 2:
                    return
                wv, wvn = load_w(("wv", l, g, pp), win_src(l, (58 + 2 * g + pp) * 128, 128))
                if d == 1:
                    tsls = [slice(PADX + 128 * m - 64, PADX + 128 * m + 64) for m in range(17)]
                elif d == 4:
                    tsls = []
                    for r in range(4):
                        for m in range(5):
                            s0 = PADX + r + 4 * (128 * m - 64)
                            tsls.append(slice(s0, s0 + 509, 4))
                else:
                    tsls = [slice(PADX + r, PADX + r + 2033, 16) for r in range(16)]
                for j0 in range(0, len(tsls), 4):
                    grp = tsls[j0:j0 + 4]
                    bank, bn = nextbank()
                    for j, sl in enumerate(grp):
                        for k in range(8):
                            mm(bank[:, j * 128:(j + 1) * 128], xT[:, k, sl], wv[:, k, 0:128], k == 0, False, ["xT", wvn], [bn])
                        mm(bank[:, j * 128:(j + 1) * 128], onesb[0:1, 0:128], vrowb[0:1, (2 * g + pp) * 128:(2 * g + pp + 1) * 128],
                           False, True, ["onesb", "vrowb"], [bn])
                    n = len(grp)
                    cp(A, Vt[:, j0:j0 + n, :], bank[:, 0:n * 128].rearrange("p (a b) -> p a b", a=n), [], [bn, "Vt"])
                if SUB == 3:
                    return
                for sbk in range(4):
                    for j in range(4):
                        if d == 1:
                            m = 4 * sbk + j
                            qsl = slice(PADX + 128 * m, PADX + 128 * m + 128)
                            chunks = [(slice(PADX + 128 * m - 64, PADX + 128 * m + 64), m),
                                      (slice(PADX + 128 * m + 64, PADX + 128 * m + 192), m + 1)]
                            mi = 1 if m == 0 else (2 if m == 15 else 0)
                        elif d == 4:
                            r, m = sbk, j
                            q0 = PADX + r + 512 * m
                            qsl = slice(q0, q0 + 509, 4)
                            k0 = PADX + r + 4 * (128 * m - 64)
                            chunks = [(slice(k0, k0 + 509, 4), r * 5 + m), (slice(k0 + 512, k0 + 512 + 509, 4), r * 5 + m + 1)]
                            mi = 1 if m == 0 else (2 if m == 3 else 0)
                        else:
                            r = 4 * sbk + j
                            qsl = slice(PADX + r, PADX + r + 2033, 16)
                            chunks = [(qsl, r)]
                            mi = 3
                        nch = len(chunks)
                        wd = nch * 128
                        for h in range(2):
                            hp_ = slice(64 * h, 64 * h + 64)
                            si = h + 2 * (cnt[0] % 2)
                            sbank, sbn = psb[si], "ps%d" % si
                            pTb, pTn = pT[h], "pT%d" % h
                            for ci, (ks, vt) in enumerate(chunks):
                                mm(sbank[:, ci * 128:(ci + 1) * 128], Kr[hp_, ks], Qr[hp_, qsl], True, True, ["Kr", "Qr"], [sbn])
                            act(pTb[:, 0:wd], sbank[:, 0:wd], AF.Exp, [], [sbn, pTn], scale=0.125)
                            tt(V, pTb[:, 0:wd], pTb[:, 0:wd], msk[:, mi, 0:wd], ALU.mult, ["msk"], [pTn])
                            for ci, (ks, vt) in enumerate(chunks):
                                mm(psb[6][hp_, j * 128:(j + 1) * 128], Vt[:, vt, 64 * h:64 * h + 64], pTb[:, ci * 128:(ci + 1) * 128],
                                   ci == 0, ci == nch - 1, ["Vt", pTn], ["ps6"])
                            for ci, (ks, vt) in enumerate(chunks):
                                mm(psb[7][hp_, j * 128:(j + 1) * 128], onesb[:, 0:64], pTb[:, ci * 128:(ci + 1) * 128],
                                   ci == 0, ci == nch - 1, ["onesb", pTn], ["ps7"])
                        cnt[0] += 1
                    if d == 1:
                        vn, vd = accn[:, 512 * sbk:512 * sbk + 512], accd[:, 512 * sbk:512 * sbk + 512]
                        bn_, bd_ = psb[6][:, :], psb[7][:, :]
                    elif d == 4:
                        vn, vd = accn[:, sbk:S:4], accd[:, sbk:S:4]
                        bn_, bd_ = psb[6][:, :], psb[7][:, :]
                    else:
                        vn = accn.rearrange("p (i r) -> p r i", r=16)[:, 4 * sbk:4 * sbk + 4, :]
                        vd = accd.rearrange("p (i r) -> p r i", r=16)[:, 4 * sbk:4 * sbk + 4, :]
                        bn_ = psb[6][:, :].rearrange("p (r i) -> p r i", r=4)
                        bd_ = psb[7][:, :].rearrange("p (r i) -> p r i", r=4)
                    if g == 0:
                        cp(V, vn, bn_, [], ["ps6", "accn"])
                        cp(A, vd, bd_, [], ["ps7", "accd"])
                    else:
                        tt(V, vn, vn, bn_, ALU.add, [], ["ps6", "accn"])
                        tt(V, vd, vd, bd_, ALU.add, [], ["ps7", "accd"])
                if os.environ.get("ATT_STOP") == str(g + 1):
                    return
            oc = 14 + pp
            inproj(l, 64 + pp, lambda t4, bank, bn, oc=oc, pp=pp: act(oT[:, oc, t4 * 512:(t4 + 1) * 512], bank[:], AF.Silu,
                                                                        ["colsb"], [bn, "oT%d" % oc], bias=col(C_BIN + 64 + pp)))
            add(V, lambda e: e.reciprocal(out=accd, in_=accd), reads=[], writes=["accd"])
            tt(V, accn, accn, accd, ALU.mult, ["accd"], ["accn"])
            tt(V, oT[:, oc, :], accn, oT[:, oc, :], ALU.mult, ["accn"], ["oT%d" % oc])

    C.attn = attn_phase

    def rwkv_phase(l):
        barrier()
        lw = abf(0, S)
        la = abf(1024, S)
        wup = abf(2048, 1024)
        aup = abf(2560, 1024)
        hbuf = af32(3072, 2050)
        tmpB = af32(3072, S)
        tmpf = af32(5124, S)
        rbf = abf(7172, S)
        kbf = abf(8196, S)
        vbf = abf(9220, S)
        kkbf = abf(10244, S)
        bonus = abf(11268, S)
        ytok = af32(12292, S).rearrange("p (c i) -> p c i", c=32)
        Vtok = abf(14340, S).rearrange("p (c i) -> p c i", c=32)
        reset = af32(15364, 512)
        STf = af32(15876, 64)
        STb = abf(15940, 64)
        Xs = abf(15972, 64)
        Us = abf(16004, 64)
        Wc = af32(16036, 8)
        totc = af32(16044, 8)
        stat = af32(16052, 128)
        ynb = rbf.rearrange("p (c i) -> p c i", c=32)
        tmpf3 = tmpf.rearrange("p (c i) -> p c i", c=32)
        sg = o8f32(0, 512)
        aa = o8f32(512, 512)
        Gc = o8f32(1024, 512)
        tmpG = o8f32(1536, 512)
        E = o8f32(2048, 512)
        bb = o8f32(2560, 512)
        kd = o8f32(3072, 512)
        AR2 = o8bf(2 * 3584, 1024).rearrange("p (c n) -> p c n", c=8)
        AR4 = o8bf(2 * 3584, 1024).rearrange("p (c a j) -> p c a j", c=8, a=2)
        BT = o8bf(2 * 4096, 512)
        KT = o8bf(2 * 4352, 512)
        BHT = o8bf(2 * 4608, 512)
        KHT = o8bf(2 * 4864, 512)
        BHtok = o8bf(2 * 5120, 512).rearrange("p (c j) -> p c j", c=8)
        KHtok = o8bf(2 * 5376, 512).rearrange("p (c j) -> p c j", c=8)
        G1s = o8bf(2 * 5632, 1024).rearrange("p (c n) -> p c n", c=8)
        G2s = o8bf(2 * 6144, 1024).rearrange("p (c n) -> p c n", c=8)
        QP = o8bf(2 * 6656, 1024).rearrange("p (c n) -> p c n", c=8)
        Pn = o8bf(2 * 7168, 512).rearrange("p (c n) -> p c n", c=8)
        m1 = [o8bf(2 * 7424, 512), o8bf(2 * 7680, 512)]
        m2 = [o8bf(2 * 7936, 256), o8bf(2 * 8064, 256)]
        t1f = E

        def c8v(ap):
            return ap.rearrange("p (c j) -> p c j", c=8)

        dma(G, wup, wup_d[l], [], ["wup"])
        dma(G, aup, aup_d[l], [], ["aup"])
        dma("sp", reset, rmask_d[0][:, 1280:1792], [], ["reset"])
        for z in range(2):
            dma(G, m1[z], rmask_d[z][:, 0:512], [], ["m1_%d" % z])
            dma(G, m2[z], rmask_d[z][:, 1024:1280], [], ["m2_%d" % z])
        memset(V, hbuf[:, 0:1], 0.0, ["hbuf"])
        memset(V, hbuf[:, 2049:2050], 0.0, ["hbuf"])

        def shifted(cg, dst, dstname):
            inproj(l, cg, lambda t4, bank, bn: act(hbuf[:, 1 + t4 * 512:1 + (t4 + 1) * 512], bank[:], AF.Identity,
                                                   ["colsb"], [bn, "hbuf"], bias=col(C_BIN + cg)))
            act(tmpf, hbuf[:, 1:2049], AF.Identity, ["hbuf", "c0col"], ["tmpf"], scale=c0col[:, cg:cg + 1])
            stt(tmpf, hbuf[:, 0:2048], col(C_MU0 + cg), tmpf, ALU.mult, ALU.add, ["hbuf", "colsb"], ["tmpf"])
            stt(dst, hbuf[:, 2:2050], col(C_MU1 + cg), tmpf, ALU.mult, ALU.add, ["hbuf", "colsb", "tmpf"], [dstname])

        def pairbank():
            return [nextbank(), nextbank()]

        shifted(24, tmpf, "tmpf")
        act(lw, tmpf, AF.Tanh, ["tmpf"], ["lw"])
        shifted(25, la, "la")

        for hp in range(8):
            shifted(hp, rbf, "rbf")
            shifted(8 + hp, kbf, "kbf")
            shifted(16 + hp, vbf, "vbf")
            inproj(l, 26 + hp, lambda t4, bank, bn, hp=hp: act(oT[:, hp, t4 * 512:(t4 + 1) * 512], bank[:], AF.Silu,
                                                               ["colsb"], [bn, "oT%d" % hp], bias=col(C_BIN + 26 + hp)))
            ts(V, tmpf, kbf, col(C_KK + hp), None, ALU.mult, None, ["kbf", "colsb"], ["tmpf"])
            act(tmpB, tmpf, AF.Square, ["tmpf"], ["hbuf"])
            for t4 in range(4):
                tsl = slice(t4 * 512, (t4 + 1) * 512)
                bank, bn = nextbank()
                mm(bank[:], bones[:], tmpB[:, tsl], True, True, ["bones", "hbuf"], [bn])
                act(tmpB[:, tsl], bank[:], AF.Sqrt, [], [bn, "hbuf"], bias=1e-12)
            add(V, lambda e: e.reciprocal(out=tmpB, in_=tmpB), reads=[], writes=["hbuf"])
            tt(V, kkbf, tmpf, tmpB, ALU.mult, ["tmpf", "hbuf"], ["kkbf"])
            stt(tmpf, rbf, col(C_RK + hp), kbf, ALU.mult, ALU.mult, ["rbf", "kbf", "colsb"], ["tmpf"])
            for t4 in range(4):
                tsl = slice(t4 * 512, (t4 + 1) * 512)
                bank, bn = nextbank()
                mm(bank[:], bones[:], tmpf[:, tsl], True, True, ["bones", "tmpf"], [bn])
                tt(V, bonus[:, tsl], bank[:], vbf[:, tsl], ALU.mult, ["vbf"], [bn, "bonus"])
            for T in range(4):
                pb = pairbank()
                for h in range(2):
                    hs = slice(64 * h, 64 * h + 64)
                    bank, bn = pb[h]
                    for c8 in range(8):
                        c = T * 8 + c8
                        mm(bank[hs, c8 * 64:(c8 + 1) * 64], vbf[hs, c * 64:(c + 1) * 64], identb[hs, 64 * h:64 * h + 64],
                           True, True, ["vbf", "identb"], [bn])
                    cp(A, Vtok[hs, T * 8:(T + 1) * 8, :], c8v(bank[hs, :]), [], [bn, "Vtok"])

            for z in range(2):
                zs = slice(64 * z, 64 * z + 64)
                memset(V, STf, 0.0, ["STf0", "STf1"])
                memset(V, STb, 0.0, ["STb0", "STb1"])
                Tord = range(4) if z == 0 else range(3, -1, -1)
                for T in Tord:
                    tsl = slice(T * 512, (T + 1) * 512)
                    b1, b1n = nextbank()
                    mm(b1[:], wup[zs, hp * 128:(hp + 1) * 128], lw[zs, tsl], True, True, ["wup", "lw"], [b1n])
                    act(sg, b1[:], AF.Sigmoid, ["colsb"], [b1n, "sg"], bias=col(C_W0 + z * 8 + hp))
                    b2, b2n = nextbank()
                    mm(b2[:], aup[zs, hp * 128:(hp + 1) * 128], la[zs, tsl], True, True, ["aup", "la"], [b2n])
                    act(aa, b2[:], AF.Sigmoid, ["colsb"], [b2n, "aa"], bias=col(C_A0 + z * 8 + hp))
                    add(V, lambda e: e.tensor_tensor_scan(out=Gc, data0=reset, data1=sg, initial=0.0, op0=ALU.mult, op1=ALU.add),
                        reads=["reset", "sg"], writes=["Gc"])
                    cp(V, totc, c8v(Gc)[:, :, 63], ["Gc"], ["totc"])
                    totb = totc.unsqueeze(2).to_broadcast([128, 8, 64])
                    if z == 1:
                        tt(V, tmpG, sg, Gc, ALU.subtract, ["sg", "Gc"], ["tmpG"])
                        tt(V, c8v(Gc), c8v(tmpG), totb, ALU.add, ["tmpG", "totc"], ["Gc"])
                    act(E, Gc, AF.Exp, ["Gc"], ["E"], scale=-CDEC)
                    tt(V, AR4[:, :, 1, :], c8v(rbf[:, tsl]), c8v(E), ALU.mult, ["rbf", "E"], ["AR"])
                    tt(V, tmpG, Gc, sg, ALU.subtract, ["Gc", "sg"], ["tmpG"])
                    act(E, tmpG, AF.Exp, ["tmpG"], ["E"], scale=-CDEC)
                    stt(AR4[:, :, 0, :], c8v(kkbf[:, tsl]), -1.0, c8v(E), ALU.mult, ALU.mult, ["kkbf", "E"], ["AR"])
                    ts(V, tmpG, aa, -1.0, col(C_KA + hp), ALU.add, ALU.mult, ["aa", "colsb"], ["tmpG"])
                    stt(kd, tmpG, 1.0, kbf[:, tsl], ALU.add, ALU.mult, ["tmpG", "kbf"], ["kd"])
                    tt(V, bb, kkbf[:, tsl], aa, ALU.mult, ["kkbf", "aa"], ["bb"])
                    act(E, Gc, AF.Exp, ["Gc"], ["E"], scale=CDEC)
                    tt(V, BT, bb, E, ALU.mult, ["bb", "E"], ["BT"])
                    tt(V, KT, kd, E, ALU.mult, ["kd", "E"], ["KT"])
                    tt(V, c8v(tmpG), c8v(Gc), totb, ALU.subtract, ["Gc", "totc"], ["tmpG"])
                    act(E, tmpG, AF.Exp, ["tmpG"], ["E"], scale=CDEC)
                    tt(V, BHT, bb, E, ALU.mult, ["bb", "E"], ["BHT"])
                    tt(V, KHT, kd, E, ALU.mult, ["kd", "E"], ["KHT"])
                    act(Wc, totc, AF.Exp, ["totc"], ["Wc"], scale=-CDEC)
                    for src, srcn, dst, dstn in ((BHT, "BHT", BHtok, "BHtok"), (KHT, "KHT", KHtok, "KHtok")):
                        pb = pairbank()
                        for h in range(2):
                            hs = slice(64 * h, 64 * h + 64)
                            bank, bn = pb[h]
                            for c8 in range(8):
                                mm(bank[hs, c8 * 64:(c8 + 1) * 64], src[hs, c8 * 64:(c8 + 1) * 64], identb[hs, 64 * h:64 * h + 64],
                                   True, True, [srcn, "identb"], [bn])
                            cp(A, dst[hs, :, :], c8v(bank[hs, :]), [], [bn, dstn + str(h)])
                    chains = []
                    for hv in range(2):
                        for h in range(2):
                            ia, ib = {(0, 0): (0, 1), (0, 1): (2, 3), (1, 0): (4, 5), (1, 1): (6, 7)}[(hv, h)]
                            chains.append((hv, h, psb[ia], "ps%d" % ia, psb[ib], "ps%d" % ib))
                    for hv, h, bA, bAn, bB, bBn in chains:
                        cs = slice(hv * 4, hv * 4 + 4)
                        hs = slice(64 * h, 64 * h + 64)
                        for cl in range(4):
                            c8 = hv * 4 + cl
                            mm(bA[hs, cl * 128:(cl + 1) * 128], BT[hs, c8 * 64:(c8 + 1) * 64], AR2[hs, c8, :], True, True,
                               ["BT", "AR"], [bAn])
                        for cl in range(4):
                            c8 = hv * 4 + cl
                            mm(bB[hs, cl * 128:(cl + 1) * 128], KT[hs, c8 * 64:(c8 + 1) * 64], AR2[hs, c8, :], True, True,
                               ["KT", "AR"], [bBn])
                    for hv, h, bA, bAn, bB, bBn in chains:
                        cs = slice(hv * 4, hv * 4 + 4)
                        hs = slice(64 * h, 64 * h + 64)
                        hn = str(h)
                        tt(V, G1s[hs, cs, :], bA[hs, :].rearrange("p (c n) -> p c n", c=4),
                           m1[z][hs, :].rearrange("p (c n) -> p c n", c=4), ALU.mult, ["m1_%d" % z], [bAn, "G1s%d%d" % (hv, h)])
                        tt(V, G2s[hs, cs, :], bB[hs, :].rearrange("p (c n) -> p c n", c=4),
                           m1[z][hs, :].rearrange("p (c n) -> p c n", c=4), ALU.mult, ["m1_%d" % z], [bBn, "G2s%d%d" % (hv, h)])
                    for hv, h, bA, bAn, bB, bBn in chains:
                        hs = slice(64 * h, 64 * h + 64)
                        for cl in range(4):
                            c8 = hv * 4 + cl
                            mm(bA[hs, cl * 64:(cl + 1) * 64], AR4[hs, c8, 0, :], BT[hs, c8 * 64:(c8 + 1) * 64], True, True,
                               ["BT", "AR"], [bAn])
                    for hv, h, bA, bAn, bB, bBn in chains:
                        cs = slice(hv * 4, hv * 4 + 4)
                        hs = slice(64 * h, 64 * h + 64)
                        nm = "%d%d" % (hv, h)
                        tt(V, Pn[hs, cs, :], bA[hs, 0:256].rearrange("p (c n) -> p c n", c=4),
                           m2[z][hs, :].rearrange("p (c n) -> p c n", c=4), ALU.mult, ["m2_%d" % z], [bAn, "Pn" + nm])
                        cp(A, QP[hs, cs, 0:64], eye2[hs, :].unsqueeze(1).to_broadcast([64, 4, 64]), ["eye2"], ["QP" + nm])
                        cp(A, QP[hs, cs, 64:128], G1s[hs, cs, 0:64], ["G1s" + nm], ["QP" + nm])
                    for k in range(6):
                        last = (k == 5)
                        wA = 64 if last else 128
                        for hv, h, bA, bAn, bB, bBn in chains:
                            hs = slice(64 * h, 64 * h + 64)
                            nm = "%d%d" % (hv, h)
                            for cl in range(4):
                                c8 = hv * 4 + cl
                                mm(bA[hs, cl * 128:cl * 128 + wA], Pn[hs, c8, :], QP[hs, c8, 0:wA], True, True,
                                   ["Pn" + nm, "QP" + nm], [bAn])
                            if not last:
                                for cl in range(4):
                                    c8 = hv * 4 + cl
                                    mm(bB[hs, cl * 64:(cl + 1) * 64], QP[hs, c8, 64:128], Pn[hs, c8, :], True, True,
                                       ["Pn" + nm, "QP" + nm], [bBn])
                        for hv, h, bA, bAn, bB, bBn in chains:
                            cs = slice(hv * 4, hv * 4 + 4)
                            hs = slice(64 * h, 64 * h + 64)
                            nm = "%d%d" % (hv, h)
                            bA3 = bA[hs, :].rearrange("p (c n) -> p c n", c=4)
                            tt(V, QP[hs, cs, 0:64], QP[hs, cs, 0:64], bA3[:, :, 0:64], ALU.add, [], [bAn, "QP" + nm])
                            if not last:
                                cp(A, QP[hs, cs, 64:128], bA3[:, :, 64:128], [], [bAn, "QP" + nm])
                                cp(A, Pn[hs, cs, :], bB[hs, 0:256].rearrange("p (c n) -> p c n", c=4), [], [bBn, "Pn" + nm])
                    cord = range(8) if z == 0 else range(7, -1, -1)
                    for c8 in cord:
                        c = T * 8 + c8
                        H = []
                        for h in range(2):
                            H.append((slice(64 * h, 64 * h + 64), str(h), "%d%d" % (c8 // 4, h), psb[4 + h], "ps%d" % (4 + h),
                                      psb[6 + h], "ps%d" % (6 + h)))
                        for hs, hn, nm, cb, cbn, yb, ybn in H:
                            mm(cb[hs, 0:64], AR4[hs, c8, 0, :], STb[hs, :], True, False, ["AR", "STb" + hn], [cbn])
                            mm(cb[hs, 0:64], G2s[hs, c8, 0:64], Vtok[hs, c, :], False, True, ["G2s" + nm, "Vtok"], [cbn])
                        for hs, hn, nm, cb, cbn, yb, ybn in H:
                            cp(A, Xs[hs, :], cb[hs, 0:64], [], [cbn, "Xs" + hn])
                        for hs, hn, nm, cb, cbn, yb, ybn in H:
                            mm(cb[hs, 64:128], QP[hs, c8, 0:64], Xs[hs, :], True, True, ["QP" + nm, "Xs" + hn], [cbn])
                        for hs, hn, nm, cb, cbn, yb, ybn in H:
                            cp(V, Us[hs, :], cb[hs, 64:128], [], [cbn, "Us" + hn])
                        for hs, hn, nm, cb, cbn, yb, ybn in H:
                            mm(cb[hs, 128:192], BHtok[hs, c8, :], Us[hs, :], True, False, ["BHtok" + hn, "Us" + hn], [cbn])
                            mm(cb[hs, 128:192], KHtok[hs, c8, :], Vtok[hs, c, :], False, True, ["KHtok" + hn, "Vtok"], [cbn])
                        for hs, hn, nm, cb, cbn, yb, ybn in H:
                            mm(yb[hs, c8 * 64:(c8 + 1) * 64], AR4[hs, c8, 1, :], STb[hs, :], True, False, ["AR", "STb" + hn], [ybn])
                            mm(yb[hs, c8 * 64:(c8 + 1) * 64], G1s[hs, c8, 64:128], Us[hs, :], False, False,
                               ["G1s" + nm, "Us" + hn], [ybn])
                            mm(yb[hs, c8 * 64:(c8 + 1) * 64], G2s[hs, c8, 64:128], Vtok[hs, c, :], False, True,
                               ["G2s" + nm, "Vtok"], [ybn])
                        for hs, hn, nm, cb, cbn, yb, ybn in H:
                            stt(STf[hs, :], STf[hs, :], Wc[hs, c8:c8 + 1], cb[hs, 128:192], ALU.mult, ALU.add,
                                ["Wc"], [cbn, "STf" + hn])
                        for hs, hn, nm, cb, cbn, yb, ybn in H:
                            cp(A, STb[hs, :], STf[hs, :], ["STf" + hn], ["STb" + hn])
                    for h in range(2):
                        hs = slice(64 * h, 64 * h + 64)
                        yb, ybn = psb[6 + h], "ps%d" % (6 + h)
                        if z == 0:
                            cp(V, ytok[hs, T * 8:(T + 1) * 8, :], c8v(yb[hs, :]), [], [ybn, "ytok"])
                        else:
                            tt(V, ytok[hs, T * 8:(T + 1) * 8, :], ytok[hs, T * 8:(T + 1) * 8, :], c8v(yb[hs, :]), ALU.add,
                               [], [ybn, "ytok"])
            add(V, lambda e: e.tensor_reduce(out=stat[:, 0:32], in_=ytok, axis=AX.X, op=ALU.add), reads=["ytok"], writes=["st0"])
            ts(V, stat[:, 32:64], stat[:, 0:32], -1.0 / 64, None, ALU.mult, None, ["st0"], ["st1"])
            tt(V, ytok, ytok, stat[:, 32:64].unsqueeze(2).to_broadcast([128, 32, 64]), ALU.add, ["st1"], ["ytok"])
            act(tmpf3, ytok, AF.Square, ["ytok"], ["tmpf"])
            add(V, lambda e: e.tensor_reduce(out=stat[:, 64:96], in_=tmpf3, axis=AX.X, op=ALU.add), reads=["tmpf"], writes=["st2"])
            act(stat[:, 96:128], stat[:, 64:96], AF.Sqrt, ["st2"], ["st3"], bias=64e-5, scale=1.0 / 64)
            add(V, lambda e: e.reciprocal(out=stat[:, 96:128], in_=stat[:, 96:128]), reads=[], writes=["st3"])
            tt(V, ynb, ytok, stat[:, 96:128].unsqueeze(2).to_broadcast([128, 32, 64]), ALU.mult, ["ytok", "st3"], ["rbf"])
            for T in range(4):
                tsl = slice(T * 512, (T + 1) * 512)
                pb = pairbank()
                for h in range(2):
                    hs = slice(64 * h, 64 * h + 64)
                    bank, bn = pb[h]
                    for c8 in range(8):
                        mm(bank[hs, c8 * 64:(c8 + 1) * 64], ynb[hs, T * 8 + c8, :], identb[hs, 64 * h:64 * h + 64], True, True,
                           ["rbf", "identb"], [bn])
                    act(t1f[hs, :], bank[hs, :], AF.Identity, ["colsb"], [bn, "t1f" + str(h)],
                        bias=colsb[hs, C_GB + hp:C_GB + hp + 1], scale=colsb[hs, C_GG + hp:C_GG + hp + 1])
                tt(V, t1f, t1f, bonus[:, tsl], ALU.add, ["bonus", "t1f0", "t1f1"], ["t1f0", "t1f1"])
                tt(V, oT[:, hp, tsl], t1f, oT[:, hp, tsl], ALU.mult, ["t1f0", "t1f1"], ["oT%d" % hp])

    C.rwkv = rwkv_phase


    dma("sp", colsb[:], cols_d[0], [], ["colsb"])
    xld = [af32(0, 1024), af32(1024, 1024)]
    for t16 in range(16):
        dma("sp", xld[t16 % 2], x_in[t16 * 128:(t16 + 1) * 128, :], [], ["xld%d" % (t16 % 2)])
        store_xT(xld[t16 % 2], "xld%d" % (t16 % 2), t16)
    import os
    STOP = int(os.environ.get("KSTOP", "9"))
    for l in range(depth if STOP > 1 else 0):
        if l > 0:
            dma("sp", colsb[:], cols_d[l], [], ["colsb"])
        dma(G, vrowb[:], vrow_d[l], [], ["vrowb"])
        tt(V, c0col[:], colsb[:, C_MU0:C_MU0 + 26], colsb[:, C_MU1:C_MU1 + 26], ALU.add, ["colsb"], ["c0col"])
        ts(V, c0col[:], c0col[:], -1.0, 1.0, ALU.mult, ALU.add, [], ["c0col"])
        if C.rwkv is not None and "A" in PH:
            C.rwkv(l)
        else:
            for c in range(8):
                memset(V, oT[:, c, :], 0.0, ["oT%d" % c])
        if "B" in PH:
            pool_phase(l)
        else:
            barrier()
            for c in range(8, 14):
                memset(V, oT[:, c, :], 0.0, ["oT%d" % c])
        if C.attn is not None and "C" in PH:
            C.attn(l)
        else:
            barrier()
            for c in range(14, 16):
                memset(V, oT[:, c, :], 0.0, ["oT%d" % c])
        if dbg is not None and l == depth - 1:
            dtmp = af32(0, S)
            for c in range(16):
                cp(V, dtmp, oT[:, c, :], ["oT%d" % c], ["dtmp"])
                dma("sp", dbg_out[:, c, :], dtmp, ["dtmp"], [])
        if STOP > 2:
            final_phase(l, l == depth - 1)
    P.emit(nc)
    st.close()
    return nc


PH = "ABC"


def EXTRA_PHASES(L):
    pass


def host_prep(inp):
    f = np.float32
    g = lambda k: np.asarray(inp[k], dtype=f)
    colv = lambda v: np.ascontiguousarray(v.reshape(-1, 128).T)
    cols = []
    for l in range(DEPTH):
        parts = [colv(g("b_in")[l]), colv(g("rwkv_mu")[l, 0]), colv(g("rwkv_mu")[l, 1]),
                 colv(g("rwkv_w0")[l, 0]), colv(g("rwkv_w0")[l, 1]), colv(g("rwkv_a0")[l, 0]), colv(g("rwkv_a0")[l, 1]),
                 colv(g("rwkv_k_k")[l]), colv(g("rwkv_k_a")[l]), colv(g("rwkv_r_k")[l].reshape(-1)),
                 colv(g("rwkv_gn_g")[l]), colv(g("rwkv_gn_b")[l]), colv(g("pool_b")[l]), colv(g("pool_scale")[l])]
        cols.append(np.concatenate(parts, axis=1))
    cols = np.stack(cols)
    assert cols.shape == (DEPTH, 128, NCOLS), cols.shape
    shared = {
        "w_in": g("w_in"), "cols": cols,
        "vrow": np.ascontiguousarray(g("b_in")[:, None, 7424:8192]),
        "w_up": np.ascontiguousarray(g("rwkv_w_up").reshape(DEPTH, 128, 1024)),
        "a_up": np.ascontiguousarray(g("rwkv_a_up").reshape(DEPTH, 128, 1024)),
        "pool_w": _blockdiag(g("pool_w")), "proj_a": g("proj_a"), "proj_b": g("proj_b"), "proj_c": g("proj_c"),
        "w_out": g("w_out"),
        "lnrow": np.ascontiguousarray(np.stack([g("ln_g"), g("ln_b")], axis=1)),
    }
    shared.update(host_consts())
    return shared


def _blockdiag(pw):
    out = np.zeros((DEPTH, 768, 768), np.float32)
    for gi in range(4):
        out[:, 192 * gi:192 * gi + 192, 192 * gi:192 * gi + 192] = pw[:, gi]
    return out


def host_consts():
    f = np.float32
    ident = np.eye(128, dtype=f)
    bones = np.zeros((128, 128), f)
    bones[:64, :64] = 1
    bones[64:, 64:] = 1
    perm = np.zeros((128, 128), f)
    for m in range(128):
        c = m % 64
        k = m + 32 if c < 32 else m - 32
        perm[k, m] = 1
    eye2 = np.concatenate([np.eye(64, dtype=f), np.eye(64, dtype=f)], 0)
    cst = np.concatenate([ident, bones, perm, eye2], axis=1)
    inv = np.power(f(10000.0), -np.arange(0, 64, 2, dtype=f) / f(64))
    ang = np.arange(S, dtype=f)[:, None] * inv[None, :]
    ang = np.concatenate([ang, ang], axis=-1).astype(f)
    cosT = np.cos(ang).T.astype(f)
    sinT = np.sin(ang).T.astype(f)
    sign = np.where(np.arange(64) < 32, -1.0, 1.0).astype(f)[:, None]
    rope = np.stack([np.concatenate([cosT, cosT], 0), np.concatenate([sinT * sign, sinT * sign], 0)]).astype(f)
    b = np.arange(128)[:, None]
    a = np.arange(128)[None, :]
    mA = (b >= a).astype(f)
    mB = (a >= b).astype(f)
    mAf = mA * (b >= 64)
    mBl = mB * (b < 64)
    m16 = (np.abs(a - b) <= 64).astype(f)
    amask = np.stack([np.concatenate([mA, mB, mA, mB], 1), np.concatenate([mAf, mB, mAf, mB], 1),
                      np.concatenate([mA, mBl, mA, mBl], 1), np.concatenate([m16, m16, m16, m16], 1)]).astype(f)
    s_ = np.arange(64)[:, None]
    t_ = np.arange(64)[None, :]
    rm = []
    for z in range(2):
        if z == 0:
            strict = (s_ < t_)
            incl = (s_ <= t_)
        else:
            strict = (s_ > t_)
            incl = (s_ >= t_)
        m1 = np.concatenate([strict, incl], 1).astype(f)
        m2 = strict.T.astype(f)
        reset = np.ones((64, 512), f)
        reset[:, ::64] = 0
        row = np.concatenate([np.tile(m1, (1, 4)), np.tile(m1, (1, 4)), np.tile(m2, (1, 4)), reset], 1)
        rm.append(np.concatenate([row, row], 0))
    rmask = np.stack(rm).astype(f)
    pedge = np.ones((128, 4, 16), f)
    for gi in range(4):
        h = 1 << gi
        for e in range(8):
            t = e
            cnt = min(t + h, S - 1) - max(t - h, 0) + 1
            pedge[:, gi, e] = (2 * h + 1) / cnt
            t = S - 8 + e
            cnt = min(t + h, S - 1) - max(t - h, 0) + 1
            pedge[:, gi, 8 + e] = (2 * h + 1) / cnt
    selc = np.zeros((128, 24), f)
    for c in range(6):
        for p in range(128):
            gi = (128 * c + p) // 192
            selc[p, c * 4 + gi] = 1.0 / (2 * (1 << gi) + 1)
    return {"selc": selc, "cst": cst, "rope": rope, "amask": amask, "rmask": rmask, "pedge": pedge}


_NC_CACHE = {}


def kernel(**inputs):
    shared = host_prep(inputs)
    x = np.asarray(inputs["x"], dtype=np.float32)
    if "nc" not in _NC_CACHE:
        _NC_CACHE["nc"] = build()
    nc = _NC_CACHE["nc"]
    in_maps = []
    for c in range(8):
        m = dict(shared)
        m["x"] = np.ascontiguousarray(x[c])
        in_maps.append(m)
    res = run_bass_kernel_spmd(nc, in_maps, core_ids=list(range(8)))
    return np.stack([np.asarray(r["y"], dtype=np.float32) for r in res.results], axis=0)
```

```python
import math
import os
import numpy as np
import ml_dtypes
import concourse.bass as bass
import concourse.mybir as mybir
from concourse.bass_utils import run_bass_kernel_spmd

F32 = mybir.dt.float32
BF16 = mybir.dt.bfloat16
AF = mybir.ActivationFunctionType
ALU = mybir.AluOpType
AX = mybir.AxisListType

ENGS = ("pe", "act", "dve", "pool", "sp")
DMA_POOL = 16


class _Buf:
    __slots__ = ("last_w", "readers", "dma_readers")

    def __init__(self):
        self.last_w = None
        self.readers = {}
        self.dma_readers = []


class _Op:
    __slots__ = ("eng", "idx", "gid", "fn", "deps", "is_dma", "signal", "sem", "val", "waits",
                 "know", "dma_n", "pre_wait")

    def __init__(self, eng, idx, gid, fn, is_dma):
        self.eng = eng
        self.idx = idx
        self.gid = gid
        self.fn = fn
        self.is_dma = is_dma
        self.deps = []
        self.signal = False
        self.sem = None
        self.val = None
        self.waits = []
        self.know = None
        self.dma_n = None
        self.pre_wait = None


class Prog:
    def __init__(self):
        self.ops = {e: [] for e in ENGS}
        self.all = []
        self.bufs = {}
        self.n_dma = {e: 0 for e in ENGS}

    def _buf(self, name):
        b = self.bufs.get(name)
        if b is None:
            b = _Buf()
            self.bufs[name] = b
        return b

    def add(self, eng, fn, reads=(), writes=(), dma=False):
        op = _Op(eng, len(self.ops[eng]), len(self.all), fn, dma)
        deps = {}
        for r in reads:
            b = self._buf(r)
            if b.last_w is not None:
                deps[b.last_w.gid] = b.last_w
        for w in writes:
            b = self._buf(w)
            if b.last_w is not None:
                deps[b.last_w.gid] = b.last_w
            for d in b.readers.values():
                deps[d.gid] = d
            for d in b.dma_readers:
                deps[d.gid] = d
        op.deps = [deps[k] for k in sorted(deps)]
        for r in reads:
            b = self._buf(r)
            if dma:
                b.dma_readers.append(op)
            else:
                b.readers[eng] = op
        for w in writes:
            b = self._buf(w)
            b.last_w = op
            b.readers = {}
            b.dma_readers = []
        if dma:
            op.dma_n = self.n_dma[eng]
            self.n_dma[eng] += 1
        self.ops[eng].append(op)
        self.all.append(op)
        return op

    def resolve(self):
        know = {e: {f: -1 for f in ENGS} for e in ENGS}
        know_dma = {e: set() for e in ENGS}
        sig_count = {e: 0 for e in ENGS}
        for op in self.all:
            E = op.eng
            kn = know[E]
            for d in op.deps:
                if d.is_dma:
                    if d.gid in know_dma[E]:
                        continue
                    know_dma[E].add(d.gid)
                    op.waits.append(d)
                    for f, v in d.know.items():
                        if v > kn[f]:
                            kn[f] = v
                    continue
                F = d.eng
                if F == E:
                    if E == "pe" or op.idx - d.idx > 2:
                        continue
                    if kn[F] >= d.idx:
                        continue
                elif kn[F] >= d.idx:
                    continue
                d.signal = True
                op.waits.append(d)
                kn[F] = max(kn[F], d.idx)
                for f, v in d.know.items():
                    if f != E and v > kn[f]:
                        kn[f] = v
            snap = dict(kn)
            if not op.is_dma:
                snap[E] = op.idx
            op.know = snap
        for e in ENGS:
            c = 0
            for op in self.ops[e]:
                if op.is_dma:
                    continue
                if op.signal:
                    c += 1
                    op.val = c

    def emit(self, nc):
        self.resolve()
        import contextlib
        with contextlib.ExitStack() as st:
            esem = {e: st.enter_context(nc.semaphore("s_" + e)) for e in ENGS}
            dsem = {e: [st.enter_context(nc.semaphore("d_%s%d" % (e, i))) for i in range(DMA_POOL)]
                    for e in ENGS if self.n_dma[e] > 0}
            block = st.enter_context(nc.Block())

            def wait_for(engine, d):
                if d.is_dma:
                    engine.wait_ge(dsem[d.eng][d.dma_n % DMA_POOL], 16 * (d.dma_n // DMA_POOL + 1))
                else:
                    engine.wait_ge(esem[d.eng], d.val)

            def run(engine, e):
                ops = self.ops[e]
                for op in ops:
                    for d in op.waits:
                        wait_for(engine, d)
                    if op.is_dma:
                        n = op.dma_n
                        if n >= DMA_POOL:
                            engine.wait_ge(dsem[e][n % DMA_POOL], 16 * (n // DMA_POOL))
                        ins = op.fn(engine)
                        ins.then_inc(dsem[e][n % DMA_POOL], 16)
                    else:
                        ins = op.fn(engine)
                        if op.signal:
                            ins.then_inc(esem[e], 1)
                nd = self.n_dma[e]
                for i in range(min(nd, DMA_POOL)):
                    n = nd - 1 - i
                    engine.wait_ge(dsem[e][n % DMA_POOL], 16 * (n // DMA_POOL + 1))

            @block.tensor
            def _(eng):
                run(eng, "pe")

            @block.scalar
            def _(eng):
                run(eng, "act")

            @block.vector
            def _(eng):
                run(eng, "dve")

            @block.gpsimd
            def _(eng):
                run(eng, "pool")

            @block.sync
            def _(eng):
                run(eng, "sp")


S = 2048
D = 1024
NIN = 11520
DEPTH = 4
PADX = 256
XW = S + 2 * PADX
ALPHA = (2 * DEPTH) ** 0.25
CDEC = math.exp(-0.5)
C_BIN = 0
C_MU0 = 90
C_MU1 = 116
C_W0 = 142
C_A0 = 158
C_KK = 174
C_KA = 182
C_RK = 190
C_GG = 198
C_GB = 206
C_PB = 214
C_PS = 220
NCOLS = 226


class Ctx:
    pass


def build(depth=DEPTH, dbg=None):
    nc = bass.Bass("TRN2", target_bir_lowering=False)
    P = Prog()
    dt_in = lambda name, shape: nc.dram_tensor(name, shape, F32, kind="ExternalInput").ap()
    x_in = dt_in("x", [S, D])
    w_in = dt_in("w_in", [DEPTH, D, NIN])
    cols_d = dt_in("cols", [DEPTH, 128, NCOLS])
    vrow_d = dt_in("vrow", [DEPTH, 1, 768])
    wup_d = dt_in("w_up", [DEPTH, 128, 1024])
    aup_d = dt_in("a_up", [DEPTH, 128, 1024])
    poolw_d = dt_in("pool_w", [DEPTH, 768, 768])
    proja_d = dt_in("proj_a", [DEPTH, 1024, 1024])
    projb_d = dt_in("proj_b", [DEPTH, 768, 1024])
    projc_d = dt_in("proj_c", [DEPTH, 256, 1024])
    wout_d = dt_in("w_out", [DEPTH, 1024, 1024])
    lnrow_d = dt_in("lnrow", [DEPTH, 2, 1024])
    cst_d = dt_in("cst", [128, 128 * 3 + 64])
    rope_d = dt_in("rope", [2, 128, S])
    amask_d = dt_in("amask", [4, 128, 512])
    rmask_d = dt_in("rmask", [2, 128, 512 + 512 + 256 + 512])
    pedge_d = dt_in("pedge", [128, 4, 16])
    y_out = nc.dram_tensor("y", [S, D], F32, kind="ExternalOutput").ap()
    xres = nc.dram_tensor("xres", [S, D], F32, kind="Internal").ap()
    dbg_out = None
    if dbg is not None:
        dbg_out = nc.dram_tensor("dbg", [128, 16, S], F32, kind="ExternalOutput").ap()

    import contextlib
    st = contextlib.ExitStack()
    sb = lambda name, shape, dt: st.enter_context(nc.sbuf_tensor(name, shape, dt))
    xT = sb("xT", [128, 8, XW], BF16)
    oT = sb("oT", [128, 16, S], BF16)
    wb = [sb("wb%d" % i, [128, 8, 512], BF16) for i in range(2)]
    ARENA = 16384
    arena = sb("arena", [128, ARENA], F32)
    colsb = sb("colsb", [128, NCOLS], F32)
    c0col = sb("c0col", [128, 26], F32)
    identb = sb("identb", [128, 128], BF16)
    permb = sb("permb", [128, 128], BF16)
    onesb = sb("onesb", [128, 128], BF16)
    eye2 = sb("eye2", [128, 64], BF16)
    bones = sb("bones", [128, 128], F32)
    vrowb = sb("vrowb", [1, 768], BF16)
    selcol = sb("selcol", [128, 24], F32)
    mixf = sb("mixf", [128, S], F32)
    selc_d = dt_in("selc", [128, 24])
    psb = [st.enter_context(nc.psum_tensor("psb%d" % i, [128, 512], F32)) for i in range(8)]

    C = Ctx()
    C.bank_i = 0

    def nextbank():
        i = C.bank_i
        C.bank_i = (i + 1) % 4
        return psb[i], "ps%d" % i

    def af32(off, n):
        return arena[:, off:off + n]

    def abf(off, n):
        return arena[:, off:off + n // 2].bitcast(BF16)

    def o8f32(off, n):
        return oT[:, 8:16, :].rearrange("p a b -> p (a b)").bitcast(F32)[:, off:off + n]

    def o8bf(off, n):
        return oT[:, 8:16, :].rearrange("p a b -> p (a b)")[:, off:off + n]

    add = P.add
    V = "dve"
    A = "act"
    G = "pool"

    def mm(out, lhsT, rhs, start, stop, rd, wr):
        add("pe", lambda e: e.matmul(out, lhsT, rhs, start=start, stop=stop), reads=rd, writes=wr)

    def act(out, in_, func, rd, wr, bias=0.0, scale=1.0):
        add(A, lambda e: e.activation(out=out, in_=in_, func=func, bias=bias, scale=scale), reads=rd, writes=wr)

    def tt(eng, out, in0, in1, op, rd, wr):
        eng = V if eng == G else eng
        add(eng, lambda e: e.tensor_tensor(out=out, in0=in0, in1=in1, op=op), reads=rd, writes=wr)

    def ts(eng, out, in0, s1, s2, op0, op1, rd, wr):
        eng = V if eng == G else eng
        if s2 is None:
            add(eng, lambda e: e.tensor_scalar(out=out, in0=in0, scalar1=s1, scalar2=None, op0=op0), reads=rd, writes=wr)
        else:
            add(eng, lambda e: e.tensor_scalar(out=out, in0=in0, scalar1=s1, scalar2=s2, op0=op0, op1=op1), reads=rd, writes=wr)

    def stt(out, in0, scalar, in1, op0, op1, rd, wr):
        add(V, lambda e: e.scalar_tensor_tensor(out=out, in0=in0, scalar=scalar, in1=in1, op0=op0, op1=op1),
            reads=rd, writes=wr)

    def cp(eng, out, in_, rd, wr):
        eng = V if eng == G else eng
        if eng == A:
            add(eng, lambda e: e.activation(out=out, in_=in_, func=AF.Copy), reads=rd, writes=wr)
        else:
            add(eng, lambda e: e.tensor_copy(out=out, in_=in_), reads=rd, writes=wr)

    def dma(q, out, in_, rd, wr):
        add(q, lambda e: e.dma_start(out=out, in_=in_), reads=rd, writes=wr, dma=True)

    def memset(eng, ap, val, wr):
        add(eng, lambda e: e.memset(ap, val), writes=wr)

    bscr = sb("bscr", [128, 8], F32)

    def barrier():
        names = [n for n in P.bufs.keys() if not n.startswith("ps")] + ["bscr"]
        mm(psb[7][:, 0:8], identb[:, 0:128], identb[:, 0:8], True, True, [], names + ["ps7"])
        act(bscr[:, 0:1], bscr[:, 1:2], AF.Copy, [], names)
        memset(V, bscr[:, 2:3], 0.0, names)
        dma("sp", bscr[0:1, 3:4], cst_d[0:1, 0:1], [], names)
        dma(G, bscr[0:1, 4:5], cst_d[0:1, 0:1], [], names)

    memset(V, bscr[:], 0.0, ["bscr"])
    dma(G, identb[:], cst_d[:, 0:128], [], ["identb"])
    dma("sp", bones[:], cst_d[:, 128:256], [], ["bones"])
    dma(G, permb[:], cst_d[:, 256:384], [], ["permb"])
    dma("sp", selcol[:], selc_d, [], ["selcol"])
    dma(G, eye2[:], cst_d[:, 384:448], [], ["eye2"])
    memset(V, onesb[:], 1.0, ["onesb"])
    memset(V, xT[:, :, 0:PADX], 0.0, ["xT"])
    memset(V, xT[:, :, PADX + S:XW], 0.0, ["xT"])

    C.wres = [None, None]
    C.wlast = 0

    def load_w(key, src3):
        for i in range(2):
            if C.wres[i] == key:
                C.wlast = i
                return wb[i], "wb%d" % i
        i = 1 - C.wlast
        C.wres[i] = key
        C.wlast = i
        kc, ncol = src3.shape[1], src3.shape[2]
        dma(G, wb[i][:, 0:kc, 0:ncol], src3, [], ["wb%d" % i])
        return wb[i], "wb%d" % i

    def win_src(l, col0, ncol):
        return w_in[l].rearrange("(k p) n -> p k n", p=128)[:, :, col0:col0 + ncol]

    def inproj(l, cg, evac, wkey=None):
        blk = cg // 4
        ncol = min(512, NIN - blk * 512)
        w, wn = load_w(("win", l, blk), win_src(l, blk * 512, ncol))
        c0 = (cg % 4) * 128
        for t4 in range(4):
            bank, bn = nextbank()
            for k in range(8):
                mm(bank[:], w[:, k, c0:c0 + 128], xT[:, k, PADX + t4 * 512:PADX + (t4 + 1) * 512],
                   k == 0, k == 7, ["xT", wn], [bn])
            evac(t4, bank, bn)

    def col(ci):
        return colsb[:, ci:ci + 1]

    xstage = sb("xstage", [128, 1024], BF16)

    def store_xT(src_f32, srcname, t16):
        cp(A, xstage[:], src_f32, [srcname], ["xstage"])
        bank, bn = nextbank()
        bb = bank[:].bitcast(BF16)
        for k in range(8):
            add("pe", lambda e, k=k: e.transpose(bb[:, k * 128:(k + 1) * 128], xstage[:, k * 128:(k + 1) * 128], identb[:]),
                reads=["xstage", "identb"], writes=[bn])
        cp(V, xT[:, :, PADX + t16 * 128:PADX + (t16 + 1) * 128], bb.rearrange("p (k t) -> p k t", k=8), [], [bn, "xT"])

    def final_phase(l, last):
        barrier()
        mergedT = abf(0, 8 * S).rearrange("p (k t) -> p k t", k=8)
        sig = abf(8192, 512)
        tmpf = af32(8448, 512)
        lng = af32(9216, 1024)
        lnb = af32(10240, 1024)
        xt_ = [af32(11264, 1024), af32(12288, 1024)]
        yt_ = [af32(13312, 1024), af32(14336, 1024)]
        stat = af32(15360, 8)
        dma("sp", lng, lnrow_d[l, 0:1, :].to_broadcast([128, 1024]), [], ["lng"])
        dma("sp", lnb, lnrow_d[l, 1:2, :].to_broadcast([128, 1024]), [], ["lnb"])
        if STOP == 4:
            return
        branches = [(proja_d, 8, 0, 66), (projb_d, 6, 8, 74), (projc_d, 2, 14, 82)]
        for bi, (pd, kc, o0, g0) in enumerate(branches):
            for eb in range(2):
                for ec in range(eb * 4, eb * 4 + 4):
                    for t4 in range(4):
                        tsl = slice(t4 * 512, (t4 + 1) * 512)
                        pw, pwn = load_w(("proj", l, bi, eb), pd[l].rearrange("(k p) n -> p k n", p=128)[:, :, eb * 512:(eb + 1) * 512])
                        b1, b1n = nextbank()
                        for k in range(kc):
                            mm(b1[:], pw[:, k, (ec % 4) * 128:(ec % 4 + 1) * 128], oT[:, o0 + k, tsl], k == 0, k == kc - 1,
                               ["oT%d" % (o0 + k), pwn], [b1n])
                        gw, gwn = load_w(("gate", l, bi, eb), win_src(l, (g0 + eb * 4) * 128, 512))
                        b2, b2n = nextbank()
                        for k in range(8):
                            mm(b2[:], gw[:, k, (ec % 4) * 128:(ec % 4 + 1) * 128], xT[:, k, PADX + t4 * 512:PADX + (t4 + 1) * 512],
                               k == 0, k == 7, ["xT", gwn], [b2n])
                        act(sig, b2[:], AF.Sigmoid, ["colsb"], [b2n, "sig"], bias=col(C_BIN + g0 + ec))
                        if bi == 0:
                            tt(V, mergedT[:, ec, tsl], b1[:], sig, ALU.mult, ["sig"], [b1n, "mg%d" % ec])
                        else:
                            tt(V, tmpf, b1[:], sig, ALU.mult, ["sig"], [b1n, "tmpf"])
                            tt(G, mergedT[:, ec, tsl], mergedT[:, ec, tsl], tmpf, ALU.add, ["tmpf"], ["mg%d" % ec])
        if STOP == 3:
            return
        wo = []
        for fh in range(2):
            wo.append(load_w(("wout", l, fh), wout_d[l].rearrange("(k p) n -> p k n", p=128)[:, :, fh * 512:(fh + 1) * 512]))
        xsrc = x_in if l == 0 else xres
        dst = y_out if last else xres
        for t16 in range(16):
            xt = xt_[t16 % 2]
            yt = yt_[t16 % 2]
            xn, yn = "xt%d" % (t16 % 2), "yt%d" % (t16 % 2)
            dma("sp", xt, xsrc[t16 * 128:(t16 + 1) * 128, :], ["xres"] if l > 0 else [], [xn])
            for fh in range(2):
                w, wn = wo[fh]
                bank, bn = nextbank()
                for k in range(8):
                    mm(bank[:], mergedT[:, k, t16 * 128:(t16 + 1) * 128], w[:, k, :], k == 0, k == 7,
                       ["mg%d" % k, wn], [bn])
                stt(yt[:, fh * 512:(fh + 1) * 512], xt[:, fh * 512:(fh + 1) * 512], ALPHA, bank[:], ALU.mult, ALU.add,
                    [xn], [bn, yn])
            if STOP == 5:
                dma("sp", dst[t16 * 128:(t16 + 1) * 128, :], yt, [yn], ["xres"])
                continue
            add(V, lambda e, yt=yt: e.tensor_reduce(out=stat[:, 0:1], in_=yt, axis=AX.X, op=ALU.add), reads=[yn], writes=["stat0"])
            ts(V, stat[:, 1:2], stat[:, 0:1], -1.0 / D, None, ALU.mult, None, ["stat0"], ["stat1"])
            ts(V, yt, yt, stat[:, 1:2], None, ALU.add, None, ["stat1"], [yn])
            add(A, lambda e, yt=yt, xt=xt: e.activation(out=xt, in_=yt, func=AF.Square, accum_out=stat[:, 2:3]),
                reads=[yn], writes=[xn, "stat2"])
            act(stat[:, 3:4], stat[:, 2:3], AF.Sqrt, ["stat2"], ["stat3"], bias=1e-5, scale=1.0 / D)
            add(V, lambda e: e.reciprocal(out=stat[:, 4:5], in_=stat[:, 3:4]), reads=["stat3"], writes=["stat4"])
            if STOP == 6:
                dma("sp", dst[t16 * 128:(t16 + 1) * 128, :], yt, [yn], ["xres"])
                continue
            if STOP != 8:
                stt(yt, yt, stat[:, 4:5], lng, ALU.mult, ALU.mult, ["stat4", "lng"], [yn])
            if STOP != 7:
                tt(V, yt, yt, lnb, ALU.add, ["lnb"], [yn])
            dma("sp", dst[t16 * 128:(t16 + 1) * 128, :], yt, [yn], ["xres"])
            if not last:
                store_xT(yt, yn, t16)

    def pool_phase(l):
        barrier()
        W = S + 32
        pbuf = af32(0, W)
        a_ = [af32(2080, W), af32(4160, W)]
        mixed = abf(6240, 6 * S).rearrange("p (c t) -> p c t", c=6)
        wgt = abf(12384, 6 * 768).rearrange("p (a b) -> p a b", a=6)
        sacc = af32(14688, 0) if False else None
        pe_t = af32(14688, 64).rearrange("p (g e) -> p g e", g=4)
        t1 = af32(14752, 512)
        memset(V, pbuf[:, 0:16], 0.0, ["pbuf"])
        memset(V, pbuf[:, W - 16:W], 0.0, ["pbuf"])
        dma("sp", pe_t, pedge_d, [], ["pe_t"])
        dma(G, wgt, poolw_d[l].rearrange("(k p) n -> p k n", p=128), [], ["wgt"])
        for c in range(6):
            inproj(l, 40 + c, lambda t4, bank, bn, c=c: act(oT[:, 8 + c, t4 * 512:(t4 + 1) * 512], bank[:], AF.Silu,
                                                             ["colsb"], [bn, "oT%d" % (8 + c)], bias=col(C_BIN + 40 + c)))
        for c in range(6):
            inproj(l, 34 + c, lambda t4, bank, bn, c=c: act(pbuf[:, 16 + t4 * 512:16 + (t4 + 1) * 512], bank[:], AF.Identity,
                                                             ["colsb"], [bn, "pbuf"], bias=col(C_BIN + 34 + c)))
            gs = sorted(set((2 * c + hf) // 3 for hf in range(2)))
            first = True
            for g in gs:
                h = 1 << g
                kk = g + 1
                src, srcn = pbuf, "pbuf"
                for j in range(kk):
                    sh = 1 << j
                    dstt = a_[j % 2]
                    n = W - (2 << j) + 1
                    tt(V, dstt[:, 0:n], src[:, 0:n], src[:, sh:sh + n], ALU.add, [srcn], ["a%d" % (j % 2)])
                    src, srcn = dstt, "a%d" % (j % 2)
                sfin = a_[kk % 2]
                sn = "a%d" % (kk % 2)
                tt(V, sfin[:, 0:S], src[:, 16 - h:16 - h + S], pbuf[:, 16 + h:16 + h + S], ALU.add, [srcn, "pbuf"], [sn])
                tt(V, sfin[:, 0:8], sfin[:, 0:8], pe_t[:, g, 0:8], ALU.mult, ["pe_t"], [sn])
                tt(V, sfin[:, S - 8:S], sfin[:, S - 8:S], pe_t[:, g, 8:16], ALU.mult, ["pe_t"], [sn])
                selw = selcol[:, c * 4 + g:c * 4 + g + 1]
                if first:
                    stt(mixf[:, :], sfin[:, 0:S], selw, pbuf[:, 16:16 + S], ALU.mult, ALU.subtract, [sn, "pbuf", "selcol"], ["mixf"])
                else:
                    stt(mixf[:, :], sfin[:, 0:S], selw, mixf[:, :], ALU.mult, ALU.add, [sn, "selcol"], ["mixf"])
                first = False
            cp(A, mixed[:, c, :], mixf[:, :], ["mixf"], ["mixed"])
        for oc in range(6):
            ics = [ic for ic in range(6) if any((2 * ic + a) // 3 == (2 * oc + b) // 3 for a in range(2) for b in range(2))]
            for t4 in range(4):
                tsl = slice(t4 * 512, (t4 + 1) * 512)
                bank, bn = nextbank()
                for n_, ic in enumerate(ics):
                    mm(bank[:], wgt[:, ic, oc * 128:(oc + 1) * 128], mixed[:, ic, tsl], n_ == 0, n_ == len(ics) - 1,
                       ["wgt", "mixed"], [bn])
                ts(V, t1, bank[:], col(C_PB + oc), col(C_PS + oc), ALU.add, ALU.mult, ["colsb"], [bn, "t1"])
                tt(V, oT[:, 8 + oc, tsl], t1, oT[:, 8 + oc, tsl], ALU.mult, ["t1"], ["oT%d" % (8 + oc)])

    C.attn = None
    C.rwkv = None
    def attn_phase(l):
        barrier()
        Qr = abf(0, XW)
        Kr = abf(1280, XW)
        qraw = abf(2560, S)
        ropec = af32(3584, S)
        ropes = af32(5632, S)
        t1 = af32(7680, 512)
        t2 = af32(8192, 512)
        accn = af32(8704, S)
        accd = af32(10752, S)
        Vt = abf(12800, 20 * 128).rearrange("p (a b) -> p a b", a=20)
        pT = [abf(14080, 512), abf(14336, 512)]
        msk = abf(14592, 4 * 512).rearrange("p (a b) -> p a b", a=4)
        dma("sp", ropec, rope_d[0], [], ["ropec"])
        dma("sp", ropes, rope_d[1], [], ["ropes"])
        for a_ in range(4):
            dma(G, msk[:, a_, :], amask_d[a_], [], ["msk"])
        for buf, nm in ((Qr, "Qr"), (Kr, "Kr")):
            memset(V, buf[:, 0:PADX], 0.0, [nm])
            memset(V, buf[:, PADX + S:XW], 0.0, [nm])
        cnt = [0]
        SUB = int(os.environ.get("ATT_SUB", "9"))
        if SUB == 1:
            return
        for pp in range(2):
            for g in range(3):
                d = (1, 4, 16)[g]
                for cg, dst, nm in ((46 + 2 * g + pp, Qr, "Qr"), (52 + 2 * g + pp, Kr, "Kr")):
                    inproj(l, cg, lambda t4, bank, bn, cg=cg: act(qraw[:, t4 * 512:(t4 + 1) * 512], bank[:], AF.Identity,
                                                                  ["colsb"], [bn, "qraw"], bias=col(C_BIN + cg)))
                    for t4 in range(4):
                        tsl = slice(t4 * 512, (t4 + 1) * 512)
                        bank, bn = nextbank()
                        mm(bank[:], permb[:], qraw[:, tsl], True, True, ["permb", "qraw"], [bn])
                        tt(V, t1, bank[:], ropes[:, tsl], ALU.mult, ["ropes"], [bn, "t1"])
                        tt(V, t2, qraw[:, tsl], ropec[:, tsl], ALU.mult, ["qraw", "ropec"], ["t2"])
                        tt(V, dst[:, PADX + t4 * 512:PADX + (t4 + 1) * 512], t1, t2, ALU.add, ["t1", "t2"], [nm])
                if SUB == 2:
                    return
                wv, wvn = load_w(("wv", l, g, pp), win_src(l, (58 + 2 * g + pp) * 128, 128))
                if d == 1:
                    tsls = [slice(PADX + 128 * m - 64, PADX + 128 * m + 64) for m in range(17)]
                elif d == 4:
                    tsls = []
                    for r in range(4):
                        for m in range(5):
                            s0 = PADX + r + 4 * (128 * m - 64)
                            tsls.append(slice(s0, s0 + 509, 4))
                else:
                    tsls = [slice(PADX + r, PADX + r + 2033, 16) for r in range(16)]
                for j0 in range(0, len(tsls), 4):
                    grp = tsls[j0:j0 + 4]
                    bank, bn = nextbank()
                    for j, sl in enumerate(grp):
                        for k in range(8):
                            mm(bank[:, j * 128:(j + 1) * 128], xT[:, k, sl], wv[:, k, 0:128], k == 0, False, ["xT", wvn], [bn])
                        mm(bank[:, j * 128:(j + 1) * 128], onesb[0:1, 0:128], vrowb[0:1, (2 * g + pp) * 128:(2 * g + pp + 1) * 128],
                           False, True, ["onesb", "vrowb"], [bn])
                    n = len(grp)
                    cp(A, Vt[:, j0:j0 + n, :], bank[:, 0:n * 128].rearrange("p (a b) -> p a b", a=n), [], [bn, "Vt"])
                if SUB == 3:
                    return
                for sbk in range(4):
                    for j in range(4):
                        if d == 1:
                            m = 4 * sbk + j
                            qsl = slice(PADX + 128 * m, PADX + 128 * m + 128)
                            chunks = [(slice(PADX + 128 * m - 64, PADX + 128 * m + 64), m),
                                      (slice(PADX + 128 * m + 64, PADX + 128 * m + 192), m + 1)]
                            mi = 1 if m == 0 else (2 if m == 15 else 0)
                        elif d == 4:
                            r, m = sbk, j
                            q0 = PADX + r + 512 * m
                            qsl = slice(q0, q0 + 509, 4)
                            k0 = PADX + r + 4 * (128 * m - 64)
                            chunks = [(slice(k0, k0 + 509, 4), r * 5 + m), (slice(k0 + 512, k0 + 512 + 509, 4), r * 5 + m + 1)]
                            mi = 1 if m == 0 else (2 if m == 3 else 0)
                        else:
                            r = 4 * sbk + j
                            qsl = slice(PADX + r, PADX + r + 2033, 16)
                            chunks = [(qsl, r)]
                            mi = 3
                        nch = len(chunks)
                        wd = nch * 128
                        for h in range(2):
                            hp_ = slice(64 * h, 64 * h + 64)
                            si = h + 2 * (cnt[0] % 2)
                            sbank, sbn = psb[si], "ps%d" % si
                            pTb, pTn = pT[h], "pT%d" % h
                            for ci, (ks, vt) in enumerate(chunks):
                                mm(sbank[:, ci * 128:(ci + 1) * 128], Kr[hp_, ks], Qr[hp_, qsl], True, True, ["Kr", "Qr"], [sbn])
                            act(pTb[:, 0:wd], sbank[:, 0:wd], AF.Exp, [], [sbn, pTn], scale=0.125)
                            tt(V, pTb[:, 0:wd], pTb[:, 0:wd], msk[:, mi, 0:wd], ALU.mult, ["msk"], [pTn])
                            for ci, (ks, vt) in enumerate(chunks):
                                mm(psb[6][hp_, j * 128:(j + 1) * 128], Vt[:, vt, 64 * h:64 * h + 64], pTb[:, ci * 128:(ci + 1) * 128],
                                   ci == 0, ci == nch - 1, ["Vt", pTn], ["ps6"])
                            for ci, (ks, vt) in enumerate(chunks):
                                mm(psb[7][hp_, j * 128:(j + 1) * 128], onesb[:, 0:64], pTb[:, ci * 128:(ci + 1) * 128],
                                   ci == 0, ci == nch - 1, ["onesb", pTn], ["ps7"])
                        cnt[0] += 1
                    if d == 1:
                        vn, vd = accn[:, 512 * sbk:512 * sbk + 512], accd[:, 512 * sbk:512 * sbk + 512]
                        bn_, bd_ = psb[6][:, :], psb[7][:, :]
                    elif d == 4:
                        vn, vd = accn[:, sbk:S:4], accd[:, sbk:S:4]
                        bn_, bd_ = psb[6][:, :], psb[7][:, :]
                    else:
                        vn = accn.rearrange("p (i r) -> p r i", r=16)[:, 4 * sbk:4 * sbk + 4, :]
                        vd = accd.rearrange("p (i r) -> p r i", r=16)[:, 4 * sbk:4 * sbk + 4, :]
                        bn_ = psb[6][:, :].rearrange("p (r i) -> p r i", r=4)
                        bd_ = psb[7][:, :].rearrange("p (r i) -> p r i", r=4)
                    if g == 0:
                        cp(V, vn, bn_, [], ["ps6", "accn"])
                        cp(A, vd, bd_, [], ["ps7", "accd"])
                    else:
                        tt(V, vn, vn, bn_, ALU.add, [], ["ps6", "accn"])
                        tt(V, vd, vd, bd_, ALU.add, [], ["ps7", "accd"])
                if os.environ.get("ATT_STOP") == str(g + 1):
                    return
            oc = 14 + pp
            inproj(l, 64 + pp, lambda t4, bank, bn, oc=oc, pp=pp: act(oT[:, oc, t4 * 512:(t4 + 1) * 512], bank[:], AF.Silu,
                                                                        ["colsb"], [bn, "oT%d" % oc], bias=col(C_BIN + 64 + pp)))
            add(V, lambda e: e.reciprocal(out=accd, in_=accd), reads=[], writes=["accd"])
            tt(V, accn, accn, accd, ALU.mult, ["accd"], ["accn"])
            tt(V, oT[:, oc, :], accn, oT[:, oc, :], ALU.mult, ["accn"], ["oT%d" % oc])

    C.attn = attn_phase

    def rwkv_phase(l):
        barrier()
        lw = abf(0, S)
        la = abf(1024, S)
        wup = abf(2048, 1024)
        aup = abf(2560, 1024)
        hbuf = af32(3072, 2050)
        tmpB = af32(3072, S)
        tmpf = af32(5124, S)
        rbf = abf(7172, S)
        kbf = abf(8196, S)
        vbf = abf(9220, S)
        kkbf = abf(10244, S)
        bonus = abf(11268, S)
        ytok = af32(12292, S).rearrange("p (c i) -> p c i", c=32)
        Vtok = abf(14340, S).rearrange("p (c i) -> p c i", c=32)
        reset = af32(15364, 512)
        STf = af32(15876, 64)
        STb = abf(15940, 64)
        Xs = abf(15972, 64)
        Us = abf(16004, 64)
        Wc = af32(16036, 8)
        totc = af32(16044, 8)
        stat = af32(16052, 128)
        ynb = rbf.rearrange("p (c i) -> p c i", c=32)
        tmpf3 = tmpf.rearrange("p (c i) -> p c i", c=32)
        sg = o8f32(0, 512)
        aa = o8f32(512, 512)
        Gc = o8f32(1024, 512)
        tmpG = o8f32(1536, 512)
        E = o8f32(2048, 512)
        bb = o8f32(2560, 512)
        kd = o8f32(3072, 512)
        AR2 = o8bf(2 * 3584, 1024).rearrange("p (c n) -> p c n", c=8)
        AR4 = o8bf(2 * 3584, 1024).rearrange("p (c a j) -> p c a j", c=8, a=2)
        BT = o8bf(2 * 4096, 512)
        KT = o8bf(2 * 4352, 512)
        BHT = o8bf(2 * 4608, 512)
        KHT = o8bf(2 * 4864, 512)
        BHtok = o8bf(2 * 5120, 512).rearrange("p (c j) -> p c j", c=8)
        KHtok = o8bf(2 * 5376, 512).rearrange("p (c j) -> p c j", c=8)
        G1s = o8bf(2 * 5632, 1024).rearrange("p (c n) -> p c n", c=8)
        G2s = o8bf(2 * 6144, 1024).rearrange("p (c n) -> p c n", c=8)
        QP = o8bf(2 * 6656, 1024).rearrange("p (c n) -> p c n", c=8)
        Pn = o8bf(2 * 7168, 512).rearrange("p (c n) -> p c n", c=8)
        m1 = [o8bf(2 * 7424, 512), o8bf(2 * 7680, 512)]
        m2 = [o8bf(2 * 7936, 256), o8bf(2 * 8064, 256)]
        t1f = E

        mfb = mixf[:, :].bitcast(BF16)
        Wc1 = af32(16180, 8)
        SETS = [
            (AR2, AR4, G1s, G2s, QP, BHtok, KHtok, Wc),
            (mfb[:, 0:1024].rearrange("p (c n) -> p c n", c=8), mfb[:, 0:1024].rearrange("p (c a j) -> p c a j", c=8, a=2),
             mfb[:, 1024:2048].rearrange("p (c n) -> p c n", c=8), mfb[:, 2048:3072].rearrange("p (c n) -> p c n", c=8),
             mfb[:, 3072:4096].rearrange("p (c n) -> p c n", c=8),
             xstage[:, 0:512].rearrange("p (c j) -> p c j", c=8), xstage[:, 512:1024].rearrange("p (c j) -> p c j", c=8), Wc1),
        ]

        STATE = [(STf, STb, Xs, Us), (af32(16188, 64), abf(16252, 64), abf(16284, 64), abf(16316, 64))]

        def c8v(ap):
            return ap.rearrange("p (c j) -> p c j", c=8)

        dma(G, wup, wup_d[l], [], ["wup"])
        dma(G, aup, aup_d[l], [], ["aup"])
        dma("sp", reset, rmask_d[0][:, 1280:1792], [], ["reset"])
        for z in range(2):
            dma(G, m1[z], rmask_d[z][:, 0:512], [], ["m1_%d" % z])
            dma(G, m2[z], rmask_d[z][:, 1024:1280], [], ["m2_%d" % z])
        memset(V, hbuf[:, 0:1], 0.0, ["hbuf"])
        memset(V, hbuf[:, 2049:2050], 0.0, ["hbuf"])

        def shifted(cg, dst, dstname):
            inproj(l, cg, lambda t4, bank, bn: act(hbuf[:, 1 + t4 * 512:1 + (t4 + 1) * 512], bank[:], AF.Identity,
                                                   ["colsb"], [bn, "hbuf"], bias=col(C_BIN + cg)))
            act(tmpf, hbuf[:, 1:2049], AF.Identity, ["hbuf", "c0col"], ["tmpf"], scale=c0col[:, cg:cg + 1])
            stt(tmpf, hbuf[:, 0:2048], col(C_MU0 + cg), tmpf, ALU.mult, ALU.add, ["hbuf", "colsb"], ["tmpf"])
            stt(dst, hbuf[:, 2:2050], col(C_MU1 + cg), tmpf, ALU.mult, ALU.add, ["hbuf", "colsb", "tmpf"], [dstname])

        def pairbank():
            return [nextbank(), nextbank()]

        shifted(24, tmpf, "tmpf")
        act(lw, tmpf, AF.Tanh, ["tmpf"], ["lw"])
        shifted(25, la, "la")

        for hp in range(8):
            shifted(hp, rbf, "rbf")
            shifted(8 + hp, kbf, "kbf")
            shifted(16 + hp, vbf, "vbf")
            inproj(l, 26 + hp, lambda t4, bank, bn, hp=hp: act(oT[:, hp, t4 * 512:(t4 + 1) * 512], bank[:], AF.Silu,
                                                               ["colsb"], [bn, "oT%d" % hp], bias=col(C_BIN + 26 + hp)))
            ts(V, tmpf, kbf, col(C_KK + hp), None, ALU.mult, None, ["kbf", "colsb"], ["tmpf"])
            act(tmpB, tmpf, AF.Square, ["tmpf"], ["hbuf"])
            for t4 in range(4):
                tsl = slice(t4 * 512, (t4 + 1) * 512)
                bank, bn = nextbank()
                mm(bank[:], bones[:], tmpB[:, tsl], True, True, ["bones", "hbuf"], [bn])
                act(tmpB[:, tsl], bank[:], AF.Sqrt, [], [bn, "hbuf"], bias=1e-12)
            add(V, lambda e: e.reciprocal(out=tmpB, in_=tmpB), reads=[], writes=["hbuf"])
            tt(V, kkbf, tmpf, tmpB, ALU.mult, ["tmpf", "hbuf"], ["kkbf"])
            stt(tmpf, rbf, col(C_RK + hp), kbf, ALU.mult, ALU.mult, ["rbf", "kbf", "colsb"], ["tmpf"])
            for t4 in range(4):
                tsl = slice(t4 * 512, (t4 + 1) * 512)
                bank, bn = nextbank()
                mm(bank[:], bones[:], tmpf[:, tsl], True, True, ["bones", "tmpf"], [bn])
                tt(V, bonus[:, tsl], bank[:], vbf[:, tsl], ALU.mult, ["vbf"], [bn, "bonus"])
            for T in range(4):
                pb = pairbank()
                for h in range(2):
                    hs = slice(64 * h, 64 * h + 64)
                    bank, bn = pb[h]
                    for c8 in range(8):
                        c = T * 8 + c8
                        mm(bank[hs, c8 * 64:(c8 + 1) * 64], vbf[hs, c * 64:(c + 1) * 64], identb[hs, 64 * h:64 * h + 64],
                           True, True, ["vbf", "identb"], [bn])
                    cp(A, Vtok[hs, T * 8:(T + 1) * 8, :], c8v(bank[hs, :]), [], [bn, "Vtok"])

            items = [(0, T) for T in range(4)] + [(1, T) for T in range(3, -1, -1)]

            def produce(z, T, s):
                zs = slice(64 * z, 64 * z + 64)
                tsl = slice(T * 512, (T + 1) * 512)
                AR2_, AR4_, G1s_, G2s_, QP_, BHtok_, KHtok_, Wc_ = SETS[s]
                ss = str(s)
                ARn = "AR" + ss
                b1, b1n = nextbank()
                mm(b1[:], wup[zs, hp * 128:(hp + 1) * 128], lw[zs, tsl], True, True, ["wup", "lw"], [b1n])
                act(sg, b1[:], AF.Sigmoid, ["colsb"], [b1n, "sg"], bias=col(C_W0 + z * 8 + hp))
                b2, b2n = nextbank()
                mm(b2[:], aup[zs, hp * 128:(hp + 1) * 128], la[zs, tsl], True, True, ["aup", "la"], [b2n])
                act(aa, b2[:], AF.Sigmoid, ["colsb"], [b2n, "aa"], bias=col(C_A0 + z * 8 + hp))
                yield
                add(V, lambda e: e.tensor_tensor_scan(out=Gc, data0=reset, data1=sg, initial=0.0, op0=ALU.mult, op1=ALU.add),
                    reads=["reset", "sg"], writes=["Gc"])
                cp(V, totc, c8v(Gc)[:, :, 63], ["Gc"], ["totc"])
                totb = totc.unsqueeze(2).to_broadcast([128, 8, 64])
                if z == 1:
                    tt(V, tmpG, sg, Gc, ALU.subtract, ["sg", "Gc"], ["tmpG"])
                    tt(V, c8v(Gc), c8v(tmpG), totb, ALU.add, ["tmpG", "totc"], ["Gc"])
                yield
                act(E, Gc, AF.Exp, ["Gc"], ["E"], scale=-CDEC)
                tt(V, AR4_[:, :, 1, :], c8v(rbf[:, tsl]), c8v(E), ALU.mult, ["rbf", "E"], [ARn])
                tt(V, tmpG, Gc, sg, ALU.subtract, ["Gc", "sg"], ["tmpG"])
                yield
                act(E, tmpG, AF.Exp, ["tmpG"], ["E"], scale=-CDEC)
                stt(AR4_[:, :, 0, :], c8v(kkbf[:, tsl]), -1.0, c8v(E), ALU.mult, ALU.mult, ["kkbf", "E"], [ARn])
                ts(V, tmpG, aa, -1.0, col(C_KA + hp), ALU.add, ALU.mult, ["aa", "colsb"], ["tmpG"])
                yield
                stt(kd, tmpG, 1.0, kbf[:, tsl], ALU.add, ALU.mult, ["tmpG", "kbf"], ["kd"])
                tt(V, bb, kkbf[:, tsl], aa, ALU.mult, ["kkbf", "aa"], ["bb"])
                act(E, Gc, AF.Exp, ["Gc"], ["E"], scale=CDEC)
                yield
                tt(V, BT, bb, E, ALU.mult, ["bb", "E"], ["BT"])
                tt(V, KT, kd, E, ALU.mult, ["kd", "E"], ["KT"])
                tt(V, c8v(tmpG), c8v(Gc), totb, ALU.subtract, ["Gc", "totc"], ["tmpG"])
                yield
                act(E, tmpG, AF.Exp, ["tmpG"], ["E"], scale=CDEC)
                tt(V, BHT, bb, E, ALU.mult, ["bb", "E"], ["BHT"])
                tt(V, KHT, kd, E, ALU.mult, ["kd", "E"], ["KHT"])
                act(Wc_, totc, AF.Exp, ["totc"], ["Wc" + ss], scale=-CDEC)
                yield
                for src, srcn, dst, dstn in ((BHT, "BHT", BHtok_, "BHtok" + ss), (KHT, "KHT", KHtok_, "KHtok" + ss)):
                    pb = pairbank()
                    for h in range(2):
                        hs = slice(64 * h, 64 * h + 64)
                        bank, bn = pb[h]
                        for c8 in range(8):
                            mm(bank[hs, c8 * 64:(c8 + 1) * 64], src[hs, c8 * 64:(c8 + 1) * 64], identb[hs, 64 * h:64 * h + 64],
                               True, True, [srcn, "identb"], [bn])
                        cp(A, dst[hs, :, :], c8v(bank[hs, :]), [], [bn, dstn + str(h)])
                    yield
                chains = []
                for hv in range(2):
                    for h in range(2):
                        ia, ib = {(0, 0): (0, 1), (0, 1): (2, 3), (1, 0): (4, 5), (1, 1): (6, 7)}[(hv, h)]
                        chains.append((hv, h, psb[ia], "ps%d" % ia, psb[ib], "ps%d" % ib))
                for hv, h, bA, bAn, bB, bBn in chains:
                    hs = slice(64 * h, 64 * h + 64)
                    for cl in range(4):
                        c8 = hv * 4 + cl
                        mm(bA[hs, cl * 128:(cl + 1) * 128], BT[hs, c8 * 64:(c8 + 1) * 64], AR2_[hs, c8, :], True, True,
                           ["BT", ARn], [bAn])
                    for cl in range(4):
                        c8 = hv * 4 + cl
                        mm(bB[hs, cl * 128:(cl + 1) * 128], KT[hs, c8 * 64:(c8 + 1) * 64], AR2_[hs, c8, :], True, True,
                           ["KT", ARn], [bBn])
                for hv, h, bA, bAn, bB, bBn in chains:
                    cs = slice(hv * 4, hv * 4 + 4)
                    hs = slice(64 * h, 64 * h + 64)
                    nm = "%s%d%d" % (ss, hv, h)
                    tt(V, G1s_[hs, cs, :], bA[hs, :].rearrange("p (c n) -> p c n", c=4),
                       m1[z][hs, :].rearrange("p (c n) -> p c n", c=4), ALU.mult, ["m1_%d" % z], [bAn, "G1s" + nm])
                    tt(V, G2s_[hs, cs, :], bB[hs, :].rearrange("p (c n) -> p c n", c=4),
                       m1[z][hs, :].rearrange("p (c n) -> p c n", c=4), ALU.mult, ["m1_%d" % z], [bBn, "G2s" + nm])
                for hv, h, bA, bAn, bB, bBn in chains:
                    hs = slice(64 * h, 64 * h + 64)
                    for cl in range(4):
                        c8 = hv * 4 + cl
                        mm(bA[hs, cl * 64:(cl + 1) * 64], AR4_[hs, c8, 0, :], BT[hs, c8 * 64:(c8 + 1) * 64], True, True,
                           ["BT", ARn], [bAn])
                for hv, h, bA, bAn, bB, bBn in chains:
                    cs = slice(hv * 4, hv * 4 + 4)
                    hs = slice(64 * h, 64 * h + 64)
                    nm = "%s%d%d" % (ss, hv, h)
                    pn = "Pn%d%d" % (hv, h)
                    tt(V, Pn[hs, cs, :], bA[hs, 0:256].rearrange("p (c n) -> p c n", c=4),
                       m2[z][hs, :].rearrange("p (c n) -> p c n", c=4), ALU.mult, ["m2_%d" % z], [bAn, pn])
                    cp(A, QP_[hs, cs, 0:64], eye2[hs, :].unsqueeze(1).to_broadcast([64, 4, 64]), ["eye2"], ["QP" + nm])
                    cp(A, QP_[hs, cs, 64:128], G1s_[hs, cs, 0:64], ["G1s" + nm], ["QP" + nm])
                for k in range(6):
                    last = (k == 5)
                    wA = 64 if last else 128
                    for hv, h, bA, bAn, bB, bBn in chains:
                        hs = slice(64 * h, 64 * h + 64)
                        nm = "%s%d%d" % (ss, hv, h)
                        pn = "Pn%d%d" % (hv, h)
                        for cl in range(4):
                            c8 = hv * 4 + cl
                            mm(bA[hs, cl * 128:cl * 128 + wA], Pn[hs, c8, :], QP_[hs, c8, 0:wA], True, True,
                               [pn, "QP" + nm], [bAn])
                        if not last:
                            for cl in range(4):
                                c8 = hv * 4 + cl
                                mm(bB[hs, cl * 64:(cl + 1) * 64], QP_[hs, c8, 64:128], Pn[hs, c8, :], True, True,
                                   [pn, "QP" + nm], [bBn])
                    for hv, h, bA, bAn, bB, bBn in chains:
                        cs = slice(hv * 4, hv * 4 + 4)
                        hs = slice(64 * h, 64 * h + 64)
                        nm = "%s%d%d" % (ss, hv, h)
                        pn = "Pn%d%d" % (hv, h)
                        bA3 = bA[hs, :].rearrange("p (c n) -> p c n", c=4)
                        tt(V, QP_[hs, cs, 0:64], QP_[hs, cs, 0:64], bA3[:, :, 0:64], ALU.add, [], [bAn, "QP" + nm])
                        if not last:
                            cp(A, QP_[hs, cs, 64:128], bA3[:, :, 64:128], [], [bAn, "QP" + nm])
                            cp(A, Pn[hs, cs, :], bB[hs, 0:256].rearrange("p (c n) -> p c n", c=4), [], [bBn, pn])
                yield

            def consume(z, T, s, first):
                AR2_, AR4_, G1s_, G2s_, QP_, BHtok_, KHtok_, Wc_ = SETS[s]
                STf_, STb_, Xs_, Us_ = STATE[z]
                ss = str(s)
                zn = str(z)
                ARn = "AR" + ss
                if first:
                    memset(V, STf_, 0.0, ["STf" + zn + "0", "STf" + zn + "1"])
                    memset(V, STb_, 0.0, ["STb" + zn + "0", "STb" + zn + "1"])
                cord = range(8) if z == 0 else range(7, -1, -1)
                for c8 in cord:
                    c = T * 8 + c8
                    H = []
                    for h in range(2):
                        H.append((slice(64 * h, 64 * h + 64), zn + str(h), "%s%d%d" % (ss, c8 // 4, h),
                                  psb[2 * z + h], "ps%d" % (2 * z + h), psb[4 + 2 * z + h], "ps%d" % (4 + 2 * z + h)))
                    for hs, hn, nm, cb, cbn, yb, ybn in H:
                        mm(cb[hs, 0:64], AR4_[hs, c8, 0, :], STb_[hs, :], True, False, [ARn, "STb" + hn], [cbn])
                        mm(cb[hs, 0:64], G2s_[hs, c8, 0:64], Vtok[hs, c, :], False, True, ["G2s" + nm, "Vtok"], [cbn])
                    yield
                    for hs, hn, nm, cb, cbn, yb, ybn in H:
                        cp(A, Xs_[hs, :], cb[hs, 0:64], [], [cbn, "Xs" + hn])
                    for hs, hn, nm, cb, cbn, yb, ybn in H:
                        mm(cb[hs, 64:128], QP_[hs, c8, 0:64], Xs_[hs, :], True, True, ["QP" + nm, "Xs" + hn], [cbn])
                    yield
                    for hs, hn, nm, cb, cbn, yb, ybn in H:
                        cp(V, Us_[hs, :], cb[hs, 64:128], [], [cbn, "Us" + hn])
                    for hs, hn, nm, cb, cbn, yb, ybn in H:
                        mm(cb[hs, 128:192], BHtok_[hs, c8, :], Us_[hs, :], True, False, ["BHtok" + ss + hn[1], "Us" + hn], [cbn])
                        mm(cb[hs, 128:192], KHtok_[hs, c8, :], Vtok[hs, c, :], False, True, ["KHtok" + ss + hn[1], "Vtok"], [cbn])
                    for hs, hn, nm, cb, cbn, yb, ybn in H:
                        mm(yb[hs, c8 * 64:(c8 + 1) * 64], AR4_[hs, c8, 1, :], STb_[hs, :], True, False, [ARn, "STb" + hn], [ybn])
                        mm(yb[hs, c8 * 64:(c8 + 1) * 64], G1s_[hs, c8, 64:128], Us_[hs, :], False, False,
                           ["G1s" + nm, "Us" + hn], [ybn])
                        mm(yb[hs, c8 * 64:(c8 + 1) * 64], G2s_[hs, c8, 64:128], Vtok[hs, c, :], False, True,
                           ["G2s" + nm, "Vtok"], [ybn])
                    yield
                    for hs, hn, nm, cb, cbn, yb, ybn in H:
                        stt(STb_[hs, :], STb_[hs, :], Wc_[hs, c8:c8 + 1], cb[hs, 128:192], ALU.mult, ALU.add,
                            ["Wc" + ss], [cbn, "STb" + hn])
                    yield
                for h in range(2):
                    hs = slice(64 * h, 64 * h + 64)
                    yb, ybn = psb[4 + 2 * z + h], "ps%d" % (4 + 2 * z + h)
                    tt(V, ytok[hs, T * 8:(T + 1) * 8, :], ytok[hs, T * 8:(T + 1) * 8, :], c8v(yb[hs, :]), ALU.add,
                       [], [ybn, "ytok%d" % T])
                yield

            for T_ in range(4):
                memset(V, ytok[:, T_ * 8:(T_ + 1) * 8, :], 0.0, ["ytok%d" % T_])
            for i in range(4):
                for _ in produce(0, i, 0):
                    pass
                for _ in produce(1, 3 - i, 1):
                    pass
                gens = [consume(0, i, 0, i == 0), consume(1, 3 - i, 1, i == 0)]
                while gens:
                    for g_ in list(gens):
                        try:
                            next(g_)
                        except StopIteration:
                            gens.remove(g_)
            add(V, lambda e: e.tensor_reduce(out=stat[:, 0:32], in_=ytok, axis=AX.X, op=ALU.add), reads=["ytok0", "ytok1", "ytok2", "ytok3"], writes=["st0"])
            ts(V, stat[:, 32:64], stat[:, 0:32], -1.0 / 64, None, ALU.mult, None, ["st0"], ["st1"])
            tt(V, ytok, ytok, stat[:, 32:64].unsqueeze(2).to_broadcast([128, 32, 64]), ALU.add, ["st1"], ["ytok0", "ytok1", "ytok2", "ytok3"])
            act(tmpf3, ytok, AF.Square, ["ytok0", "ytok1", "ytok2", "ytok3"], ["tmpf"])
            add(V, lambda e: e.tensor_reduce(out=stat[:, 64:96], in_=tmpf3, axis=AX.X, op=ALU.add), reads=["tmpf"], writes=["st2"])
            act(stat[:, 96:128], stat[:, 64:96], AF.Sqrt, ["st2"], ["st3"], bias=64e-5, scale=1.0 / 64)
            add(V, lambda e: e.reciprocal(out=stat[:, 96:128], in_=stat[:, 96:128]), reads=[], writes=["st3"])
            tt(V, ynb, ytok, stat[:, 96:128].unsqueeze(2).to_broadcast([128, 32, 64]), ALU.mult, ["ytok0", "ytok1", "ytok2", "ytok3", "st3"], ["rbf"])
            for T in range(4):
                tsl = slice(T * 512, (T + 1) * 512)
                pb = pairbank()
                for h in range(2):
                    hs = slice(64 * h, 64 * h + 64)
                    bank, bn = pb[h]
                    for c8 in range(8):
                        mm(bank[hs, c8 * 64:(c8 + 1) * 64], ynb[hs, T * 8 + c8, :], identb[hs, 64 * h:64 * h + 64], True, True,
                           ["rbf", "identb"], [bn])
                    act(t1f[hs, :], bank[hs, :], AF.Identity, ["colsb"], [bn, "t1f" + str(h)],
                        bias=colsb[hs, C_GB + hp:C_GB + hp + 1], scale=colsb[hs, C_GG + hp:C_GG + hp + 1])
                tt(V, t1f, t1f, bonus[:, tsl], ALU.add, ["bonus", "t1f0", "t1f1"], ["t1f0", "t1f1"])
                tt(V, oT[:, hp, tsl], t1f, oT[:, hp, tsl], ALU.mult, ["t1f0", "t1f1"], ["oT%d" % hp])

    C.rwkv = rwkv_phase


    dma("sp", colsb[:], cols_d[0], [], ["colsb"])
    xld = [af32(0, 1024), af32(1024, 1024)]
    for t16 in range(16):
        dma("sp", xld[t16 % 2], x_in[t16 * 128:(t16 + 1) * 128, :], [], ["xld%d" % (t16 % 2)])
        store_xT(xld[t16 % 2], "xld%d" % (t16 % 2), t16)
    import os
    STOP = int(os.environ.get("KSTOP", "9"))
    for l in range(depth if STOP > 1 else 0):
        if l > 0:
            dma("sp", colsb[:], cols_d[l], [], ["colsb"])
        dma(G, vrowb[:], vrow_d[l], [], ["vrowb"])
        tt(V, c0col[:], colsb[:, C_MU0:C_MU0 + 26], colsb[:, C_MU1:C_MU1 + 26], ALU.add, ["colsb"], ["c0col"])
        ts(V, c0col[:], c0col[:], -1.0, 1.0, ALU.mult, ALU.add, [], ["c0col"])
        if C.rwkv is not None and "A" in PH:
            C.rwkv(l)
        else:
            for c in range(8):
                memset(V, oT[:, c, :], 0.0, ["oT%d" % c])
        if "B" in PH:
            pool_phase(l)
        else:
            barrier()
            for c in range(8, 14):
                memset(V, oT[:, c, :], 0.0, ["oT%d" % c])
        if C.attn is not None and "C" in PH:
            C.attn(l)
        else:
            barrier()
            for c in range(14, 16):
                memset(V, oT[:, c, :], 0.0, ["oT%d" % c])
        if dbg is not None and l == depth - 1:
            dtmp = af32(0, S)
            for c in range(16):
                cp(V, dtmp, oT[:, c, :], ["oT%d" % c], ["dtmp"])
                dma("sp", dbg_out[:, c, :], dtmp, ["dtmp"], [])
        if STOP > 2:
            final_phase(l, l == depth - 1)
    P.emit(nc)
    st.close()
    return nc


PH = "ABC"


def EXTRA_PHASES(L):
    pass


def host_prep(inp):
    f = np.float32
    g = lambda k: np.asarray(inp[k], dtype=f)
    colv = lambda v: np.ascontiguousarray(v.reshape(-1, 128).T)
    cols = []
    for l in range(DEPTH):
        parts = [colv(g("b_in")[l]), colv(g("rwkv_mu")[l, 0]), colv(g("rwkv_mu")[l, 1]),
                 colv(g("rwkv_w0")[l, 0]), colv(g("rwkv_w0")[l, 1]), colv(g("rwkv_a0")[l, 0]), colv(g("rwkv_a0")[l, 1]),
                 colv(g("rwkv_k_k")[l]), colv(g("rwkv_k_a")[l]), colv(g("rwkv_r_k")[l].reshape(-1)),
                 colv(g("rwkv_gn_g")[l]), colv(g("rwkv_gn_b")[l]), colv(g("pool_b")[l]), colv(g("pool_scale")[l])]
        cols.append(np.concatenate(parts, axis=1))
    cols = np.stack(cols)
    assert cols.shape == (DEPTH, 128, NCOLS), cols.shape
    shared = {
        "w_in": g("w_in"), "cols": cols,
        "vrow": np.ascontiguousarray(g("b_in")[:, None, 7424:8192]),
        "w_up": np.ascontiguousarray(g("rwkv_w_up").reshape(DEPTH, 128, 1024)),
        "a_up": np.ascontiguousarray(g("rwkv_a_up").reshape(DEPTH, 128, 1024)),
        "pool_w": _blockdiag(g("pool_w")), "proj_a": g("proj_a"), "proj_b": g("proj_b"), "proj_c": g("proj_c"),
        "w_out": g("w_out"),
        "lnrow": np.ascontiguousarray(np.stack([g("ln_g"), g("ln_b")], axis=1)),
    }
    shared.update(host_consts())
    return shared


def _blockdiag(pw):
    out = np.zeros((DEPTH, 768, 768), np.float32)
    for gi in range(4):
        out[:, 192 * gi:192 * gi + 192, 192 * gi:192 * gi + 192] = pw[:, gi]
    return out


def host_consts():
    f = np.float32
    ident = np.eye(128, dtype=f)
    bones = np.zeros((128, 128), f)
    bones[:64, :64] = 1
    bones[64:, 64:] = 1
    perm = np.zeros((128, 128), f)
    for m in range(128):
        c = m % 64
        k = m + 32 if c < 32 else m - 32
        perm[k, m] = 1
    eye2 = np.concatenate([np.eye(64, dtype=f), np.eye(64, dtype=f)], 0)
    cst = np.concatenate([ident, bones, perm, eye2], axis=1)
    inv = np.power(f(10000.0), -np.arange(0, 64, 2, dtype=f) / f(64))
    ang = np.arange(S, dtype=f)[:, None] * inv[None, :]
    ang = np.concatenate([ang, ang], axis=-1).astype(f)
    cosT = np.cos(ang).T.astype(f)
    sinT = np.sin(ang).T.astype(f)
    sign = np.where(np.arange(64) < 32, -1.0, 1.0).astype(f)[:, None]
    rope = np.stack([np.concatenate([cosT, cosT], 0), np.concatenate([sinT * sign, sinT * sign], 0)]).astype(f)
    b = np.arange(128)[:, None]
    a = np.arange(128)[None, :]
    mA = (b >= a).astype(f)
    mB = (a >= b).astype(f)
    mAf = mA * (b >= 64)
    mBl = mB * (b < 64)
    m16 = (np.abs(a - b) <= 64).astype(f)
    amask = np.stack([np.concatenate([mA, mB, mA, mB], 1), np.concatenate([mAf, mB, mAf, mB], 1),
                      np.concatenate([mA, mBl, mA, mBl], 1), np.concatenate([m16, m16, m16, m16], 1)]).astype(f)
    s_ = np.arange(64)[:, None]
    t_ = np.arange(64)[None, :]
    rm = []
    for z in range(2):
        if z == 0:
            strict = (s_ < t_)
            incl = (s_ <= t_)
        else:
            strict = (s_ > t_)
            incl = (s_ >= t_)
        m1 = np.concatenate([strict, incl], 1).astype(f)
        m2 = strict.T.astype(f)
        reset = np.ones((64, 512), f)
        reset[:, ::64] = 0
        row = np.concatenate([np.tile(m1, (1, 4)), np.tile(m1, (1, 4)), np.tile(m2, (1, 4)), reset], 1)
        rm.append(np.concatenate([row, row], 0))
    rmask = np.stack(rm).astype(f)
    pedge = np.ones((128, 4, 16), f)
    for gi in range(4):
        h = 1 << gi
        for e in range(8):
            t = e
            cnt = min(t + h, S - 1) - max(t - h, 0) + 1
            pedge[:, gi, e] = (2 * h + 1) / cnt
            t = S - 8 + e
            cnt = min(t + h, S - 1) - max(t - h, 0) + 1
            pedge[:, gi, 8 + e] = (2 * h + 1) / cnt
    selc = np.zeros((128, 24), f)
    for c in range(6):
        for p in range(128):
            gi = (128 * c + p) // 192
            selc[p, c * 4 + gi] = 1.0 / (2 * (1 << gi) + 1)
    return {"selc": selc, "cst": cst, "rope": rope, "amask": amask, "rmask": rmask, "pedge": pedge}


_NC_CACHE = {}


def kernel(**inputs):
    shared = host_prep(inputs)
    x = np.asarray(inputs["x"], dtype=np.float32)
    if "nc" not in _NC_CACHE:
        _NC_CACHE["nc"] = build()
    nc = _NC_CACHE["nc"]
    in_maps = []
    for c in range(8):
        m = dict(shared)
        m["x"] = np.ascontiguousarray(x[c])
        in_maps.append(m)
    res = run_bass_kernel_spmd(nc, in_maps, core_ids=list(range(8)))
    return np.stack([np.asarray(r["y"], dtype=np.float32) for r in res.results], axis=0)
```

```python
import math
import os
import numpy as np
import ml_dtypes
import concourse.bass as bass
import concourse.mybir as mybir
from concourse.bass_utils import run_bass_kernel_spmd

F32 = mybir.dt.float32
BF16 = mybir.dt.bfloat16
AF = mybir.ActivationFunctionType
ALU = mybir.AluOpType
AX = mybir.AxisListType

ENGS = ("pe", "act", "dve", "pool", "sp")
DMA_POOL = 16


class _Buf:
    __slots__ = ("last_w", "readers", "dma_readers")

    def __init__(self):
        self.last_w = None
        self.readers = {}
        self.dma_readers = []


class _Op:
    __slots__ = ("eng", "idx", "gid", "fn", "deps", "is_dma", "signal", "sem", "val", "waits",
                 "know", "dma_n", "pre_wait")

    def __init__(self, eng, idx, gid, fn, is_dma):
        self.eng = eng
        self.idx = idx
        self.gid = gid
        self.fn = fn
        self.is_dma = is_dma
        self.deps = []
        self.signal = False
        self.sem = None
        self.val = None
        self.waits = []
        self.know = None
        self.dma_n = None
        self.pre_wait = None


class Prog:
    def __init__(self):
        self.ops = {e: [] for e in ENGS}
        self.all = []
        self.bufs = {}
        self.n_dma = {e: 0 for e in ENGS}

    def _buf(self, name):
        b = self.bufs.get(name)
        if b is None:
            b = _Buf()
            self.bufs[name] = b
        return b

    def add(self, eng, fn, reads=(), writes=(), dma=False):
        op = _Op(eng, len(self.ops[eng]), len(self.all), fn, dma)
        deps = {}
        for r in reads:
            b = self._buf(r)
            if b.last_w is not None:
                deps[b.last_w.gid] = b.last_w
        for w in writes:
            b = self._buf(w)
            if b.last_w is not None:
                deps[b.last_w.gid] = b.last_w
            for d in b.readers.values():
                deps[d.gid] = d
            for d in b.dma_readers:
                deps[d.gid] = d
        op.deps = [deps[k] for k in sorted(deps)]
        for r in reads:
            b = self._buf(r)
            if dma:
                b.dma_readers.append(op)
            else:
                b.readers[eng] = op
        for w in writes:
            b = self._buf(w)
            b.last_w = op
            b.readers = {}
            b.dma_readers = []
        if dma:
            op.dma_n = self.n_dma[eng]
            self.n_dma[eng] += 1
        self.ops[eng].append(op)
        self.all.append(op)
        return op

    def resolve(self):
        know = {e: {f: -1 for f in ENGS} for e in ENGS}
        know_dma = {e: set() for e in ENGS}
        sig_count = {e: 0 for e in ENGS}
        for op in self.all:
            E = op.eng
            kn = know[E]
            for d in op.deps:
                if d.is_dma:
                    if d.gid in know_dma[E]:
                        continue
                    know_dma[E].add(d.gid)
                    op.waits.append(d)
                    for f, v in d.know.items():
                        if v > kn[f]:
                            kn[f] = v
                    continue
                F = d.eng
                if F == E:
                    if E == "pe" or op.idx - d.idx > 2:
                        continue
                    if kn[F] >= d.idx:
                        continue
                elif kn[F] >= d.idx:
                    continue
                d.signal = True
                op.waits.append(d)
                kn[F] = max(kn[F], d.idx)
                for f, v in d.know.items():
                    if f != E and v > kn[f]:
                        kn[f] = v
            snap = dict(kn)
            if not op.is_dma:
                snap[E] = op.idx
            op.know = snap
        for e in ENGS:
            c = 0
            for op in self.ops[e]:
                if op.is_dma:
                    continue
                if op.signal:
                    c += 1
                    op.val = c

    def emit(self, nc):
        self.resolve()
        import contextlib
        with contextlib.ExitStack() as st:
            esem = {e: st.enter_context(nc.semaphore("s_" + e)) for e in ENGS}
            dsem = {e: [st.enter_context(nc.semaphore("d_%s%d" % (e, i))) for i in range(DMA_POOL)]
                    for e in ENGS if self.n_dma[e] > 0}
            block = st.enter_context(nc.Block())

            def wait_for(engine, d):
                if d.is_dma:
                    engine.wait_ge(dsem[d.eng][d.dma_n % DMA_POOL], 16 * (d.dma_n // DMA_POOL + 1))
                else:
                    engine.wait_ge(esem[d.eng], d.val)

            def run(engine, e):
                ops = self.ops[e]
                for op in ops:
                    for d in op.waits:
                        wait_for(engine, d)
                    if op.is_dma:
                        n = op.dma_n
                        if n >= DMA_POOL:
                            engine.wait_ge(dsem[e][n % DMA_POOL], 16 * (n // DMA_POOL))
                        ins = op.fn(engine)
                        ins.then_inc(dsem[e][n % DMA_POOL], 16)
                    else:
                        ins = op.fn(engine)
                        if op.signal:
                            ins.then_inc(esem[e], 1)
                nd = self.n_dma[e]
                for i in range(min(nd, DMA_POOL)):
                    n = nd - 1 - i
                    engine.wait_ge(dsem[e][n % DMA_POOL], 16 * (n // DMA_POOL + 1))

            @block.tensor
            def _(eng):
                run(eng, "pe")

            @block.scalar
            def _(eng):
                run(eng, "act")

            @block.vector
            def _(eng):
                run(eng, "dve")

            @block.gpsimd
            def _(eng):
                run(eng, "pool")

            @block.sync
            def _(eng):
                run(eng, "sp")


S = 2048
D = 1024
NIN = 11520
DEPTH = 4
PADX = 256
XW = S + 2 * PADX
ALPHA = (2 * DEPTH) ** 0.25
CDEC = math.exp(-0.5)
C_BIN = 0
C_MU0 = 90
C_MU1 = 116
C_W0 = 142
C_A0 = 158
C_KK = 174
C_KA = 182
C_RK = 190
C_GG = 198
C_GB = 206
C_PB = 214
C_PS = 220
NCOLS = 226


class Ctx:
    pass


def build(depth=DEPTH, dbg=None):
    nc = bass.Bass("TRN2", target_bir_lowering=False)
    P = Prog()
    dt_in = lambda name, shape: nc.dram_tensor(name, shape, F32, kind="ExternalInput").ap()
    x_in = dt_in("x", [S, D])
    w_in = dt_in("w_in", [DEPTH, D, NIN])
    cols_d = dt_in("cols", [DEPTH, 128, NCOLS])
    vrow_d = dt_in("vrow", [DEPTH, 1, 768])
    wup_d = dt_in("w_up", [DEPTH, 128, 1024])
    aup_d = dt_in("a_up", [DEPTH, 128, 1024])
    poolw_d = dt_in("pool_w", [DEPTH, 768, 768])
    proja_d = dt_in("proj_a", [DEPTH, 1024, 1024])
    projb_d = dt_in("proj_b", [DEPTH, 768, 1024])
    projc_d = dt_in("proj_c", [DEPTH, 256, 1024])
    wout_d = dt_in("w_out", [DEPTH, 1024, 1024])
    lnrow_d = dt_in("lnrow", [DEPTH, 2, 1024])
    cst_d = dt_in("cst", [128, 128 * 3 + 64])
    rope_d = dt_in("rope", [2, 128, S])
    amask_d = dt_in("amask", [4, 128, 512])
    rmask_d = dt_in("rmask", [2, 128, 512 + 512 + 256 + 512])
    pedge_d = dt_in("pedge", [128, 4, 16])
    y_out = nc.dram_tensor("y", [S, D], F32, kind="ExternalOutput").ap()
    xres = nc.dram_tensor("xres", [S, D], F32, kind="Internal").ap()
    dbg_out = None
    if dbg is not None:
        dbg_out = nc.dram_tensor("dbg", [128, 16, S], F32, kind="ExternalOutput").ap()

    import contextlib
    st = contextlib.ExitStack()
    sb = lambda name, shape, dt: st.enter_context(nc.sbuf_tensor(name, shape, dt))
    xT = sb("xT", [128, 8, XW], BF16)
    oT = sb("oT", [128, 16, S], BF16)
    wb = [sb("wb%d" % i, [128, 8, 512], BF16) for i in range(2)]
    ARENA = 16384
    arena = sb("arena", [128, ARENA], F32)
    colsb = sb("colsb", [128, NCOLS], F32)
    c0col = sb("c0col", [128, 26], F32)
    identb = sb("identb", [128, 128], BF16)
    permb = sb("permb", [128, 128], BF16)
    onesb = sb("onesb", [128, 128], BF16)
    eye2 = sb("eye2", [128, 64], BF16)
    bones = sb("bones", [128, 128], F32)
    vrowb = sb("vrowb", [1, 768], BF16)
    selcol = sb("selcol", [128, 24], F32)
    mixf = sb("mixf", [128, S], F32)
    selc_d = dt_in("selc", [128, 24])
    psb = [st.enter_context(nc.psum_tensor("psb%d" % i, [128, 512], F32)) for i in range(8)]

    C = Ctx()
    C.bank_i = 0

    def nextbank():
        i = C.bank_i
        C.bank_i = (i + 1) % 4
        return psb[i], "ps%d" % i

    def af32(off, n):
        return arena[:, off:off + n]

    def abf(off, n):
        return arena[:, off:off + n // 2].bitcast(BF16)

    def o8f32(off, n):
        return oT[:, 8:16, :].rearrange("p a b -> p (a b)").bitcast(F32)[:, off:off + n]

    def o8bf(off, n):
        return oT[:, 8:16, :].rearrange("p a b -> p (a b)")[:, off:off + n]

    add = P.add
    V = "dve"
    A = "act"
    G = "pool"

    def mm(out, lhsT, rhs, start, stop, rd, wr):
        add("pe", lambda e: e.matmul(out, lhsT, rhs, start=start, stop=stop), reads=rd, writes=wr)

    def act(out, in_, func, rd, wr, bias=0.0, scale=1.0):
        add(A, lambda e: e.activation(out=out, in_=in_, func=func, bias=bias, scale=scale), reads=rd, writes=wr)

    def tt(eng, out, in0, in1, op, rd, wr):
        eng = V if eng == G else eng
        add(eng, lambda e: e.tensor_tensor(out=out, in0=in0, in1=in1, op=op), reads=rd, writes=wr)

    def ts(eng, out, in0, s1, s2, op0, op1, rd, wr):
        eng = V if eng == G else eng
        if s2 is None:
            add(eng, lambda e: e.tensor_scalar(out=out, in0=in0, scalar1=s1, scalar2=None, op0=op0), reads=rd, writes=wr)
        else:
            add(eng, lambda e: e.tensor_scalar(out=out, in0=in0, scalar1=s1, scalar2=s2, op0=op0, op1=op1), reads=rd, writes=wr)

    def stt(out, in0, scalar, in1, op0, op1, rd, wr):
        add(V, lambda e: e.scalar_tensor_tensor(out=out, in0=in0, scalar=scalar, in1=in1, op0=op0, op1=op1),
            reads=rd, writes=wr)

    def cp(eng, out, in_, rd, wr):
        eng = V if eng == G else eng
        if eng == A:
            add(eng, lambda e: e.activation(out=out, in_=in_, func=AF.Copy), reads=rd, writes=wr)
        else:
            add(eng, lambda e: e.tensor_copy(out=out, in_=in_), reads=rd, writes=wr)

    def dma(q, out, in_, rd, wr):
        add(q, lambda e: e.dma_start(out=out, in_=in_), reads=rd, writes=wr, dma=True)

    def memset(eng, ap, val, wr):
        add(eng, lambda e: e.memset(ap, val), writes=wr)

    bscr = sb("bscr", [128, 8], F32)

    def barrier():
        names = [n for n in P.bufs.keys() if not n.startswith("ps")] + ["bscr"]
        mm(psb[7][:, 0:8], identb[:, 0:128], identb[:, 0:8], True, True, [], names + ["ps7"])
        act(bscr[:, 0:1], bscr[:, 1:2], AF.Copy, [], names)
        memset(V, bscr[:, 2:3], 0.0, names)
        dma("sp", bscr[0:1, 3:4], cst_d[0:1, 0:1], [], names)
        dma(G, bscr[0:1, 4:5], cst_d[0:1, 0:1], [], names)

    memset(V, bscr[:], 0.0, ["bscr"])
    dma(G, identb[:], cst_d[:, 0:128], [], ["identb"])
    dma("sp", bones[:], cst_d[:, 128:256], [], ["bones"])
    dma(G, permb[:], cst_d[:, 256:384], [], ["permb"])
    dma("sp", selcol[:], selc_d, [], ["selcol"])
    dma(G, eye2[:], cst_d[:, 384:448], [], ["eye2"])
    memset(V, onesb[:], 1.0, ["onesb"])
    memset(V, xT[:, :, 0:PADX], 0.0, ["xT"])
    memset(V, xT[:, :, PADX + S:XW], 0.0, ["xT"])

    C.wres = [None, None]
    C.wlast = 0

    def load_w(key, src3):
        for i in range(2):
            if C.wres[i] == key:
                C.wlast = i
                return wb[i], "wb%d" % i
        i = 1 - C.wlast
        C.wres[i] = key
        C.wlast = i
        kc, ncol = src3.shape[1], src3.shape[2]
        dma(G, wb[i][:, 0:kc, 0:ncol], src3, [], ["wb%d" % i])
        return wb[i], "wb%d" % i

    def win_src(l, col0, ncol):
        return w_in[l].rearrange("(k p) n -> p k n", p=128)[:, :, col0:col0 + ncol]

    def inproj(l, cg, evac, wkey=None):
        blk = cg // 4
        ncol = min(512, NIN - blk * 512)
        w, wn = load_w(("win", l, blk), win_src(l, blk * 512, ncol))
        c0 = (cg % 4) * 128
        for t4 in range(4):
            bank, bn = nextbank()
            for k in range(8):
                mm(bank[:], w[:, k, c0:c0 + 128], xT[:, k, PADX + t4 * 512:PADX + (t4 + 1) * 512],
                   k == 0, k == 7, ["xT", wn], [bn])
            evac(t4, bank, bn)

    def col(ci):
        return colsb[:, ci:ci + 1]

    xstage = sb("xstage", [128, 1024], BF16)

    def store_xT(src_f32, srcname, t16):
        cp(A, xstage[:], src_f32, [srcname], ["xstage"])
        bank, bn = nextbank()
        bb = bank[:].bitcast(BF16)
        for k in range(8):
            add("pe", lambda e, k=k: e.transpose(bb[:, k * 128:(k + 1) * 128], xstage[:, k * 128:(k + 1) * 128], identb[:]),
                reads=["xstage", "identb"], writes=[bn])
        cp(V, xT[:, :, PADX + t16 * 128:PADX + (t16 + 1) * 128], bb.rearrange("p (k t) -> p k t", k=8), [], [bn, "xT"])

    def final_phase(l, last):
        barrier()
        mergedT = abf(0, 8 * S).rearrange("p (k t) -> p k t", k=8)
        sig = abf(8192, 512)
        tmpf = af32(8448, 512)
        lng = af32(9216, 1024)
        lnb = af32(10240, 1024)
        xt_ = [af32(11264, 1024), af32(12288, 1024)]
        yt_ = [af32(13312, 1024), af32(14336, 1024)]
        stat = af32(15360, 8)
        dma("sp", lng, lnrow_d[l, 0:1, :].to_broadcast([128, 1024]), [], ["lng"])
        dma("sp", lnb, lnrow_d[l, 1:2, :].to_broadcast([128, 1024]), [], ["lnb"])
        if STOP == 4:
            return
        branches = [(proja_d, 8, 0, 66), (projb_d, 6, 8, 74), (projc_d, 2, 14, 82)]
        for bi, (pd, kc, o0, g0) in enumerate(branches):
            for eb in range(2):
                for ec in range(eb * 4, eb * 4 + 4):
                    for t4 in range(4):
                        tsl = slice(t4 * 512, (t4 + 1) * 512)
                        pw, pwn = load_w(("proj", l, bi, eb), pd[l].rearrange("(k p) n -> p k n", p=128)[:, :, eb * 512:(eb + 1) * 512])
                        b1, b1n = nextbank()
                        for k in range(kc):
                            mm(b1[:], pw[:, k, (ec % 4) * 128:(ec % 4 + 1) * 128], oT[:, o0 + k, tsl], k == 0, k == kc - 1,
                               ["oT%d" % (o0 + k), pwn], [b1n])
                        gw, gwn = load_w(("gate", l, bi, eb), win_src(l, (g0 + eb * 4) * 128, 512))
                        b2, b2n = nextbank()
                        for k in range(8):
                            mm(b2[:], gw[:, k, (ec % 4) * 128:(ec % 4 + 1) * 128], xT[:, k, PADX + t4 * 512:PADX + (t4 + 1) * 512],
                               k == 0, k == 7, ["xT", gwn], [b2n])
                        act(sig, b2[:], AF.Sigmoid, ["colsb"], [b2n, "sig"], bias=col(C_BIN + g0 + ec))
                        if bi == 0:
                            tt(V, mergedT[:, ec, tsl], b1[:], sig, ALU.mult, ["sig"], [b1n, "mg%d" % ec])
                        else:
                            tt(V, tmpf, b1[:], sig, ALU.mult, ["sig"], [b1n, "tmpf"])
                            tt(G, mergedT[:, ec, tsl], mergedT[:, ec, tsl], tmpf, ALU.add, ["tmpf"], ["mg%d" % ec])
        if STOP == 3:
            return
        wo = []
        for fh in range(2):
            wo.append(load_w(("wout", l, fh), wout_d[l].rearrange("(k p) n -> p k n", p=128)[:, :, fh * 512:(fh + 1) * 512]))
        xsrc = x_in if l == 0 else xres
        dst = y_out if last else xres
        for t16 in range(16):
            xt = xt_[t16 % 2]
            yt = yt_[t16 % 2]
            xn, yn = "xt%d" % (t16 % 2), "yt%d" % (t16 % 2)
            dma("sp", xt, xsrc[t16 * 128:(t16 + 1) * 128, :], ["xres"] if l > 0 else [], [xn])
            for fh in range(2):
                w, wn = wo[fh]
                bank, bn = nextbank()
                for k in range(8):
                    mm(bank[:], mergedT[:, k, t16 * 128:(t16 + 1) * 128], w[:, k, :], k == 0, k == 7,
                       ["mg%d" % k, wn], [bn])
                stt(yt[:, fh * 512:(fh + 1) * 512], xt[:, fh * 512:(fh + 1) * 512], ALPHA, bank[:], ALU.mult, ALU.add,
                    [xn], [bn, yn])
            if STOP == 5:
                dma("sp", dst[t16 * 128:(t16 + 1) * 128, :], yt, [yn], ["xres"])
                continue
            add(V, lambda e, yt=yt: e.tensor_reduce(out=stat[:, 0:1], in_=yt, axis=AX.X, op=ALU.add), reads=[yn], writes=["stat0"])
            ts(V, stat[:, 1:2], stat[:, 0:1], -1.0 / D, None, ALU.mult, None, ["stat0"], ["stat1"])
            ts(V, yt, yt, stat[:, 1:2], None, ALU.add, None, ["stat1"], [yn])
            add(A, lambda e, yt=yt, xt=xt: e.activation(out=xt, in_=yt, func=AF.Square, accum_out=stat[:, 2:3]),
                reads=[yn], writes=[xn, "stat2"])
            act(stat[:, 3:4], stat[:, 2:3], AF.Sqrt, ["stat2"], ["stat3"], bias=1e-5, scale=1.0 / D)
            add(V, lambda e: e.reciprocal(out=stat[:, 4:5], in_=stat[:, 3:4]), reads=["stat3"], writes=["stat4"])
            if STOP == 6:
                dma("sp", dst[t16 * 128:(t16 + 1) * 128, :], yt, [yn], ["xres"])
                continue
            if STOP != 8:
                stt(yt, yt, stat[:, 4:5], lng, ALU.mult, ALU.mult, ["stat4", "lng"], [yn])
            if STOP != 7:
                tt(V, yt, yt, lnb, ALU.add, ["lnb"], [yn])
            dma("sp", dst[t16 * 128:(t16 + 1) * 128, :], yt, [yn], ["xres"])
            if not last:
                store_xT(yt, yn, t16)

    def pool_phase(l):
        barrier()
        W = S + 32
        pbuf = af32(0, W)
        a_ = [af32(2080, W), af32(4160, W)]
        mixed = abf(6240, 6 * S).rearrange("p (c t) -> p c t", c=6)
        wgt = abf(12384, 6 * 768).rearrange("p (a b) -> p a b", a=6)
        sacc = af32(14688, 0) if False else None
        pe_t = af32(14688, 64).rearrange("p (g e) -> p g e", g=4)
        t1 = af32(14752, 512)
        memset(V, pbuf[:, 0:16], 0.0, ["pbuf"])
        memset(V, pbuf[:, W - 16:W], 0.0, ["pbuf"])
        dma("sp", pe_t, pedge_d, [], ["pe_t"])
        dma(G, wgt, poolw_d[l].rearrange("(k p) n -> p k n", p=128), [], ["wgt"])
        for c in range(6):
            inproj(l, 40 + c, lambda t4, bank, bn, c=c: act(oT[:, 8 + c, t4 * 512:(t4 + 1) * 512], bank[:], AF.Silu,
                                                             ["colsb"], [bn, "oT%d" % (8 + c)], bias=col(C_BIN + 40 + c)))
        for c in range(6):
            inproj(l, 34 + c, lambda t4, bank, bn, c=c: act(pbuf[:, 16 + t4 * 512:16 + (t4 + 1) * 512], bank[:], AF.Identity,
                                                             ["colsb"], [bn, "pbuf"], bias=col(C_BIN + 34 + c)))
            gs = sorted(set((2 * c + hf) // 3 for hf in range(2)))
            first = True
            for g in gs:
                h = 1 << g
                kk = g + 1
                src, srcn = pbuf, "pbuf"
                for j in range(kk):
                    sh = 1 << j
                    dstt = a_[j % 2]
                    n = W - (2 << j) + 1
                    tt(V, dstt[:, 0:n], src[:, 0:n], src[:, sh:sh + n], ALU.add, [srcn], ["a%d" % (j % 2)])
                    src, srcn = dstt, "a%d" % (j % 2)
                sfin = a_[kk % 2]
                sn = "a%d" % (kk % 2)
                tt(V, sfin[:, 0:S], src[:, 16 - h:16 - h + S], pbuf[:, 16 + h:16 + h + S], ALU.add, [srcn, "pbuf"], [sn])
                tt(V, sfin[:, 0:8], sfin[:, 0:8], pe_t[:, g, 0:8], ALU.mult, ["pe_t"], [sn])
                tt(V, sfin[:, S - 8:S], sfin[:, S - 8:S], pe_t[:, g, 8:16], ALU.mult, ["pe_t"], [sn])
                selw = selcol[:, c * 4 + g:c * 4 + g + 1]
                if first:
                    stt(mixf[:, :], sfin[:, 0:S], selw, pbuf[:, 16:16 + S], ALU.mult, ALU.subtract, [sn, "pbuf", "selcol"], ["mixf"])
                else:
                    stt(mixf[:, :], sfin[:, 0:S], selw, mixf[:, :], ALU.mult, ALU.add, [sn, "selcol"], ["mixf"])
                first = False
            cp(A, mixed[:, c, :], mixf[:, :], ["mixf"], ["mixed"])
        for oc in range(6):
            ics = [ic for ic in range(6) if any((2 * ic + a) // 3 == (2 * oc + b) // 3 for a in range(2) for b in range(2))]
            for t4 in range(4):
                tsl = slice(t4 * 512, (t4 + 1) * 512)
                bank, bn = nextbank()
                for n_, ic in enumerate(ics):
                    mm(bank[:], wgt[:, ic, oc * 128:(oc + 1) * 128], mixed[:, ic, tsl], n_ == 0, n_ == len(ics) - 1,
                       ["wgt", "mixed"], [bn])
                ts(V, t1, bank[:], col(C_PB + oc), col(C_PS + oc), ALU.add, ALU.mult, ["colsb"], [bn, "t1"])
                tt(V, oT[:, 8 + oc, tsl], t1, oT[:, 8 + oc, tsl], ALU.mult, ["t1"], ["oT%d" % (8 + oc)])

    C.attn = None
    C.rwkv = None
    def attn_phase(l):
        barrier()
        Qr = abf(0, XW)
        Kr = abf(1280, XW)
        qraw = abf(2560, S)
        ropec = af32(3584, S)
        ropes = af32(5632, S)
        t1 = af32(7680, 512)
        t2 = af32(8192, 512)
        accn = af32(8704, S)
        accd = af32(10752, S)
        Vt = abf(12800, 20 * 128).rearrange("p (a b) -> p a b", a=20)
        pT = [abf(14080, 512), abf(14336, 512)]
        msk = abf(14592, 4 * 512).rearrange("p (a b) -> p a b", a=4)
        dma("sp", ropec, rope_d[0], [], ["ropec"])
        dma("sp", ropes, rope_d[1], [], ["ropes"])
        for a_ in range(4):
            dma(G, msk[:, a_, :], amask_d[a_], [], ["msk"])
        for buf, nm in ((Qr, "Qr"), (Kr, "Kr")):
            memset(V, buf[:, 0:PADX], 0.0, [nm])
            memset(V, buf[:, PADX + S:XW], 0.0, [nm])
        cnt = [0]
        SUB = int(os.environ.get("ATT_SUB", "9"))
        if SUB == 1:
            return
        for pp in range(2):
            for g in range(3):
                d = (1, 4, 16)[g]
                for cg, dst, nm in ((46 + 2 * g + pp, Qr, "Qr"), (52 + 2 * g + pp, Kr, "Kr")):
                    inproj(l, cg, lambda t4, bank, bn, cg=cg: act(qraw[:, t4 * 512:(t4 + 1) * 512], bank[:], AF.Identity,
                                                                  ["colsb"], [bn, "qraw"], bias=col(C_BIN + cg)))
                    for t4 in range(4):
                        tsl = slice(t4 * 512, (t4 + 1) * 512)
                        bank, bn = nextbank()
                        mm(bank[:], permb[:], qraw[:, tsl], True, True, ["permb", "qraw"], [bn])
                        tt(V, t1, bank[:], ropes[:, tsl], ALU.mult, ["ropes"], [bn, "t1"])
                        tt(V, t2, qraw[:, tsl], ropec[:, tsl], ALU.mult, ["qraw", "ropec"], ["t2"])
                        tt(V, dst[:, PADX + t4 * 512:PADX + (t4 + 1) * 512], t1, t2, ALU.add, ["t1", "t2"], [nm])
                if SUB == 2:
                    return
                wv, wvn = load_w(("wv", l, g, pp), win_src(l, (58 + 2 * g + pp) * 128, 128))
                if d == 1:
                    tsls = [slice(PADX + 128 * m - 64, PADX + 128 * m + 64) for m in range(17)]
                elif d == 4:
                    tsls = []
                    for r in range(4):
                        for m in range(5):
                            s0 = PADX + r + 4 * (128 * m - 64)
                            tsls.append(slice(s0, s0 + 509, 4))
                else:
                    tsls = [slice(PADX + r, PADX + r + 2033, 16) for r in range(16)]
                for j0 in range(0, len(tsls), 4):
                    grp = tsls[j0:j0 + 4]
                    bank, bn = nextbank()
                    for j, sl in enumerate(grp):
                        for k in range(8):
                            mm(bank[:, j * 128:(j + 1) * 128], xT[:, k, sl], wv[:, k, 0:128], k == 0, False, ["xT", wvn], [bn])
                        mm(bank[:, j * 128:(j + 1) * 128], onesb[0:1, 0:128], vrowb[0:1, (2 * g + pp) * 128:(2 * g + pp + 1) * 128],
                           False, True, ["onesb", "vrowb"], [bn])
                    n = len(grp)
                    cp(A, Vt[:, j0:j0 + n, :], bank[:, 0:n * 128].rearrange("p (a b) -> p a b", a=n), [], [bn, "Vt"])
                if SUB == 3:
                    return
                for sbk in range(4):
                    for j in range(4):
                        if d == 1:
                            m = 4 * sbk + j
                            qsl = slice(PADX + 128 * m, PADX + 128 * m + 128)
                            chunks = [(slice(PADX + 128 * m - 64, PADX + 128 * m + 64), m),
                                      (slice(PADX + 128 * m + 64, PADX + 128 * m + 192), m + 1)]
                            mi = 1 if m == 0 else (2 if m == 15 else 0)
                        elif d == 4:
                            r, m = sbk, j
                            q0 = PADX + r + 512 * m
                            qsl = slice(q0, q0 + 509, 4)
                            k0 = PADX + r + 4 * (128 * m - 64)
                            chunks = [(slice(k0, k0 + 509, 4), r * 5 + m), (slice(k0 + 512, k0 + 512 + 509, 4), r * 5 + m + 1)]
                            mi = 1 if m == 0 else (2 if m == 3 else 0)
                        else:
                            r = 4 * sbk + j
                            qsl = slice(PADX + r, PADX + r + 2033, 16)
                            chunks = [(qsl, r)]
                            mi = 3
                        nch = len(chunks)
                        wd = nch * 128
                        for h in range(2):
                            hp_ = slice(64 * h, 64 * h + 64)
                            si = h + 2 * (cnt[0] % 2)
                            sbank, sbn = psb[si], "ps%d" % si
                            pTb, pTn = pT[h], "pT%d" % h
                            for ci, (ks, vt) in enumerate(chunks):
                                mm(sbank[:, ci * 128:(ci + 1) * 128], Kr[hp_, ks], Qr[hp_, qsl], True, True, ["Kr", "Qr"], [sbn])
                            act(pTb[:, 0:wd], sbank[:, 0:wd], AF.Exp, [], [sbn, pTn], scale=0.125)
                            tt(V, pTb[:, 0:wd], pTb[:, 0:wd], msk[:, mi, 0:wd], ALU.mult, ["msk"], [pTn])
                            for ci, (ks, vt) in enumerate(chunks):
                                mm(psb[6][hp_, j * 128:(j + 1) * 128], Vt[:, vt, 64 * h:64 * h + 64], pTb[:, ci * 128:(ci + 1) * 128],
                                   ci == 0, ci == nch - 1, ["Vt", pTn], ["ps6"])
                            for ci, (ks, vt) in enumerate(chunks):
                                mm(psb[7][hp_, j * 128:(j + 1) * 128], onesb[:, 0:64], pTb[:, ci * 128:(ci + 1) * 128],
                                   ci == 0, ci == nch - 1, ["onesb", pTn], ["ps7"])
                        cnt[0] += 1
                    if d == 1:
                        vn, vd = accn[:, 512 * sbk:512 * sbk + 512], accd[:, 512 * sbk:512 * sbk + 512]
                        bn_, bd_ = psb[6][:, :], psb[7][:, :]
                    elif d == 4:
                        vn, vd = accn[:, sbk:S:4], accd[:, sbk:S:4]
                        bn_, bd_ = psb[6][:, :], psb[7][:, :]
                    else:
                        vn = accn.rearrange("p (i r) -> p r i", r=16)[:, 4 * sbk:4 * sbk + 4, :]
                        vd = accd.rearrange("p (i r) -> p r i", r=16)[:, 4 * sbk:4 * sbk + 4, :]
                        bn_ = psb[6][:, :].rearrange("p (r i) -> p r i", r=4)
                        bd_ = psb[7][:, :].rearrange("p (r i) -> p r i", r=4)
                    if g == 0:
                        cp(V, vn, bn_, [], ["ps6", "accn"])
                        cp(A, vd, bd_, [], ["ps7", "accd"])
                    else:
                        tt(V, vn, vn, bn_, ALU.add, [], ["ps6", "accn"])
                        tt(V, vd, vd, bd_, ALU.add, [], ["ps7", "accd"])
                if os.environ.get("ATT_STOP") == str(g + 1):
                    return
            oc = 14 + pp
            inproj(l, 64 + pp, lambda t4, bank, bn, oc=oc, pp=pp: act(oT[:, oc, t4 * 512:(t4 + 1) * 512], bank[:], AF.Silu,
                                                                        ["colsb"], [bn, "oT%d" % oc], bias=col(C_BIN + 64 + pp)))
            add(V, lambda e: e.reciprocal(out=accd, in_=accd), reads=[], writes=["accd"])
            tt(V, accn, accn, accd, ALU.mult, ["accd"], ["accn"])
            tt(V, oT[:, oc, :], accn, oT[:, oc, :], ALU.mult, ["accn"], ["oT%d" % oc])

    C.attn = attn_phase

    def rwkv_phase(l):
        barrier()
        lw = abf(0, S)
        la = abf(1024, S)
        wup = abf(2048, 1024)
        aup = abf(2560, 1024)
        hbuf = af32(3072, 2050)
        tmpB = af32(3072, S)
        tmpf = af32(5124, S)
        rbf = abf(7172, S)
        kbf = abf(8196, S)
        vbf = abf(9220, S)
        kkbf = abf(10244, S)
        bonus = abf(11268, S)
        ytok = af32(12292, S).rearrange("p (c i) -> p c i", c=32)
        Vtok = abf(14340, S).rearrange("p (c i) -> p c i", c=32)
        reset = af32(15364, 512)
        STf = af32(15876, 64)
        STb = abf(15940, 64)
        Xs = abf(15972, 64)
        Us = abf(16004, 64)
        Wc = af32(16036, 8)
        totc = af32(16044, 8)
        stat = af32(16052, 128)
        ynb = rbf.rearrange("p (c i) -> p c i", c=32)
        tmpf3 = tmpf.rearrange("p (c i) -> p c i", c=32)
        sg = o8f32(0, 512)
        aa = o8f32(512, 512)
        Gc = o8f32(1024, 512)
        tmpG = o8f32(1536, 512)
        E = o8f32(2048, 512)
        bb = o8f32(2560, 512)
        kd = o8f32(3072, 512)
        AR2 = o8bf(2 * 3584, 1024).rearrange("p (c n) -> p c n", c=8)
        AR4 = o8bf(2 * 3584, 1024).rearrange("p (c a j) -> p c a j", c=8, a=2)
        BT = o8bf(2 * 4096, 512)
        KT = o8bf(2 * 4352, 512)
        BHT = o8bf(2 * 4608, 512)
        KHT = o8bf(2 * 4864, 512)
        BHtok = o8bf(2 * 5120, 512).rearrange("p (c j) -> p c j", c=8)
        KHtok = o8bf(2 * 5376, 512).rearrange("p (c j) -> p c j", c=8)
        G1s = o8bf(2 * 5632, 1024).rearrange("p (c n) -> p c n", c=8)
        G2s = o8bf(2 * 6144, 1024).rearrange("p (c n) -> p c n", c=8)
        QP = o8bf(2 * 6656, 1024).rearrange("p (c n) -> p c n", c=8)
        Pn = o8bf(2 * 7168, 512).rearrange("p (c n) -> p c n", c=8)
        m1 = [o8bf(2 * 7424, 512), o8bf(2 * 7680, 512)]
        m2 = [o8bf(2 * 7936, 256), o8bf(2 * 8064, 256)]
        t1f = E

        mfb = mixf[:, :].bitcast(BF16)
        Wc1 = af32(16180, 8)
        SETS = [
            (AR2, AR4, G1s, G2s, QP, BHtok, KHtok, Wc),
            (mfb[:, 0:1024].rearrange("p (c n) -> p c n", c=8), mfb[:, 0:1024].rearrange("p (c a j) -> p c a j", c=8, a=2),
             mfb[:, 1024:2048].rearrange("p (c n) -> p c n", c=8), mfb[:, 2048:3072].rearrange("p (c n) -> p c n", c=8),
             mfb[:, 3072:4096].rearrange("p (c n) -> p c n", c=8),
             xstage[:, 0:512].rearrange("p (c j) -> p c j", c=8), xstage[:, 512:1024].rearrange("p (c j) -> p c j", c=8), Wc1),
        ]

        STATE = [(STf, STb, Xs, Us), (af32(16188, 64), abf(16252, 64), abf(16284, 64), abf(16316, 64))]

        def c8v(ap):
            return ap.rearrange("p (c j) -> p c j", c=8)

        dma(G, wup, wup_d[l], [], ["wup"])
        dma(G, aup, aup_d[l], [], ["aup"])
        dma("sp", reset, rmask_d[0][:, 1280:1792], [], ["reset"])
        for z in range(2):
            dma(G, m1[z], rmask_d[z][:, 0:512], [], ["m1_%d" % z])
            dma(G, m2[z], rmask_d[z][:, 1024:1280], [], ["m2_%d" % z])
        memset(V, hbuf[:, 0:1], 0.0, ["hbuf"])
        memset(V, hbuf[:, 2049:2050], 0.0, ["hbuf"])

        def shifted(cg, dst, dstname):
            inproj(l, cg, lambda t4, bank, bn: act(hbuf[:, 1 + t4 * 512:1 + (t4 + 1) * 512], bank[:], AF.Identity,
                                                   ["colsb"], [bn, "hbuf"], bias=col(C_BIN + cg)))
            act(tmpf, hbuf[:, 1:2049], AF.Identity, ["hbuf", "c0col"], ["tmpf"], scale=c0col[:, cg:cg + 1])
            stt(tmpf, hbuf[:, 0:2048], col(C_MU0 + cg), tmpf, ALU.mult, ALU.add, ["hbuf", "colsb"], ["tmpf"])
            stt(dst, hbuf[:, 2:2050], col(C_MU1 + cg), tmpf, ALU.mult, ALU.add, ["hbuf", "colsb", "tmpf"], [dstname])

        def pairbank():
            return [nextbank(), nextbank()]

        shifted(24, tmpf, "tmpf")
        act(lw, tmpf, AF.Tanh, ["tmpf"], ["lw"])
        shifted(25, la, "la")

        for hp in range(8):
            shifted(hp, rbf, "rbf")
            shifted(8 + hp, kbf, "kbf")
            shifted(16 + hp, vbf, "vbf")
            inproj(l, 26 + hp, lambda t4, bank, bn, hp=hp: act(oT[:, hp, t4 * 512:(t4 + 1) * 512], bank[:], AF.Silu,
                                                               ["colsb"], [bn, "oT%d" % hp], bias=col(C_BIN + 26 + hp)))
            ts(V, tmpf, kbf, col(C_KK + hp), None, ALU.mult, None, ["kbf", "colsb"], ["tmpf"])
            act(tmpB, tmpf, AF.Square, ["tmpf"], ["hbuf"])
            for t4 in range(4):
                tsl = slice(t4 * 512, (t4 + 1) * 512)
                bank, bn = nextbank()
                mm(bank[:], bones[:], tmpB[:, tsl], True, True, ["bones", "hbuf"], [bn])
                act(tmpB[:, tsl], bank[:], AF.Sqrt, [], [bn, "hbuf"], bias=1e-12)
            add(V, lambda e: e.reciprocal(out=tmpB, in_=tmpB), reads=[], writes=["hbuf"])
            tt(V, kkbf, tmpf, tmpB, ALU.mult, ["tmpf", "hbuf"], ["kkbf"])
            stt(tmpf, rbf, col(C_RK + hp), kbf, ALU.mult, ALU.mult, ["rbf", "kbf", "colsb"], ["tmpf"])
            for t4 in range(4):
                tsl = slice(t4 * 512, (t4 + 1) * 512)
                bank, bn = nextbank()
                mm(bank[:], bones[:], tmpf[:, tsl], True, True, ["bones", "tmpf"], [bn])
                tt(V, bonus[:, tsl], bank[:], vbf[:, tsl], ALU.mult, ["vbf"], [bn, "bonus"])
            for T in range(4):
                pb = pairbank()
                for h in range(2):
                    hs = slice(64 * h, 64 * h + 64)
                    bank, bn = pb[h]
                    for c8 in range(8):
                        c = T * 8 + c8
                        mm(bank[hs, c8 * 64:(c8 + 1) * 64], vbf[hs, c * 64:(c + 1) * 64], identb[hs, 64 * h:64 * h + 64],
                           True, True, ["vbf", "identb"], [bn])
                    cp(A, Vtok[hs, T * 8:(T + 1) * 8, :], c8v(bank[hs, :]), [], [bn, "Vtok"])

            items = [(0, T) for T in range(4)] + [(1, T) for T in range(3, -1, -1)]

            def produce(z, T, s):
                zs = slice(64 * z, 64 * z + 64)
                tsl = slice(T * 512, (T + 1) * 512)
                AR2_, AR4_, G1s_, G2s_, QP_, BHtok_, KHtok_, Wc_ = SETS[s]
                ss = str(s)
                ARn = "AR" + ss
                b1, b1n = nextbank()
                mm(b1[:], wup[zs, hp * 128:(hp + 1) * 128], lw[zs, tsl], True, True, ["wup", "lw"], [b1n])
                act(sg, b1[:], AF.Sigmoid, ["colsb"], [b1n, "sg"], bias=col(C_W0 + z * 8 + hp))
                b2, b2n = nextbank()
                mm(b2[:], aup[zs, hp * 128:(hp + 1) * 128], la[zs, tsl], True, True, ["aup", "la"], [b2n])
                act(aa, b2[:], AF.Sigmoid, ["colsb"], [b2n, "aa"], bias=col(C_A0 + z * 8 + hp))
                yield
                add(V, lambda e: e.tensor_tensor_scan(out=Gc, data0=reset, data1=sg, initial=0.0, op0=ALU.mult, op1=ALU.add),
                    reads=["reset", "sg"], writes=["Gc"])
                cp(V, totc, c8v(Gc)[:, :, 63], ["Gc"], ["totc"])
                totb = totc.unsqueeze(2).to_broadcast([128, 8, 64])
                if z == 1:
                    tt(V, tmpG, sg, Gc, ALU.subtract, ["sg", "Gc"], ["tmpG"])
                    tt(V, c8v(Gc), c8v(tmpG), totb, ALU.add, ["tmpG", "totc"], ["Gc"])
                yield
                act(E, Gc, AF.Exp, ["Gc"], ["E"], scale=-CDEC)
                tt(V, AR4_[:, :, 1, :], c8v(rbf[:, tsl]), c8v(E), ALU.mult, ["rbf", "E"], [ARn])
                tt(V, tmpG, Gc, sg, ALU.subtract, ["Gc", "sg"], ["tmpG"])
                yield
                act(E, tmpG, AF.Exp, ["tmpG"], ["E"], scale=-CDEC)
                stt(AR4_[:, :, 0, :], c8v(kkbf[:, tsl]), -1.0, c8v(E), ALU.mult, ALU.mult, ["kkbf", "E"], [ARn])
                ts(V, tmpG, aa, -1.0, col(C_KA + hp), ALU.add, ALU.mult, ["aa", "colsb"], ["tmpG"])
                yield
                stt(kd, tmpG, 1.0, kbf[:, tsl], ALU.add, ALU.mult, ["tmpG", "kbf"], ["kd"])
                tt(V, bb, kkbf[:, tsl], aa, ALU.mult, ["kkbf", "aa"], ["bb"])
                act(E, Gc, AF.Exp, ["Gc"], ["E"], scale=CDEC)
                yield
                tt(V, BT, bb, E, ALU.mult, ["bb", "E"], ["BT"])
                tt(V, KT, kd, E, ALU.mult, ["kd", "E"], ["KT"])
                tt(V, c8v(tmpG), c8v(Gc), totb, ALU.subtract, ["Gc", "totc"], ["tmpG"])
                yield
                act(E, tmpG, AF.Exp, ["tmpG"], ["E"], scale=CDEC)
                tt(V, BHT, bb, E, ALU.mult, ["bb", "E"], ["BHT"])
                tt(V, KHT, kd, E, ALU.mult, ["kd", "E"], ["KHT"])
                act(Wc_, totc, AF.Exp, ["totc"], ["Wc" + ss], scale=-CDEC)
                yield
                for src, srcn, dst, dstn in ((BHT, "BHT", BHtok_, "BHtok" + ss), (KHT, "KHT", KHtok_, "KHtok" + ss)):
                    pb = pairbank()
                    for h in range(2):
                        hs = slice(64 * h, 64 * h + 64)
                        bank, bn = pb[h]
                        for c8 in range(8):
                            mm(bank[hs, c8 * 64:(c8 + 1) * 64], src[hs, c8 * 64:(c8 + 1) * 64], identb[hs, 64 * h:64 * h + 64],
                               True, True, [srcn, "identb"], [bn])
                        cp(A, dst[hs, :, :], c8v(bank[hs, :]), [], [bn, dstn + str(h)])
                    yield
                chains = []
                for hv in range(2):
                    for h in range(2):
                        ia, ib = {(0, 0): (0, 1), (0, 1): (2, 3), (1, 0): (4, 5), (1, 1): (6, 7)}[(hv, h)]
                        chains.append((hv, h, psb[ia], "ps%d" % ia, psb[ib], "ps%d" % ib))
                for hv, h, bA, bAn, bB, bBn in chains:
                    hs = slice(64 * h, 64 * h + 64)
                    for cl in range(4):
                        c8 = hv * 4 + cl
                        mm(bA[hs, cl * 128:(cl + 1) * 128], BT[hs, c8 * 64:(c8 + 1) * 64], AR2_[hs, c8, :], True, True,
                           ["BT", ARn], [bAn])
                    for cl in range(4):
                        c8 = hv * 4 + cl
                        mm(bB[hs, cl * 128:(cl + 1) * 128], KT[hs, c8 * 64:(c8 + 1) * 64], AR2_[hs, c8, :], True, True,
                           ["KT", ARn], [bBn])
                for hv, h, bA, bAn, bB, bBn in chains:
                    cs = slice(hv * 4, hv * 4 + 4)
                    hs = slice(64 * h, 64 * h + 64)
                    nm = "%s%d%d" % (ss, hv, h)
                    tt(V, G1s_[hs, cs, :], bA[hs, :].rearrange("p (c n) -> p c n", c=4),
                       m1[z][hs, :].rearrange("p (c n) -> p c n", c=4), ALU.mult, ["m1_%d" % z], [bAn, "G1s" + nm])
                    tt(V, G2s_[hs, cs, :], bB[hs, :].rearrange("p (c n) -> p c n", c=4),
                       m1[z][hs, :].rearrange("p (c n) -> p c n", c=4), ALU.mult, ["m1_%d" % z], [bBn, "G2s" + nm])
                for hv, h, bA, bAn, bB, bBn in chains:
                    hs = slice(64 * h, 64 * h + 64)
                    for cl in range(4):
                        c8 = hv * 4 + cl
                        mm(bA[hs, cl * 64:(cl + 1) * 64], AR4_[hs, c8, 0, :], BT[hs, c8 * 64:(c8 + 1) * 64], True, True,
                           ["BT", ARn], [bAn])
                for hv, h, bA, bAn, bB, bBn in chains:
                    cs = slice(hv * 4, hv * 4 + 4)
                    hs = slice(64 * h, 64 * h + 64)
                    nm = "%s%d%d" % (ss, hv, h)
                    pn = "Pn%d%d" % (hv, h)
                    tt(V, Pn[hs, cs, :], bA[hs, 0:256].rearrange("p (c n) -> p c n", c=4),
                       m2[z][hs, :].rearrange("p (c n) -> p c n", c=4), ALU.mult, ["m2_%d" % z], [bAn, pn])
                    cp(A, QP_[hs, cs, 0:64], eye2[hs, :].unsqueeze(1).to_broadcast([64, 4, 64]), ["eye2"], ["QP" + nm])
                    cp(A, QP_[hs, cs, 64:128], G1s_[hs, cs, 0:64], ["G1s" + nm], ["QP" + nm])
                for k in range(6):
                    last = (k == 5)
                    wA = 64 if last else 128
                    for hv, h, bA, bAn, bB, bBn in chains:
                        hs = slice(64 * h, 64 * h + 64)
                        nm = "%s%d%d" % (ss, hv, h)
                        pn = "Pn%d%d" % (hv, h)
                        for cl in range(4):
                            c8 = hv * 4 + cl
                            mm(bA[hs, cl * 128:cl * 128 + wA], Pn[hs, c8, :], QP_[hs, c8, 0:wA], True, True,
                               [pn, "QP" + nm], [bAn])
                        if not last:
                            for cl in range(4):
                                c8 = hv * 4 + cl
                                mm(bB[hs, cl * 64:(cl + 1) * 64], QP_[hs, c8, 64:128], Pn[hs, c8, :], True, True,
                                   [pn, "QP" + nm], [bBn])
                    for hv, h, bA, bAn, bB, bBn in chains:
                        cs = slice(hv * 4, hv * 4 + 4)
                        hs = slice(64 * h, 64 * h + 64)
                        nm = "%s%d%d" % (ss, hv, h)
                        pn = "Pn%d%d" % (hv, h)
                        bA3 = bA[hs, :].rearrange("p (c n) -> p c n", c=4)
                        tt(V, QP_[hs, cs, 0:64], QP_[hs, cs, 0:64], bA3[:, :, 0:64], ALU.add, [], [bAn, "QP" + nm])
                        if not last:
                            cp(A, QP_[hs, cs, 64:128], bA3[:, :, 64:128], [], [bAn, "QP" + nm])
                            cp(A if hv == 0 else V, Pn[hs, cs, :], bB[hs, 0:256].rearrange("p (c n) -> p c n", c=4), [], [bBn, pn])
                yield

            def consume(z, T, s, first):
                AR2_, AR4_, G1s_, G2s_, QP_, BHtok_, KHtok_, Wc_ = SETS[s]
                STf_, STb_, Xs_, Us_ = STATE[z]
                ss = str(s)
                zn = str(z)
                ARn = "AR" + ss
                if first:
                    memset(V, STf_, 0.0, ["STf" + zn + "0", "STf" + zn + "1"])
                    memset(V, STb_, 0.0, ["STb" + zn + "0", "STb" + zn + "1"])
                cord = range(8) if z == 0 else range(7, -1, -1)
                for c8 in cord:
                    c = T * 8 + c8
                    H = []
                    for h in range(2):
                        H.append((slice(64 * h, 64 * h + 64), zn + str(h), "%s%d%d" % (ss, c8 // 4, h),
                                  psb[2 * z + h], "ps%d" % (2 * z + h), psb[4 + 2 * z + h], "ps%d" % (4 + 2 * z + h)))
                    for hs, hn, nm, cb, cbn, yb, ybn in H:
                        mm(cb[hs, 0:64], AR4_[hs, c8, 0, :], STb_[hs, :], True, False, [ARn, "STb" + hn], [cbn])
                        mm(cb[hs, 0:64], G2s_[hs, c8, 0:64], Vtok[hs, c, :], False, True, ["G2s" + nm, "Vtok"], [cbn])
                    yield
                    for hs, hn, nm, cb, cbn, yb, ybn in H:
                        cp(A if z == 0 else V, Xs_[hs, :], cb[hs, 0:64], [], [cbn, "Xs" + hn])
                    for hs, hn, nm, cb, cbn, yb, ybn in H:
                        mm(cb[hs, 64:128], QP_[hs, c8, 0:64], Xs_[hs, :], True, True, ["QP" + nm, "Xs" + hn], [cbn])
                    yield
                    for hs, hn, nm, cb, cbn, yb, ybn in H:
                        cp(V if z == 0 else A, Us_[hs, :], cb[hs, 64:128], [], [cbn, "Us" + hn])
                    for hs, hn, nm, cb, cbn, yb, ybn in H:
                        mm(cb[hs, 128:192], BHtok_[hs, c8, :], Us_[hs, :], True, False, ["BHtok" + ss + hn[1], "Us" + hn], [cbn])
                        mm(cb[hs, 128:192], KHtok_[hs, c8, :], Vtok[hs, c, :], False, True, ["KHtok" + ss + hn[1], "Vtok"], [cbn])
                    for hs, hn, nm, cb, cbn, yb, ybn in H:
                        mm(yb[hs, c8 * 64:(c8 + 1) * 64], AR4_[hs, c8, 1, :], STb_[hs, :], True, False, [ARn, "STb" + hn], [ybn])
                        mm(yb[hs, c8 * 64:(c8 + 1) * 64], G1s_[hs, c8, 64:128], Us_[hs, :], False, False,
                           ["G1s" + nm, "Us" + hn], [ybn])
                        mm(yb[hs, c8 * 64:(c8 + 1) * 64], G2s_[hs, c8, 64:128], Vtok[hs, c, :], False, True,
                           ["G2s" + nm, "Vtok"], [ybn])
                    yield
                    for hs, hn, nm, cb, cbn, yb, ybn in H:
                        stt(STb_[hs, :], STb_[hs, :], Wc_[hs, c8:c8 + 1], cb[hs, 128:192], ALU.mult, ALU.add,
                            ["Wc" + ss], [cbn, "STb" + hn])
                    yield
                for h in range(2):
                    hs = slice(64 * h, 64 * h + 64)
                    yb, ybn = psb[4 + 2 * z + h], "ps%d" % (4 + 2 * z + h)
                    tt(V, ytok[hs, T * 8:(T + 1) * 8, :], ytok[hs, T * 8:(T + 1) * 8, :], c8v(yb[hs, :]), ALU.add,
                       [], [ybn, "ytok%d" % T])
                yield

            for T_ in range(4):
                memset(V, ytok[:, T_ * 8:(T_ + 1) * 8, :], 0.0, ["ytok%d" % T_])
            for i in range(4):
                for _ in produce(0, i, 0):
                    pass
                for _ in produce(1, 3 - i, 1):
                    pass
                gens = [consume(0, i, 0, i == 0), consume(1, 3 - i, 1, i == 0)]
                while gens:
                    for g_ in list(gens):
                        try:
                            next(g_)
                        except StopIteration:
                            gens.remove(g_)
            add(V, lambda e: e.tensor_reduce(out=stat[:, 0:32], in_=ytok, axis=AX.X, op=ALU.add), reads=["ytok0", "ytok1", "ytok2", "ytok3"], writes=["st0"])
            ts(V, stat[:, 32:64], stat[:, 0:32], -1.0 / 64, None, ALU.mult, None, ["st0"], ["st1"])
            tt(V, ytok, ytok, stat[:, 32:64].unsqueeze(2).to_broadcast([128, 32, 64]), ALU.add, ["st1"], ["ytok0", "ytok1", "ytok2", "ytok3"])
            act(tmpf3, ytok, AF.Square, ["ytok0", "ytok1", "ytok2", "ytok3"], ["tmpf"])
            add(V, lambda e: e.tensor_reduce(out=stat[:, 64:96], in_=tmpf3, axis=AX.X, op=ALU.add), reads=["tmpf"], writes=["st2"])
            act(stat[:, 96:128], stat[:, 64:96], AF.Sqrt, ["st2"], ["st3"], bias=64e-5, scale=1.0 / 64)
            add(V, lambda e: e.reciprocal(out=stat[:, 96:128], in_=stat[:, 96:128]), reads=[], writes=["st3"])
            tt(V, ynb, ytok, stat[:, 96:128].unsqueeze(2).to_broadcast([128, 32, 64]), ALU.mult, ["ytok0", "ytok1", "ytok2", "ytok3", "st3"], ["rbf"])
            for T in range(4):
                tsl = slice(T * 512, (T + 1) * 512)
                pb = pairbank()
                for h in range(2):
                    hs = slice(64 * h, 64 * h + 64)
                    bank, bn = pb[h]
                    for c8 in range(8):
                        mm(bank[hs, c8 * 64:(c8 + 1) * 64], ynb[hs, T * 8 + c8, :], identb[hs, 64 * h:64 * h + 64], True, True,
                           ["rbf", "identb"], [bn])
                    act(t1f[hs, :], bank[hs, :], AF.Identity, ["colsb"], [bn, "t1f" + str(h)],
                        bias=colsb[hs, C_GB + hp:C_GB + hp + 1], scale=colsb[hs, C_GG + hp:C_GG + hp + 1])
                tt(V, t1f, t1f, bonus[:, tsl], ALU.add, ["bonus", "t1f0", "t1f1"], ["t1f0", "t1f1"])
                tt(V, oT[:, hp, tsl], t1f, oT[:, hp, tsl], ALU.mult, ["t1f0", "t1f1"], ["oT%d" % hp])

    C.rwkv = rwkv_phase


    dma("sp", colsb[:], cols_d[0], [], ["colsb"])
    xld = [af32(0, 1024), af32(1024, 1024)]
    for t16 in range(16):
        dma("sp", xld[t16 % 2], x_in[t16 * 128:(t16 + 1) * 128, :], [], ["xld%d" % (t16 % 2)])
        store_xT(xld[t16 % 2], "xld%d" % (t16 % 2), t16)
    import os
    STOP = int(os.environ.get("KSTOP", "9"))
    for l in range(depth if STOP > 1 else 0):
        if l > 0:
            dma("sp", colsb[:], cols_d[l], [], ["colsb"])
        dma(G, vrowb[:], vrow_d[l], [], ["vrowb"])
        tt(V, c0col[:], colsb[:, C_MU0:C_MU0 + 26], colsb[:, C_MU1:C_MU1 + 26], ALU.add, ["colsb"], ["c0col"])
        ts(V, c0col[:], c0col[:], -1.0, 1.0, ALU.mult, ALU.add, [], ["c0col"])
        if C.rwkv is not None and "A" in PH:
            C.rwkv(l)
        else:
            for c in range(8):
                memset(V, oT[:, c, :], 0.0, ["oT%d" % c])
        if "B" in PH:
            pool_phase(l)
        else:
            barrier()
            for c in range(8, 14):
                memset(V, oT[:, c, :], 0.0, ["oT%d" % c])
        if C.attn is not None and "C" in PH:
            C.attn(l)
        else:
            barrier()
            for c in range(14, 16):
                memset(V, oT[:, c, :], 0.0, ["oT%d" % c])
        if dbg is not None and l == depth - 1:
            dtmp = af32(0, S)
            for c in range(16):
                cp(V, dtmp, oT[:, c, :], ["oT%d" % c], ["dtmp"])
                dma("sp", dbg_out[:, c, :], dtmp, ["dtmp"], [])
        if STOP > 2:
            final_phase(l, l == depth - 1)
    P.emit(nc)
    st.close()
    return nc


PH = "ABC"


def EXTRA_PHASES(L):
    pass


def host_prep(inp):
    f = np.float32
    g = lambda k: np.asarray(inp[k], dtype=f)
    colv = lambda v: np.ascontiguousarray(v.reshape(-1, 128).T)
    cols = []
    for l in range(DEPTH):
        parts = [colv(g("b_in")[l]), colv(g("rwkv_mu")[l, 0]), colv(g("rwkv_mu")[l, 1]),
                 colv(g("rwkv_w0")[l, 0]), colv(g("rwkv_w0")[l, 1]), colv(g("rwkv_a0")[l, 0]), colv(g("rwkv_a0")[l, 1]),
                 colv(g("rwkv_k_k")[l]), colv(g("rwkv_k_a")[l]), colv(g("rwkv_r_k")[l].reshape(-1)),
                 colv(g("rwkv_gn_g")[l]), colv(g("rwkv_gn_b")[l]), colv(g("pool_b")[l]), colv(g("pool_scale")[l])]
        cols.append(np.concatenate(parts, axis=1))
    cols = np.stack(cols)
    assert cols.shape == (DEPTH, 128, NCOLS), cols.shape
    shared = {
        "w_in": g("w_in"), "cols": cols,
        "vrow": np.ascontiguousarray(g("b_in")[:, None, 7424:8192]),
        "w_up": np.ascontiguousarray(g("rwkv_w_up").reshape(DEPTH, 128, 1024)),
        "a_up": np.ascontiguousarray(g("rwkv_a_up").reshape(DEPTH, 128, 1024)),
        "pool_w": _blockdiag(g("pool_w")), "proj_a": g("proj_a"), "proj_b": g("proj_b"), "proj_c": g("proj_c"),
        "w_out": g("w_out"),
        "lnrow": np.ascontiguousarray(np.stack([g("ln_g"), g("ln_b")], axis=1)),
    }
    shared.update(host_consts())
    return shared


def _blockdiag(pw):
    out = np.zeros((DEPTH, 768, 768), np.float32)
    for gi in range(4):
        out[:, 192 * gi:192 * gi + 192, 192 * gi:192 * gi + 192] = pw[:, gi]
    return out


def host_consts():
    f = np.float32
    ident = np.eye(128, dtype=f)
    bones = np.zeros((128, 128), f)
    bones[:64, :64] = 1
    bones[64:, 64:] = 1
    perm = np.zeros((128, 128), f)
    for m in range(128):
        c = m % 64
        k = m + 32 if c < 32 else m - 32
        perm[k, m] = 1
    eye2 = np.concatenate([np.eye(64, dtype=f), np.eye(64, dtype=f)], 0)
    cst = np.concatenate([ident, bones, perm, eye2], axis=1)
    inv = np.power(f(10000.0), -np.arange(0, 64, 2, dtype=f) / f(64))
    ang = np.arange(S, dtype=f)[:, None] * inv[None, :]
    ang = np.concatenate([ang, ang], axis=-1).astype(f)
    cosT = np.cos(ang).T.astype(f)
    sinT = np.sin(ang).T.astype(f)
    sign = np.where(np.arange(64) < 32, -1.0, 1.0).astype(f)[:, None]
    rope = np.stack([np.concatenate([cosT, cosT], 0), np.concatenate([sinT * sign, sinT * sign], 0)]).astype(f)
    b = np.arange(128)[:, None]
    a = np.arange(128)[None, :]
    mA = (b >= a).astype(f)
    mB = (a >= b).astype(f)
    mAf = mA * (b >= 64)
    mBl = mB * (b < 64)
    m16 = (np.abs(a - b) <= 64).astype(f)
    amask = np.stack([np.concatenate([mA, mB, mA, mB], 1), np.concatenate([mAf, mB, mAf, mB], 1),
                      np.concatenate([mA, mBl, mA, mBl], 1), np.concatenate([m16, m16, m16, m16], 1)]).astype(f)
    s_ = np.arange(64)[:, None]
    t_ = np.arange(64)[None, :]
    rm = []
    for z in range(2):
        if z == 0:
            strict = (s_ < t_)
            incl = (s_ <= t_)
        else:
            strict = (s_ > t_)
            incl = (s_ >= t_)
        m1 = np.concatenate([strict, incl], 1).astype(f)
        m2 = strict.T.astype(f)
        reset = np.ones((64, 512), f)
        reset[:, ::64] = 0
        row = np.concatenate([np.tile(m1, (1, 4)), np.tile(m1, (1, 4)), np.tile(m2, (1, 4)), reset], 1)
        rm.append(np.concatenate([row, row], 0))
    rmask = np.stack(rm).astype(f)
    pedge = np.ones((128, 4, 16), f)
    for gi in range(4):
        h = 1 << gi
        for e in range(8):
            t = e
            cnt = min(t + h, S - 1) - max(t - h, 0) + 1
            pedge[:, gi, e] = (2 * h + 1) / cnt
            t = S - 8 + e
            cnt = min(t + h, S - 1) - max(t - h, 0) + 1
            pedge[:, gi, 8 + e] = (2 * h + 1) / cnt
    selc = np.zeros((128, 24), f)
    for c in range(6):
        for p in range(128):
            gi = (128 * c + p) // 192
            selc[p, c * 4 + gi] = 1.0 / (2 * (1 << gi) + 1)
    return {"selc": selc, "cst": cst, "rope": rope, "amask": amask, "rmask": rmask, "pedge": pedge}


_NC_CACHE = {}


def kernel(**inputs):
    shared = host_prep(inputs)
    x = np.asarray(inputs["x"], dtype=np.float32)
    if "nc" not in _NC_CACHE:
        _NC_CACHE["nc"] = build()
    nc = _NC_CACHE["nc"]
    in_maps = []
    for c in range(8):
        m = dict(shared)
        m["x"] = np.ascontiguousarray(x[c])
        in_maps.append(m)
    res = run_bass_kernel_spmd(nc, in_maps, core_ids=list(range(8)))
    return np.stack([np.asarray(r["y"], dtype=np.float32) for r in res.results], axis=0)
```

```python
import math
import os
import numpy as np
import ml_dtypes
import concourse.bass as bass
import concourse.mybir as mybir
from concourse.bass_utils import run_bass_kernel_spmd

F32 = mybir.dt.float32
BF16 = mybir.dt.bfloat16
AF = mybir.ActivationFunctionType
ALU = mybir.AluOpType
AX = mybir.AxisListType

ENGS = ("pe", "act", "dve", "pool", "sp")
DMA_POOL = 16


class _Buf:
    __slots__ = ("last_w", "readers", "dma_readers")

    def __init__(self):
        self.last_w = None
        self.readers = {}
        self.dma_readers = []


class _Op:
    __slots__ = ("eng", "idx", "gid", "fn", "deps", "is_dma", "signal", "sem", "val", "waits",
                 "know", "dma_n", "pre_wait")

    def __init__(self, eng, idx, gid, fn, is_dma):
        self.eng = eng
        self.idx = idx
        self.gid = gid
        self.fn = fn
        self.is_dma = is_dma
        self.deps = []
        self.signal = False
        self.sem = None
        self.val = None
        self.waits = []
        self.know = None
        self.dma_n = None
        self.pre_wait = None


class Prog:
    def __init__(self):
        self.ops = {e: [] for e in ENGS}
        self.all = []
        self.bufs = {}
        self.n_dma = {e: 0 for e in ENGS}

    def _buf(self, name):
        b = self.bufs.get(name)
        if b is None:
            b = _Buf()
            self.bufs[name] = b
        return b

    def add(self, eng, fn, reads=(), writes=(), dma=False):
        op = _Op(eng, len(self.ops[eng]), len(self.all), fn, dma)
        deps = {}
        for r in reads:
            b = self._buf(r)
            if b.last_w is not None:
                deps[b.last_w.gid] = b.last_w
        for w in writes:
            b = self._buf(w)
            if b.last_w is not None:
                deps[b.last_w.gid] = b.last_w
            for d in b.readers.values():
                deps[d.gid] = d
            for d in b.dma_readers:
                deps[d.gid] = d
        op.deps = [deps[k] for k in sorted(deps)]
        for r in reads:
            b = self._buf(r)
            if dma:
                b.dma_readers.append(op)
            else:
                b.readers[eng] = op
        for w in writes:
            b = self._buf(w)
            b.last_w = op
            b.readers = {}
            b.dma_readers = []
        if dma:
            op.dma_n = self.n_dma[eng]
            self.n_dma[eng] += 1
        self.ops[eng].append(op)
        self.all.append(op)
        return op

    def resolve(self):
        know = {e: {f: -1 for f in ENGS} for e in ENGS}
        know_dma = {e: set() for e in ENGS}
        sig_count = {e: 0 for e in ENGS}
        for op in self.all:
            E = op.eng
            kn = know[E]
            for d in op.deps:
                if d.is_dma:
                    if d.gid in know_dma[E]:
                        continue
                    know_dma[E].add(d.gid)
                    op.waits.append(d)
                    for f, v in d.know.items():
                        if v > kn[f]:
                            kn[f] = v
                    continue
                F = d.eng
                if F == E:
                    if E == "pe" or op.idx - d.idx > 2:
                        continue
                    if kn[F] >= d.idx:
                        continue
                elif kn[F] >= d.idx:
                    continue
                d.signal = True
                op.waits.append(d)
                kn[F] = max(kn[F], d.idx)
                for f, v in d.know.items():
                    if f != E and v > kn[f]:
                        kn[f] = v
            snap = dict(kn)
            if not op.is_dma:
                snap[E] = op.idx
            op.know = snap
        for e in ENGS:
            c = 0
            for op in self.ops[e]:
                if op.is_dma:
                    continue
                if op.signal:
                    c += 1
                    op.val = c

    def emit(self, nc):
        self.resolve()
        import contextlib
        with contextlib.ExitStack() as st:
            esem = {e: st.enter_context(nc.semaphore("s_" + e)) for e in ENGS}
            dsem = {e: [st.enter_context(nc.semaphore("d_%s%d" % (e, i))) for i in range(DMA_POOL)]
                    for e in ENGS if self.n_dma[e] > 0}
            block = st.enter_context(nc.Block())

            def wait_for(engine, d):
                if d.is_dma:
                    engine.wait_ge(dsem[d.eng][d.dma_n % DMA_POOL], 16 * (d.dma_n // DMA_POOL + 1))
                else:
                    engine.wait_ge(esem[d.eng], d.val)

            def run(engine, e):
                ops = self.ops[e]
                for op in ops:
                    for d in op.waits:
                        wait_for(engine, d)
                    if op.is_dma:
                        n = op.dma_n
                        if n >= DMA_POOL:
                            engine.wait_ge(dsem[e][n % DMA_POOL], 16 * (n // DMA_POOL))
                        ins = op.fn(engine)
                        ins.then_inc(dsem[e][n % DMA_POOL], 16)
                    else:
                        ins = op.fn(engine)
                        if op.signal:
                            ins.then_inc(esem[e], 1)
                nd = self.n_dma[e]
                for i in range(min(nd, DMA_POOL)):
                    n = nd - 1 - i
                    engine.wait_ge(dsem[e][n % DMA_POOL], 16 * (n // DMA_POOL + 1))

            @block.tensor
            def _(eng):
                run(eng, "pe")

            @block.scalar
            def _(eng):
                run(eng, "act")

            @block.vector
            def _(eng):
                run(eng, "dve")

            @block.gpsimd
            def _(eng):
                run(eng, "pool")

            @block.sync
            def _(eng):
                run(eng, "sp")


S = 2048
D = 1024
NIN = 11520
DEPTH = 4
PADX = 256
XW = S + 2 * PADX
ALPHA = (2 * DEPTH) ** 0.25
CDEC = math.exp(-0.5)
C_BIN = 0
C_MU0 = 90
C_MU1 = 116
C_W0 = 142
C_A0 = 158
C_KK = 174
C_KA = 182
C_RK = 190
C_GG = 198
C_GB = 206
C_PB = 214
C_PS = 220
NCOLS = 226


class Ctx:
    pass


def build(depth=DEPTH, dbg=None):
    nc = bass.Bass("TRN2", target_bir_lowering=False)
    P = Prog()
    dt_in = lambda name, shape: nc.dram_tensor(name, shape, F32, kind="ExternalInput").ap()
    x_in = dt_in("x", [S, D])
    w_in = dt_in("w_in", [DEPTH, D, NIN])
    cols_d = dt_in("cols", [DEPTH, 128, NCOLS])
    vrow_d = dt_in("vrow", [DEPTH, 1, 768])
    wup_d = dt_in("w_up", [DEPTH, 128, 1024])
    aup_d = dt_in("a_up", [DEPTH, 128, 1024])
    poolw_d = dt_in("pool_w", [DEPTH, 768, 768])
    proja_d = dt_in("proj_a", [DEPTH, 1024, 1024])
    projb_d = dt_in("proj_b", [DEPTH, 768, 1024])
    projc_d = dt_in("proj_c", [DEPTH, 256, 1024])
    wout_d = dt_in("w_out", [DEPTH, 1024, 1024])
    lnrow_d = dt_in("lnrow", [DEPTH, 2, 1024])
    cst_d = dt_in("cst", [128, 128 * 3 + 64])
    rope_d = dt_in("rope", [2, 128, S])
    amask_d = dt_in("amask", [4, 128, 512])
    rmask_d = dt_in("rmask", [2, 128, 512 + 512 + 256 + 512])
    pedge_d = dt_in("pedge", [128, 4, 16])
    y_out = nc.dram_tensor("y", [S, D], F32, kind="ExternalOutput").ap()
    xres = nc.dram_tensor("xres", [S, D], F32, kind="Internal").ap()
    dbg_out = None
    if dbg is not None:
        dbg_out = nc.dram_tensor("dbg", [128, 16, S], F32, kind="ExternalOutput").ap()

    import contextlib
    st = contextlib.ExitStack()
    sb = lambda name, shape, dt: st.enter_context(nc.sbuf_tensor(name, shape, dt))
    xT = sb("xT", [128, 8, XW], BF16)
    oT = sb("oT", [128, 16, S], BF16)
    wb = [sb("wb%d" % i, [128, 8, 512], BF16) for i in range(2)]
    ARENA = 16384
    arena = sb("arena", [128, ARENA], F32)
    colsb = sb("colsb", [128, NCOLS], F32)
    c0col = sb("c0col", [128, 26], F32)
    identb = sb("identb", [128, 128], BF16)
    permb = sb("permb", [128, 128], BF16)
    onesb = sb("onesb", [128, 128], BF16)
    eye2 = sb("eye2", [128, 64], BF16)
    bones = sb("bones", [128, 128], F32)
    vrowb = sb("vrowb", [1, 768], BF16)
    selcol = sb("selcol", [128, 24], F32)
    mixf = sb("mixf", [128, S], F32)
    selc_d = dt_in("selc", [128, 24])
    psb = [st.enter_context(nc.psum_tensor("psb%d" % i, [128, 512], F32)) for i in range(8)]

    C = Ctx()
    C.bank_i = 0

    def nextbank():
        i = C.bank_i
        C.bank_i = (i + 1) % 4
        return psb[i], "ps%d" % i

    def af32(off, n):
        return arena[:, off:off + n]

    def abf(off, n):
        return arena[:, off:off + n // 2].bitcast(BF16)

    def o8f32(off, n):
        return oT[:, 8:16, :].rearrange("p a b -> p (a b)").bitcast(F32)[:, off:off + n]

    def o8bf(off, n):
        return oT[:, 8:16, :].rearrange("p a b -> p (a b)")[:, off:off + n]

    add = P.add
    V = "dve"
    A = "act"
    G = "pool"

    def mm(out, lhsT, rhs, start, stop, rd, wr):
        add("pe", lambda e: e.matmul(out, lhsT, rhs, start=start, stop=stop), reads=rd, writes=wr)

    def act(out, in_, func, rd, wr, bias=0.0, scale=1.0):
        add(A, lambda e: e.activation(out=out, in_=in_, func=func, bias=bias, scale=scale), reads=rd, writes=wr)

    def tt(eng, out, in0, in1, op, rd, wr):
        eng = V if eng == G else eng
        add(eng, lambda e: e.tensor_tensor(out=out, in0=in0, in1=in1, op=op), reads=rd, writes=wr)

    def ts(eng, out, in0, s1, s2, op0, op1, rd, wr):
        eng = V if eng == G else eng
        if s2 is None:
            add(eng, lambda e: e.tensor_scalar(out=out, in0=in0, scalar1=s1, scalar2=None, op0=op0), reads=rd, writes=wr)
        else:
            add(eng, lambda e: e.tensor_scalar(out=out, in0=in0, scalar1=s1, scalar2=s2, op0=op0, op1=op1), reads=rd, writes=wr)

    def stt(out, in0, scalar, in1, op0, op1, rd, wr):
        add(V, lambda e: e.scalar_tensor_tensor(out=out, in0=in0, scalar=scalar, in1=in1, op0=op0, op1=op1),
            reads=rd, writes=wr)

    def cp(eng, out, in_, rd, wr):
        eng = V if eng == G else eng
        if eng == A:
            add(eng, lambda e: e.copy(out=out, in_=in_), reads=rd, writes=wr)
        else:
            add(eng, lambda e: e.tensor_copy(out=out, in_=in_), reads=rd, writes=wr)

    def dma(q, out, in_, rd, wr):
        add(q, lambda e: e.dma_start(out=out, in_=in_), reads=rd, writes=wr, dma=True)

    def memset(eng, ap, val, wr):
        add(eng, lambda e: e.memset(ap, val), writes=wr)

    bscr = sb("bscr", [128, 8], F32)

    def barrier():
        names = [n for n in P.bufs.keys() if not n.startswith("ps")] + ["bscr"]
        mm(psb[7][:, 0:8], identb[:, 0:128], identb[:, 0:8], True, True, [], names + ["ps7"])
        act(bscr[:, 0:1], bscr[:, 1:2], AF.Copy, [], names)
        memset(V, bscr[:, 2:3], 0.0, names)
        dma("sp", bscr[0:1, 3:4], cst_d[0:1, 0:1], [], names)
        dma(G, bscr[0:1, 4:5], cst_d[0:1, 0:1], [], names)

    memset(V, bscr[:], 0.0, ["bscr"])
    dma(G, identb[:], cst_d[:, 0:128], [], ["identb"])
    dma("sp", bones[:], cst_d[:, 128:256], [], ["bones"])
    dma(G, permb[:], cst_d[:, 256:384], [], ["permb"])
    dma("sp", selcol[:], selc_d, [], ["selcol"])
    dma(G, eye2[:], cst_d[:, 384:448], [], ["eye2"])
    memset(V, onesb[:], 1.0, ["onesb"])
    memset(V, xT[:, :, 0:PADX], 0.0, ["xT"])
    memset(V, xT[:, :, PADX + S:XW], 0.0, ["xT"])

    C.wres = [None, None]
    C.wlast = 0

    def load_w(key, src3):
        for i in range(2):
            if C.wres[i] == key:
                C.wlast = i
                return wb[i], "wb%d" % i
        i = 1 - C.wlast
        C.wres[i] = key
        C.wlast = i
        kc, ncol = src3.shape[1], src3.shape[2]
        dma(G, wb[i][:, 0:kc, 0:ncol], src3, [], ["wb%d" % i])
        return wb[i], "wb%d" % i

    def win_src(l, col0, ncol):
        return w_in[l].rearrange("(k p) n -> p k n", p=128)[:, :, col0:col0 + ncol]

    def inproj(l, cg, evac, wkey=None):
        blk = cg // 4
        ncol = min(512, NIN - blk * 512)
        w, wn = load_w(("win", l, blk), win_src(l, blk * 512, ncol))
        c0 = (cg % 4) * 128
        for t4 in range(4):
            bank, bn = nextbank()
            for k in range(8):
                mm(bank[:], w[:, k, c0:c0 + 128], xT[:, k, PADX + t4 * 512:PADX + (t4 + 1) * 512],
                   k == 0, k == 7, ["xT", wn], [bn])
            evac(t4, bank, bn)

    def col(ci):
        return colsb[:, ci:ci + 1]

    xstage = sb("xstage", [128, 1024], BF16)

    def store_xT(src_f32, srcname, t16):
        cp(A, xstage[:], src_f32, [srcname], ["xstage"])
        bank, bn = nextbank()
        bb = bank[:].bitcast(BF16)
        for k in range(8):
            add("pe", lambda e, k=k: e.transpose(bb[:, k * 128:(k + 1) * 128], xstage[:, k * 128:(k + 1) * 128], identb[:]),
                reads=["xstage", "identb"], writes=[bn])
        cp(V, xT[:, :, PADX + t16 * 128:PADX + (t16 + 1) * 128], bb.rearrange("p (k t) -> p k t", k=8), [], [bn, "xT"])

    def final_phase(l, last):
        barrier()
        mergedT = abf(0, 8 * S).rearrange("p (k t) -> p k t", k=8)
        sig = abf(8192, 512)
        tmpf = af32(8448, 512)
        lng = af32(9216, 1024)
        lnb = af32(10240, 1024)
        xt_ = [af32(11264, 1024), af32(12288, 1024)]
        yt_ = [af32(13312, 1024), af32(14336, 1024)]
        stat = af32(15360, 8)
        dma("sp", lng, lnrow_d[l, 0:1, :].to_broadcast([128, 1024]), [], ["lng"])
        dma("sp", lnb, lnrow_d[l, 1:2, :].to_broadcast([128, 1024]), [], ["lnb"])
        if STOP == 4:
            return
        branches = [(proja_d, 8, 0, 66), (projb_d, 6, 8, 74), (projc_d, 2, 14, 82)]
        for bi, (pd, kc, o0, g0) in enumerate(branches):
            for eb in range(2):
                for ec in range(eb * 4, eb * 4 + 4):
                    for t4 in range(4):
                        tsl = slice(t4 * 512, (t4 + 1) * 512)
                        pw, pwn = load_w(("proj", l, bi, eb), pd[l].rearrange("(k p) n -> p k n", p=128)[:, :, eb * 512:(eb + 1) * 512])
                        b1, b1n = nextbank()
                        for k in range(kc):
                            mm(b1[:], pw[:, k, (ec % 4) * 128:(ec % 4 + 1) * 128], oT[:, o0 + k, tsl], k == 0, k == kc - 1,
                               ["oT%d" % (o0 + k), pwn], [b1n])
                        gw, gwn = load_w(("gate", l, bi, eb), win_src(l, (g0 + eb * 4) * 128, 512))
                        b2, b2n = nextbank()
                        for k in range(8):
                            mm(b2[:], gw[:, k, (ec % 4) * 128:(ec % 4 + 1) * 128], xT[:, k, PADX + t4 * 512:PADX + (t4 + 1) * 512],
                               k == 0, k == 7, ["xT", gwn], [b2n])
                        act(sig, b2[:], AF.Sigmoid, ["colsb"], [b2n, "sig"], bias=col(C_BIN + g0 + ec))
                        if bi == 0:
                            tt(V, mergedT[:, ec, tsl], b1[:], sig, ALU.mult, ["sig"], [b1n, "mg%d" % ec])
                        else:
                            tt(V, tmpf, b1[:], sig, ALU.mult, ["sig"], [b1n, "tmpf"])
                            tt(G, mergedT[:, ec, tsl], mergedT[:, ec, tsl], tmpf, ALU.add, ["tmpf"], ["mg%d" % ec])
        if STOP == 3:
            return
        wo = []
        for fh in range(2):
            wo.append(load_w(("wout", l, fh), wout_d[l].rearrange("(k p) n -> p k n", p=128)[:, :, fh * 512:(fh + 1) * 512]))
        xsrc = x_in if l == 0 else xres
        dst = y_out if last else xres
        for t16 in range(16):
            xt = xt_[t16 % 2]
            yt = yt_[t16 % 2]
            xn, yn = "xt%d" % (t16 % 2), "yt%d" % (t16 % 2)
            dma("sp", xt, xsrc[t16 * 128:(t16 + 1) * 128, :], ["xres"] if l > 0 else [], [xn])
            for fh in range(2):
                w, wn = wo[fh]
                bank, bn = nextbank()
                for k in range(8):
                    mm(bank[:], mergedT[:, k, t16 * 128:(t16 + 1) * 128], w[:, k, :], k == 0, k == 7,
                       ["mg%d" % k, wn], [bn])
                stt(yt[:, fh * 512:(fh + 1) * 512], xt[:, fh * 512:(fh + 1) * 512], ALPHA, bank[:], ALU.mult, ALU.add,
                    [xn], [bn, yn])
            if STOP == 5:
                dma("sp", dst[t16 * 128:(t16 + 1) * 128, :], yt, [yn], ["xres"])
                continue
            add(V, lambda e, yt=yt: e.tensor_reduce(out=stat[:, 0:1], in_=yt, axis=AX.X, op=ALU.add), reads=[yn], writes=["stat0"])
            ts(V, stat[:, 1:2], stat[:, 0:1], -1.0 / D, None, ALU.mult, None, ["stat0"], ["stat1"])
            ts(V, yt, yt, stat[:, 1:2], None, ALU.add, None, ["stat1"], [yn])
            add(A, lambda e, yt=yt, xt=xt: e.activation(out=xt, in_=yt, func=AF.Square, accum_out=stat[:, 2:3]),
                reads=[yn], writes=[xn, "stat2"])
            act(stat[:, 3:4], stat[:, 2:3], AF.Sqrt, ["stat2"], ["stat3"], bias=1e-5, scale=1.0 / D)
            add(V, lambda e: e.reciprocal(out=stat[:, 4:5], in_=stat[:, 3:4]), reads=["stat3"], writes=["stat4"])
            if STOP == 6:
                dma("sp", dst[t16 * 128:(t16 + 1) * 128, :], yt, [yn], ["xres"])
                continue
            if STOP != 8:
                stt(yt, yt, stat[:, 4:5], lng, ALU.mult, ALU.mult, ["stat4", "lng"], [yn])
            if STOP != 7:
                tt(V, yt, yt, lnb, ALU.add, ["lnb"], [yn])
            dma("sp", dst[t16 * 128:(t16 + 1) * 128, :], yt, [yn], ["xres"])
            if not last:
                store_xT(yt, yn, t16)

    def pool_phase(l):
        barrier()
        W = S + 32
        pbuf = af32(0, W)
        a_ = [af32(2080, W), af32(4160, W)]
        mixed = abf(6240, 6 * S).rearrange("p (c t) -> p c t", c=6)
        wgt = abf(12384, 6 * 768).rearrange("p (a b) -> p a b", a=6)
        sacc = af32(14688, 0) if False else None
        pe_t = af32(14688, 64).rearrange("p (g e) -> p g e", g=4)
        t1 = af32(14752, 512)
        memset(V, pbuf[:, 0:16], 0.0, ["pbuf"])
        memset(V, pbuf[:, W - 16:W], 0.0, ["pbuf"])
        dma("sp", pe_t, pedge_d, [], ["pe_t"])
        dma(G, wgt, poolw_d[l].rearrange("(k p) n -> p k n", p=128), [], ["wgt"])
        for c in range(6):
            inproj(l, 40 + c, lambda t4, bank, bn, c=c: act(oT[:, 8 + c, t4 * 512:(t4 + 1) * 512], bank[:], AF.Silu,
                                                             ["colsb"], [bn, "oT%d" % (8 + c)], bias=col(C_BIN + 40 + c)))
        for c in range(6):
            inproj(l, 34 + c, lambda t4, bank, bn, c=c: act(pbuf[:, 16 + t4 * 512:16 + (t4 + 1) * 512], bank[:], AF.Identity,
                                                             ["colsb"], [bn, "pbuf"], bias=col(C_BIN + 34 + c)))
            gs = sorted(set((2 * c + hf) // 3 for hf in range(2)))
            first = True
            for g in gs:
                h = 1 << g
                kk = g + 1
                src, srcn = pbuf, "pbuf"
                for j in range(kk):
                    sh = 1 << j
                    dstt = a_[j % 2]
                    n = W - (2 << j) + 1
                    tt(V, dstt[:, 0:n], src[:, 0:n], src[:, sh:sh + n], ALU.add, [srcn], ["a%d" % (j % 2)])
                    src, srcn = dstt, "a%d" % (j % 2)
                sfin = a_[kk % 2]
                sn = "a%d" % (kk % 2)
                tt(V, sfin[:, 0:S], src[:, 16 - h:16 - h + S], pbuf[:, 16 + h:16 + h + S], ALU.add, [srcn, "pbuf"], [sn])
                tt(V, sfin[:, 0:8], sfin[:, 0:8], pe_t[:, g, 0:8], ALU.mult, ["pe_t"], [sn])
                tt(V, sfin[:, S - 8:S], sfin[:, S - 8:S], pe_t[:, g, 8:16], ALU.mult, ["pe_t"], [sn])
                selw = selcol[:, c * 4 + g:c * 4 + g + 1]
                if first:
                    stt(mixf[:, :], sfin[:, 0:S], selw, pbuf[:, 16:16 + S], ALU.mult, ALU.subtract, [sn, "pbuf", "selcol"], ["mixf"])
                else:
                    stt(mixf[:, :], sfin[:, 0:S], selw, mixf[:, :], ALU.mult, ALU.add, [sn, "selcol"], ["mixf"])
                first = False
            cp(A, mixed[:, c, :], mixf[:, :], ["mixf"], ["mixed"])
        for oc in range(6):
            ics = [ic for ic in range(6) if any((2 * ic + a) // 3 == (2 * oc + b) // 3 for a in range(2) for b in range(2))]
            for t4 in range(4):
                tsl = slice(t4 * 512, (t4 + 1) * 512)
                bank, bn = nextbank()
                for n_, ic in enumerate(ics):
                    mm(bank[:], wgt[:, ic, oc * 128:(oc + 1) * 128], mixed[:, ic, tsl], n_ == 0, n_ == len(ics) - 1,
                       ["wgt", "mixed"], [bn])
                ts(V, t1, bank[:], col(C_PB + oc), col(C_PS + oc), ALU.add, ALU.mult, ["colsb"], [bn, "t1"])
                tt(V, oT[:, 8 + oc, tsl], t1, oT[:, 8 + oc, tsl], ALU.mult, ["t1"], ["oT%d" % (8 + oc)])

    C.attn = None
    C.rwkv = None
    def attn_phase(l):
        barrier()
        Qr = abf(0, XW)
        Kr = abf(1280, XW)
        qraw = abf(2560, S)
        ropec = af32(3584, S)
        ropes = af32(5632, S)
        t1 = af32(7680, 512)
        t2 = af32(8192, 512)
        accn = af32(8704, S)
        accd = af32(10752, S)
        Vt = abf(12800, 20 * 128).rearrange("p (a b) -> p a b", a=20)
        pT = [abf(14080, 512), abf(14336, 512)]
        msk = abf(14592, 4 * 512).rearrange("p (a b) -> p a b", a=4)
        dma("sp", ropec, rope_d[0], [], ["ropec"])
        dma("sp", ropes, rope_d[1], [], ["ropes"])
        for a_ in range(4):
            dma(G, msk[:, a_, :], amask_d[a_], [], ["msk"])
        for buf, nm in ((Qr, "Qr"), (Kr, "Kr")):
            memset(V, buf[:, 0:PADX], 0.0, [nm])
            memset(V, buf[:, PADX + S:XW], 0.0, [nm])
        cnt = [0]
        SUB = int(os.environ.get("ATT_SUB", "9"))
        if SUB == 1:
            return
        for pp in range(2):
            for g in range(3):
                d = (1, 4, 16)[g]
                for cg, dst, nm in ((46 + 2 * g + pp, Qr, "Qr"), (52 + 2 * g + pp, Kr, "Kr")):
                    inproj(l, cg, lambda t4, bank, bn, cg=cg: act(qraw[:, t4 * 512:(t4 + 1) * 512], bank[:], AF.Identity,
                                                                  ["colsb"], [bn, "qraw"], bias=col(C_BIN + cg)))
                    for t4 in range(4):
                        tsl = slice(t4 * 512, (t4 + 1) * 512)
                        bank, bn = nextbank()
                        mm(bank[:], permb[:], qraw[:, tsl], True, True, ["permb", "qraw"], [bn])
                        tt(V, t1, bank[:], ropes[:, tsl], ALU.mult, ["ropes"], [bn, "t1"])
                        tt(V, t2, qraw[:, tsl], ropec[:, tsl], ALU.mult, ["qraw", "ropec"], ["t2"])
                        tt(V, dst[:, PADX + t4 * 512:PADX + (t4 + 1) * 512], t1, t2, ALU.add, ["t1", "t2"], [nm])
                if SUB == 2:
                    return
                wv, wvn = load_w(("wv", l, g, pp), win_src(l, (58 + 2 * g + pp) * 128, 128))
                if d == 1:
                    tsls = [slice(PADX + 128 * m - 64, PADX + 128 * m + 64) for m in range(17)]
                elif d == 4:
                    tsls = []
                    for r in range(4):
                        for m in range(5):
                            s0 = PADX + r + 4 * (128 * m - 64)
                            tsls.append(slice(s0, s0 + 509, 4))
                else:
                    tsls = [slice(PADX + r, PADX + r + 2033, 16) for r in range(16)]
                for j0 in range(0, len(tsls), 4):
                    grp = tsls[j0:j0 + 4]
                    bank, bn = nextbank()
                    for j, sl in enumerate(grp):
                        for k in range(8):
                            mm(bank[:, j * 128:(j + 1) * 128], xT[:, k, sl], wv[:, k, 0:128], k == 0, False, ["xT", wvn], [bn])
                        mm(bank[:, j * 128:(j + 1) * 128], onesb[0:1, 0:128], vrowb[0:1, (2 * g + pp) * 128:(2 * g + pp + 1) * 128],
                           False, True, ["onesb", "vrowb"], [bn])
                    n = len(grp)
                    cp(A, Vt[:, j0:j0 + n, :], bank[:, 0:n * 128].rearrange("p (a b) -> p a b", a=n), [], [bn, "Vt"])
                if SUB == 3:
                    return
                for sbk in range(4):
                    for j in range(4):
                        if d == 1:
                            m = 4 * sbk + j
                            qsl = slice(PADX + 128 * m, PADX + 128 * m + 128)
                            chunks = [(slice(PADX + 128 * m - 64, PADX + 128 * m + 64), m),
                                      (slice(PADX + 128 * m + 64, PADX + 128 * m + 192), m + 1)]
                            mi = 1 if m == 0 else (2 if m == 15 else 0)
                        elif d == 4:
                            r, m = sbk, j
                            q0 = PADX + r + 512 * m
                            qsl = slice(q0, q0 + 509, 4)
                            k0 = PADX + r + 4 * (128 * m - 64)
                            chunks = [(slice(k0, k0 + 509, 4), r * 5 + m), (slice(k0 + 512, k0 + 512 + 509, 4), r * 5 + m + 1)]
                            mi = 1 if m == 0 else (2 if m == 3 else 0)
                        else:
                            r = 4 * sbk + j
                            qsl = slice(PADX + r, PADX + r + 2033, 16)
                            chunks = [(qsl, r)]
                            mi = 3
                        nch = len(chunks)
                        wd = nch * 128
                        for h in range(2):
                            hp_ = slice(64 * h, 64 * h + 64)
                            si = h + 2 * (cnt[0] % 2)
                            sbank, sbn = psb[si], "ps%d" % si
                            pTb, pTn = pT[h], "pT%d" % h
                            for ci, (ks, vt) in enumerate(chunks):
                                mm(sbank[:, ci * 128:(ci + 1) * 128], Kr[hp_, ks], Qr[hp_, qsl], True, True, ["Kr", "Qr"], [sbn])
                            act(pTb[:, 0:wd], sbank[:, 0:wd], AF.Exp, [], [sbn, pTn], scale=0.125)
                            tt(V, pTb[:, 0:wd], pTb[:, 0:wd], msk[:, mi, 0:wd], ALU.mult, ["msk"], [pTn])
                            for ci, (ks, vt) in enumerate(chunks):
                                mm(psb[6][hp_, j * 128:(j + 1) * 128], Vt[:, vt, 64 * h:64 * h + 64], pTb[:, ci * 128:(ci + 1) * 128],
                                   ci == 0, ci == nch - 1, ["Vt", pTn], ["ps6"])
                            for ci, (ks, vt) in enumerate(chunks):
                                mm(psb[7][hp_, j * 128:(j + 1) * 128], onesb[:, 0:64], pTb[:, ci * 128:(ci + 1) * 128],
                                   ci == 0, ci == nch - 1, ["onesb", pTn], ["ps7"])
                        cnt[0] += 1
                    if d == 1:
                        vn, vd = accn[:, 512 * sbk:512 * sbk + 512], accd[:, 512 * sbk:512 * sbk + 512]
                        bn_, bd_ = psb[6][:, :], psb[7][:, :]
                    elif d == 4:
                        vn, vd = accn[:, sbk:S:4], accd[:, sbk:S:4]
                        bn_, bd_ = psb[6][:, :], psb[7][:, :]
                    else:
                        vn = accn.rearrange("p (i r) -> p r i", r=16)[:, 4 * sbk:4 * sbk + 4, :]
                        vd = accd.rearrange("p (i r) -> p r i", r=16)[:, 4 * sbk:4 * sbk + 4, :]
                        bn_ = psb[6][:, :].rearrange("p (r i) -> p r i", r=4)
                        bd_ = psb[7][:, :].rearrange("p (r i) -> p r i", r=4)
                    if g == 0:
                        cp(V, vn, bn_, [], ["ps6", "accn"])
                        cp(A, vd, bd_, [], ["ps7", "accd"])
                    else:
                        tt(V, vn, vn, bn_, ALU.add, [], ["ps6", "accn"])
                        tt(V, vd, vd, bd_, ALU.add, [], ["ps7", "accd"])
                if os.environ.get("ATT_STOP") == str(g + 1):
                    return
            oc = 14 + pp
            inproj(l, 64 + pp, lambda t4, bank, bn, oc=oc, pp=pp: act(oT[:, oc, t4 * 512:(t4 + 1) * 512], bank[:], AF.Silu,
                                                                        ["colsb"], [bn, "oT%d" % oc], bias=col(C_BIN + 64 + pp)))
            add(V, lambda e: e.reciprocal(out=accd, in_=accd), reads=[], writes=["accd"])
            tt(V, accn, accn, accd, ALU.mult, ["accd"], ["accn"])
            tt(V, oT[:, oc, :], accn, oT[:, oc, :], ALU.mult, ["accn"], ["oT%d" % oc])

    C.attn = attn_phase

    def rwkv_phase(l):
        barrier()
        lw = abf(0, S)
        la = abf(1024, S)
        wup = abf(2048, 1024)
        aup = abf(2560, 1024)
        hbuf = af32(3072, 2050)
        tmpB = af32(3072, S)
        tmpf = af32(5124, S)
        rbf = abf(7172, S)
        kbf = abf(8196, S)
        vbf = abf(9220, S)
        kkbf = abf(10244, S)
        bonus = abf(11268, S)
        ytok = af32(12292, S).rearrange("p (c i) -> p c i", c=32)
        Vtok = abf(14340, S).rearrange("p (c i) -> p c i", c=32)
        reset = af32(15364, 512)
        STf = af32(15876, 64)
        STb = abf(15940, 64)
        Xs = abf(15972, 64)
        Us = abf(16004, 64)
        Wc = af32(16036, 8)
        totc = af32(16044, 8)
        stat = af32(16052, 128)
        ynb = rbf.rearrange("p (c i) -> p c i", c=32)
        tmpf3 = tmpf.rearrange("p (c i) -> p c i", c=32)
        sg = o8f32(0, 512)
        aa = o8f32(512, 512)
        Gc = o8f32(1024, 512)
        tmpG = o8f32(1536, 512)
        E = o8f32(2048, 512)
        bb = o8f32(2560, 512)
        kd = o8f32(3072, 512)
        AR2 = o8bf(2 * 3584, 1024).rearrange("p (c n) -> p c n", c=8)
        AR4 = o8bf(2 * 3584, 1024).rearrange("p (c a j) -> p c a j", c=8, a=2)
        BT = o8bf(2 * 4096, 512)
        KT = o8bf(2 * 4352, 512)
        BHT = o8bf(2 * 4608, 512)
        KHT = o8bf(2 * 4864, 512)
        BHtok = o8bf(2 * 5120, 512).rearrange("p (c j) -> p c j", c=8)
        KHtok = o8bf(2 * 5376, 512).rearrange("p (c j) -> p c j", c=8)
        G1s = o8bf(2 * 5632, 1024).rearrange("p (c n) -> p c n", c=8)
        G2s = o8bf(2 * 6144, 1024).rearrange("p (c n) -> p c n", c=8)
        QP = o8bf(2 * 6656, 1024).rearrange("p (c n) -> p c n", c=8)
        Pn = o8bf(2 * 7168, 512).rearrange("p (c n) -> p c n", c=8)
        m1 = [o8bf(2 * 7424, 512), o8bf(2 * 7680, 512)]
        m2 = [o8bf(2 * 7936, 256), o8bf(2 * 8064, 256)]
        t1f = E

        mfb = mixf[:, :].bitcast(BF16)
        Wc1 = af32(16180, 8)
        SETS = [
            (AR2, AR4, G1s, G2s, QP, BHtok, KHtok, Wc),
            (mfb[:, 0:1024].rearrange("p (c n) -> p c n", c=8), mfb[:, 0:1024].rearrange("p (c a j) -> p c a j", c=8, a=2),
             mfb[:, 1024:2048].rearrange("p (c n) -> p c n", c=8), mfb[:, 2048:3072].rearrange("p (c n) -> p c n", c=8),
             mfb[:, 3072:4096].rearrange("p (c n) -> p c n", c=8),
             xstage[:, 0:512].rearrange("p (c j) -> p c j", c=8), xstage[:, 512:1024].rearrange("p (c j) -> p c j", c=8), Wc1),
        ]

        STATE = [(STf, STb, Xs, Us), (af32(16188, 64), abf(16252, 64), abf(16284, 64), abf(16316, 64))]

        def c8v(ap):
            return ap.rearrange("p (c j) -> p c j", c=8)

        dma(G, wup, wup_d[l], [], ["wup"])
        dma(G, aup, aup_d[l], [], ["aup"])
        dma("sp", reset, rmask_d[0][:, 1280:1792], [], ["reset"])
        for z in range(2):
            dma(G, m1[z], rmask_d[z][:, 0:512], [], ["m1_%d" % z])
            dma(G, m2[z], rmask_d[z][:, 1024:1280], [], ["m2_%d" % z])
        memset(V, hbuf[:, 0:1], 0.0, ["hbuf"])
        memset(V, hbuf[:, 2049:2050], 0.0, ["hbuf"])

        def shifted(cg, dst, dstname):
            inproj(l, cg, lambda t4, bank, bn: act(hbuf[:, 1 + t4 * 512:1 + (t4 + 1) * 512], bank[:], AF.Identity,
                                                   ["colsb"], [bn, "hbuf"], bias=col(C_BIN + cg)))
            act(tmpf, hbuf[:, 1:2049], AF.Identity, ["hbuf", "c0col"], ["tmpf"], scale=c0col[:, cg:cg + 1])
            stt(tmpf, hbuf[:, 0:2048], col(C_MU0 + cg), tmpf, ALU.mult, ALU.add, ["hbuf", "colsb"], ["tmpf"])
            stt(dst, hbuf[:, 2:2050], col(C_MU1 + cg), tmpf, ALU.mult, ALU.add, ["hbuf", "colsb", "tmpf"], [dstname])

        def pairbank():
            return [nextbank(), nextbank()]

        shifted(24, tmpf, "tmpf")
        act(lw, tmpf, AF.Tanh, ["tmpf"], ["lw"])
        shifted(25, la, "la")

        for hp in range(8):
            shifted(hp, rbf, "rbf")
            shifted(8 + hp, kbf, "kbf")
            shifted(16 + hp, vbf, "vbf")
            inproj(l, 26 + hp, lambda t4, bank, bn, hp=hp: act(oT[:, hp, t4 * 512:(t4 + 1) * 512], bank[:], AF.Silu,
                                                               ["colsb"], [bn, "oT%d" % hp], bias=col(C_BIN + 26 + hp)))
            ts(V, tmpf, kbf, col(C_KK + hp), None, ALU.mult, None, ["kbf", "colsb"], ["tmpf"])
            act(tmpB, tmpf, AF.Square, ["tmpf"], ["hbuf"])
            for t4 in range(4):
                tsl = slice(t4 * 512, (t4 + 1) * 512)
                bank, bn = nextbank()
                mm(bank[:], bones[:], tmpB[:, tsl], True, True, ["bones", "hbuf"], [bn])
                act(tmpB[:, tsl], bank[:], AF.Sqrt, [], [bn, "hbuf"], bias=1e-12)
            add(V, lambda e: e.reciprocal(out=tmpB, in_=tmpB), reads=[], writes=["hbuf"])
            tt(V, kkbf, tmpf, tmpB, ALU.mult, ["tmpf", "hbuf"], ["kkbf"])
            stt(tmpf, rbf, col(C_RK + hp), kbf, ALU.mult, ALU.mult, ["rbf", "kbf", "colsb"], ["tmpf"])
            for t4 in range(4):
                tsl = slice(t4 * 512, (t4 + 1) * 512)
                bank, bn = nextbank()
                mm(bank[:], bones[:], tmpf[:, tsl], True, True, ["bones", "tmpf"], [bn])
                tt(V, bonus[:, tsl], bank[:], vbf[:, tsl], ALU.mult, ["vbf"], [bn, "bonus"])
            for T in range(4):
                pb = pairbank()
                for h in range(2):
                    hs = slice(64 * h, 64 * h + 64)
                    bank, bn = pb[h]
                    for c8 in range(8):
                        c = T * 8 + c8
                        mm(bank[hs, c8 * 64:(c8 + 1) * 64], vbf[hs, c * 64:(c + 1) * 64], identb[hs, 64 * h:64 * h + 64],
                           True, True, ["vbf", "identb"], [bn])
                    cp(A, Vtok[hs, T * 8:(T + 1) * 8, :], c8v(bank[hs, :]), [], [bn, "Vtok"])

            items = [(0, T) for T in range(4)] + [(1, T) for T in range(3, -1, -1)]

            def produce(z, T, s):
                zs = slice(64 * z, 64 * z + 64)
                tsl = slice(T * 512, (T + 1) * 512)
                AR2_, AR4_, G1s_, G2s_, QP_, BHtok_, KHtok_, Wc_ = SETS[s]
                ss = str(s)
                ARn = "AR" + ss
                b1, b1n = nextbank()
                mm(b1[:], wup[zs, hp * 128:(hp + 1) * 128], lw[zs, tsl], True, True, ["wup", "lw"], [b1n])
                act(sg, b1[:], AF.Sigmoid, ["colsb"], [b1n, "sg"], bias=col(C_W0 + z * 8 + hp))
                b2, b2n = nextbank()
                mm(b2[:], aup[zs, hp * 128:(hp + 1) * 128], la[zs, tsl], True, True, ["aup", "la"], [b2n])
                act(aa, b2[:], AF.Sigmoid, ["colsb"], [b2n, "aa"], bias=col(C_A0 + z * 8 + hp))
                yield
                add(V, lambda e: e.tensor_tensor_scan(out=Gc, data0=reset, data1=sg, initial=0.0, op0=ALU.mult, op1=ALU.add),
                    reads=["reset", "sg"], writes=["Gc"])
                cp(V, totc, c8v(Gc)[:, :, 63], ["Gc"], ["totc"])
                totb = totc.unsqueeze(2).to_broadcast([128, 8, 64])
                if z == 1:
                    tt(V, tmpG, sg, Gc, ALU.subtract, ["sg", "Gc"], ["tmpG"])
                    tt(V, c8v(Gc), c8v(tmpG), totb, ALU.add, ["tmpG", "totc"], ["Gc"])
                yield
                tt(V, sg, Gc, sg, ALU.subtract, ["Gc"], ["sg"])
                tt(V, c8v(tmpG), c8v(Gc), totb, ALU.subtract, ["Gc", "totc"], ["tmpG"])
                act(E, Gc, AF.Exp, ["Gc"], ["E"], scale=-CDEC)
                act(sg, sg, AF.Exp, [], ["sg"], scale=-CDEC)
                act(Gc, Gc, AF.Exp, [], ["Gc"], scale=CDEC)
                act(tmpG, tmpG, AF.Exp, [], ["tmpG"], scale=CDEC)
                act(Wc_, totc, AF.Exp, ["totc"], ["Wc" + ss], scale=-CDEC)
                yield
                ts(V, kd, aa, -1.0, col(C_KA + hp), ALU.add, ALU.mult, ["aa", "colsb"], ["kd"])
                stt(kd, kd, 1.0, kbf[:, tsl], ALU.add, ALU.mult, ["kbf"], ["kd"])
                tt(V, bb, kkbf[:, tsl], aa, ALU.mult, ["kkbf", "aa"], ["bb"])
                yield
                tt(V, AR4_[:, :, 1, :], c8v(rbf[:, tsl]), c8v(E), ALU.mult, ["rbf", "E"], [ARn])
                stt(AR4_[:, :, 0, :], c8v(kkbf[:, tsl]), -1.0, c8v(sg), ALU.mult, ALU.mult, ["kkbf", "sg"], [ARn])
                yield
                tt(V, BT, bb, Gc, ALU.mult, ["bb", "Gc"], ["BT"])
                tt(V, KT, kd, Gc, ALU.mult, ["kd", "Gc"], ["KT"])
                tt(V, BHT, bb, tmpG, ALU.mult, ["bb", "tmpG"], ["BHT"])
                tt(V, KHT, kd, tmpG, ALU.mult, ["kd", "tmpG"], ["KHT"])
                yield
                for src, srcn, dst, dstn in ((BHT, "BHT", BHtok_, "BHtok" + ss), (KHT, "KHT", KHtok_, "KHtok" + ss)):
                    pb = pairbank()
                    for h in range(2):
                        hs = slice(64 * h, 64 * h + 64)
                        bank, bn = pb[h]
                        for c8 in range(8):
                            mm(bank[hs, c8 * 64:(c8 + 1) * 64], src[hs, c8 * 64:(c8 + 1) * 64], identb[hs, 64 * h:64 * h + 64],
                               True, True, [srcn, "identb"], [bn])
                        cp(A, dst[hs, :, :], c8v(bank[hs, :]), [], [bn, dstn + str(h)])
                    yield
                chains = []
                for hv in range(2):
                    for h in range(2):
                        ia, ib = {(0, 0): (0, 1), (0, 1): (2, 3), (1, 0): (4, 5), (1, 1): (6, 7)}[(hv, h)]
                        chains.append((hv, h, psb[ia], "ps%d" % ia, psb[ib], "ps%d" % ib))
                for hv, h, bA, bAn, bB, bBn in chains:
                    hs = slice(64 * h, 64 * h + 64)
                    for cl in range(4):
                        c8 = hv * 4 + cl
                        mm(bA[hs, cl * 128:(cl + 1) * 128], BT[hs, c8 * 64:(c8 + 1) * 64], AR2_[hs, c8, :], True, True,
                           ["BT", ARn], [bAn])
                    for cl in range(4):
                        c8 = hv * 4 + cl
                        mm(bB[hs, cl * 128:(cl + 1) * 128], KT[hs, c8 * 64:(c8 + 1) * 64], AR2_[hs, c8, :], True, True,
                           ["KT", ARn], [bBn])
                for hv, h, bA, bAn, bB, bBn in chains:
                    cs = slice(hv * 4, hv * 4 + 4)
                    hs = slice(64 * h, 64 * h + 64)
                    nm = "%s%d%d" % (ss, hv, h)
                    tt(V, G1s_[hs, cs, :], bA[hs, :].rearrange("p (c n) -> p c n", c=4),
                       m1[z][hs, :].rearrange("p (c n) -> p c n", c=4), ALU.mult, ["m1_%d" % z], [bAn, "G1s" + nm])
                    tt(V, G2s_[hs, cs, :], bB[hs, :].rearrange("p (c n) -> p c n", c=4),
                       m1[z][hs, :].rearrange("p (c n) -> p c n", c=4), ALU.mult, ["m1_%d" % z], [bBn, "G2s" + nm])
                for hv, h, bA, bAn, bB, bBn in chains:
                    hs = slice(64 * h, 64 * h + 64)
                    for cl in range(4):
                        c8 = hv * 4 + cl
                        mm(bA[hs, cl * 64:(cl + 1) * 64], AR4_[hs, c8, 0, :], BT[hs, c8 * 64:(c8 + 1) * 64], True, True,
                           ["BT", ARn], [bAn])
                for hv, h, bA, bAn, bB, bBn in chains:
                    cs = slice(hv * 4, hv * 4 + 4)
                    hs = slice(64 * h, 64 * h + 64)
                    nm = "%s%d%d" % (ss, hv, h)
                    pn = "Pn%d%d" % (hv, h)
                    tt(V, Pn[hs, cs, :], bA[hs, 0:256].rearrange("p (c n) -> p c n", c=4),
                       m2[z][hs, :].rearrange("p (c n) -> p c n", c=4), ALU.mult, ["m2_%d" % z], [bAn, pn])
                    cp(A, QP_[hs, cs, 0:64], eye2[hs, :].unsqueeze(1).to_broadcast([64, 4, 64]), ["eye2"], ["QP" + nm])
                    cp(A, QP_[hs, cs, 64:128], G1s_[hs, cs, 0:64], ["G1s" + nm], ["QP" + nm])
                for k in range(6):
                    last = (k == 5)
                    wA = 64 if last else 128
                    for hv, h, bA, bAn, bB, bBn in chains:
                        hs = slice(64 * h, 64 * h + 64)
                        nm = "%s%d%d" % (ss, hv, h)
                        pn = "Pn%d%d" % (hv, h)
                        for cl in range(4):
                            c8 = hv * 4 + cl
                            mm(bA[hs, cl * 128:cl * 128 + wA], Pn[hs, c8, :], QP_[hs, c8, 0:wA], True, True,
                               [pn, "QP" + nm], [bAn])
                        if not last:
                            for cl in range(4):
                                c8 = hv * 4 + cl
                                mm(bB[hs, cl * 64:(cl + 1) * 64], QP_[hs, c8, 64:128], Pn[hs, c8, :], True, True,
                                   [pn, "QP" + nm], [bBn])
                    for hv, h, bA, bAn, bB, bBn in chains:
                        cs = slice(hv * 4, hv * 4 + 4)
                        hs = slice(64 * h, 64 * h + 64)
                        nm = "%s%d%d" % (ss, hv, h)
                        pn = "Pn%d%d" % (hv, h)
                        bA3 = bA[hs, :].rearrange("p (c n) -> p c n", c=4)
                        tt(V, QP_[hs, cs, 0:64], QP_[hs, cs, 0:64], bA3[:, :, 0:64], ALU.add, [], [bAn, "QP" + nm])
                        if not last:
                            cp(A, QP_[hs, cs, 64:128], bA3[:, :, 64:128], [], [bAn, "QP" + nm])
                            cp(A, Pn[hs, cs, :], bB[hs, 0:256].rearrange("p (c n) -> p c n", c=4), [], [bBn, pn])
                yield

            def consume(z, T, s, first):
                AR2_, AR4_, G1s_, G2s_, QP_, BHtok_, KHtok_, Wc_ = SETS[s]
                STf_, STb_, Xs_, Us_ = STATE[z]
                ss = str(s)
                zn = str(z)
                ARn = "AR" + ss
                if first:
                    memset(V, STf_, 0.0, ["STf" + zn + "0", "STf" + zn + "1"])
                    memset(V, STb_, 0.0, ["STb" + zn + "0", "STb" + zn + "1"])
                cord = range(8) if z == 0 else range(7, -1, -1)
                for c8 in cord:
                    c = T * 8 + c8
                    H = []
                    for h in range(2):
                        H.append((slice(64 * h, 64 * h + 64), zn + str(h), "%s%d%d" % (ss, c8 // 4, h),
                                  psb[2 * z + h], "ps%d" % (2 * z + h), psb[4 + 2 * z + h], "ps%d" % (4 + 2 * z + h)))
                    for hs, hn, nm, cb, cbn, yb, ybn in H:
                        mm(cb[hs, 0:64], AR4_[hs, c8, 0, :], STb_[hs, :], True, False, [ARn, "STb" + hn], [cbn])
                        mm(cb[hs, 0:64], G2s_[hs, c8, 0:64], Vtok[hs, c, :], False, True, ["G2s" + nm, "Vtok"], [cbn])
                    yield
                    for hs, hn, nm, cb, cbn, yb, ybn in H:
                        cp(A, Xs_[hs, :], cb[hs, 0:64], [], [cbn, "Xs" + hn])
                    for hs, hn, nm, cb, cbn, yb, ybn in H:
                        mm(cb[hs, 64:128], QP_[hs, c8, 0:64], Xs_[hs, :], True, True, ["QP" + nm, "Xs" + hn], [cbn])
                    yield
                    for hs, hn, nm, cb, cbn, yb, ybn in H:
                        cp(V if z == 0 else A, Us_[hs, :], cb[hs, 64:128], [], [cbn, "Us" + hn])
                    for hs, hn, nm, cb, cbn, yb, ybn in H:
                        mm(cb[hs, 128:192], BHtok_[hs, c8, :], Us_[hs, :], True, False, ["BHtok" + ss + hn[1], "Us" + hn], [cbn])
                        mm(cb[hs, 128:192], KHtok_[hs, c8, :], Vtok[hs, c, :], False, True, ["KHtok" + ss + hn[1], "Vtok"], [cbn])
                    for hs, hn, nm, cb, cbn, yb, ybn in H:
                        mm(yb[hs, c8 * 64:(c8 + 1) * 64], AR4_[hs, c8, 1, :], STb_[hs, :], True, False, [ARn, "STb" + hn], [ybn])
                        mm(yb[hs, c8 * 64:(c8 + 1) * 64], G1s_[hs, c8, 64:128], Us_[hs, :], False, False,
                           ["G1s" + nm, "Us" + hn], [ybn])
                        mm(yb[hs, c8 * 64:(c8 + 1) * 64], G2s_[hs, c8, 64:128], Vtok[hs, c, :], False, True,
                           ["G2s" + nm, "Vtok"], [ybn])
                    yield
                    for hs, hn, nm, cb, cbn, yb, ybn in H:
                        stt(STb_[hs, :], STb_[hs, :], Wc_[hs, c8:c8 + 1], cb[hs, 128:192], ALU.mult, ALU.add,
                            ["Wc" + ss], [cbn, "STb" + hn])
                    yield
                for h in range(2):
                    hs = slice(64 * h, 64 * h + 64)
                    yb, ybn = psb[4 + 2 * z + h], "ps%d" % (4 + 2 * z + h)
                    tt(V, ytok[hs, T * 8:(T + 1) * 8, :], ytok[hs, T * 8:(T + 1) * 8, :], c8v(yb[hs, :]), ALU.add,
                       [], [ybn, "ytok%d" % T])
                yield

            for T_ in range(4):
                memset(V, ytok[:, T_ * 8:(T_ + 1) * 8, :], 0.0, ["ytok%d" % T_])
            for i in range(4):
                for _ in produce(0, i, 0):
                    pass
                for _ in produce(1, 3 - i, 1):
                    pass
                gens = [consume(0, i, 0, i == 0), consume(1, 3 - i, 1, i == 0)]
                while gens:
                    for g_ in list(gens):
                        try:
                            next(g_)
                        except StopIteration:
                            gens.remove(g_)
            add(V, lambda e: e.tensor_reduce(out=stat[:, 0:32], in_=ytok, axis=AX.X, op=ALU.add), reads=["ytok0", "ytok1", "ytok2", "ytok3"], writes=["st0"])
            ts(V, stat[:, 32:64], stat[:, 0:32], -1.0 / 64, None, ALU.mult, None, ["st0"], ["st1"])
            tt(V, ytok, ytok, stat[:, 32:64].unsqueeze(2).to_broadcast([128, 32, 64]), ALU.add, ["st1"], ["ytok0", "ytok1", "ytok2", "ytok3"])
            act(tmpf3, ytok, AF.Square, ["ytok0", "ytok1", "ytok2", "ytok3"], ["tmpf"])
            add(V, lambda e: e.tensor_reduce(out=stat[:, 64:96], in_=tmpf3, axis=AX.X, op=ALU.add), reads=["tmpf"], writes=["st2"])
            act(stat[:, 96:128], stat[:, 64:96], AF.Sqrt, ["st2"], ["st3"], bias=64e-5, scale=1.0 / 64)
            add(V, lambda e: e.reciprocal(out=stat[:, 96:128], in_=stat[:, 96:128]), reads=[], writes=["st3"])
            tt(V, ynb, ytok, stat[:, 96:128].unsqueeze(2).to_broadcast([128, 32, 64]), ALU.mult, ["ytok0", "ytok1", "ytok2", "ytok3", "st3"], ["rbf"])
            for T in range(4):
                tsl = slice(T * 512, (T + 1) * 512)
                pb = pairbank()
                for h in range(2):
                    hs = slice(64 * h, 64 * h + 64)
                    bank, bn = pb[h]
                    for c8 in range(8):
                        mm(bank[hs, c8 * 64:(c8 + 1) * 64], ynb[hs, T * 8 + c8, :], identb[hs, 64 * h:64 * h + 64], True, True,
                           ["rbf", "identb"], [bn])
                    act(t1f[hs, :], bank[hs, :], AF.Identity, ["colsb"], [bn, "t1f" + str(h)],
                        bias=colsb[hs, C_GB + hp:C_GB + hp + 1], scale=colsb[hs, C_GG + hp:C_GG + hp + 1])
                tt(V, t1f, t1f, bonus[:, tsl], ALU.add, ["bonus", "t1f0", "t1f1"], ["t1f0", "t1f1"])
                tt(V, oT[:, hp, tsl], t1f, oT[:, hp, tsl], ALU.mult, ["t1f0", "t1f1"], ["oT%d" % hp])

    C.rwkv = rwkv_phase


    dma("sp", colsb[:], cols_d[0], [], ["colsb"])
    xld = [af32(0, 1024), af32(1024, 1024)]
    for t16 in range(16):
        dma("sp", xld[t16 % 2], x_in[t16 * 128:(t16 + 1) * 128, :], [], ["xld%d" % (t16 % 2)])
        store_xT(xld[t16 % 2], "xld%d" % (t16 % 2), t16)
    import os
    STOP = int(os.environ.get("KSTOP", "9"))
    for l in range(depth if STOP > 1 else 0):
        if l > 0:
            dma("sp", colsb[:], cols_d[l], [], ["colsb"])
        dma(G, vrowb[:], vrow_d[l], [], ["vrowb"])
        tt(V, c0col[:], colsb[:, C_MU0:C_MU0 + 26], colsb[:, C_MU1:C_MU1 + 26], ALU.add, ["colsb"], ["c0col"])
        ts(V, c0col[:], c0col[:], -1.0, 1.0, ALU.mult, ALU.add, [], ["c0col"])
        if C.rwkv is not None and "A" in PH:
            C.rwkv(l)
        else:
            for c in range(8):
                memset(V, oT[:, c, :], 0.0, ["oT%d" % c])
        if "B" in PH:
            pool_phase(l)
        else:
            barrier()
            for c in range(8, 14):
                memset(V, oT[:, c, :], 0.0, ["oT%d" % c])
        if C.attn is not None and "C" in PH:
            C.attn(l)
        else:
            barrier()
            for c in range(14, 16):
                memset(V, oT[:, c, :], 0.0, ["oT%d" % c])
        if dbg is not None and l == depth - 1:
            dtmp = af32(0, S)
            for c in range(16):
                cp(V, dtmp, oT[:, c, :], ["oT%d" % c], ["dtmp"])
                dma("sp", dbg_out[:, c, :], dtmp, ["dtmp"], [])
        if STOP > 2:
            final_phase(l, l == depth - 1)
    P.emit(nc)
    st.close()
    return nc


PH = "ABC"


def EXTRA_PHASES(L):
    pass


def host_prep(inp):
    f = np.float32
    g = lambda k: np.asarray(inp[k], dtype=f)
    colv = lambda v: np.ascontiguousarray(v.reshape(-1, 128).T)
    cols = []
    for l in range(DEPTH):
        parts = [colv(g("b_in")[l]), colv(g("rwkv_mu")[l, 0]), colv(g("rwkv_mu")[l, 1]),
                 colv(g("rwkv_w0")[l, 0]), colv(g("rwkv_w0")[l, 1]), colv(g("rwkv_a0")[l, 0]), colv(g("rwkv_a0")[l, 1]),
                 colv(g("rwkv_k_k")[l]), colv(g("rwkv_k_a")[l]), colv(g("rwkv_r_k")[l].reshape(-1)),
                 colv(g("rwkv_gn_g")[l]), colv(g("rwkv_gn_b")[l]), colv(g("pool_b")[l]), colv(g("pool_scale")[l])]
        cols.append(np.concatenate(parts, axis=1))
    cols = np.stack(cols)
    assert cols.shape == (DEPTH, 128, NCOLS), cols.shape
    shared = {
        "w_in": g("w_in"), "cols": cols,
        "vrow": np.ascontiguousarray(g("b_in")[:, None, 7424:8192]),
        "w_up": np.ascontiguousarray(g("rwkv_w_up").reshape(DEPTH, 128, 1024)),
        "a_up": np.ascontiguousarray(g("rwkv_a_up").reshape(DEPTH, 128, 1024)),
        "pool_w": _blockdiag(g("pool_w")), "proj_a": g("proj_a"), "proj_b": g("proj_b"), "proj_c": g("proj_c"),
        "w_out": g("w_out"),
        "lnrow": np.ascontiguousarray(np.stack([g("ln_g"), g("ln_b")], axis=1)),
    }
    shared.update(host_consts())
    return shared


def _blockdiag(pw):
    out = np.zeros((DEPTH, 768, 768), np.float32)
    for gi in range(4):
        out[:, 192 * gi:192 * gi + 192, 192 * gi:192 * gi + 192] = pw[:, gi]
    return out


def host_consts():
    f = np.float32
    ident = np.eye(128, dtype=f)
    bones = np.zeros((128, 128), f)
    bones[:64, :64] = 1
    bones[64:, 64:] = 1
    perm = np.zeros((128, 128), f)
    for m in range(128):
        c = m % 64
        k = m + 32 if c < 32 else m - 32
        perm[k, m] = 1
    eye2 = np.concatenate([np.eye(64, dtype=f), np.eye(64, dtype=f)], 0)
    cst = np.concatenate([ident, bones, perm, eye2], axis=1)
    inv = np.power(f(10000.0), -np.arange(0, 64, 2, dtype=f) / f(64))
    ang = np.arange(S, dtype=f)[:, None] * inv[None, :]
    ang = np.concatenate([ang, ang], axis=-1).astype(f)
    cosT = np.cos(ang).T.astype(f)
    sinT = np.sin(ang).T.astype(f)
    sign = np.where(np.arange(64) < 32, -1.0, 1.0).astype(f)[:, None]
    rope = np.stack([np.concatenate([cosT, cosT], 0), np.concatenate([sinT * sign, sinT * sign], 0)]).astype(f)
    b = np.arange(128)[:, None]
    a = np.arange(128)[None, :]
    mA = (b >= a).astype(f)
    mB = (a >= b).astype(f)
    mAf = mA * (b >= 64)
    mBl = mB * (b < 64)
    m16 = (np.abs(a - b) <= 64).astype(f)
    amask = np.stack([np.concatenate([mA, mB, mA, mB], 1), np.concatenate([mAf, mB, mAf, mB], 1),
                      np.concatenate([mA, mBl, mA, mBl], 1), np.concatenate([m16, m16, m16, m16], 1)]).astype(f)
    s_ = np.arange(64)[:, None]
    t_ = np.arange(64)[None, :]
    rm = []
    for z in range(2):
        if z == 0:
            strict = (s_ < t_)
            incl = (s_ <= t_)
        else:
            strict = (s_ > t_)
            incl = (s_ >= t_)
        m1 = np.concatenate([strict, incl], 1).astype(f)
        m2 = strict.T.astype(f)
        reset = np.ones((64, 512), f)
        reset[:, ::64] = 0
        row = np.concatenate([np.tile(m1, (1, 4)), np.tile(m1, (1, 4)), np.tile(m2, (1, 4)), reset], 1)
        rm.append(np.concatenate([row, row], 0))
    rmask = np.stack(rm).astype(f)
    pedge = np.ones((128, 4, 16), f)
    for gi in range(4):
        h = 1 << gi
        for e in range(8):
            t = e
            cnt = min(t + h, S - 1) - max(t - h, 0) + 1
            pedge[:, gi, e] = (2 * h + 1) / cnt
            t = S - 8 + e
            cnt = min(t + h, S - 1) - max(t - h, 0) + 1
            pedge[:, gi, 8 + e] = (2 * h + 1) / cnt
    selc = np.zeros((128, 24), f)
    for c in range(6):
        for p in range(128):
            gi = (128 * c + p) // 192
            selc[p, c * 4 + gi] = 1.0 / (2 * (1 << gi) + 1)
    return {"selc": selc, "cst": cst, "rope": rope, "amask": amask, "rmask": rmask, "pedge": pedge}


_NC_CACHE = {}


def kernel(**inputs):
    shared = host_prep(inputs)
    x = np.asarray(inputs["x"], dtype=np.float32)
    if "nc" not in _NC_CACHE:
        _NC_CACHE["nc"] = build()
    nc = _NC_CACHE["nc"]
    in_maps = []
    for c in range(8):
        m = dict(shared)
        m["x"] = np.ascontiguousarray(x[c])
        in_maps.append(m)
    res = run_bass_kernel_spmd(nc, in_maps, core_ids=list(range(8)))
    return np.stack([np.asarray(r["y"], dtype=np.float32) for r in res.results], axis=0)
```

```python
import math
import os
import numpy as np
import ml_dtypes
import concourse.bass as bass
import concourse.mybir as mybir
from concourse.bass_utils import run_bass_kernel_spmd

F32 = mybir.dt.float32
BF16 = mybir.dt.bfloat16
AF = mybir.ActivationFunctionType
ALU = mybir.AluOpType
AX = mybir.AxisListType

ENGS = ("pe", "act", "dve", "pool", "sp")
DMA_POOL = 16


class _Buf:
    __slots__ = ("last_w", "readers", "dma_readers")

    def __init__(self):
        self.last_w = None
        self.readers = {}
        self.dma_readers = []


class _Op:
    __slots__ = ("eng", "idx", "gid", "fn", "deps", "is_dma", "signal", "sem", "val", "waits",
                 "know", "dma_n", "pre_wait")

    def __init__(self, eng, idx, gid, fn, is_dma):
        self.eng = eng
        self.idx = idx
        self.gid = gid
        self.fn = fn
        self.is_dma = is_dma
        self.deps = []
        self.signal = False
        self.sem = None
        self.val = None
        self.waits = []
        self.know = None
        self.dma_n = None
        self.pre_wait = None


class Prog:
    def __init__(self):
        self.ops = {e: [] for e in ENGS}
        self.all = []
        self.bufs = {}
        self.n_dma = {e: 0 for e in ENGS}

    def _buf(self, name):
        b = self.bufs.get(name)
        if b is None:
            b = _Buf()
            self.bufs[name] = b
        return b

    def add(self, eng, fn, reads=(), writes=(), dma=False):
        op = _Op(eng, len(self.ops[eng]), len(self.all), fn, dma)
        deps = {}
        for r in reads:
            b = self._buf(r)
            if b.last_w is not None:
                deps[b.last_w.gid] = b.last_w
        for w in writes:
            b = self._buf(w)
            if b.last_w is not None:
                deps[b.last_w.gid] = b.last_w
            for d in b.readers.values():
                deps[d.gid] = d
            for d in b.dma_readers:
                deps[d.gid] = d
        op.deps = [deps[k] for k in sorted(deps)]
        for r in reads:
            b = self._buf(r)
            if dma:
                b.dma_readers.append(op)
            else:
                b.readers[eng] = op
        for w in writes:
            b = self._buf(w)
            b.last_w = op
            b.readers = {}
            b.dma_readers = []
        if dma:
            op.dma_n = self.n_dma[eng]
            self.n_dma[eng] += 1
        self.ops[eng].append(op)
        self.all.append(op)
        return op

    def resolve(self):
        know = {e: {f: -1 for f in ENGS} for e in ENGS}
        know_dma = {e: set() for e in ENGS}
        sig_count = {e: 0 for e in ENGS}
        for op in self.all:
            E = op.eng
            kn = know[E]
            for d in op.deps:
                if d.is_dma:
                    if d.gid in know_dma[E]:
                        continue
                    know_dma[E].add(d.gid)
                    op.waits.append(d)
                    for f, v in d.know.items():
                        if v > kn[f]:
                            kn[f] = v
                    continue
                F = d.eng
                if F == E:
                    if E == "pe" or op.idx - d.idx > 2:
                        continue
                    if kn[F] >= d.idx:
                        continue
                elif kn[F] >= d.idx:
                    continue
                d.signal = True
                op.waits.append(d)
                kn[F] = max(kn[F], d.idx)
                for f, v in d.know.items():
                    if f != E and v > kn[f]:
                        kn[f] = v
            snap = dict(kn)
            if not op.is_dma:
                snap[E] = op.idx
            op.know = snap
        for e in ENGS:
            c = 0
            for op in self.ops[e]:
                if op.is_dma:
                    continue
                if op.signal:
                    c += 1
                    op.val = c

    def emit(self, nc):
        self.resolve()
        import contextlib
        with contextlib.ExitStack() as st:
            esem = {e: st.enter_context(nc.semaphore("s_" + e)) for e in ENGS}
            dsem = {e: [st.enter_context(nc.semaphore("d_%s%d" % (e, i))) for i in range(DMA_POOL)]
                    for e in ENGS if self.n_dma[e] > 0}
            block = st.enter_context(nc.Block())

            def wait_for(engine, d):
                if d.is_dma:
                    engine.wait_ge(dsem[d.eng][d.dma_n % DMA_POOL], 16 * (d.dma_n // DMA_POOL + 1))
                else:
                    engine.wait_ge(esem[d.eng], d.val)

            def run(engine, e):
                ops = self.ops[e]
                for op in ops:
                    for d in op.waits:
                        wait_for(engine, d)
                    if op.is_dma:
                        n = op.dma_n
                        if n >= DMA_POOL:
                            engine.wait_ge(dsem[e][n % DMA_POOL], 16 * (n // DMA_POOL))
                        ins = op.fn(engine)
                        ins.then_inc(dsem[e][n % DMA_POOL], 16)
                    else:
                        ins = op.fn(engine)
                        if op.signal:
                            ins.then_inc(esem[e], 1)
                nd = self.n_dma[e]
                for i in range(min(nd, DMA_POOL)):
                    n = nd - 1 - i
                    engine.wait_ge(dsem[e][n % DMA_POOL], 16 * (n // DMA_POOL + 1))

            @block.tensor
            def _(eng):
                run(eng, "pe")

            @block.scalar
            def _(eng):
                run(eng, "act")

            @block.vector
            def _(eng):
                run(eng, "dve")

            @block.gpsimd
            def _(eng):
                run(eng, "pool")

            @block.sync
            def _(eng):
                run(eng, "sp")


S = 2048
D = 1024
NIN = 11520
DEPTH = 4
PADX = 256
XW = S + 2 * PADX
ALPHA = (2 * DEPTH) ** 0.25
CDEC = math.exp(-0.5)
C_BIN = 0
C_MU0 = 90
C_MU1 = 116
C_W0 = 142
C_A0 = 158
C_KK = 174
C_KA = 182
C_RK = 190
C_GG = 198
C_GB = 206
C_PB = 214
C_PS = 220
NCOLS = 226


class Ctx:
    pass


def build(depth=DEPTH, dbg=None):
    nc = bass.Bass("TRN2", target_bir_lowering=False)
    P = Prog()
    dt_in = lambda name, shape: nc.dram_tensor(name, shape, F32, kind="ExternalInput").ap()
    x_in = dt_in("x", [S, D])
    w_in = dt_in("w_in", [DEPTH, D, NIN])
    cols_d = dt_in("cols", [DEPTH, 128, NCOLS])
    vrow_d = dt_in("vrow", [DEPTH, 1, 768])
    wup_d = dt_in("w_up", [DEPTH, 128, 1024])
    aup_d = dt_in("a_up", [DEPTH, 128, 1024])
    poolw_d = dt_in("pool_w", [DEPTH, 768, 768])
    proja_d = dt_in("proj_a", [DEPTH, 1024, 1024])
    projb_d = dt_in("proj_b", [DEPTH, 768, 1024])
    projc_d = dt_in("proj_c", [DEPTH, 256, 1024])
    wout_d = dt_in("w_out", [DEPTH, 1024, 1024])
    lnrow_d = dt_in("lnrow", [DEPTH, 2, 1024])
    cst_d = dt_in("cst", [128, 128 * 3 + 64])
    rope_d = dt_in("rope", [2, 128, S])
    amask_d = dt_in("amask", [4, 128, 512])
    rmask_d = dt_in("rmask", [2, 128, 512 + 512 + 256 + 512])
    pedge_d = dt_in("pedge", [128, 4, 16])
    y_out = nc.dram_tensor("y", [S, D], F32, kind="ExternalOutput").ap()
    xres = nc.dram_tensor("xres", [S, D], F32, kind="Internal").ap()
    dbg_out = None
    if dbg is not None:
        dbg_out = nc.dram_tensor("dbg", [128, 16, S], F32, kind="ExternalOutput").ap()

    import contextlib
    st = contextlib.ExitStack()
    sb = lambda name, shape, dt: st.enter_context(nc.sbuf_tensor(name, shape, dt))
    xT = sb("xT", [128, 8, XW], BF16)
    oT = sb("oT", [128, 16, S], BF16)
    wb = [sb("wb%d" % i, [128, 8, 512], BF16) for i in range(2)]
    ARENA = 16384
    arena = sb("arena", [128, ARENA], F32)
    colsb = sb("colsb", [128, NCOLS], F32)
    c0col = sb("c0col", [128, 26], F32)
    identb = sb("identb", [128, 128], BF16)
    permb = sb("permb", [128, 128], BF16)
    onesb = sb("onesb", [128, 128], BF16)
    eye2 = sb("eye2", [128, 64], BF16)
    bones = sb("bones", [128, 128], F32)
    vrowb = sb("vrowb", [1, 768], BF16)
    selcol = sb("selcol", [128, 24], F32)
    mixf = sb("mixf", [128, S], F32)
    selc_d = dt_in("selc", [128, 24])
    psb = [st.enter_context(nc.psum_tensor("psb%d" % i, [128, 512], F32)) for i in range(8)]

    C = Ctx()
    C.bank_i = 0

    def nextbank():
        i = C.bank_i
        C.bank_i = (i + 1) % 4
        return psb[i], "ps%d" % i

    def af32(off, n):
        return arena[:, off:off + n]

    def abf(off, n):
        return arena[:, off:off + n // 2].bitcast(BF16)

    def o8f32(off, n):
        return oT[:, 8:16, :].rearrange("p a b -> p (a b)").bitcast(F32)[:, off:off + n]

    def o8bf(off, n):
        return oT[:, 8:16, :].rearrange("p a b -> p (a b)")[:, off:off + n]

    add = P.add
    V = "dve"
    A = "act"
    G = "pool"

    def mm(out, lhsT, rhs, start, stop, rd, wr):
        add("pe", lambda e: e.matmul(out, lhsT, rhs, start=start, stop=stop), reads=rd, writes=wr)

    def act(out, in_, func, rd, wr, bias=0.0, scale=1.0):
        add(A, lambda e: e.activation(out=out, in_=in_, func=func, bias=bias, scale=scale), reads=rd, writes=wr)

    def tt(eng, out, in0, in1, op, rd, wr):
        eng = V if eng == G else eng
        add(eng, lambda e: e.tensor_tensor(out=out, in0=in0, in1=in1, op=op), reads=rd, writes=wr)

    def ts(eng, out, in0, s1, s2, op0, op1, rd, wr):
        eng = V if eng == G else eng
        if s2 is None:
            add(eng, lambda e: e.tensor_scalar(out=out, in0=in0, scalar1=s1, scalar2=None, op0=op0), reads=rd, writes=wr)
        else:
            add(eng, lambda e: e.tensor_scalar(out=out, in0=in0, scalar1=s1, scalar2=s2, op0=op0, op1=op1), reads=rd, writes=wr)

    def stt(out, in0, scalar, in1, op0, op1, rd, wr):
        add(V, lambda e: e.scalar_tensor_tensor(out=out, in0=in0, scalar=scalar, in1=in1, op0=op0, op1=op1),
            reads=rd, writes=wr)

    def cp(eng, out, in_, rd, wr):
        eng = V if eng == G else eng
        if eng == A:
            add(eng, lambda e: e.copy(out=out, in_=in_), reads=rd, writes=wr)
        else:
            add(eng, lambda e: e.tensor_copy(out=out, in_=in_), reads=rd, writes=wr)

    def dma(q, out, in_, rd, wr):
        add(q, lambda e: e.dma_start(out=out, in_=in_), reads=rd, writes=wr, dma=True)

    def memset(eng, ap, val, wr):
        add(eng, lambda e: e.memset(ap, val), writes=wr)

    bscr = sb("bscr", [128, 8], F32)
    epsc = sb("epsc", [128, 1], F32)

    def barrier():
        names = [n for n in P.bufs.keys() if not n.startswith("ps")] + ["bscr"]
        mm(psb[7][:, 0:8], identb[:, 0:128], identb[:, 0:8], True, True, [], names + ["ps7"])
        act(bscr[:, 0:1], bscr[:, 1:2], AF.Copy, [], names)
        memset(V, bscr[:, 2:3], 0.0, names)
        dma("sp", bscr[0:1, 3:4], cst_d[0:1, 0:1], [], names)
        dma(G, bscr[0:1, 4:5], cst_d[0:1, 0:1], [], names)

    memset(V, bscr[:], 0.0, ["bscr"])
    memset(V, epsc[:], 1e-12, ["epsc"])
    dma(G, identb[:], cst_d[:, 0:128], [], ["identb"])
    dma("sp", bones[:], cst_d[:, 128:256], [], ["bones"])
    dma(G, permb[:], cst_d[:, 256:384], [], ["permb"])
    dma("sp", selcol[:], selc_d, [], ["selcol"])
    dma(G, eye2[:], cst_d[:, 384:448], [], ["eye2"])
    memset(V, onesb[:], 1.0, ["onesb"])
    memset(V, xT[:, :, 0:PADX], 0.0, ["xT"])
    memset(V, xT[:, :, PADX + S:XW], 0.0, ["xT"])

    C.wres = [None, None]
    C.wlast = 0

    def load_w(key, src3):
        for i in range(2):
            if C.wres[i] == key:
                C.wlast = i
                return wb[i], "wb%d" % i
        i = 1 - C.wlast
        C.wres[i] = key
        C.wlast = i
        kc, ncol = src3.shape[1], src3.shape[2]
        dma(G, wb[i][:, 0:kc, 0:ncol], src3, [], ["wb%d" % i])
        return wb[i], "wb%d" % i

    def win_src(l, col0, ncol):
        return w_in[l].rearrange("(k p) n -> p k n", p=128)[:, :, col0:col0 + ncol]

    def inproj(l, cg, evac, wkey=None):
        blk = cg // 4
        ncol = min(512, NIN - blk * 512)
        w, wn = load_w(("win", l, blk), win_src(l, blk * 512, ncol))
        c0 = (cg % 4) * 128
        for t4 in range(4):
            bank, bn = nextbank()
            for k in range(8):
                mm(bank[:], w[:, k, c0:c0 + 128], xT[:, k, PADX + t4 * 512:PADX + (t4 + 1) * 512],
                   k == 0, k == 7, ["xT", wn], [bn])
            evac(t4, bank, bn)

    def col(ci):
        return colsb[:, ci:ci + 1]

    xstage = sb("xstage", [128, 1024], BF16)

    def store_xT(src_f32, srcname, t16):
        cp(A, xstage[:], src_f32, [srcname], ["xstage"])
        bank, bn = nextbank()
        bb = bank[:].bitcast(BF16)
        for k in range(8):
            add("pe", lambda e, k=k: e.transpose(bb[:, k * 128:(k + 1) * 128], xstage[:, k * 128:(k + 1) * 128], identb[:]),
                reads=["xstage", "identb"], writes=[bn])
        cp(V, xT[:, :, PADX + t16 * 128:PADX + (t16 + 1) * 128], bb.rearrange("p (k t) -> p k t", k=8), [], [bn, "xT"])

    def final_phase(l, last):
        barrier()
        mergedT = abf(0, 8 * S).rearrange("p (k t) -> p k t", k=8)
        sig = abf(8192, 512)
        tmpf = af32(8448, 512)
        lng = af32(9216, 1024)
        lnb = af32(10240, 1024)
        xt_ = [af32(11264, 1024), af32(12288, 1024)]
        yt_ = [af32(13312, 1024), af32(14336, 1024)]
        stat = af32(15360, 8)
        dma("sp", lng, lnrow_d[l, 0:1, :].to_broadcast([128, 1024]), [], ["lng"])
        dma("sp", lnb, lnrow_d[l, 1:2, :].to_broadcast([128, 1024]), [], ["lnb"])
        if STOP == 4:
            return
        branches = [(proja_d, 8, 0, 66), (projb_d, 6, 8, 74), (projc_d, 2, 14, 82)]
        for bi, (pd, kc, o0, g0) in enumerate(branches):
            for eb in range(2):
                for ec in range(eb * 4, eb * 4 + 4):
                    for t4 in range(4):
                        tsl = slice(t4 * 512, (t4 + 1) * 512)
                        pw, pwn = load_w(("proj", l, bi, eb), pd[l].rearrange("(k p) n -> p k n", p=128)[:, :, eb * 512:(eb + 1) * 512])
                        b1, b1n = nextbank()
                        for k in range(kc):
                            mm(b1[:], pw[:, k, (ec % 4) * 128:(ec % 4 + 1) * 128], oT[:, o0 + k, tsl], k == 0, k == kc - 1,
                               ["oT%d" % (o0 + k), pwn], [b1n])
                        gw, gwn = load_w(("gate", l, bi, eb), win_src(l, (g0 + eb * 4) * 128, 512))
                        b2, b2n = nextbank()
                        for k in range(8):
                            mm(b2[:], gw[:, k, (ec % 4) * 128:(ec % 4 + 1) * 128], xT[:, k, PADX + t4 * 512:PADX + (t4 + 1) * 512],
                               k == 0, k == 7, ["xT", gwn], [b2n])
                        act(sig, b2[:], AF.Sigmoid, ["colsb"], [b2n, "sig"], bias=col(C_BIN + g0 + ec))
                        if bi == 0:
                            tt(V, mergedT[:, ec, tsl], b1[:], sig, ALU.mult, ["sig"], [b1n, "mg%d" % ec])
                        else:
                            tt(V, tmpf, b1[:], sig, ALU.mult, ["sig"], [b1n, "tmpf"])
                            tt(G, mergedT[:, ec, tsl], mergedT[:, ec, tsl], tmpf, ALU.add, ["tmpf"], ["mg%d" % ec])
        if STOP == 3:
            return
        wo = []
        for fh in range(2):
            wo.append(load_w(("wout", l, fh), wout_d[l].rearrange("(k p) n -> p k n", p=128)[:, :, fh * 512:(fh + 1) * 512]))
        xsrc = x_in if l == 0 else xres
        dst = y_out if last else xres
        for t16 in range(16):
            xt = xt_[t16 % 2]
            yt = yt_[t16 % 2]
            xn, yn = "xt%d" % (t16 % 2), "yt%d" % (t16 % 2)
            dma("sp", xt, xsrc[t16 * 128:(t16 + 1) * 128, :], ["xres"] if l > 0 else [], [xn])
            for fh in range(2):
                w, wn = wo[fh]
                bank, bn = nextbank()
                for k in range(8):
                    mm(bank[:], mergedT[:, k, t16 * 128:(t16 + 1) * 128], w[:, k, :], k == 0, k == 7,
                       ["mg%d" % k, wn], [bn])
                stt(yt[:, fh * 512:(fh + 1) * 512], xt[:, fh * 512:(fh + 1) * 512], ALPHA, bank[:], ALU.mult, ALU.add,
                    [xn], [bn, yn])
            if STOP == 5:
                dma("sp", dst[t16 * 128:(t16 + 1) * 128, :], yt, [yn], ["xres"])
                continue
            add(V, lambda e, yt=yt: e.tensor_reduce(out=stat[:, 0:1], in_=yt, axis=AX.X, op=ALU.add), reads=[yn], writes=["stat0"])
            ts(V, stat[:, 1:2], stat[:, 0:1], -1.0 / D, None, ALU.mult, None, ["stat0"], ["stat1"])
            ts(V, yt, yt, stat[:, 1:2], None, ALU.add, None, ["stat1"], [yn])
            add(A, lambda e, yt=yt, xt=xt: e.activation(out=xt, in_=yt, func=AF.Square, accum_out=stat[:, 2:3]),
                reads=[yn], writes=[xn, "stat2"])
            act(stat[:, 3:4], stat[:, 2:3], AF.Sqrt, ["stat2"], ["stat3"], bias=1e-5, scale=1.0 / D)
            add(V, lambda e: e.reciprocal(out=stat[:, 4:5], in_=stat[:, 3:4]), reads=["stat3"], writes=["stat4"])
            if STOP == 6:
                dma("sp", dst[t16 * 128:(t16 + 1) * 128, :], yt, [yn], ["xres"])
                continue
            if STOP != 8:
                stt(yt, yt, stat[:, 4:5], lng, ALU.mult, ALU.mult, ["stat4", "lng"], [yn])
            if STOP != 7:
                tt(V, yt, yt, lnb, ALU.add, ["lnb"], [yn])
            dma("sp", dst[t16 * 128:(t16 + 1) * 128, :], yt, [yn], ["xres"])
            if not last:
                store_xT(yt, yn, t16)

    def pool_phase(l):
        barrier()
        W = S + 32
        pbuf = af32(0, W)
        a_ = [af32(2080, W), af32(4160, W)]
        mixed = abf(6240, 6 * S).rearrange("p (c t) -> p c t", c=6)
        wgt = abf(12384, 6 * 768).rearrange("p (a b) -> p a b", a=6)
        sacc = af32(14688, 0) if False else None
        pe_t = af32(14688, 64).rearrange("p (g e) -> p g e", g=4)
        t1 = af32(14752, 512)
        memset(V, pbuf[:, 0:16], 0.0, ["pbuf"])
        memset(V, pbuf[:, W - 16:W], 0.0, ["pbuf"])
        dma("sp", pe_t, pedge_d, [], ["pe_t"])
        dma(G, wgt, poolw_d[l].rearrange("(k p) n -> p k n", p=128), [], ["wgt"])
        for c in range(6):
            inproj(l, 40 + c, lambda t4, bank, bn, c=c: act(oT[:, 8 + c, t4 * 512:(t4 + 1) * 512], bank[:], AF.Silu,
                                                             ["colsb"], [bn, "oT%d" % (8 + c)], bias=col(C_BIN + 40 + c)))
        for c in range(6):
            inproj(l, 34 + c, lambda t4, bank, bn, c=c: act(pbuf[:, 16 + t4 * 512:16 + (t4 + 1) * 512], bank[:], AF.Identity,
                                                             ["colsb"], [bn, "pbuf"], bias=col(C_BIN + 34 + c)))
            gs = sorted(set((2 * c + hf) // 3 for hf in range(2)))
            first = True
            for g in gs:
                h = 1 << g
                kk = g + 1
                src, srcn = pbuf, "pbuf"
                for j in range(kk):
                    sh = 1 << j
                    dstt = a_[j % 2]
                    n = W - (2 << j) + 1
                    tt(V, dstt[:, 0:n], src[:, 0:n], src[:, sh:sh + n], ALU.add, [srcn], ["a%d" % (j % 2)])
                    src, srcn = dstt, "a%d" % (j % 2)
                sfin = a_[kk % 2]
                sn = "a%d" % (kk % 2)
                tt(V, sfin[:, 0:S], src[:, 16 - h:16 - h + S], pbuf[:, 16 + h:16 + h + S], ALU.add, [srcn, "pbuf"], [sn])
                tt(V, sfin[:, 0:8], sfin[:, 0:8], pe_t[:, g, 0:8], ALU.mult, ["pe_t"], [sn])
                tt(V, sfin[:, S - 8:S], sfin[:, S - 8:S], pe_t[:, g, 8:16], ALU.mult, ["pe_t"], [sn])
                selw = selcol[:, c * 4 + g:c * 4 + g + 1]
                if first:
                    stt(mixf[:, :], sfin[:, 0:S], selw, pbuf[:, 16:16 + S], ALU.mult, ALU.subtract, [sn, "pbuf", "selcol"], ["mixf"])
                else:
                    stt(mixf[:, :], sfin[:, 0:S], selw, mixf[:, :], ALU.mult, ALU.add, [sn, "selcol"], ["mixf"])
                first = False
            cp(A, mixed[:, c, :], mixf[:, :], ["mixf"], ["mixed"])
        for oc in range(6):
            ics = [ic for ic in range(6) if any((2 * ic + a) // 3 == (2 * oc + b) // 3 for a in range(2) for b in range(2))]
            for t4 in range(4):
                tsl = slice(t4 * 512, (t4 + 1) * 512)
                bank, bn = nextbank()
                for n_, ic in enumerate(ics):
                    mm(bank[:], wgt[:, ic, oc * 128:(oc + 1) * 128], mixed[:, ic, tsl], n_ == 0, n_ == len(ics) - 1,
                       ["wgt", "mixed"], [bn])
                ts(V, t1, bank[:], col(C_PB + oc), col(C_PS + oc), ALU.add, ALU.mult, ["colsb"], [bn, "t1"])
                tt(V, oT[:, 8 + oc, tsl], t1, oT[:, 8 + oc, tsl], ALU.mult, ["t1"], ["oT%d" % (8 + oc)])

    C.attn = None
    C.rwkv = None
    def attn_phase(l):
        barrier()
        Qr = abf(0, XW)
        Kr = abf(1280, XW)
        qraw = abf(2560, S)
        ropec = af32(3584, S)
        ropes = af32(5632, S)
        t1 = af32(7680, 512)
        t2 = af32(8192, 512)
        accn = af32(8704, S)
        accd = af32(10752, S)
        Vt = abf(12800, 20 * 128).rearrange("p (a b) -> p a b", a=20)
        pT = [abf(14080, 512), abf(14336, 512)]
        msk = abf(14592, 4 * 512).rearrange("p (a b) -> p a b", a=4)
        dma("sp", ropec, rope_d[0], [], ["ropec"])
        dma("sp", ropes, rope_d[1], [], ["ropes"])
        for a_ in range(4):
            dma(G, msk[:, a_, :], amask_d[a_], [], ["msk"])
        for buf, nm in ((Qr, "Qr"), (Kr, "Kr")):
            memset(V, buf[:, 0:PADX], 0.0, [nm])
            memset(V, buf[:, PADX + S:XW], 0.0, [nm])
        cnt = [0]
        SUB = int(os.environ.get("ATT_SUB", "9"))
        if SUB == 1:
            return
        for pp in range(2):
            for g in range(3):
                d = (1, 4, 16)[g]
                for cg, dst, nm in ((46 + 2 * g + pp, Qr, "Qr"), (52 + 2 * g + pp, Kr, "Kr")):
                    inproj(l, cg, lambda t4, bank, bn, cg=cg: act(qraw[:, t4 * 512:(t4 + 1) * 512], bank[:], AF.Identity,
                                                                  ["colsb"], [bn, "qraw"], bias=col(C_BIN + cg)))
                    for t4 in range(4):
                        tsl = slice(t4 * 512, (t4 + 1) * 512)
                        bank, bn = nextbank()
                        mm(bank[:], permb[:], qraw[:, tsl], True, True, ["permb", "qraw"], [bn])
                        tt(V, t1, bank[:], ropes[:, tsl], ALU.mult, ["ropes"], [bn, "t1"])
                        tt(V, t2, qraw[:, tsl], ropec[:, tsl], ALU.mult, ["qraw", "ropec"], ["t2"])
                        tt(V, dst[:, PADX + t4 * 512:PADX + (t4 + 1) * 512], t1, t2, ALU.add, ["t1", "t2"], [nm])
                if SUB == 2:
                    return
                wv, wvn = load_w(("wv", l, g, pp), win_src(l, (58 + 2 * g + pp) * 128, 128))
                if d == 1:
                    tsls = [slice(PADX + 128 * m - 64, PADX + 128 * m + 64) for m in range(17)]
                elif d == 4:
                    tsls = []
                    for r in range(4):
                        for m in range(5):
                            s0 = PADX + r + 4 * (128 * m - 64)
                            tsls.append(slice(s0, s0 + 509, 4))
                else:
                    tsls = [slice(PADX + r, PADX + r + 2033, 16) for r in range(16)]
                for j0 in range(0, len(tsls), 4):
                    grp = tsls[j0:j0 + 4]
                    bank, bn = nextbank()
                    for j, sl in enumerate(grp):
                        for k in range(8):
                            mm(bank[:, j * 128:(j + 1) * 128], xT[:, k, sl], wv[:, k, 0:128], k == 0, False, ["xT", wvn], [bn])
                        mm(bank[:, j * 128:(j + 1) * 128], onesb[0:1, 0:128], vrowb[0:1, (2 * g + pp) * 128:(2 * g + pp + 1) * 128],
                           False, True, ["onesb", "vrowb"], [bn])
                    n = len(grp)
                    cp(A, Vt[:, j0:j0 + n, :], bank[:, 0:n * 128].rearrange("p (a b) -> p a b", a=n), [], [bn, "Vt"])
                if SUB == 3:
                    return
                for sbk in range(4):
                    for j in range(4):
                        if d == 1:
                            m = 4 * sbk + j
                            qsl = slice(PADX + 128 * m, PADX + 128 * m + 128)
                            chunks = [(slice(PADX + 128 * m - 64, PADX + 128 * m + 64), m),
                                      (slice(PADX + 128 * m + 64, PADX + 128 * m + 192), m + 1)]
                            mi = 1 if m == 0 else (2 if m == 15 else 0)
                        elif d == 4:
                            r, m = sbk, j
                            q0 = PADX + r + 512 * m
                            qsl = slice(q0, q0 + 509, 4)
                            k0 = PADX + r + 4 * (128 * m - 64)
                            chunks = [(slice(k0, k0 + 509, 4), r * 5 + m), (slice(k0 + 512, k0 + 512 + 509, 4), r * 5 + m + 1)]
                            mi = 1 if m == 0 else (2 if m == 3 else 0)
                        else:
                            r = 4 * sbk + j
                            qsl = slice(PADX + r, PADX + r + 2033, 16)
                            chunks = [(qsl, r)]
                            mi = 3
                        nch = len(chunks)
                        wd = nch * 128
                        for h in range(2):
                            hp_ = slice(64 * h, 64 * h + 64)
                            si = h + 2 * (cnt[0] % 2)
                            sbank, sbn = psb[si], "ps%d" % si
                            par = cnt[0] % 2
                            pTb, pTn = pT[h][:, par * 256:par * 256 + 256], "pT%d_%d" % (h, par)
                            for ci, (ks, vt) in enumerate(chunks):
                                mm(sbank[:, ci * 128:(ci + 1) * 128], Kr[hp_, ks], Qr[hp_, qsl], True, True, ["Kr", "Qr"], [sbn])
                            act(pTb[:, 0:wd], sbank[:, 0:wd], AF.Exp, [], [sbn, pTn], scale=0.125)
                            tt(V, pTb[:, 0:wd], pTb[:, 0:wd], msk[:, mi, 0:wd], ALU.mult, ["msk"], [pTn])
                            for ci, (ks, vt) in enumerate(chunks):
                                mm(psb[6][hp_, j * 128:(j + 1) * 128], Vt[:, vt, 64 * h:64 * h + 64], pTb[:, ci * 128:(ci + 1) * 128],
                                   ci == 0, ci == nch - 1, ["Vt", pTn], ["ps6"])
                            for ci, (ks, vt) in enumerate(chunks):
                                mm(psb[7][hp_, j * 128:(j + 1) * 128], onesb[:, 0:64], pTb[:, ci * 128:(ci + 1) * 128],
                                   ci == 0, ci == nch - 1, ["onesb", pTn], ["ps7"])
                        cnt[0] += 1
                    if d == 1:
                        vn, vd = accn[:, 512 * sbk:512 * sbk + 512], accd[:, 512 * sbk:512 * sbk + 512]
                        bn_, bd_ = psb[6][:, :], psb[7][:, :]
                    elif d == 4:
                        vn, vd = accn[:, sbk:S:4], accd[:, sbk:S:4]
                        bn_, bd_ = psb[6][:, :], psb[7][:, :]
                    else:
                        vn = accn.rearrange("p (i r) -> p r i", r=16)[:, 4 * sbk:4 * sbk + 4, :]
                        vd = accd.rearrange("p (i r) -> p r i", r=16)[:, 4 * sbk:4 * sbk + 4, :]
                        bn_ = psb[6][:, :].rearrange("p (r i) -> p r i", r=4)
                        bd_ = psb[7][:, :].rearrange("p (r i) -> p r i", r=4)
                    if g == 0:
                        cp(V, vn, bn_, [], ["ps6", "accn"])
                        cp(A, vd, bd_, [], ["ps7", "accd"])
                    else:
                        tt(V, vn, vn, bn_, ALU.add, [], ["ps6", "accn"])
                        tt(V, vd, vd, bd_, ALU.add, [], ["ps7", "accd"])
                if os.environ.get("ATT_STOP") == str(g + 1):
                    return
            oc = 14 + pp
            inproj(l, 64 + pp, lambda t4, bank, bn, oc=oc, pp=pp: act(oT[:, oc, t4 * 512:(t4 + 1) * 512], bank[:], AF.Silu,
                                                                        ["colsb"], [bn, "oT%d" % oc], bias=col(C_BIN + 64 + pp)))
            act(accd, accd, AF.Ln, [], ["accd"])
            act(accd, accd, AF.Exp, [], ["accd"], scale=-1.0)
            tt(V, accn, accn, accd, ALU.mult, ["accd"], ["accn"])
            tt(V, oT[:, oc, :], accn, oT[:, oc, :], ALU.mult, ["accn"], ["oT%d" % oc])

    C.attn = attn_phase

    def rwkv_phase(l):
        barrier()
        lw = abf(0, S)
        la = abf(1024, S)
        wup = abf(2048, 1024)
        aup = abf(2560, 1024)
        hbuf = af32(3072, 2050)
        tmpB = af32(3072, S)
        tmpf = af32(5124, S)
        rbf = abf(7172, S)
        kbf = abf(8196, S)
        vbf = abf(9220, S)
        kkbf = abf(10244, S)
        bonus = abf(11268, S)
        ytok = af32(12292, S).rearrange("p (c i) -> p c i", c=32)
        Vtok = abf(14340, S).rearrange("p (c i) -> p c i", c=32)
        reset = af32(15364, 512)
        STf = af32(15876, 64)
        STb = abf(15940, 64)
        Xs = abf(15972, 64)
        Us = abf(16004, 64)
        Wc = af32(16036, 8)
        totc = af32(16044, 8)
        stat = af32(16052, 128)
        ynb = rbf.rearrange("p (c i) -> p c i", c=32)
        tmpf3 = tmpf.rearrange("p (c i) -> p c i", c=32)
        sg = o8f32(0, 512)
        aa = o8f32(512, 512)
        Gc = o8f32(1024, 512)
        tmpG = o8f32(1536, 512)
        E = o8f32(2048, 512)
        bb = o8f32(2560, 512)
        kd = o8f32(3072, 512)
        AR2 = o8bf(2 * 3584, 1024).rearrange("p (c n) -> p c n", c=8)
        AR4 = o8bf(2 * 3584, 1024).rearrange("p (c a j) -> p c a j", c=8, a=2)
        BT = o8bf(2 * 4096, 512)
        KT = o8bf(2 * 4352, 512)
        BHT = o8bf(2 * 4608, 512)
        KHT = o8bf(2 * 4864, 512)
        BHtok = o8bf(2 * 5120, 512).rearrange("p (c j) -> p c j", c=8)
        KHtok = o8bf(2 * 5376, 512).rearrange("p (c j) -> p c j", c=8)
        G1s = o8bf(2 * 5632, 1024).rearrange("p (c n) -> p c n", c=8)
        G2s = o8bf(2 * 6144, 1024).rearrange("p (c n) -> p c n", c=8)
        QP = o8bf(2 * 6656, 1024).rearrange("p (c n) -> p c n", c=8)
        Pn = o8bf(2 * 7168, 512).rearrange("p (c n) -> p c n", c=8)
        m1 = [o8bf(2 * 7424, 512), o8bf(2 * 7680, 512)]
        m2 = [o8bf(2 * 7936, 256), o8bf(2 * 8064, 256)]
        t1f = E

        mfb = mixf[:, :].bitcast(BF16)
        Wc1 = af32(16180, 8)
        SETS = [
            (AR2, AR4, G1s, G2s, QP, BHtok, KHtok, Wc),
            (mfb[:, 0:1024].rearrange("p (c n) -> p c n", c=8), mfb[:, 0:1024].rearrange("p (c a j) -> p c a j", c=8, a=2),
             mfb[:, 1024:2048].rearrange("p (c n) -> p c n", c=8), mfb[:, 2048:3072].rearrange("p (c n) -> p c n", c=8),
             mfb[:, 3072:4096].rearrange("p (c n) -> p c n", c=8),
             xstage[:, 0:512].rearrange("p (c j) -> p c j", c=8), xstage[:, 512:1024].rearrange("p (c j) -> p c j", c=8), Wc1),
        ]

        STATE = [(STf, STb, Xs, Us), (af32(16188, 64), abf(16252, 64), abf(16284, 64), abf(16316, 64))]

        def c8v(ap):
            return ap.rearrange("p (c j) -> p c j", c=8)

        dma(G, wup, wup_d[l], [], ["wup"])
        dma(G, aup, aup_d[l], [], ["aup"])
        dma("sp", reset, rmask_d[0][:, 1280:1792], [], ["reset"])
        for z in range(2):
            dma(G, m1[z], rmask_d[z][:, 0:512], [], ["m1_%d" % z])
            dma(G, m2[z], rmask_d[z][:, 1024:1280], [], ["m2_%d" % z])
        memset(V, hbuf[:, 0:1], 0.0, ["hbuf"])
        memset(V, hbuf[:, 2049:2050], 0.0, ["hbuf"])

        def shifted(cg, dst, dstname):
            inproj(l, cg, lambda t4, bank, bn: act(hbuf[:, 1 + t4 * 512:1 + (t4 + 1) * 512], bank[:], AF.Identity,
                                                   ["colsb"], [bn, "hbuf"], bias=col(C_BIN + cg)))
            act(tmpf, hbuf[:, 1:2049], AF.Identity, ["hbuf", "c0col"], ["tmpf"], scale=c0col[:, cg:cg + 1])
            stt(tmpf, hbuf[:, 0:2048], col(C_MU0 + cg), tmpf, ALU.mult, ALU.add, ["hbuf", "colsb"], ["tmpf"])
            stt(dst, hbuf[:, 2:2050], col(C_MU1 + cg), tmpf, ALU.mult, ALU.add, ["hbuf", "colsb", "tmpf"], [dstname])

        def pairbank():
            return [nextbank(), nextbank()]

        shifted(24, tmpf, "tmpf")
        act(lw, tmpf, AF.Tanh, ["tmpf"], ["lw"])
        shifted(25, la, "la")

        for hp in range(8):
            shifted(hp, rbf, "rbf")
            shifted(8 + hp, kbf, "kbf")
            shifted(16 + hp, vbf, "vbf")
            inproj(l, 26 + hp, lambda t4, bank, bn, hp=hp: act(oT[:, hp, t4 * 512:(t4 + 1) * 512], bank[:], AF.Silu,
                                                               ["colsb"], [bn, "oT%d" % hp], bias=col(C_BIN + 26 + hp)))
            ts(V, tmpf, kbf, col(C_KK + hp), None, ALU.mult, None, ["kbf", "colsb"], ["tmpf"])
            act(tmpB, tmpf, AF.Square, ["tmpf"], ["hbuf"])
            for t4 in range(4):
                tsl = slice(t4 * 512, (t4 + 1) * 512)
                bank, bn = nextbank()
                mm(bank[:], bones[:], tmpB[:, tsl], True, True, ["bones", "hbuf"], [bn])
                act(tmpB[:, tsl], bank[:], AF.Ln, ["epsc"], [bn, "hbuf"], bias=epsc[:, 0:1])
                act(tmpB[:, tsl], tmpB[:, tsl], AF.Exp, [], ["hbuf"], scale=-0.5)
            tt(V, kkbf, tmpf, tmpB, ALU.mult, ["tmpf", "hbuf"], ["kkbf"])
            stt(tmpf, rbf, col(C_RK + hp), kbf, ALU.mult, ALU.mult, ["rbf", "kbf", "colsb"], ["tmpf"])
            for t4 in range(4):
                tsl = slice(t4 * 512, (t4 + 1) * 512)
                bank, bn = nextbank()
                mm(bank[:], bones[:], tmpf[:, tsl], True, True, ["bones", "tmpf"], [bn])
                tt(V, bonus[:, tsl], bank[:], vbf[:, tsl], ALU.mult, ["vbf"], [bn, "bonus"])
            for T in range(4):
                pb = pairbank()
                for h in range(2):
                    hs = slice(64 * h, 64 * h + 64)
                    bank, bn = pb[h]
                    for c8 in range(8):
                        c = T * 8 + c8
                        mm(bank[hs, c8 * 64:(c8 + 1) * 64], vbf[hs, c * 64:(c + 1) * 64], identb[hs, 64 * h:64 * h + 64],
                           True, True, ["vbf", "identb"], [bn])
                    cp(A, Vtok[hs, T * 8:(T + 1) * 8, :], c8v(bank[hs, :]), [], [bn, "Vtok"])

            items = [(0, T) for T in range(4)] + [(1, T) for T in range(3, -1, -1)]

            def produce(z, T, s):
                zs = slice(64 * z, 64 * z + 64)
                tsl = slice(T * 512, (T + 1) * 512)
                AR2_, AR4_, G1s_, G2s_, QP_, BHtok_, KHtok_, Wc_ = SETS[s]
                ss = str(s)
                ARn = "AR" + ss
                b1, b1n = nextbank()
                mm(b1[:], wup[zs, hp * 128:(hp + 1) * 128], lw[zs, tsl], True, True, ["wup", "lw"], [b1n])
                act(sg, b1[:], AF.Sigmoid, ["colsb"], [b1n, "sg"], bias=col(C_W0 + z * 8 + hp))
                b2, b2n = nextbank()
                mm(b2[:], aup[zs, hp * 128:(hp + 1) * 128], la[zs, tsl], True, True, ["aup", "la"], [b2n])
                act(aa, b2[:], AF.Sigmoid, ["colsb"], [b2n, "aa"], bias=col(C_A0 + z * 8 + hp))
                yield
                add(V, lambda e: e.tensor_tensor_scan(out=Gc, data0=reset, data1=sg, initial=0.0, op0=ALU.mult, op1=ALU.add),
                    reads=["reset", "sg"], writes=["Gc"])
                cp(V, totc, c8v(Gc)[:, :, 63], ["Gc"], ["totc"])
                totb = totc.unsqueeze(2).to_broadcast([128, 8, 64])
                if z == 1:
                    tt(V, tmpG, sg, Gc, ALU.subtract, ["sg", "Gc"], ["tmpG"])
                    tt(V, c8v(Gc), c8v(tmpG), totb, ALU.add, ["tmpG", "totc"], ["Gc"])
                yield
                tt(V, sg, Gc, sg, ALU.subtract, ["Gc"], ["sg"])
                tt(V, c8v(tmpG), c8v(Gc), totb, ALU.subtract, ["Gc", "totc"], ["tmpG"])
                act(E, Gc, AF.Exp, ["Gc"], ["E"], scale=-CDEC)
                act(sg, sg, AF.Exp, [], ["sg"], scale=-CDEC)
                act(Gc, Gc, AF.Exp, [], ["Gc"], scale=CDEC)
                act(tmpG, tmpG, AF.Exp, [], ["tmpG"], scale=CDEC)
                act(Wc_, totc, AF.Exp, ["totc"], ["Wc" + ss], scale=-CDEC)
                yield
                ts(V, kd, aa, -1.0, col(C_KA + hp), ALU.add, ALU.mult, ["aa", "colsb"], ["kd"])
                stt(kd, kd, 1.0, kbf[:, tsl], ALU.add, ALU.mult, ["kbf"], ["kd"])
                tt(V, bb, kkbf[:, tsl], aa, ALU.mult, ["kkbf", "aa"], ["bb"])
                yield
                tt(V, AR4_[:, :, 1, :], c8v(rbf[:, tsl]), c8v(E), ALU.mult, ["rbf", "E"], [ARn])
                stt(AR4_[:, :, 0, :], c8v(kkbf[:, tsl]), -1.0, c8v(sg), ALU.mult, ALU.mult, ["kkbf", "sg"], [ARn])
                yield
                tt(V, BT, bb, Gc, ALU.mult, ["bb", "Gc"], ["BT"])
                tt(V, KT, kd, Gc, ALU.mult, ["kd", "Gc"], ["KT"])
                tt(V, BHT, bb, tmpG, ALU.mult, ["bb", "tmpG"], ["BHT"])
                tt(V, KHT, kd, tmpG, ALU.mult, ["kd", "tmpG"], ["KHT"])
                yield
                for src, srcn, dst, dstn in ((BHT, "BHT", BHtok_, "BHtok" + ss), (KHT, "KHT", KHtok_, "KHtok" + ss)):
                    pb = pairbank()
                    for h in range(2):
                        hs = slice(64 * h, 64 * h + 64)
                        bank, bn = pb[h]
                        for c8 in range(8):
                            mm(bank[hs, c8 * 64:(c8 + 1) * 64], src[hs, c8 * 64:(c8 + 1) * 64], identb[hs, 64 * h:64 * h + 64],
                               True, True, [srcn, "identb"], [bn])
                        cp(A, dst[hs, :, :], c8v(bank[hs, :]), [], [bn, dstn + str(h)])
                    yield
                chains = []
                for hv in range(2):
                    for h in range(2):
                        ia, ib = {(0, 0): (0, 1), (0, 1): (2, 3), (1, 0): (4, 5), (1, 1): (6, 7)}[(hv, h)]
                        chains.append((hv, h, psb[ia], "ps%d" % ia, psb[ib], "ps%d" % ib))
                for hv, h, bA, bAn, bB, bBn in chains:
                    hs = slice(64 * h, 64 * h + 64)
                    for cl in range(4):
                        c8 = hv * 4 + cl
                        mm(bA[hs, cl * 128:(cl + 1) * 128], BT[hs, c8 * 64:(c8 + 1) * 64], AR2_[hs, c8, :], True, True,
                           ["BT", ARn], [bAn])
                    for cl in range(4):
                        c8 = hv * 4 + cl
                        mm(bB[hs, cl * 128:(cl + 1) * 128], KT[hs, c8 * 64:(c8 + 1) * 64], AR2_[hs, c8, :], True, True,
                           ["KT", ARn], [bBn])
                for hv, h, bA, bAn, bB, bBn in chains:
                    cs = slice(hv * 4, hv * 4 + 4)
                    hs = slice(64 * h, 64 * h + 64)
                    nm = "%s%d%d" % (ss, hv, h)
                    tt(V, G1s_[hs, cs, :], bA[hs, :].rearrange("p (c n) -> p c n", c=4),
                       m1[z][hs, :].rearrange("p (c n) -> p c n", c=4), ALU.mult, ["m1_%d" % z], [bAn, "G1s" + nm])
                    tt(V, G2s_[hs, cs, :], bB[hs, :].rearrange("p (c n) -> p c n", c=4),
                       m1[z][hs, :].rearrange("p (c n) -> p c n", c=4), ALU.mult, ["m1_%d" % z], [bBn, "G2s" + nm])
                for hv, h, bA, bAn, bB, bBn in chains:
                    hs = slice(64 * h, 64 * h + 64)
                    for cl in range(4):
                        c8 = hv * 4 + cl
                        mm(bA[hs, cl * 64:(cl + 1) * 64], AR4_[hs, c8, 0, :], BT[hs, c8 * 64:(c8 + 1) * 64], True, True,
                           ["BT", ARn], [bAn])
                for hv, h, bA, bAn, bB, bBn in chains:
                    cs = slice(hv * 4, hv * 4 + 4)
                    hs = slice(64 * h, 64 * h + 64)
                    nm = "%s%d%d" % (ss, hv, h)
                    pn = "Pn%d%d" % (hv, h)
                    tt(V, Pn[hs, cs, :], bA[hs, 0:256].rearrange("p (c n) -> p c n", c=4),
                       m2[z][hs, :].rearrange("p (c n) -> p c n", c=4), ALU.mult, ["m2_%d" % z], [bAn, pn])
                    cp(A, QP_[hs, cs, 0:64], eye2[hs, :].unsqueeze(1).to_broadcast([64, 4, 64]), ["eye2"], ["QP" + nm])
                    cp(A, QP_[hs, cs, 64:128], G1s_[hs, cs, 0:64], ["G1s" + nm], ["QP" + nm])
                for k in range(6):
                    last = (k == 5)
                    wA = 64 if last else 128
                    for hv, h, bA, bAn, bB, bBn in chains:
                        hs = slice(64 * h, 64 * h + 64)
                        nm = "%s%d%d" % (ss, hv, h)
                        pn = "Pn%d%d" % (hv, h)
                        for cl in range(4):
                            c8 = hv * 4 + cl
                            mm(bA[hs, cl * 128:cl * 128 + wA], Pn[hs, c8, :], QP_[hs, c8, 0:wA], True, True,
                               [pn, "QP" + nm], [bAn])
                        if not last:
                            for cl in range(4):
                                c8 = hv * 4 + cl
                                mm(bB[hs, cl * 64:(cl + 1) * 64], QP_[hs, c8, 64:128], Pn[hs, c8, :], True, True,
                                   [pn, "QP" + nm], [bBn])
                    for hv, h, bA, bAn, bB, bBn in chains:
                        cs = slice(hv * 4, hv * 4 + 4)
                        hs = slice(64 * h, 64 * h + 64)
                        nm = "%s%d%d" % (ss, hv, h)
                        pn = "Pn%d%d" % (hv, h)
                        bA3 = bA[hs, :].rearrange("p (c n) -> p c n", c=4)
                        tt(V, QP_[hs, cs, 0:64], QP_[hs, cs, 0:64], bA3[:, :, 0:64], ALU.add, [], [bAn, "QP" + nm])
                        if not last:
                            cp(A, QP_[hs, cs, 64:128], bA3[:, :, 64:128], [], [bAn, "QP" + nm])
                            cp(A, Pn[hs, cs, :], bB[hs, 0:256].rearrange("p (c n) -> p c n", c=4), [], [bBn, pn])
                yield

            def consume(z, T, s, first):
                AR2_, AR4_, G1s_, G2s_, QP_, BHtok_, KHtok_, Wc_ = SETS[s]
                STf_, STb_, Xs_, Us_ = STATE[z]
                ss = str(s)
                zn = str(z)
                ARn = "AR" + ss
                if first:
                    memset(V, STf_, 0.0, ["STf" + zn + "0", "STf" + zn + "1"])
                    memset(V, STb_, 0.0, ["STb" + zn + "0", "STb" + zn + "1"])
                cord = range(8) if z == 0 else range(7, -1, -1)
                for c8 in cord:
                    c = T * 8 + c8
                    H = []
                    for h in range(2):
                        H.append((slice(64 * h, 64 * h + 64), zn + str(h), "%s%d%d" % (ss, c8 // 4, h),
                                  psb[2 * z + h], "ps%d" % (2 * z + h), psb[4 + 2 * z + h], "ps%d" % (4 + 2 * z + h)))
                    for hs, hn, nm, cb, cbn, yb, ybn in H:
                        mm(cb[hs, 0:64], AR4_[hs, c8, 0, :], STb_[hs, :], True, False, [ARn, "STb" + hn], [cbn])
                        mm(cb[hs, 0:64], G2s_[hs, c8, 0:64], Vtok[hs, c, :], False, True, ["G2s" + nm, "Vtok"], [cbn])
                    yield
                    for hs, hn, nm, cb, cbn, yb, ybn in H:
                        cp(A, Xs_[hs, :], cb[hs, 0:64], [], [cbn, "Xs" + hn])
                    for hs, hn, nm, cb, cbn, yb, ybn in H:
                        mm(cb[hs, 64:128], QP_[hs, c8, 0:64], Xs_[hs, :], True, True, ["QP" + nm, "Xs" + hn], [cbn])
                    yield
                    for hs, hn, nm, cb, cbn, yb, ybn in H:
                        cp(V if z == 0 else A, Us_[hs, :], cb[hs, 64:128], [], [cbn, "Us" + hn])
                    for hs, hn, nm, cb, cbn, yb, ybn in H:
                        mm(cb[hs, 128:192], BHtok_[hs, c8, :], Us_[hs, :], True, False, ["BHtok" + ss + hn[1], "Us" + hn], [cbn])
                        mm(cb[hs, 128:192], KHtok_[hs, c8, :], Vtok[hs, c, :], False, True, ["KHtok" + ss + hn[1], "Vtok"], [cbn])
                    for hs, hn, nm, cb, cbn, yb, ybn in H:
                        mm(yb[hs, c8 * 64:(c8 + 1) * 64], AR4_[hs, c8, 1, :], STb_[hs, :], True, False, [ARn, "STb" + hn], [ybn])
                        mm(yb[hs, c8 * 64:(c8 + 1) * 64], G1s_[hs, c8, 64:128], Us_[hs, :], False, False,
                           ["G1s" + nm, "Us" + hn], [ybn])
                        mm(yb[hs, c8 * 64:(c8 + 1) * 64], G2s_[hs, c8, 64:128], Vtok[hs, c, :], False, True,
                           ["G2s" + nm, "Vtok"], [ybn])
                    yield
                    for hs, hn, nm, cb, cbn, yb, ybn in H:
                        stt(STb_[hs, :], STb_[hs, :], Wc_[hs, c8:c8 + 1], cb[hs, 128:192], ALU.mult, ALU.add,
                            ["Wc" + ss], [cbn, "STb" + hn])
                    yield
                for h in range(2):
                    hs = slice(64 * h, 64 * h + 64)
                    yb, ybn = psb[4 + 2 * z + h], "ps%d" % (4 + 2 * z + h)
                    tt(V, ytok[hs, T * 8:(T + 1) * 8, :], ytok[hs, T * 8:(T + 1) * 8, :], c8v(yb[hs, :]), ALU.add,
                       [], [ybn, "ytok%d" % T])
                yield

            for T_ in range(4):
                memset(V, ytok[:, T_ * 8:(T_ + 1) * 8, :], 0.0, ["ytok%d" % T_])
            for i in range(4):
                for _ in produce(0, i, 0):
                    pass
                for _ in produce(1, 3 - i, 1):
                    pass
                gens = [consume(0, i, 0, i == 0), consume(1, 3 - i, 1, i == 0)]
                while gens:
                    for g_ in list(gens):
                        try:
                            next(g_)
                        except StopIteration:
                            gens.remove(g_)
            add(V, lambda e: e.tensor_reduce(out=stat[:, 0:32], in_=ytok, axis=AX.X, op=ALU.add), reads=["ytok0", "ytok1", "ytok2", "ytok3"], writes=["st0"])
            ts(V, stat[:, 32:64], stat[:, 0:32], -1.0 / 64, None, ALU.mult, None, ["st0"], ["st1"])
            tt(V, ytok, ytok, stat[:, 32:64].unsqueeze(2).to_broadcast([128, 32, 64]), ALU.add, ["st1"], ["ytok0", "ytok1", "ytok2", "ytok3"])
            act(tmpf3, ytok, AF.Square, ["ytok0", "ytok1", "ytok2", "ytok3"], ["tmpf"])
            add(V, lambda e: e.tensor_reduce(out=stat[:, 64:96], in_=tmpf3, axis=AX.X, op=ALU.add), reads=["tmpf"], writes=["st2"])
            act(stat[:, 96:128], stat[:, 64:96], AF.Sqrt, ["st2"], ["st3"], bias=64e-5, scale=1.0 / 64)
            add(V, lambda e: e.reciprocal(out=stat[:, 96:128], in_=stat[:, 96:128]), reads=[], writes=["st3"])
            tt(V, ynb, ytok, stat[:, 96:128].unsqueeze(2).to_broadcast([128, 32, 64]), ALU.mult, ["ytok0", "ytok1", "ytok2", "ytok3", "st3"], ["rbf"])
            for T in range(4):
                tsl = slice(T * 512, (T + 1) * 512)
                pb = pairbank()
                for h in range(2):
                    hs = slice(64 * h, 64 * h + 64)
                    bank, bn = pb[h]
                    for c8 in range(8):
                        mm(bank[hs, c8 * 64:(c8 + 1) * 64], ynb[hs, T * 8 + c8, :], identb[hs, 64 * h:64 * h + 64], True, True,
                           ["rbf", "identb"], [bn])
                    act(t1f[hs, :], bank[hs, :], AF.Identity, ["colsb"], [bn, "t1f" + str(h)],
                        bias=colsb[hs, C_GB + hp:C_GB + hp + 1], scale=colsb[hs, C_GG + hp:C_GG + hp + 1])
                tt(V, t1f, t1f, bonus[:, tsl], ALU.add, ["bonus", "t1f0", "t1f1"], ["t1f0", "t1f1"])
                tt(V, oT[:, hp, tsl], t1f, oT[:, hp, tsl], ALU.mult, ["t1f0", "t1f1"], ["oT%d" % hp])

    C.rwkv = rwkv_phase


    dma("sp", colsb[:], cols_d[0], [], ["colsb"])
    xld = [af32(0, 1024), af32(1024, 1024)]
    for t16 in range(16):
        dma("sp", xld[t16 % 2], x_in[t16 * 128:(t16 + 1) * 128, :], [], ["xld%d" % (t16 % 2)])
        store_xT(xld[t16 % 2], "xld%d" % (t16 % 2), t16)
    import os
    STOP = int(os.environ.get("KSTOP", "9"))
    for l in range(depth if STOP > 1 else 0):
        if l > 0:
            dma("sp", colsb[:], cols_d[l], [], ["colsb"])
        dma(G, vrowb[:], vrow_d[l], [], ["vrowb"])
        tt(V, c0col[:], colsb[:, C_MU0:C_MU0 + 26], colsb[:, C_MU1:C_MU1 + 26], ALU.add, ["colsb"], ["c0col"])
        ts(V, c0col[:], c0col[:], -1.0, 1.0, ALU.mult, ALU.add, [], ["c0col"])
        if C.rwkv is not None and "A" in PH:
            C.rwkv(l)
        else:
            for c in range(8):
                memset(V, oT[:, c, :], 0.0, ["oT%d" % c])
        if "B" in PH:
            pool_phase(l)
        else:
            barrier()
            for c in range(8, 14):
                memset(V, oT[:, c, :], 0.0, ["oT%d" % c])
        if C.attn is not None and "C" in PH:
            C.attn(l)
        else:
            barrier()
            for c in range(14, 16):
                memset(V, oT[:, c, :], 0.0, ["oT%d" % c])
        if dbg is not None and l == depth - 1:
            dtmp = af32(0, S)
            for c in range(16):
                cp(V, dtmp, oT[:, c, :], ["oT%d" % c], ["dtmp"])
                dma("sp", dbg_out[:, c, :], dtmp, ["dtmp"], [])
        if STOP > 2:
            final_phase(l, l == depth - 1)
    P.emit(nc)
    st.close()
    return nc


PH = "ABC"


def EXTRA_PHASES(L):
    pass


def host_prep(inp):
    f = np.float32
    g = lambda k: np.asarray(inp[k], dtype=f)
    colv = lambda v: np.ascontiguousarray(v.reshape(-1, 128).T)
    cols = []
    for l in range(DEPTH):
        parts = [colv(g("b_in")[l]), colv(g("rwkv_mu")[l, 0]), colv(g("rwkv_mu")[l, 1]),
                 colv(g("rwkv_w0")[l, 0]), colv(g("rwkv_w0")[l, 1]), colv(g("rwkv_a0")[l, 0]), colv(g("rwkv_a0")[l, 1]),
                 colv(g("rwkv_k_k")[l]), colv(g("rwkv_k_a")[l]), colv(g("rwkv_r_k")[l].reshape(-1)),
                 colv(g("rwkv_gn_g")[l]), colv(g("rwkv_gn_b")[l]), colv(g("pool_b")[l]), colv(g("pool_scale")[l])]
        cols.append(np.concatenate(parts, axis=1))
    cols = np.stack(cols)
    assert cols.shape == (DEPTH, 128, NCOLS), cols.shape
    shared = {
        "w_in": g("w_in"), "cols": cols,
        "vrow": np.ascontiguousarray(g("b_in")[:, None, 7424:8192]),
        "w_up": np.ascontiguousarray(g("rwkv_w_up").reshape(DEPTH, 128, 1024)),
        "a_up": np.ascontiguousarray(g("rwkv_a_up").reshape(DEPTH, 128, 1024)),
        "pool_w": _blockdiag(g("pool_w")), "proj_a": g("proj_a"), "proj_b": g("proj_b"), "proj_c": g("proj_c"),
        "w_out": g("w_out"),
        "lnrow": np.ascontiguousarray(np.stack([g("ln_g"), g("ln_b")], axis=1)),
    }
    shared.update(host_consts())
    return shared


def _blockdiag(pw):
    out = np.zeros((DEPTH, 768, 768), np.float32)
    for gi in range(4):
        out[:, 192 * gi:192 * gi + 192, 192 * gi:192 * gi + 192] = pw[:, gi]
    return out


def host_consts():
    f = np.float32
    ident = np.eye(128, dtype=f)
    bones = np.zeros((128, 128), f)
    bones[:64, :64] = 1
    bones[64:, 64:] = 1
    perm = np.zeros((128, 128), f)
    for m in range(128):
        c = m % 64
        k = m + 32 if c < 32 else m - 32
        perm[k, m] = 1
    eye2 = np.concatenate([np.eye(64, dtype=f), np.eye(64, dtype=f)], 0)
    cst = np.concatenate([ident, bones, perm, eye2], axis=1)
    inv = np.power(f(10000.0), -np.arange(0, 64, 2, dtype=f) / f(64))
    ang = np.arange(S, dtype=f)[:, None] * inv[None, :]
    ang = np.concatenate([ang, ang], axis=-1).astype(f)
    cosT = np.cos(ang).T.astype(f)
    sinT = np.sin(ang).T.astype(f)
    sign = np.where(np.arange(64) < 32, -1.0, 1.0).astype(f)[:, None]
    rope = np.stack([np.concatenate([cosT, cosT], 0), np.concatenate([sinT * sign, sinT * sign], 0)]).astype(f)
    b = np.arange(128)[:, None]
    a = np.arange(128)[None, :]
    mA = (b >= a).astype(f)
    mB = (a >= b).astype(f)
    mAf = mA * (b >= 64)
    mBl = mB * (b < 64)
    m16 = (np.abs(a - b) <= 64).astype(f)
    amask = np.stack([np.concatenate([mA, mB, mA, mB], 1), np.concatenate([mAf, mB, mAf, mB], 1),
                      np.concatenate([mA, mBl, mA, mBl], 1), np.concatenate([m16, m16, m16, m16], 1)]).astype(f)
    s_ = np.arange(64)[:, None]
    t_ = np.arange(64)[None, :]
    rm = []
    for z in range(2):
        if z == 0:
            strict = (s_ < t_)
            incl = (s_ <= t_)
        else:
            strict = (s_ > t_)
            incl = (s_ >= t_)
        m1 = np.concatenate([strict, incl], 1).astype(f)
        m2 = strict.T.astype(f)
        reset = np.ones((64, 512), f)
        reset[:, ::64] = 0
        row = np.concatenate([np.tile(m1, (1, 4)), np.tile(m1, (1, 4)), np.tile(m2, (1, 4)), reset], 1)
        rm.append(np.concatenate([row, row], 0))
    rmask = np.stack(rm).astype(f)
    pedge = np.ones((128, 4, 16), f)
    for gi in range(4):
        h = 1 << gi
        for e in range(8):
            t = e
            cnt = min(t + h, S - 1) - max(t - h, 0) + 1
            pedge[:, gi, e] = (2 * h + 1) / cnt
            t = S - 8 + e
            cnt = min(t + h, S - 1) - max(t - h, 0) + 1
            pedge[:, gi, 8 + e] = (2 * h + 1) / cnt
    selc = np.zeros((128, 24), f)
    for c in range(6):
        for p in range(128):
            gi = (128 * c + p) // 192
            selc[p, c * 4 + gi] = 1.0 / (2 * (1 << gi) + 1)
    return {"selc": selc, "cst": cst, "rope": rope, "amask": amask, "rmask": rmask, "pedge": pedge}


_NC_CACHE = {}


def kernel(**inputs):
    shared = host_prep(inputs)
    x = np.asarray(inputs["x"], dtype=np.float32)
    if "nc" not in _NC_CACHE:
        _NC_CACHE["nc"] = build()
    nc = _NC_CACHE["nc"]
    in_maps = []
    for c in range(8):
        m = dict(shared)
        m["x"] = np.ascontiguousarray(x[c])
        in_maps.append(m)
    res = run_bass_kernel_spmd(nc, in_maps, core_ids=list(range(8)))
    return np.stack([np.asarray(r["y"], dtype=np.float32) for r in res.results], axis=0)
```

```python
import math
import os
import numpy as np
import ml_dtypes
import concourse.bass as bass
import concourse.mybir as mybir
from concourse.bass_utils import run_bass_kernel_spmd

F32 = mybir.dt.float32
BF16 = mybir.dt.bfloat16
AF = mybir.ActivationFunctionType
ALU = mybir.AluOpType
AX = mybir.AxisListType

ENGS = ("pe", "act", "dve", "pool", "sp")
DMA_POOL = 16


class _Buf:
    __slots__ = ("last_w", "readers", "dma_readers")

    def __init__(self):
        self.last_w = None
        self.readers = {}
        self.dma_readers = []


class _Op:
    __slots__ = ("eng", "idx", "gid", "fn", "deps", "is_dma", "signal", "sem", "val", "waits",
                 "know", "dma_n", "pre_wait")

    def __init__(self, eng, idx, gid, fn, is_dma):
        self.eng = eng
        self.idx = idx
        self.gid = gid
        self.fn = fn
        self.is_dma = is_dma
        self.deps = []
        self.signal = False
        self.sem = None
        self.val = None
        self.waits = []
        self.know = None
        self.dma_n = None
        self.pre_wait = None


class Prog:
    def __init__(self):
        self.ops = {e: [] for e in ENGS}
        self.all = []
        self.bufs = {}
        self.n_dma = {e: 0 for e in ENGS}

    def _buf(self, name):
        b = self.bufs.get(name)
        if b is None:
            b = _Buf()
            self.bufs[name] = b
        return b

    def add(self, eng, fn, reads=(), writes=(), dma=False):
        op = _Op(eng, len(self.ops[eng]), len(self.all), fn, dma)
        deps = {}
        for r in reads:
            b = self._buf(r)
            if b.last_w is not None:
                deps[b.last_w.gid] = b.last_w
        for w in writes:
            b = self._buf(w)
            if b.last_w is not None:
                deps[b.last_w.gid] = b.last_w
            for d in b.readers.values():
                deps[d.gid] = d
            for d in b.dma_readers:
                deps[d.gid] = d
        op.deps = [deps[k] for k in sorted(deps)]
        for r in reads:
            b = self._buf(r)
            if dma:
                b.dma_readers.append(op)
            else:
                b.readers[eng] = op
        for w in writes:
            b = self._buf(w)
            b.last_w = op
            b.readers = {}
            b.dma_readers = []
        if dma:
            op.dma_n = self.n_dma[eng]
            self.n_dma[eng] += 1
        self.ops[eng].append(op)
        self.all.append(op)
        return op

    def resolve(self):
        know = {e: {f: -1 for f in ENGS} for e in ENGS}
        know_dma = {e: set() for e in ENGS}
        sig_count = {e: 0 for e in ENGS}
        for op in self.all:
            E = op.eng
            kn = know[E]
            for d in op.deps:
                if d.is_dma:
                    if d.gid in know_dma[E]:
                        continue
                    know_dma[E].add(d.gid)
                    op.waits.append(d)
                    for f, v in d.know.items():
                        if v > kn[f]:
                            kn[f] = v
                    continue
                F = d.eng
                if F == E:
                    if E == "pe" or op.idx - d.idx > 2:
                        continue
                    if kn[F] >= d.idx:
                        continue
                elif kn[F] >= d.idx:
                    continue
                d.signal = True
                op.waits.append(d)
                kn[F] = max(kn[F], d.idx)
                for f, v in d.know.items():
                    if f != E and v > kn[f]:
                        kn[f] = v
            snap = dict(kn)
            if not op.is_dma:
                snap[E] = op.idx
            op.know = snap
        for e in ENGS:
            c = 0
            for op in self.ops[e]:
                if op.is_dma:
                    continue
                if op.signal:
                    c += 1
                    op.val = c

    def emit(self, nc):
        self.resolve()
        import contextlib
        with contextlib.ExitStack() as st:
            esem = {e: st.enter_context(nc.semaphore("s_" + e)) for e in ENGS}
            dsem = {e: [st.enter_context(nc.semaphore("d_%s%d" % (e, i))) for i in range(DMA_POOL)]
                    for e in ENGS if self.n_dma[e] > 0}
            block = st.enter_context(nc.Block())

            def wait_for(engine, d):
                if d.is_dma:
                    engine.wait_ge(dsem[d.eng][d.dma_n % DMA_POOL], 16 * (d.dma_n // DMA_POOL + 1))
                else:
                    engine.wait_ge(esem[d.eng], d.val)

            def run(engine, e):
                ops = self.ops[e]
                for op in ops:
                    for d in op.waits:
                        wait_for(engine, d)
                    if op.is_dma:
                        n = op.dma_n
                        if n >= DMA_POOL:
                            engine.wait_ge(dsem[e][n % DMA_POOL], 16 * (n // DMA_POOL))
                        ins = op.fn(engine)
                        ins.then_inc(dsem[e][n % DMA_POOL], 16)
                    else:
                        ins = op.fn(engine)
                        if op.signal:
                            ins.then_inc(esem[e], 1)
                nd = self.n_dma[e]
                for i in range(min(nd, DMA_POOL)):
                    n = nd - 1 - i
                    engine.wait_ge(dsem[e][n % DMA_POOL], 16 * (n // DMA_POOL + 1))

            @block.tensor
            def _(eng):
                run(eng, "pe")

            @block.scalar
            def _(eng):
                run(eng, "act")

            @block.vector
            def _(eng):
                run(eng, "dve")

            @block.gpsimd
            def _(eng):
                run(eng, "pool")

            @block.sync
            def _(eng):
                run(eng, "sp")


S = 2048
D = 1024
NIN = 11520
DEPTH = 4
PADX = 256
XW = S + 2 * PADX
ALPHA = (2 * DEPTH) ** 0.25
CDEC = math.exp(-0.5)
C_BIN = 0
C_MU0 = 90
C_MU1 = 116
C_W0 = 142
C_A0 = 158
C_KK = 174
C_KA = 182
C_RK = 190
C_GG = 198
C_GB = 206
C_PB = 214
C_PS = 220
NCOLS = 226


class Ctx:
    pass


def build(depth=DEPTH, dbg=None):
    nc = bass.Bass("TRN2", target_bir_lowering=False)
    P = Prog()
    dt_in = lambda name, shape: nc.dram_tensor(name, shape, F32, kind="ExternalInput").ap()
    x_in = dt_in("x", [S, D])
    w_in = dt_in("w_in", [DEPTH, D, NIN])
    cols_d = dt_in("cols", [DEPTH, 128, NCOLS])
    vrow_d = dt_in("vrow", [DEPTH, 1, 768])
    wup_d = dt_in("w_up", [DEPTH, 128, 1024])
    aup_d = dt_in("a_up", [DEPTH, 128, 1024])
    poolw_d = dt_in("pool_w", [DEPTH, 768, 768])
    proja_d = dt_in("proj_a", [DEPTH, 1024, 1024])
    projb_d = dt_in("proj_b", [DEPTH, 768, 1024])
    projc_d = dt_in("proj_c", [DEPTH, 256, 1024])
    wout_d = dt_in("w_out", [DEPTH, 1024, 1024])
    lnrow_d = dt_in("lnrow", [DEPTH, 2, 1024])
    cst_d = dt_in("cst", [128, 128 * 3 + 64])
    rope_d = dt_in("rope", [2, 128, S])
    amask_d = dt_in("amask", [4, 128, 512])
    rmask_d = dt_in("rmask", [2, 128, 512 + 512 + 256 + 512])
    pedge_d = dt_in("pedge", [128, 4, 16])
    y_out = nc.dram_tensor("y", [S, D], F32, kind="ExternalOutput").ap()
    xres = nc.dram_tensor("xres", [S, D], F32, kind="Internal").ap()
    dbg_out = None
    if dbg is not None:
        dbg_out = nc.dram_tensor("dbg", [128, 16, S], F32, kind="ExternalOutput").ap()

    import contextlib
    st = contextlib.ExitStack()
    sb = lambda name, shape, dt: st.enter_context(nc.sbuf_tensor(name, shape, dt))
    xT = sb("xT", [128, 8, XW], BF16)
    oT = sb("oT", [128, 16, S], BF16)
    wb = [sb("wb%d" % i, [128, 8, 512], BF16) for i in range(2)]
    ARENA = 16384
    arena = sb("arena", [128, ARENA], F32)
    colsb = sb("colsb", [128, NCOLS], F32)
    c0col = sb("c0col", [128, 26], F32)
    identb = sb("identb", [128, 128], BF16)
    permb = sb("permb", [128, 128], BF16)
    onesb = sb("onesb", [128, 128], BF16)
    eye2 = sb("eye2", [128, 64], BF16)
    bones = sb("bones", [128, 128], F32)
    vrowb = sb("vrowb", [1, 768], BF16)
    selcol = sb("selcol", [128, 24], F32)
    mixf = sb("mixf", [128, S], F32)
    selc_d = dt_in("selc", [128, 24])
    psb = [st.enter_context(nc.psum_tensor("psb%d" % i, [128, 512], F32)) for i in range(8)]

    C = Ctx()
    C.bank_i = 0

    def nextbank():
        i = C.bank_i
        C.bank_i = (i + 1) % 4
        return psb[i], "ps%d" % i

    def af32(off, n):
        return arena[:, off:off + n]

    def abf(off, n):
        return arena[:, off:off + n // 2].bitcast(BF16)

    def o8f32(off, n):
        return oT[:, 8:16, :].rearrange("p a b -> p (a b)").bitcast(F32)[:, off:off + n]

    def o8bf(off, n):
        return oT[:, 8:16, :].rearrange("p a b -> p (a b)")[:, off:off + n]

    add = P.add
    V = "dve"
    A = "act"
    G = "pool"

    def mm(out, lhsT, rhs, start, stop, rd, wr):
        add("pe", lambda e: e.matmul(out, lhsT, rhs, start=start, stop=stop), reads=rd, writes=wr)

    def act(out, in_, func, rd, wr, bias=0.0, scale=1.0):
        add(A, lambda e: e.activation(out=out, in_=in_, func=func, bias=bias, scale=scale), reads=rd, writes=wr)

    def tt(eng, out, in0, in1, op, rd, wr):
        eng = V if eng == G else eng
        add(eng, lambda e: e.tensor_tensor(out=out, in0=in0, in1=in1, op=op), reads=rd, writes=wr)

    def ts(eng, out, in0, s1, s2, op0, op1, rd, wr):
        eng = V if eng == G else eng
        if s2 is None:
            add(eng, lambda e: e.tensor_scalar(out=out, in0=in0, scalar1=s1, scalar2=None, op0=op0), reads=rd, writes=wr)
        else:
            add(eng, lambda e: e.tensor_scalar(out=out, in0=in0, scalar1=s1, scalar2=s2, op0=op0, op1=op1), reads=rd, writes=wr)

    def stt(out, in0, scalar, in1, op0, op1, rd, wr):
        add(V, lambda e: e.scalar_tensor_tensor(out=out, in0=in0, scalar=scalar, in1=in1, op0=op0, op1=op1),
            reads=rd, writes=wr)

    def cp(eng, out, in_, rd, wr):
        eng = V if eng == G else eng
        if eng == A:
            add(eng, lambda e: e.copy(out=out, in_=in_), reads=rd, writes=wr)
        else:
            add(eng, lambda e: e.tensor_copy(out=out, in_=in_), reads=rd, writes=wr)

    def dma(q, out, in_, rd, wr):
        add(q, lambda e: e.dma_start(out=out, in_=in_), reads=rd, writes=wr, dma=True)

    def memset(eng, ap, val, wr):
        add(eng, lambda e: e.memset(ap, val), writes=wr)

    bscr = sb("bscr", [128, 8], F32)
    epsc = sb("epsc", [128, 1], F32)
    halfcol = sb("halfcol", [128, 32], F32)
    hbuf2 = sb("hbuf2", [128, 2050], F32)

    def barrier():
        names = [n for n in P.bufs.keys() if not n.startswith("ps")] + ["bscr"]
        mm(psb[7][:, 0:8], identb[:, 0:128], identb[:, 0:8], True, True, [], names + ["ps7"])
        act(bscr[:, 0:1], bscr[:, 1:2], AF.Copy, [], names)
        memset(V, bscr[:, 2:3], 0.0, names)
        dma("sp", bscr[0:1, 3:4], cst_d[0:1, 0:1], [], names)
        dma(G, bscr[0:1, 4:5], cst_d[0:1, 0:1], [], names)

    memset(V, bscr[:], 0.0, ["bscr"])
    memset(V, epsc[:], 1e-12, ["epsc"])
    dma(G, identb[:], cst_d[:, 0:128], [], ["identb"])
    dma("sp", bones[:], cst_d[:, 128:256], [], ["bones"])
    dma(G, permb[:], cst_d[:, 256:384], [], ["permb"])
    dma("sp", selcol[:], selc_d, [], ["selcol"])
    dma(G, eye2[:], cst_d[:, 384:448], [], ["eye2"])
    memset(V, onesb[:], 1.0, ["onesb"])
    memset(V, xT[:, :, 0:PADX], 0.0, ["xT"])
    memset(V, xT[:, :, PADX + S:XW], 0.0, ["xT"])

    C.wres = [None, None]
    C.wlast = 0

    def load_w(key, src3):
        for i in range(2):
            if C.wres[i] == key:
                C.wlast = i
                return wb[i], "wb%d" % i
        i = 1 - C.wlast
        C.wres[i] = key
        C.wlast = i
        kc, ncol = src3.shape[1], src3.shape[2]
        dma(G, wb[i][:, 0:kc, 0:ncol], src3, [], ["wb%d" % i])
        return wb[i], "wb%d" % i

    def win_src(l, col0, ncol):
        return w_in[l].rearrange("(k p) n -> p k n", p=128)[:, :, col0:col0 + ncol]

    def inproj(l, cg, evac, wkey=None):
        blk = cg // 4
        ncol = min(512, NIN - blk * 512)
        w, wn = load_w(("win", l, blk), win_src(l, blk * 512, ncol))
        c0 = (cg % 4) * 128
        for t4 in range(4):
            bank, bn = nextbank()
            for k in range(8):
                mm(bank[:], w[:, k, c0:c0 + 128], xT[:, k, PADX + t4 * 512:PADX + (t4 + 1) * 512],
                   k == 0, k == 7, ["xT", wn], [bn])
            evac(t4, bank, bn)

    def col(ci):
        return colsb[:, ci:ci + 1]

    xstage = sb("xstage", [128, 1024], BF16)

    def store_xT(src_f32, srcname, t16):
        cp(A, xstage[:], src_f32, [srcname], ["xstage"])
        bank, bn = nextbank()
        bb = bank[:].bitcast(BF16)
        for k in range(8):
            add("pe", lambda e, k=k: e.transpose(bb[:, k * 128:(k + 1) * 128], xstage[:, k * 128:(k + 1) * 128], identb[:]),
                reads=["xstage", "identb"], writes=[bn])
        cp(V, xT[:, :, PADX + t16 * 128:PADX + (t16 + 1) * 128], bb.rearrange("p (k t) -> p k t", k=8), [], [bn, "xT"])

    def final_phase(l, last):
        barrier()
        mergedT = abf(0, 8 * S).rearrange("p (k t) -> p k t", k=8)
        sig = abf(8192, 512)
        tmpf = af32(8448, 512)
        lng = af32(9216, 1024)
        lnb = af32(10240, 1024)
        xt_ = [af32(11264, 1024), af32(12288, 1024)]
        yt_ = [af32(13312, 1024), af32(14336, 1024)]
        stat = af32(15360, 8)
        dma("sp", lng, lnrow_d[l, 0:1, :].to_broadcast([128, 1024]), [], ["lng"])
        dma("sp", lnb, lnrow_d[l, 1:2, :].to_broadcast([128, 1024]), [], ["lnb"])
        if STOP == 4:
            return
        branches = [(proja_d, 8, 0, 66), (projb_d, 6, 8, 74), (projc_d, 2, 14, 82)]
        for bi, (pd, kc, o0, g0) in enumerate(branches):
            for eb in range(2):
                for ec in range(eb * 4, eb * 4 + 4):
                    for t4 in range(4):
                        tsl = slice(t4 * 512, (t4 + 1) * 512)
                        pw, pwn = load_w(("proj", l, bi, eb), pd[l].rearrange("(k p) n -> p k n", p=128)[:, :, eb * 512:(eb + 1) * 512])
                        b1, b1n = nextbank()
                        for k in range(kc):
                            mm(b1[:], pw[:, k, (ec % 4) * 128:(ec % 4 + 1) * 128], oT[:, o0 + k, tsl], k == 0, k == kc - 1,
                               ["oT%d" % (o0 + k), pwn], [b1n])
                        gw, gwn = load_w(("gate", l, bi, eb), win_src(l, (g0 + eb * 4) * 128, 512))
                        b2, b2n = nextbank()
                        for k in range(8):
                            mm(b2[:], gw[:, k, (ec % 4) * 128:(ec % 4 + 1) * 128], xT[:, k, PADX + t4 * 512:PADX + (t4 + 1) * 512],
                               k == 0, k == 7, ["xT", gwn], [b2n])
                        act(sig, b2[:], AF.Sigmoid, ["colsb"], [b2n, "sig"], bias=col(C_BIN + g0 + ec))
                        if bi == 0:
                            tt(V, mergedT[:, ec, tsl], b1[:], sig, ALU.mult, ["sig"], [b1n, "mg%d" % ec])
                        else:
                            tt(V, tmpf, b1[:], sig, ALU.mult, ["sig"], [b1n, "tmpf"])
                            tt(G, mergedT[:, ec, tsl], mergedT[:, ec, tsl], tmpf, ALU.add, ["tmpf"], ["mg%d" % ec])
        if STOP == 3:
            return
        wo = []
        for fh in range(2):
            wo.append(load_w(("wout", l, fh), wout_d[l].rearrange("(k p) n -> p k n", p=128)[:, :, fh * 512:(fh + 1) * 512]))
        xsrc = x_in if l == 0 else xres
        dst = y_out if last else xres
        for t16 in range(16):
            xt = xt_[t16 % 2]
            yt = yt_[t16 % 2]
            xn, yn = "xt%d" % (t16 % 2), "yt%d" % (t16 % 2)
            dma("sp", xt, xsrc[t16 * 128:(t16 + 1) * 128, :], ["xres"] if l > 0 else [], [xn])
            for fh in range(2):
                w, wn = wo[fh]
                bank, bn = nextbank()
                for k in range(8):
                    mm(bank[:], mergedT[:, k, t16 * 128:(t16 + 1) * 128], w[:, k, :], k == 0, k == 7,
                       ["mg%d" % k, wn], [bn])
                stt(yt[:, fh * 512:(fh + 1) * 512], xt[:, fh * 512:(fh + 1) * 512], ALPHA, bank[:], ALU.mult, ALU.add,
                    [xn], [bn, yn])
            if STOP == 5:
                dma("sp", dst[t16 * 128:(t16 + 1) * 128, :], yt, [yn], ["xres"])
                continue
            add(V, lambda e, yt=yt: e.tensor_reduce(out=stat[:, 0:1], in_=yt, axis=AX.X, op=ALU.add), reads=[yn], writes=["stat0"])
            ts(V, stat[:, 1:2], stat[:, 0:1], -1.0 / D, None, ALU.mult, None, ["stat0"], ["stat1"])
            ts(V, yt, yt, stat[:, 1:2], None, ALU.add, None, ["stat1"], [yn])
            add(A, lambda e, yt=yt, xt=xt: e.activation(out=xt, in_=yt, func=AF.Square, accum_out=stat[:, 2:3]),
                reads=[yn], writes=[xn, "stat2"])
            act(stat[:, 3:4], stat[:, 2:3], AF.Sqrt, ["stat2"], ["stat3"], bias=1e-5, scale=1.0 / D)
            add(V, lambda e: e.reciprocal(out=stat[:, 4:5], in_=stat[:, 3:4]), reads=["stat3"], writes=["stat4"])
            if STOP == 6:
                dma("sp", dst[t16 * 128:(t16 + 1) * 128, :], yt, [yn], ["xres"])
                continue
            if STOP != 8:
                stt(yt, yt, stat[:, 4:5], lng, ALU.mult, ALU.mult, ["stat4", "lng"], [yn])
            if STOP != 7:
                tt(V, yt, yt, lnb, ALU.add, ["lnb"], [yn])
            dma("sp", dst[t16 * 128:(t16 + 1) * 128, :], yt, [yn], ["xres"])
            if not last:
                store_xT(yt, yn, t16)

    def pool_phase(l):
        barrier()
        W = S + 32
        pbuf = af32(0, W)
        a_ = [af32(2080, W), af32(4160, W)]
        mixed = abf(6240, 6 * S).rearrange("p (c t) -> p c t", c=6)
        wgt = abf(12384, 6 * 768).rearrange("p (a b) -> p a b", a=6)
        sacc = af32(14688, 0) if False else None
        pe_t = af32(14688, 64).rearrange("p (g e) -> p g e", g=4)
        t1 = af32(14752, 512)
        memset(V, pbuf[:, 0:16], 0.0, ["pbuf"])
        memset(V, pbuf[:, W - 16:W], 0.0, ["pbuf"])
        dma("sp", pe_t, pedge_d, [], ["pe_t"])
        dma(G, wgt, poolw_d[l].rearrange("(k p) n -> p k n", p=128), [], ["wgt"])
        for c in range(6):
            inproj(l, 40 + c, lambda t4, bank, bn, c=c: act(oT[:, 8 + c, t4 * 512:(t4 + 1) * 512], bank[:], AF.Silu,
                                                             ["colsb"], [bn, "oT%d" % (8 + c)], bias=col(C_BIN + 40 + c)))
        for c in range(6):
            inproj(l, 34 + c, lambda t4, bank, bn, c=c: act(pbuf[:, 16 + t4 * 512:16 + (t4 + 1) * 512], bank[:], AF.Identity,
                                                             ["colsb"], [bn, "pbuf"], bias=col(C_BIN + 34 + c)))
            gs = sorted(set((2 * c + hf) // 3 for hf in range(2)))
            first = True
            for g in gs:
                h = 1 << g
                kk = g + 1
                src, srcn = pbuf, "pbuf"
                for j in range(kk):
                    sh = 1 << j
                    dstt = a_[j % 2]
                    n = W - (2 << j) + 1
                    tt(V, dstt[:, 0:n], src[:, 0:n], src[:, sh:sh + n], ALU.add, [srcn], ["a%d" % (j % 2)])
                    src, srcn = dstt, "a%d" % (j % 2)
                sfin = a_[kk % 2]
                sn = "a%d" % (kk % 2)
                tt(V, sfin[:, 0:S], src[:, 16 - h:16 - h + S], pbuf[:, 16 + h:16 + h + S], ALU.add, [srcn, "pbuf"], [sn])
                tt(V, sfin[:, 0:8], sfin[:, 0:8], pe_t[:, g, 0:8], ALU.mult, ["pe_t"], [sn])
                tt(V, sfin[:, S - 8:S], sfin[:, S - 8:S], pe_t[:, g, 8:16], ALU.mult, ["pe_t"], [sn])
                selw = selcol[:, c * 4 + g:c * 4 + g + 1]
                if first:
                    stt(mixf[:, :], sfin[:, 0:S], selw, pbuf[:, 16:16 + S], ALU.mult, ALU.subtract, [sn, "pbuf", "selcol"], ["mixf"])
                else:
                    stt(mixf[:, :], sfin[:, 0:S], selw, mixf[:, :], ALU.mult, ALU.add, [sn, "selcol"], ["mixf"])
                first = False
            cp(A, mixed[:, c, :], mixf[:, :], ["mixf"], ["mixed"])
        for oc in range(6):
            ics = [ic for ic in range(6) if any((2 * ic + a) // 3 == (2 * oc + b) // 3 for a in range(2) for b in range(2))]
            for t4 in range(4):
                tsl = slice(t4 * 512, (t4 + 1) * 512)
                bank, bn = nextbank()
                for n_, ic in enumerate(ics):
                    mm(bank[:], wgt[:, ic, oc * 128:(oc + 1) * 128], mixed[:, ic, tsl], n_ == 0, n_ == len(ics) - 1,
                       ["wgt", "mixed"], [bn])
                ts(V, t1, bank[:], col(C_PB + oc), col(C_PS + oc), ALU.add, ALU.mult, ["colsb"], [bn, "t1"])
                tt(V, oT[:, 8 + oc, tsl], t1, oT[:, 8 + oc, tsl], ALU.mult, ["t1"], ["oT%d" % (8 + oc)])

    C.attn = None
    C.rwkv = None
    def attn_phase(l):
        barrier()
        Qr = abf(0, XW)
        Kr = abf(1280, XW)
        qraw = abf(2560, S)
        ropec = af32(3584, S)
        ropes = af32(5632, S)
        t1 = af32(7680, 512)
        t2 = af32(8192, 512)
        accn = af32(8704, S)
        accd = af32(10752, S)
        Vt = abf(12800, 20 * 128).rearrange("p (a b) -> p a b", a=20)
        pT = [abf(14080, 512), abf(14336, 512)]
        msk = abf(14592, 4 * 512).rearrange("p (a b) -> p a b", a=4)
        dma("sp", ropec, rope_d[0], [], ["ropec"])
        dma("sp", ropes, rope_d[1], [], ["ropes"])
        for a_ in range(4):
            dma(G, msk[:, a_, :], amask_d[a_], [], ["msk"])
        for buf, nm in ((Qr, "Qr"), (Kr, "Kr")):
            memset(V, buf[:, 0:PADX], 0.0, [nm])
            memset(V, buf[:, PADX + S:XW], 0.0, [nm])
        cnt = [0]
        SUB = int(os.environ.get("ATT_SUB", "9"))
        if SUB == 1:
            return
        for pp in range(2):
            for g in range(3):
                d = (1, 4, 16)[g]
                for cg, dst, nm in ((46 + 2 * g + pp, Qr, "Qr"), (52 + 2 * g + pp, Kr, "Kr")):
                    inproj(l, cg, lambda t4, bank, bn, cg=cg: act(qraw[:, t4 * 512:(t4 + 1) * 512], bank[:], AF.Identity,
                                                                  ["colsb"], [bn, "qraw"], bias=col(C_BIN + cg)))
                    for t4 in range(4):
                        tsl = slice(t4 * 512, (t4 + 1) * 512)
                        bank, bn = nextbank()
                        mm(bank[:], permb[:], qraw[:, tsl], True, True, ["permb", "qraw"], [bn])
                        tt(V, t1, bank[:], ropes[:, tsl], ALU.mult, ["ropes"], [bn, "t1"])
                        tt(V, t2, qraw[:, tsl], ropec[:, tsl], ALU.mult, ["qraw", "ropec"], ["t2"])
                        tt(V, dst[:, PADX + t4 * 512:PADX + (t4 + 1) * 512], t1, t2, ALU.add, ["t1", "t2"], [nm])
                if SUB == 2:
                    return
                wv, wvn = load_w(("wv", l, g, pp), win_src(l, (58 + 2 * g + pp) * 128, 128))
                if d == 1:
                    tsls = [slice(PADX + 128 * m - 64, PADX + 128 * m + 64) for m in range(17)]
                elif d == 4:
                    tsls = []
                    for r in range(4):
                        for m in range(5):
                            s0 = PADX + r + 4 * (128 * m - 64)
                            tsls.append(slice(s0, s0 + 509, 4))
                else:
                    tsls = [slice(PADX + r, PADX + r + 2033, 16) for r in range(16)]
                for j0 in range(0, len(tsls), 4):
                    grp = tsls[j0:j0 + 4]
                    bank, bn = nextbank()
                    for j, sl in enumerate(grp):
                        for k in range(8):
                            mm(bank[:, j * 128:(j + 1) * 128], xT[:, k, sl], wv[:, k, 0:128], k == 0, False, ["xT", wvn], [bn])
                        mm(bank[:, j * 128:(j + 1) * 128], onesb[0:1, 0:128], vrowb[0:1, (2 * g + pp) * 128:(2 * g + pp + 1) * 128],
                           False, True, ["onesb", "vrowb"], [bn])
                    n = len(grp)
                    cp(A, Vt[:, j0:j0 + n, :], bank[:, 0:n * 128].rearrange("p (a b) -> p a b", a=n), [], [bn, "Vt"])
                if SUB == 3:
                    return
                for sbk in range(4):
                    for j in range(4):
                        if d == 1:
                            m = 4 * sbk + j
                            qsl = slice(PADX + 128 * m, PADX + 128 * m + 128)
                            chunks = [(slice(PADX + 128 * m - 64, PADX + 128 * m + 64), m),
                                      (slice(PADX + 128 * m + 64, PADX + 128 * m + 192), m + 1)]
                            mi = 1 if m == 0 else (2 if m == 15 else 0)
                        elif d == 4:
                            r, m = sbk, j
                            q0 = PADX + r + 512 * m
                            qsl = slice(q0, q0 + 509, 4)
                            k0 = PADX + r + 4 * (128 * m - 64)
                            chunks = [(slice(k0, k0 + 509, 4), r * 5 + m), (slice(k0 + 512, k0 + 512 + 509, 4), r * 5 + m + 1)]
                            mi = 1 if m == 0 else (2 if m == 3 else 0)
                        else:
                            r = 4 * sbk + j
                            qsl = slice(PADX + r, PADX + r + 2033, 16)
                            chunks = [(qsl, r)]
                            mi = 3
                        nch = len(chunks)
                        wd = nch * 128
                        for h in range(2):
                            hp_ = slice(64 * h, 64 * h + 64)
                            si = h + 2 * (cnt[0] % 2)
                            sbank, sbn = psb[si], "ps%d" % si
                            par = cnt[0] % 2
                            pTb, pTn = pT[h][:, par * 256:par * 256 + 256], "pT%d_%d" % (h, par)
                            for ci, (ks, vt) in enumerate(chunks):
                                mm(sbank[:, ci * 128:(ci + 1) * 128], Kr[hp_, ks], Qr[hp_, qsl], True, True, ["Kr", "Qr"], [sbn])
                            act(pTb[:, 0:wd], sbank[:, 0:wd], AF.Exp, [], [sbn, pTn], scale=0.125)
                            tt(V, pTb[:, 0:wd], pTb[:, 0:wd], msk[:, mi, 0:wd], ALU.mult, ["msk"], [pTn])
                            for ci, (ks, vt) in enumerate(chunks):
                                mm(psb[6][hp_, j * 128:(j + 1) * 128], Vt[:, vt, 64 * h:64 * h + 64], pTb[:, ci * 128:(ci + 1) * 128],
                                   ci == 0, ci == nch - 1, ["Vt", pTn], ["ps6"])
                            for ci, (ks, vt) in enumerate(chunks):
                                mm(psb[7][hp_, j * 128:(j + 1) * 128], onesb[:, 0:64], pTb[:, ci * 128:(ci + 1) * 128],
                                   ci == 0, ci == nch - 1, ["onesb", pTn], ["ps7"])
                        cnt[0] += 1
                    if d == 1:
                        vn, vd = accn[:, 512 * sbk:512 * sbk + 512], accd[:, 512 * sbk:512 * sbk + 512]
                        bn_, bd_ = psb[6][:, :], psb[7][:, :]
                    elif d == 4:
                        vn, vd = accn[:, sbk:S:4], accd[:, sbk:S:4]
                        bn_, bd_ = psb[6][:, :], psb[7][:, :]
                    else:
                        vn = accn.rearrange("p (i r) -> p r i", r=16)[:, 4 * sbk:4 * sbk + 4, :]
                        vd = accd.rearrange("p (i r) -> p r i", r=16)[:, 4 * sbk:4 * sbk + 4, :]
                        bn_ = psb[6][:, :].rearrange("p (r i) -> p r i", r=4)
                        bd_ = psb[7][:, :].rearrange("p (r i) -> p r i", r=4)
                    if g == 0:
                        cp(V, vn, bn_, [], ["ps6", "accn"])
                        cp(A, vd, bd_, [], ["ps7", "accd"])
                    else:
                        tt(V, vn, vn, bn_, ALU.add, [], ["ps6", "accn"])
                        tt(V, vd, vd, bd_, ALU.add, [], ["ps7", "accd"])
                if os.environ.get("ATT_STOP") == str(g + 1):
                    return
            oc = 14 + pp
            inproj(l, 64 + pp, lambda t4, bank, bn, oc=oc, pp=pp: act(oT[:, oc, t4 * 512:(t4 + 1) * 512], bank[:], AF.Silu,
                                                                        ["colsb"], [bn, "oT%d" % oc], bias=col(C_BIN + 64 + pp)))
            act(accd, accd, AF.Ln, [], ["accd"])
            act(accd, accd, AF.Exp, [], ["accd"], scale=-1.0)
            tt(V, accn, accn, accd, ALU.mult, ["accd"], ["accn"])
            tt(V, oT[:, oc, :], accn, oT[:, oc, :], ALU.mult, ["accn"], ["oT%d" % oc])

    C.attn = attn_phase

    def rwkv_phase(l):
        barrier()
        lw = abf(0, S)
        la = abf(1024, S)
        wup = abf(2048, 1024)
        aup = abf(2560, 1024)
        hbuf = af32(3072, 2050)
        tmpB = af32(3072, S)
        tmpf = af32(5124, S)
        rbf = abf(7172, S)
        kbf = abf(8196, S)
        vbf = abf(9220, S)
        kkbf = abf(10244, S)
        bonus = abf(11268, S)
        ytok = af32(12292, S).rearrange("p (c i) -> p c i", c=32)
        Vtok = abf(14340, S).rearrange("p (c i) -> p c i", c=32)
        reset = af32(15364, 512)
        STf = af32(15876, 64)
        STb = abf(15940, 64)
        Xs = abf(15972, 64)
        Us = abf(16004, 64)
        Wc = af32(16036, 8)
        totc = af32(16044, 8)
        stat = af32(16052, 128)
        ynb = rbf.rearrange("p (c i) -> p c i", c=32)
        tmpf3 = tmpf.rearrange("p (c i) -> p c i", c=32)
        sg = o8f32(0, 512)
        aa = o8f32(512, 512)
        Gc = o8f32(1024, 512)
        tmpG = o8f32(1536, 512)
        E = o8f32(2048, 512)
        bb = o8f32(2560, 512)
        kd = o8f32(3072, 512)
        AR2 = o8bf(2 * 3584, 1024).rearrange("p (c n) -> p c n", c=8)
        AR4 = o8bf(2 * 3584, 1024).rearrange("p (c a j) -> p c a j", c=8, a=2)
        BT = o8bf(2 * 4096, 512)
        KT = o8bf(2 * 4352, 512)
        BHT = o8bf(2 * 4608, 512)
        KHT = o8bf(2 * 4864, 512)
        BHtok = o8bf(2 * 5120, 512).rearrange("p (c j) -> p c j", c=8)
        KHtok = o8bf(2 * 5376, 512).rearrange("p (c j) -> p c j", c=8)
        G1s = o8bf(2 * 5632, 1024).rearrange("p (c n) -> p c n", c=8)
        G2s = o8bf(2 * 6144, 1024).rearrange("p (c n) -> p c n", c=8)
        QP = o8bf(2 * 6656, 1024).rearrange("p (c n) -> p c n", c=8)
        Pn = o8bf(2 * 7168, 512).rearrange("p (c n) -> p c n", c=8)
        m1 = [o8bf(2 * 7424, 512), o8bf(2 * 7680, 512)]
        m2 = [o8bf(2 * 7936, 256), o8bf(2 * 8064, 256)]
        t1f = E

        mfb = mixf[:, :].bitcast(BF16)
        Wc1 = af32(16180, 8)
        SETS = [
            (AR2, AR4, G1s, G2s, QP, BHtok, KHtok, Wc),
            (mfb[:, 0:1024].rearrange("p (c n) -> p c n", c=8), mfb[:, 0:1024].rearrange("p (c a j) -> p c a j", c=8, a=2),
             mfb[:, 1024:2048].rearrange("p (c n) -> p c n", c=8), mfb[:, 2048:3072].rearrange("p (c n) -> p c n", c=8),
             mfb[:, 3072:4096].rearrange("p (c n) -> p c n", c=8),
             xstage[:, 0:512].rearrange("p (c j) -> p c j", c=8), xstage[:, 512:1024].rearrange("p (c j) -> p c j", c=8), Wc1),
        ]

        STATE = [(STf, STb, Xs, Us), (af32(16188, 64), abf(16252, 64), abf(16284, 64), abf(16316, 64))]

        def c8v(ap):
            return ap.rearrange("p (c j) -> p c j", c=8)

        dma(G, wup, wup_d[l], [], ["wup"])
        dma(G, aup, aup_d[l], [], ["aup"])
        dma("sp", reset, rmask_d[0][:, 1280:1792], [], ["reset"])
        for z in range(2):
            dma(G, m1[z], rmask_d[z][:, 0:512], [], ["m1_%d" % z])
            dma(G, m2[z], rmask_d[z][:, 1024:1280], [], ["m2_%d" % z])
        memset(V, hbuf[:, 0:1], 0.0, ["hbuf"])
        memset(V, hbuf[:, 2049:2050], 0.0, ["hbuf"])
        memset(V, hbuf2[:, 0:1], 0.0, ["hbuf2"])
        memset(V, hbuf2[:, 2049:2050], 0.0, ["hbuf2"])
        hbs = [(hbuf, "hbuf"), (hbuf2, "hbuf2")]
        hbi = [0]
        ts(V, halfcol[:], colsb[:, C_W0:C_W0 + 32], 0.5, None, ALU.mult, None, ["colsb"], ["halfcol"])

        def shifted(cg, dst, dstname):
            hb_, hbn = hbs[hbi[0]]
            hbi[0] ^= 1
            inproj(l, cg, lambda t4, bank, bn: act(hb_[:, 1 + t4 * 512:1 + (t4 + 1) * 512], bank[:], AF.Identity,
                                                   ["colsb"], [bn, hbn], bias=col(C_BIN + cg)))
            act(tmpf, hb_[:, 1:2049], AF.Identity, [hbn, "c0col"], ["tmpf"], scale=c0col[:, cg:cg + 1])
            stt(tmpf, hb_[:, 0:2048], col(C_MU0 + cg), tmpf, ALU.mult, ALU.add, [hbn, "colsb"], ["tmpf"])
            stt(dst, hb_[:, 2:2050], col(C_MU1 + cg), tmpf, ALU.mult, ALU.add, [hbn, "colsb", "tmpf"], [dstname])

        def pairbank():
            return [nextbank(), nextbank()]

        shifted(24, tmpf, "tmpf")
        act(lw, tmpf, AF.Tanh, ["tmpf"], ["lw"])
        shifted(25, la, "la")

        for hp in range(8):
            shifted(hp, rbf, "rbf")
            shifted(8 + hp, kbf, "kbf")
            shifted(16 + hp, vbf, "vbf")
            inproj(l, 26 + hp, lambda t4, bank, bn, hp=hp: act(oT[:, hp, t4 * 512:(t4 + 1) * 512], bank[:], AF.Silu,
                                                               ["colsb"], [bn, "oT%d" % hp], bias=col(C_BIN + 26 + hp)))
            ts(V, tmpf, kbf, col(C_KK + hp), None, ALU.mult, None, ["kbf", "colsb"], ["tmpf"])
            act(tmpB, tmpf, AF.Square, ["tmpf"], ["hbuf"])
            for t4 in range(4):
                tsl = slice(t4 * 512, (t4 + 1) * 512)
                bank, bn = nextbank()
                mm(bank[:], bones[:], tmpB[:, tsl], True, True, ["bones", "hbuf"], [bn])
                act(tmpB[:, tsl], bank[:], AF.Ln, ["epsc"], [bn, "hbuf"], bias=epsc[:, 0:1])
                act(tmpB[:, tsl], tmpB[:, tsl], AF.Exp, [], ["hbuf"], scale=-0.5)
            tt(V, kkbf, tmpf, tmpB, ALU.mult, ["tmpf", "hbuf"], ["kkbf"])
            stt(tmpf, rbf, col(C_RK + hp), kbf, ALU.mult, ALU.mult, ["rbf", "kbf", "colsb"], ["tmpf"])
            for t4 in range(4):
                tsl = slice(t4 * 512, (t4 + 1) * 512)
                bank, bn = nextbank()
                mm(bank[:], bones[:], tmpf[:, tsl], True, True, ["bones", "tmpf"], [bn])
                tt(V, bonus[:, tsl], bank[:], vbf[:, tsl], ALU.mult, ["vbf"], [bn, "bonus"])
            for T in range(4):
                pb = pairbank()
                for h in range(2):
                    hs = slice(64 * h, 64 * h + 64)
                    bank, bn = pb[h]
                    for c8 in range(8):
                        c = T * 8 + c8
                        mm(bank[hs, c8 * 64:(c8 + 1) * 64], vbf[hs, c * 64:(c + 1) * 64], identb[hs, 64 * h:64 * h + 64],
                           True, True, ["vbf", "identb"], [bn])
                    cp(A, Vtok[hs, T * 8:(T + 1) * 8, :], c8v(bank[hs, :]), [], [bn, "Vtok"])

            items = [(0, T) for T in range(4)] + [(1, T) for T in range(3, -1, -1)]

            def produce(z, T, s):
                zs = slice(64 * z, 64 * z + 64)
                tsl = slice(T * 512, (T + 1) * 512)
                AR2_, AR4_, G1s_, G2s_, QP_, BHtok_, KHtok_, Wc_ = SETS[s]
                ss = str(s)
                ARn = "AR" + ss
                b1, b1n = nextbank()
                mm(b1[:], wup[zs, hp * 128:(hp + 1) * 128], lw[zs, tsl], True, True, ["wup", "lw"], [b1n])
                act(sg, b1[:], AF.Tanh, ["halfcol"], [b1n, "sg"], bias=halfcol[:, z * 8 + hp:z * 8 + hp + 1], scale=0.5)
                ts(V, sg, sg, 0.5, 0.5, ALU.mult, ALU.add, [], ["sg"])
                b2, b2n = nextbank()
                mm(b2[:], aup[zs, hp * 128:(hp + 1) * 128], la[zs, tsl], True, True, ["aup", "la"], [b2n])
                act(aa, b2[:], AF.Tanh, ["halfcol"], [b2n, "aa"], bias=halfcol[:, 16 + z * 8 + hp:16 + z * 8 + hp + 1], scale=0.5)
                ts(V, aa, aa, 0.5, 0.5, ALU.mult, ALU.add, [], ["aa"])
                yield
                add(V, lambda e: e.tensor_tensor_scan(out=Gc, data0=reset, data1=sg, initial=0.0, op0=ALU.mult, op1=ALU.add),
                    reads=["reset", "sg"], writes=["Gc"])
                cp(V, totc, c8v(Gc)[:, :, 63], ["Gc"], ["totc"])
                totb = totc.unsqueeze(2).to_broadcast([128, 8, 64])
                if z == 1:
                    tt(V, tmpG, sg, Gc, ALU.subtract, ["sg", "Gc"], ["tmpG"])
                    tt(V, c8v(Gc), c8v(tmpG), totb, ALU.add, ["tmpG", "totc"], ["Gc"])
                yield
                tt(V, sg, Gc, sg, ALU.subtract, ["Gc"], ["sg"])
                tt(V, c8v(tmpG), c8v(Gc), totb, ALU.subtract, ["Gc", "totc"], ["tmpG"])
                act(E, Gc, AF.Exp, ["Gc"], ["E"], scale=-CDEC)
                act(sg, sg, AF.Exp, [], ["sg"], scale=-CDEC)
                act(Gc, Gc, AF.Exp, [], ["Gc"], scale=CDEC)
                act(tmpG, tmpG, AF.Exp, [], ["tmpG"], scale=CDEC)
                act(Wc_, totc, AF.Exp, ["totc"], ["Wc" + ss], scale=-CDEC)
                yield
                ts(V, kd, aa, -1.0, col(C_KA + hp), ALU.add, ALU.mult, ["aa", "colsb"], ["kd"])
                stt(kd, kd, 1.0, kbf[:, tsl], ALU.add, ALU.mult, ["kbf"], ["kd"])
                tt(V, bb, kkbf[:, tsl], aa, ALU.mult, ["kkbf", "aa"], ["bb"])
                yield
                tt(V, AR4_[:, :, 1, :], c8v(rbf[:, tsl]), c8v(E), ALU.mult, ["rbf", "E"], [ARn])
                stt(AR4_[:, :, 0, :], c8v(kkbf[:, tsl]), -1.0, c8v(sg), ALU.mult, ALU.mult, ["kkbf", "sg"], [ARn])
                yield
                tt(V, BT, bb, Gc, ALU.mult, ["bb", "Gc"], ["BT"])
                tt(V, KT, kd, Gc, ALU.mult, ["kd", "Gc"], ["KT"])
                tt(V, BHT, bb, tmpG, ALU.mult, ["bb", "tmpG"], ["BHT"])
                tt(V, KHT, kd, tmpG, ALU.mult, ["kd", "tmpG"], ["KHT"])
                yield
                for src, srcn, dst, dstn in ((BHT, "BHT", BHtok_, "BHtok" + ss), (KHT, "KHT", KHtok_, "KHtok" + ss)):
                    pb = pairbank()
                    for h in range(2):
                        hs = slice(64 * h, 64 * h + 64)
                        bank, bn = pb[h]
                        for c8 in range(8):
                            mm(bank[hs, c8 * 64:(c8 + 1) * 64], src[hs, c8 * 64:(c8 + 1) * 64], identb[hs, 64 * h:64 * h + 64],
                               True, True, [srcn, "identb"], [bn])
                        cp(A, dst[hs, :, :], c8v(bank[hs, :]), [], [bn, dstn + str(h)])
                    yield
                chains = []
                for hv in range(2):
                    for h in range(2):
                        ia, ib = {(0, 0): (0, 1), (0, 1): (2, 3), (1, 0): (4, 5), (1, 1): (6, 7)}[(hv, h)]
                        chains.append((hv, h, psb[ia], "ps%d" % ia, psb[ib], "ps%d" % ib))
                for hv, h, bA, bAn, bB, bBn in chains:
                    hs = slice(64 * h, 64 * h + 64)
                    for cl in range(4):
                        c8 = hv * 4 + cl
                        mm(bA[hs, cl * 128:(cl + 1) * 128], BT[hs, c8 * 64:(c8 + 1) * 64], AR2_[hs, c8, :], True, True,
                           ["BT", ARn], [bAn])
                    for cl in range(4):
                        c8 = hv * 4 + cl
                        mm(bB[hs, cl * 128:(cl + 1) * 128], KT[hs, c8 * 64:(c8 + 1) * 64], AR2_[hs, c8, :], True, True,
                           ["KT", ARn], [bBn])
                for hv, h, bA, bAn, bB, bBn in chains:
                    cs = slice(hv * 4, hv * 4 + 4)
                    hs = slice(64 * h, 64 * h + 64)
                    nm = "%s%d%d" % (ss, hv, h)
                    tt(V, G1s_[hs, cs, :], bA[hs, :].rearrange("p (c n) -> p c n", c=4),
                       m1[z][hs, :].rearrange("p (c n) -> p c n", c=4), ALU.mult, ["m1_%d" % z], [bAn, "G1s" + nm])
                    tt(V, G2s_[hs, cs, :], bB[hs, :].rearrange("p (c n) -> p c n", c=4),
                       m1[z][hs, :].rearrange("p (c n) -> p c n", c=4), ALU.mult, ["m1_%d" % z], [bBn, "G2s" + nm])
                for hv, h, bA, bAn, bB, bBn in chains:
                    hs = slice(64 * h, 64 * h + 64)
                    for cl in range(4):
                        c8 = hv * 4 + cl
                        mm(bA[hs, cl * 64:(cl + 1) * 64], AR4_[hs, c8, 0, :], BT[hs, c8 * 64:(c8 + 1) * 64], True, True,
                           ["BT", ARn], [bAn])
                for hv, h, bA, bAn, bB, bBn in chains:
                    cs = slice(hv * 4, hv * 4 + 4)
                    hs = slice(64 * h, 64 * h + 64)
                    nm = "%s%d%d" % (ss, hv, h)
                    pn = "Pn%d%d" % (hv, h)
                    tt(V, Pn[hs, cs, :], bA[hs, 0:256].rearrange("p (c n) -> p c n", c=4),
                       m2[z][hs, :].rearrange("p (c n) -> p c n", c=4), ALU.mult, ["m2_%d" % z], [bAn, pn])
                    cp(A, QP_[hs, cs, 0:64], eye2[hs, :].unsqueeze(1).to_broadcast([64, 4, 64]), ["eye2"], ["QP" + nm])
                    cp(A, QP_[hs, cs, 64:128], G1s_[hs, cs, 0:64], ["G1s" + nm], ["QP" + nm])
                for k in range(6):
                    last = (k == 5)
                    wA = 64 if last else 128
                    for hv, h, bA, bAn, bB, bBn in chains:
                        hs = slice(64 * h, 64 * h + 64)
                        nm = "%s%d%d" % (ss, hv, h)
                        pn = "Pn%d%d" % (hv, h)
                        for cl in range(4):
                            c8 = hv * 4 + cl
                            mm(bA[hs, cl * 128:cl * 128 + wA], Pn[hs, c8, :], QP_[hs, c8, 0:wA], True, True,
                               [pn, "QP" + nm], [bAn])
                        if not last:
                            for cl in range(4):
                                c8 = hv * 4 + cl
                                mm(bB[hs, cl * 64:(cl + 1) * 64], QP_[hs, c8, 64:128], Pn[hs, c8, :], True, True,
                                   [pn, "QP" + nm], [bBn])
                    for hv, h, bA, bAn, bB, bBn in chains:
                        cs = slice(hv * 4, hv * 4 + 4)
                        hs = slice(64 * h, 64 * h + 64)
                        nm = "%s%d%d" % (ss, hv, h)
                        pn = "Pn%d%d" % (hv, h)
                        bA3 = bA[hs, :].rearrange("p (c n) -> p c n", c=4)
                        tt(V, QP_[hs, cs, 0:64], QP_[hs, cs, 0:64], bA3[:, :, 0:64], ALU.add, [], [bAn, "QP" + nm])
                        if not last:
                            cp(A, QP_[hs, cs, 64:128], bA3[:, :, 64:128], [], [bAn, "QP" + nm])
                            cp(A, Pn[hs, cs, :], bB[hs, 0:256].rearrange("p (c n) -> p c n", c=4), [], [bBn, pn])
                yield

            def consume(z, T, s, first):
                AR2_, AR4_, G1s_, G2s_, QP_, BHtok_, KHtok_, Wc_ = SETS[s]
                STf_, STb_, Xs_, Us_ = STATE[z]
                ss = str(s)
                zn = str(z)
                ARn = "AR" + ss
                if first:
                    memset(V, STf_, 0.0, ["STf" + zn + "0", "STf" + zn + "1"])
                    memset(V, STb_, 0.0, ["STb" + zn + "0", "STb" + zn + "1"])
                cord = range(8) if z == 0 else range(7, -1, -1)
                for c8 in cord:
                    c = T * 8 + c8
                    H = []
                    for h in range(2):
                        H.append((slice(64 * h, 64 * h + 64), zn + str(h), "%s%d%d" % (ss, c8 // 4, h),
                                  psb[2 * z + h], "ps%d" % (2 * z + h), psb[4 + 2 * z + h], "ps%d" % (4 + 2 * z + h)))
                    for hs, hn, nm, cb, cbn, yb, ybn in H:
                        mm(cb[hs, 0:64], AR4_[hs, c8, 0, :], STb_[hs, :], True, False, [ARn, "STb" + hn], [cbn])
                        mm(cb[hs, 0:64], G2s_[hs, c8, 0:64], Vtok[hs, c, :], False, True, ["G2s" + nm, "Vtok"], [cbn])
                    yield
                    for hs, hn, nm, cb, cbn, yb, ybn in H:
                        cp(A, Xs_[hs, :], cb[hs, 0:64], [], [cbn, "Xs" + hn])
                    for hs, hn, nm, cb, cbn, yb, ybn in H:
                        mm(cb[hs, 64:128], QP_[hs, c8, 0:64], Xs_[hs, :], True, True, ["QP" + nm, "Xs" + hn], [cbn])
                    yield
                    for hs, hn, nm, cb, cbn, yb, ybn in H:
                        cp(V if z == 0 else A, Us_[hs, :], cb[hs, 64:128], [], [cbn, "Us" + hn])
                    for hs, hn, nm, cb, cbn, yb, ybn in H:
                        mm(cb[hs, 128:192], BHtok_[hs, c8, :], Us_[hs, :], True, False, ["BHtok" + ss + hn[1], "Us" + hn], [cbn])
                        mm(cb[hs, 128:192], KHtok_[hs, c8, :], Vtok[hs, c, :], False, True, ["KHtok" + ss + hn[1], "Vtok"], [cbn])
                    for hs, hn, nm, cb, cbn, yb, ybn in H:
                        mm(yb[hs, c8 * 64:(c8 + 1) * 64], AR4_[hs, c8, 1, :], STb_[hs, :], True, False, [ARn, "STb" + hn], [ybn])
                        mm(yb[hs, c8 * 64:(c8 + 1) * 64], G1s_[hs, c8, 64:128], Us_[hs, :], False, False,
                           ["G1s" + nm, "Us" + hn], [ybn])
                        mm(yb[hs, c8 * 64:(c8 + 1) * 64], G2s_[hs, c8, 64:128], Vtok[hs, c, :], False, True,
                           ["G2s" + nm, "Vtok"], [ybn])
                    yield
                    for hs, hn, nm, cb, cbn, yb, ybn in H:
                        stt(STb_[hs, :], STb_[hs, :], Wc_[hs, c8:c8 + 1], cb[hs, 128:192], ALU.mult, ALU.add,
                            ["Wc" + ss], [cbn, "STb" + hn])
                    yield
                for h in range(2):
                    hs = slice(64 * h, 64 * h + 64)
                    yb, ybn = psb[4 + 2 * z + h], "ps%d" % (4 + 2 * z + h)
                    tt(V, ytok[hs, T * 8:(T + 1) * 8, :], ytok[hs, T * 8:(T + 1) * 8, :], c8v(yb[hs, :]), ALU.add,
                       [], [ybn, "ytok%d" % T])
                yield

            for T_ in range(4):
                memset(V, ytok[:, T_ * 8:(T_ + 1) * 8, :], 0.0, ["ytok%d" % T_])
            for i in range(4):
                for _ in produce(0, i, 0):
                    pass
                for _ in produce(1, 3 - i, 1):
                    pass
                gens = [consume(0, i, 0, i == 0), consume(1, 3 - i, 1, i == 0)]
                while gens:
                    for g_ in list(gens):
                        try:
                            next(g_)
                        except StopIteration:
                            gens.remove(g_)
            add(V, lambda e: e.tensor_reduce(out=stat[:, 0:32], in_=ytok, axis=AX.X, op=ALU.add), reads=["ytok0", "ytok1", "ytok2", "ytok3"], writes=["st0"])
            ts(V, stat[:, 32:64], stat[:, 0:32], -1.0 / 64, None, ALU.mult, None, ["st0"], ["st1"])
            tt(V, ytok, ytok, stat[:, 32:64].unsqueeze(2).to_broadcast([128, 32, 64]), ALU.add, ["st1"], ["ytok0", "ytok1", "ytok2", "ytok3"])
            act(tmpf3, ytok, AF.Square, ["ytok0", "ytok1", "ytok2", "ytok3"], ["tmpf"])
            add(V, lambda e: e.tensor_reduce(out=stat[:, 64:96], in_=tmpf3, axis=AX.X, op=ALU.add), reads=["tmpf"], writes=["st2"])
            act(stat[:, 96:128], stat[:, 64:96], AF.Sqrt, ["st2"], ["st3"], bias=64e-5, scale=1.0 / 64)
            add(V, lambda e: e.reciprocal(out=stat[:, 96:128], in_=stat[:, 96:128]), reads=[], writes=["st3"])
            tt(V, ynb, ytok, stat[:, 96:128].unsqueeze(2).to_broadcast([128, 32, 64]), ALU.mult, ["ytok0", "ytok1", "ytok2", "ytok3", "st3"], ["rbf"])
            for T in range(4):
                tsl = slice(T * 512, (T + 1) * 512)
                pb = pairbank()
                for h in range(2):
                    hs = slice(64 * h, 64 * h + 64)
                    bank, bn = pb[h]
                    for c8 in range(8):
                        mm(bank[hs, c8 * 64:(c8 + 1) * 64], ynb[hs, T * 8 + c8, :], identb[hs, 64 * h:64 * h + 64], True, True,
                           ["rbf", "identb"], [bn])
                    act(t1f[hs, :], bank[hs, :], AF.Identity, ["colsb"], [bn, "t1f" + str(h)],
                        bias=colsb[hs, C_GB + hp:C_GB + hp + 1], scale=colsb[hs, C_GG + hp:C_GG + hp + 1])
                tt(V, t1f, t1f, bonus[:, tsl], ALU.add, ["bonus", "t1f0", "t1f1"], ["t1f0", "t1f1"])
                tt(V, oT[:, hp, tsl], t1f, oT[:, hp, tsl], ALU.mult, ["t1f0", "t1f1"], ["oT%d" % hp])

    C.rwkv = rwkv_phase


    dma("sp", colsb[:], cols_d[0], [], ["colsb"])
    xld = [af32(0, 1024), af32(1024, 1024)]
    for t16 in range(16):
        dma("sp", xld[t16 % 2], x_in[t16 * 128:(t16 + 1) * 128, :], [], ["xld%d" % (t16 % 2)])
        store_xT(xld[t16 % 2], "xld%d" % (t16 % 2), t16)
    import os
    STOP = int(os.environ.get("KSTOP", "9"))
    for l in range(depth if STOP > 1 else 0):
        if l > 0:
            dma("sp", colsb[:], cols_d[l], [], ["colsb"])
        dma(G, vrowb[:], vrow_d[l], [], ["vrowb"])
        tt(V, c0col[:], colsb[:, C_MU0:C_MU0 + 26], colsb[:, C_MU1:C_MU1 + 26], ALU.add, ["colsb"], ["c0col"])
        ts(V, c0col[:], c0col[:], -1.0, 1.0, ALU.mult, ALU.add, [], ["c0col"])
        if C.rwkv is not None and "A" in PH:
            C.rwkv(l)
        else:
            for c in range(8):
                memset(V, oT[:, c, :], 0.0, ["oT%d" % c])
        if "B" in PH:
            pool_phase(l)
        else:
            barrier()
            for c in range(8, 14):
                memset(V, oT[:, c, :], 0.0, ["oT%d" % c])
        if C.attn is not None and "C" in PH:
            C.attn(l)
        else:
            barrier()
            for c in range(14, 16):
                memset(V, oT[:, c, :], 0.0, ["oT%d" % c])
        if dbg is not None and l == depth - 1:
            dtmp = af32(0, S)
            for c in range(16):
                cp(V, dtmp, oT[:, c, :], ["oT%d" % c], ["dtmp"])
                dma("sp", dbg_out[:, c, :], dtmp, ["dtmp"], [])
        if STOP > 2:
            final_phase(l, l == depth - 1)
    P.emit(nc)
    st.close()
    return nc


PH = "ABC"


def EXTRA_PHASES(L):
    pass


def host_prep(inp):
    f = np.float32
    g = lambda k: np.asarray(inp[k], dtype=f)
    colv = lambda v: np.ascontiguousarray(v.reshape(-1, 128).T)
    cols = []
    for l in range(DEPTH):
        parts = [colv(g("b_in")[l]), colv(g("rwkv_mu")[l, 0]), colv(g("rwkv_mu")[l, 1]),
                 colv(g("rwkv_w0")[l, 0]), colv(g("rwkv_w0")[l, 1]), colv(g("rwkv_a0")[l, 0]), colv(g("rwkv_a0")[l, 1]),
                 colv(g("rwkv_k_k")[l]), colv(g("rwkv_k_a")[l]), colv(g("rwkv_r_k")[l].reshape(-1)),
                 colv(g("rwkv_gn_g")[l]), colv(g("rwkv_gn_b")[l]), colv(g("pool_b")[l]), colv(g("pool_scale")[l])]
        cols.append(np.concatenate(parts, axis=1))
    cols = np.stack(cols)
    assert cols.shape == (DEPTH, 128, NCOLS), cols.shape
    shared = {
        "w_in": g("w_in"), "cols": cols,
        "vrow": np.ascontiguousarray(g("b_in")[:, None, 7424:8192]),
        "w_up": np.ascontiguousarray(g("rwkv_w_up").reshape(DEPTH, 128, 1024)),
        "a_up": np.ascontiguousarray(g("rwkv_a_up").reshape(DEPTH, 128, 1024)),
        "pool_w": _blockdiag(g("pool_w")), "proj_a": g("proj_a"), "proj_b": g("proj_b"), "proj_c": g("proj_c"),
        "w_out": g("w_out"),
        "lnrow": np.ascontiguousarray(np.stack([g("ln_g"), g("ln_b")], axis=1)),
    }
    shared.update(host_consts())
    return shared


def _blockdiag(pw):
    out = np.zeros((DEPTH, 768, 768), np.float32)
    for gi in range(4):
        out[:, 192 * gi:192 * gi + 192, 192 * gi:192 * gi + 192] = pw[:, gi]
    return out


def host_consts():
    f = np.float32
    ident = np.eye(128, dtype=f)
    bones = np.zeros((128, 128), f)
    bones[:64, :64] = 1
    bones[64:, 64:] = 1
    perm = np.zeros((128, 128), f)
    for m in range(128):
        c = m % 64
        k = m + 32 if c < 32 else m - 32
        perm[k, m] = 1
    eye2 = np.concatenate([np.eye(64, dtype=f), np.eye(64, dtype=f)], 0)
    cst = np.concatenate([ident, bones, perm, eye2], axis=1)
    inv = np.power(f(10000.0), -np.arange(0, 64, 2, dtype=f) / f(64))
    ang = np.arange(S, dtype=f)[:, None] * inv[None, :]
    ang = np.concatenate([ang, ang], axis=-1).astype(f)
    cosT = np.cos(ang).T.astype(f)
    sinT = np.sin(ang).T.astype(f)
    sign = np.where(np.arange(64) < 32, -1.0, 1.0).astype(f)[:, None]
    rope = np.stack([np.concatenate([cosT, cosT], 0), np.concatenate([sinT * sign, sinT * sign], 0)]).astype(f)
    b = np.arange(128)[:, None]
    a = np.arange(128)[None, :]
    mA = (b >= a).astype(f)
    mB = (a >= b).astype(f)
    mAf = mA * (b >= 64)
    mBl = mB * (b < 64)
    m16 = (np.abs(a - b) <= 64).astype(f)
    amask = np.stack([np.concatenate([mA, mB, mA, mB], 1), np.concatenate([mAf, mB, mAf, mB], 1),
                      np.concatenate([mA, mBl, mA, mBl], 1), np.concatenate([m16, m16, m16, m16], 1)]).astype(f)
    s_ = np.arange(64)[:, None]
    t_ = np.arange(64)[None, :]
    rm = []
    for z in range(2):
        if z == 0:
            strict = (s_ < t_)
            incl = (s_ <= t_)
        else:
            strict = (s_ > t_)
            incl = (s_ >= t_)
        m1 = np.concatenate([strict, incl], 1).astype(f)
        m2 = strict.T.astype(f)
        reset = np.ones((64, 512), f)
        reset[:, ::64] = 0
        row = np.concatenate([np.tile(m1, (1, 4)), np.tile(m1, (1, 4)), np.tile(m2, (1, 4)), reset], 1)
        rm.append(np.concatenate([row, row], 0))
    rmask = np.stack(rm).astype(f)
    pedge = np.ones((128, 4, 16), f)
    for gi in range(4):
        h = 1 << gi
        for e in range(8):
            t = e
            cnt = min(t + h, S - 1) - max(t - h, 0) + 1
            pedge[:, gi, e] = (2 * h + 1) / cnt
            t = S - 8 + e
            cnt = min(t + h, S - 1) - max(t - h, 0) + 1
            pedge[:, gi, 8 + e] = (2 * h + 1) / cnt
    selc = np.zeros((128, 24), f)
    for c in range(6):
        for p in range(128):
            gi = (128 * c + p) // 192
            selc[p, c * 4 + gi] = 1.0 / (2 * (1 << gi) + 1)
    return {"selc": selc, "cst": cst, "rope": rope, "amask": amask, "rmask": rmask, "pedge": pedge}


_NC_CACHE = {}


def kernel(**inputs):
    shared = host_prep(inputs)
    x = np.asarray(inputs["x"], dtype=np.float32)
    if "nc" not in _NC_CACHE:
        _NC_CACHE["nc"] = build()
    nc = _NC_CACHE["nc"]
    in_maps = []
    for c in range(8):
        m = dict(shared)
        m["x"] = np.ascontiguousarray(x[c])
        in_maps.append(m)
    res = run_bass_kernel_spmd(nc, in_maps, core_ids=list(range(8)))
    return np.stack([np.asarray(r["y"], dtype=np.float32) for r in res.results], axis=0)
```

```python
import math
import os
import numpy as np
import ml_dtypes
import concourse.bass as bass
import concourse.mybir as mybir
from concourse.bass_utils import run_bass_kernel_spmd

F32 = mybir.dt.float32
BF16 = mybir.dt.bfloat16
AF = mybir.ActivationFunctionType
ALU = mybir.AluOpType
AX = mybir.AxisListType

ENGS = ("pe", "act", "dve", "pool", "sp")
DMA_POOL = 16


class _Buf:
    __slots__ = ("last_w", "readers", "dma_readers")

    def __init__(self):
        self.last_w = None
        self.readers = {}
        self.dma_readers = []


class _Op:
    __slots__ = ("eng", "idx", "gid", "fn", "deps", "is_dma", "signal", "sem", "val", "waits",
                 "know", "dma_n", "pre_wait")

    def __init__(self, eng, idx, gid, fn, is_dma):
        self.eng = eng
        self.idx = idx
        self.gid = gid
        self.fn = fn
        self.is_dma = is_dma
        self.deps = []
        self.signal = False
        self.sem = None
        self.val = None
        self.waits = []
        self.know = None
        self.dma_n = None
        self.pre_wait = None


class Prog:
    def __init__(self):
        self.ops = {e: [] for e in ENGS}
        self.all = []
        self.bufs = {}
        self.n_dma = {e: 0 for e in ENGS}

    def _buf(self, name):
        b = self.bufs.get(name)
        if b is None:
            b = _Buf()
            self.bufs[name] = b
        return b

    def add(self, eng, fn, reads=(), writes=(), dma=False):
        op = _Op(eng, len(self.ops[eng]), len(self.all), fn, dma)
        deps = {}
        for r in reads:
            b = self._buf(r)
            if b.last_w is not None:
                deps[b.last_w.gid] = b.last_w
        for w in writes:
            b = self._buf(w)
            if b.last_w is not None:
                deps[b.last_w.gid] = b.last_w
            for d in b.readers.values():
                deps[d.gid] = d
            for d in b.dma_readers:
                deps[d.gid] = d
        op.deps = [deps[k] for k in sorted(deps)]
        for r in reads:
            b = self._buf(r)
            if dma:
                b.dma_readers.append(op)
            else:
                b.readers[eng] = op
        for w in writes:
            b = self._buf(w)
            b.last_w = op
            b.readers = {}
            b.dma_readers = []
        if dma:
            op.dma_n = self.n_dma[eng]
            self.n_dma[eng] += 1
        self.ops[eng].append(op)
        self.all.append(op)
        return op

    def resolve(self):
        know = {e: {f: -1 for f in ENGS} for e in ENGS}
        know_dma = {e: set() for e in ENGS}
        sig_count = {e: 0 for e in ENGS}
        for op in self.all:
            E = op.eng
            kn = know[E]
            for d in op.deps:
                if d.is_dma:
                    if d.gid in know_dma[E]:
                        continue
                    know_dma[E].add(d.gid)
                    op.waits.append(d)
                    for f, v in d.know.items():
                        if v > kn[f]:
                            kn[f] = v
                    continue
                F = d.eng
                if F == E:
                    if E == "pe" or op.idx - d.idx > 2:
                        continue
                    if kn[F] >= d.idx:
                        continue
                elif kn[F] >= d.idx:
                    continue
                d.signal = True
                op.waits.append(d)
                kn[F] = max(kn[F], d.idx)
                for f, v in d.know.items():
                    if f != E and v > kn[f]:
                        kn[f] = v
            snap = dict(kn)
            if not op.is_dma:
                snap[E] = op.idx
            op.know = snap
        for e in ENGS:
            c = 0
            for op in self.ops[e]:
                if op.is_dma:
                    continue
                if op.signal:
                    c += 1
                    op.val = c

    def emit(self, nc):
        self.resolve()
        import contextlib
        with contextlib.ExitStack() as st:
            esem = {e: st.enter_context(nc.semaphore("s_" + e)) for e in ENGS}
            dsem = {e: [st.enter_context(nc.semaphore("d_%s%d" % (e, i))) for i in range(DMA_POOL)]
                    for e in ENGS if self.n_dma[e] > 0}
            block = st.enter_context(nc.Block())

            def wait_for(engine, d):
                if d.is_dma:
                    engine.wait_ge(dsem[d.eng][d.dma_n % DMA_POOL], 16 * (d.dma_n // DMA_POOL + 1))
                else:
                    engine.wait_ge(esem[d.eng], d.val)

            def run(engine, e):
                ops = self.ops[e]
                for op in ops:
                    for d in op.waits:
                        wait_for(engine, d)
                    if op.is_dma:
                        n = op.dma_n
                        if n >= DMA_POOL:
                            engine.wait_ge(dsem[e][n % DMA_POOL], 16 * (n // DMA_POOL))
                        ins = op.fn(engine)
                        ins.then_inc(dsem[e][n % DMA_POOL], 16)
                    else:
                        ins = op.fn(engine)
                        if op.signal:
                            ins.then_inc(esem[e], 1)
                nd = self.n_dma[e]
                for i in range(min(nd, DMA_POOL)):
                    n = nd - 1 - i
                    engine.wait_ge(dsem[e][n % DMA_POOL], 16 * (n // DMA_POOL + 1))

            @block.tensor
            def _(eng):
                run(eng, "pe")

            @block.scalar
            def _(eng):
                run(eng, "act")

            @block.vector
            def _(eng):
                run(eng, "dve")

            @block.gpsimd
            def _(eng):
                run(eng, "pool")

            @block.sync
            def _(eng):
                run(eng, "sp")


S = 2048
D = 1024
NIN = 11520
DEPTH = 4
PADX = 256
XW = S + 2 * PADX
ALPHA = (2 * DEPTH) ** 0.25
CDEC = math.exp(-0.5)
C_BIN = 0
C_MU0 = 90
C_MU1 = 116
C_W0 = 142
C_A0 = 158
C_KK = 174
C_KA = 182
C_RK = 190
C_GG = 198
C_GB = 206
C_PB = 214
C_PS = 220
NCOLS = 226


class Ctx:
    pass


def build(depth=DEPTH, dbg=None):
    nc = bass.Bass("TRN2", target_bir_lowering=False)
    P = Prog()
    dt_in = lambda name, shape: nc.dram_tensor(name, shape, F32, kind="ExternalInput").ap()
    x_in = dt_in("x", [S, D])
    w_in = dt_in("w_in", [DEPTH, D, NIN])
    cols_d = dt_in("cols", [DEPTH, 128, NCOLS])
    vrow_d = dt_in("vrow", [DEPTH, 1, 768])
    wup_d = dt_in("w_up", [DEPTH, 128, 1024])
    aup_d = dt_in("a_up", [DEPTH, 128, 1024])
    poolw_d = dt_in("pool_w", [DEPTH, 768, 768])
    proja_d = dt_in("proj_a", [DEPTH, 1024, 1024])
    projb_d = dt_in("proj_b", [DEPTH, 768, 1024])
    projc_d = dt_in("proj_c", [DEPTH, 256, 1024])
    wout_d = dt_in("w_out", [DEPTH, 1024, 1024])
    lnrow_d = dt_in("lnrow", [DEPTH, 2, 1024])
    cst_d = dt_in("cst", [128, 128 * 3 + 64])
    rope_d = dt_in("rope", [2, 128, S])
    amask_d = dt_in("amask", [4, 128, 512])
    rmask_d = dt_in("rmask", [2, 128, 512 + 512 + 256 + 512])
    pedge_d = dt_in("pedge", [128, 4, 16])
    y_out = nc.dram_tensor("y", [S, D], F32, kind="ExternalOutput").ap()
    xres = nc.dram_tensor("xres", [S, D], F32, kind="Internal").ap()
    dbg_out = None
    if dbg is not None:
        dbg_out = nc.dram_tensor("dbg", [128, 16, S], F32, kind="ExternalOutput").ap()

    import contextlib
    st = contextlib.ExitStack()
    sb = lambda name, shape, dt: st.enter_context(nc.sbuf_tensor(name, shape, dt))
    xT = sb("xT", [128, 8, XW], BF16)
    oT = sb("oT", [128, 16, S], BF16)
    wb = [sb("wb%d" % i, [128, 8, 512], BF16) for i in range(2)]
    ARENA = 16384
    arena = sb("arena", [128, ARENA], F32)
    colsb = sb("colsb", [128, NCOLS], F32)
    c0col = sb("c0col", [128, 26], F32)
    identb = sb("identb", [128, 128], BF16)
    permb = sb("permb", [128, 128], BF16)
    onesb = sb("onesb", [128, 128], BF16)
    eye2 = sb("eye2", [128, 64], BF16)
    bones = sb("bones", [128, 128], F32)
    vrowb = sb("vrowb", [1, 768], BF16)
    selcol = sb("selcol", [128, 24], F32)
    mixf = sb("mixf", [128, S], F32)
    selc_d = dt_in("selc", [128, 24])
    psb = [st.enter_context(nc.psum_tensor("psb%d" % i, [128, 512], F32)) for i in range(8)]

    C = Ctx()
    C.bank_i = 0

    def nextbank():
        i = C.bank_i
        C.bank_i = (i + 1) % 4
        return psb[i], "ps%d" % i

    def af32(off, n):
        return arena[:, off:off + n]

    def abf(off, n):
        return arena[:, off:off + n // 2].bitcast(BF16)

    def o8f32(off, n):
        return oT[:, 8:16, :].rearrange("p a b -> p (a b)").bitcast(F32)[:, off:off + n]

    def o8bf(off, n):
        return oT[:, 8:16, :].rearrange("p a b -> p (a b)")[:, off:off + n]

    add = P.add
    V = "dve"
    A = "act"
    G = "pool"

    def mm(out, lhsT, rhs, start, stop, rd, wr):
        add("pe", lambda e: e.matmul(out, lhsT, rhs, start=start, stop=stop), reads=rd, writes=wr)

    def act(out, in_, func, rd, wr, bias=0.0, scale=1.0):
        add(A, lambda e: e.activation(out=out, in_=in_, func=func, bias=bias, scale=scale), reads=rd, writes=wr)

    def tt(eng, out, in0, in1, op, rd, wr):
        eng = V if eng == G else eng
        add(eng, lambda e: e.tensor_tensor(out=out, in0=in0, in1=in1, op=op), reads=rd, writes=wr)

    def ts(eng, out, in0, s1, s2, op0, op1, rd, wr):
        eng = V if eng == G else eng
        if s2 is None:
            add(eng, lambda e: e.tensor_scalar(out=out, in0=in0, scalar1=s1, scalar2=None, op0=op0), reads=rd, writes=wr)
        else:
            add(eng, lambda e: e.tensor_scalar(out=out, in0=in0, scalar1=s1, scalar2=s2, op0=op0, op1=op1), reads=rd, writes=wr)

    def stt(out, in0, scalar, in1, op0, op1, rd, wr):
        add(V, lambda e: e.scalar_tensor_tensor(out=out, in0=in0, scalar=scalar, in1=in1, op0=op0, op1=op1),
            reads=rd, writes=wr)

    def cp(eng, out, in_, rd, wr):
        eng = V if eng == G else eng
        if eng == A:
            add(eng, lambda e: e.copy(out=out, in_=in_), reads=rd, writes=wr)
        else:
            add(eng, lambda e: e.tensor_copy(out=out, in_=in_), reads=rd, writes=wr)

    def dma(q, out, in_, rd, wr):
        add(q, lambda e: e.dma_start(out=out, in_=in_), reads=rd, writes=wr, dma=True)

    def memset(eng, ap, val, wr):
        add(eng, lambda e: e.memset(ap, val), writes=wr)

    bscr = sb("bscr", [128, 8], F32)
    epsc = sb("epsc", [128, 1], F32)

    def barrier():
        names = [n for n in P.bufs.keys() if not n.startswith("ps")] + ["bscr"]
        mm(psb[7][:, 0:8], identb[:, 0:128], identb[:, 0:8], True, True, [], names + ["ps7"])
        act(bscr[:, 0:1], bscr[:, 1:2], AF.Copy, [], names)
        memset(V, bscr[:, 2:3], 0.0, names)
        dma("sp", bscr[0:1, 3:4], cst_d[0:1, 0:1], [], names)
        dma(G, bscr[0:1, 4:5], cst_d[0:1, 0:1], [], names)

    memset(V, bscr[:], 0.0, ["bscr"])
    memset(V, epsc[:], 1e-12, ["epsc"])
    dma(G, identb[:], cst_d[:, 0:128], [], ["identb"])
    dma("sp", bones[:], cst_d[:, 128:256], [], ["bones"])
    dma(G, permb[:], cst_d[:, 256:384], [], ["permb"])
    dma("sp", selcol[:], selc_d, [], ["selcol"])
    dma(G, eye2[:], cst_d[:, 384:448], [], ["eye2"])
    memset(V, onesb[:], 1.0, ["onesb"])
    memset(V, xT[:, :, 0:PADX], 0.0, ["xT"])
    memset(V, xT[:, :, PADX + S:XW], 0.0, ["xT"])

    C.wres = [None, None]
    C.wlast = 0

    def load_w(key, src3):
        for i in range(2):
            if C.wres[i] == key:
                C.wlast = i
                return wb[i], "wb%d" % i
        i = 1 - C.wlast
        C.wres[i] = key
        C.wlast = i
        kc, ncol = src3.shape[1], src3.shape[2]
        dma(G, wb[i][:, 0:kc, 0:ncol], src3, [], ["wb%d" % i])
        return wb[i], "wb%d" % i

    def win_src(l, col0, ncol):
        return w_in[l].rearrange("(k p) n -> p k n", p=128)[:, :, col0:col0 + ncol]

    def inproj(l, cg, evac, wkey=None):
        blk = cg // 4
        ncol = min(512, NIN - blk * 512)
        w, wn = load_w(("win", l, blk), win_src(l, blk * 512, ncol))
        c0 = (cg % 4) * 128
        for t4 in range(4):
            bank, bn = nextbank()
            for k in range(8):
                mm(bank[:], w[:, k, c0:c0 + 128], xT[:, k, PADX + t4 * 512:PADX + (t4 + 1) * 512],
                   k == 0, k == 7, ["xT", wn], [bn])
            evac(t4, bank, bn)

    def col(ci):
        return colsb[:, ci:ci + 1]

    xstage = sb("xstage", [128, 1024], BF16)

    def store_xT(src_f32, srcname, t16):
        cp(A, xstage[:], src_f32, [srcname], ["xstage"])
        bank, bn = nextbank()
        bb = bank[:].bitcast(BF16)
        for k in range(8):
            add("pe", lambda e, k=k: e.transpose(bb[:, k * 128:(k + 1) * 128], xstage[:, k * 128:(k + 1) * 128], identb[:]),
                reads=["xstage", "identb"], writes=[bn])
        cp(V, xT[:, :, PADX + t16 * 128:PADX + (t16 + 1) * 128], bb.rearrange("p (k t) -> p k t", k=8), [], [bn, "xT"])

    def final_phase(l, last):
        barrier()
        mergedT = abf(0, 8 * S).rearrange("p (k t) -> p k t", k=8)
        sig = abf(8192, 512)
        tmpf = af32(8448, 512)
        lng = af32(9216, 1024)
        lnb = af32(10240, 1024)
        xt_ = [af32(11264, 1024), af32(12288, 1024)]
        yt_ = [af32(13312, 1024), af32(14336, 1024)]
        stat = af32(15360, 8)
        dma("sp", lng, lnrow_d[l, 0:1, :].to_broadcast([128, 1024]), [], ["lng"])
        dma("sp", lnb, lnrow_d[l, 1:2, :].to_broadcast([128, 1024]), [], ["lnb"])
        if STOP == 4:
            return
        branches = [(proja_d, 8, 0, 66), (projb_d, 6, 8, 74), (projc_d, 2, 14, 82)]
        for bi, (pd, kc, o0, g0) in enumerate(branches):
            for eb in range(2):
                for ec in range(eb * 4, eb * 4 + 4):
                    for t4 in range(4):
                        tsl = slice(t4 * 512, (t4 + 1) * 512)
                        pw, pwn = load_w(("proj", l, bi, eb), pd[l].rearrange("(k p) n -> p k n", p=128)[:, :, eb * 512:(eb + 1) * 512])
                        b1, b1n = nextbank()
                        for k in range(kc):
                            mm(b1[:], pw[:, k, (ec % 4) * 128:(ec % 4 + 1) * 128], oT[:, o0 + k, tsl], k == 0, k == kc - 1,
                               ["oT%d" % (o0 + k), pwn], [b1n])
                        gw, gwn = load_w(("gate", l, bi, eb), win_src(l, (g0 + eb * 4) * 128, 512))
                        b2, b2n = nextbank()
                        for k in range(8):
                            mm(b2[:], gw[:, k, (ec % 4) * 128:(ec % 4 + 1) * 128], xT[:, k, PADX + t4 * 512:PADX + (t4 + 1) * 512],
                               k == 0, k == 7, ["xT", gwn], [b2n])
                        act(sig, b2[:], AF.Sigmoid, ["colsb"], [b2n, "sig"], bias=col(C_BIN + g0 + ec))
                        if bi == 0:
                            tt(V, mergedT[:, ec, tsl], b1[:], sig, ALU.mult, ["sig"], [b1n, "mg%d" % ec])
                        else:
                            tt(V, tmpf, b1[:], sig, ALU.mult, ["sig"], [b1n, "tmpf"])
                            tt(G, mergedT[:, ec, tsl], mergedT[:, ec, tsl], tmpf, ALU.add, ["tmpf"], ["mg%d" % ec])
        if STOP == 3:
            return
        wo = []
        for fh in range(2):
            wo.append(load_w(("wout", l, fh), wout_d[l].rearrange("(k p) n -> p k n", p=128)[:, :, fh * 512:(fh + 1) * 512]))
        xsrc = x_in if l == 0 else xres
        dst = y_out if last else xres
        def ld_x(t):
            dma("sp", xt_[t % 2], xsrc[t * 128:(t + 1) * 128, :], ["xres%d" % t] if l > 0 else [], ["xt%d" % (t % 2)])

        ld_x(0)
        for t16 in range(16):
            xt = xt_[t16 % 2]
            yt = yt_[t16 % 2]
            par = t16 % 2
            st_ = af32(15360 + 8 * par, 8)
            xn, yn = "xt%d" % par, "yt%d" % par
            sn = lambda k, par=par: "stat%d_%d" % (k, par)
            for fh in range(2):
                w, wn = wo[fh]
                bank, bn = nextbank()
                for k in range(8):
                    mm(bank[:], mergedT[:, k, t16 * 128:(t16 + 1) * 128], w[:, k, :], k == 0, k == 7,
                       ["mg%d" % k, wn], [bn])
                stt(yt[:, fh * 512:(fh + 1) * 512], xt[:, fh * 512:(fh + 1) * 512], ALPHA, bank[:], ALU.mult, ALU.add,
                    [xn], [bn, yn])
            add(V, lambda e, yt=yt, st_=st_: e.tensor_reduce(out=st_[:, 0:1], in_=yt, axis=AX.X, op=ALU.add), reads=[yn], writes=[sn(0)])
            ts(V, st_[:, 1:2], st_[:, 0:1], -1.0 / D, None, ALU.mult, None, [sn(0)], [sn(1)])
            ts(V, yt, yt, st_[:, 1:2], None, ALU.add, None, [sn(1)], [yn])
            add(A, lambda e, yt=yt, xt=xt, st_=st_: e.activation(out=xt, in_=yt, func=AF.Square, accum_out=st_[:, 2:3]),
                reads=[yn], writes=[xn, sn(2)])
            if t16 + 1 < 16:
                ld_x(t16 + 1)
            act(st_[:, 3:4], st_[:, 2:3], AF.Sqrt, [sn(2)], [sn(3)], bias=1e-5, scale=1.0 / D)
            add(V, lambda e, st_=st_: e.reciprocal(out=st_[:, 4:5], in_=st_[:, 3:4]), reads=[sn(3)], writes=[sn(4)])
            stt(yt, yt, st_[:, 4:5], lng, ALU.mult, ALU.mult, [sn(4), "lng"], [yn])
            tt(V, yt, yt, lnb, ALU.add, ["lnb"], [yn])
            dma("sp", dst[t16 * 128:(t16 + 1) * 128, :], yt, [yn], ["xres%d" % t16])
            if not last:
                store_xT(yt, yn, t16)

    def pool_phase(l):
        barrier()
        W = S + 32
        pbuf = af32(0, W)
        a_ = [af32(2080, W), af32(4160, W)]
        mixed = abf(6240, 6 * S).rearrange("p (c t) -> p c t", c=6)
        wgt = abf(12384, 6 * 768).rearrange("p (a b) -> p a b", a=6)
        sacc = af32(14688, 0) if False else None
        pe_t = af32(14688, 64).rearrange("p (g e) -> p g e", g=4)
        t1 = af32(14752, 512)
        memset(V, pbuf[:, 0:16], 0.0, ["pbuf"])
        memset(V, pbuf[:, W - 16:W], 0.0, ["pbuf"])
        dma("sp", pe_t, pedge_d, [], ["pe_t"])
        dma(G, wgt, poolw_d[l].rearrange("(k p) n -> p k n", p=128), [], ["wgt"])
        for c in range(6):
            inproj(l, 40 + c, lambda t4, bank, bn, c=c: act(oT[:, 8 + c, t4 * 512:(t4 + 1) * 512], bank[:], AF.Silu,
                                                             ["colsb"], [bn, "oT%d" % (8 + c)], bias=col(C_BIN + 40 + c)))
        for c in range(6):
            inproj(l, 34 + c, lambda t4, bank, bn, c=c: act(pbuf[:, 16 + t4 * 512:16 + (t4 + 1) * 512], bank[:], AF.Identity,
                                                             ["colsb"], [bn, "pbuf"], bias=col(C_BIN + 34 + c)))
            gs = sorted(set((2 * c + hf) // 3 for hf in range(2)))
            first = True
            for g in gs:
                h = 1 << g
                kk = g + 1
                src, srcn = pbuf, "pbuf"
                for j in range(kk):
                    sh = 1 << j
                    dstt = a_[j % 2]
                    n = W - (2 << j) + 1
                    tt(V, dstt[:, 0:n], src[:, 0:n], src[:, sh:sh + n], ALU.add, [srcn], ["a%d" % (j % 2)])
                    src, srcn = dstt, "a%d" % (j % 2)
                sfin = a_[kk % 2]
                sn = "a%d" % (kk % 2)
                tt(V, sfin[:, 0:S], src[:, 16 - h:16 - h + S], pbuf[:, 16 + h:16 + h + S], ALU.add, [srcn, "pbuf"], [sn])
                tt(V, sfin[:, 0:8], sfin[:, 0:8], pe_t[:, g, 0:8], ALU.mult, ["pe_t"], [sn])
                tt(V, sfin[:, S - 8:S], sfin[:, S - 8:S], pe_t[:, g, 8:16], ALU.mult, ["pe_t"], [sn])
                selw = selcol[:, c * 4 + g:c * 4 + g + 1]
                if first:
                    stt(mixf[:, :], sfin[:, 0:S], selw, pbuf[:, 16:16 + S], ALU.mult, ALU.subtract, [sn, "pbuf", "selcol"], ["mixf"])
                else:
                    stt(mixf[:, :], sfin[:, 0:S], selw, mixf[:, :], ALU.mult, ALU.add, [sn, "selcol"], ["mixf"])
                first = False
            cp(A, mixed[:, c, :], mixf[:, :], ["mixf"], ["mixed"])
        for oc in range(6):
            ics = [ic for ic in range(6) if any((2 * ic + a) // 3 == (2 * oc + b) // 3 for a in range(2) for b in range(2))]
            for t4 in range(4):
                tsl = slice(t4 * 512, (t4 + 1) * 512)
                bank, bn = nextbank()
                for n_, ic in enumerate(ics):
                    mm(bank[:], wgt[:, ic, oc * 128:(oc + 1) * 128], mixed[:, ic, tsl], n_ == 0, n_ == len(ics) - 1,
                       ["wgt", "mixed"], [bn])
                ts(V, t1, bank[:], col(C_PB + oc), col(C_PS + oc), ALU.add, ALU.mult, ["colsb"], [bn, "t1"])
                tt(V, oT[:, 8 + oc, tsl], t1, oT[:, 8 + oc, tsl], ALU.mult, ["t1"], ["oT%d" % (8 + oc)])

    C.attn = None
    C.rwkv = None
    def attn_phase(l):
        barrier()
        Qr = abf(0, XW)
        Kr = abf(1280, XW)
        qraw = abf(2560, S)
        ropec = af32(3584, S)
        ropes = af32(5632, S)
        t1 = af32(7680, 512)
        t2 = af32(8192, 512)
        accn = af32(8704, S)
        accd = af32(10752, S)
        Vt = abf(12800, 20 * 128).rearrange("p (a b) -> p a b", a=20)
        pT = [abf(14080, 512), abf(14336, 512)]
        msk = abf(14592, 4 * 512).rearrange("p (a b) -> p a b", a=4)
        dma("sp", ropec, rope_d[0], [], ["ropec"])
        dma("sp", ropes, rope_d[1], [], ["ropes"])
        for a_ in range(4):
            dma(G, msk[:, a_, :], amask_d[a_], [], ["msk"])
        for buf, nm in ((Qr, "Qr"), (Kr, "Kr")):
            memset(V, buf[:, 0:PADX], 0.0, [nm])
            memset(V, buf[:, PADX + S:XW], 0.0, [nm])
        cnt = [0]
        SUB = int(os.environ.get("ATT_SUB", "9"))
        if SUB == 1:
            return
        for pp in range(2):
            for g in range(3):
                d = (1, 4, 16)[g]
                for cg, dst, nm in ((46 + 2 * g + pp, Qr, "Qr"), (52 + 2 * g + pp, Kr, "Kr")):
                    inproj(l, cg, lambda t4, bank, bn, cg=cg: act(qraw[:, t4 * 512:(t4 + 1) * 512], bank[:], AF.Identity,
                                                                  ["colsb"], [bn, "qraw"], bias=col(C_BIN + cg)))
                    for t4 in range(4):
                        tsl = slice(t4 * 512, (t4 + 1) * 512)
                        bank, bn = nextbank()
                        mm(bank[:], permb[:], qraw[:, tsl], True, True, ["permb", "qraw"], [bn])
                        tt(V, t1, bank[:], ropes[:, tsl], ALU.mult, ["ropes"], [bn, "t1"])
                        tt(V, t2, qraw[:, tsl], ropec[:, tsl], ALU.mult, ["qraw", "ropec"], ["t2"])
                        tt(V, dst[:, PADX + t4 * 512:PADX + (t4 + 1) * 512], t1, t2, ALU.add, ["t1", "t2"], [nm])
                if SUB == 2:
                    return
                wv, wvn = load_w(("wv", l, g, pp), win_src(l, (58 + 2 * g + pp) * 128, 128))
                if d == 1:
                    tsls = [slice(PADX + 128 * m - 64, PADX + 128 * m + 64) for m in range(17)]
                elif d == 4:
                    tsls = []
                    for r in range(4):
                        for m in range(5):
                            s0 = PADX + r + 4 * (128 * m - 64)
                            tsls.append(slice(s0, s0 + 509, 4))
                else:
                    tsls = [slice(PADX + r, PADX + r + 2033, 16) for r in range(16)]
                for j0 in range(0, len(tsls), 4):
                    grp = tsls[j0:j0 + 4]
                    bank, bn = nextbank()
                    for j, sl in enumerate(grp):
                        for k in range(8):
                            mm(bank[:, j * 128:(j + 1) * 128], xT[:, k, sl], wv[:, k, 0:128], k == 0, False, ["xT", wvn], [bn])
                        mm(bank[:, j * 128:(j + 1) * 128], onesb[0:1, 0:128], vrowb[0:1, (2 * g + pp) * 128:(2 * g + pp + 1) * 128],
                           False, True, ["onesb", "vrowb"], [bn])
                    n = len(grp)
                    cp(A, Vt[:, j0:j0 + n, :], bank[:, 0:n * 128].rearrange("p (a b) -> p a b", a=n), [], [bn, "Vt"])
                if SUB == 3:
                    return
                for sbk in range(4):
                    for j in range(4):
                        if d == 1:
                            m = 4 * sbk + j
                            qsl = slice(PADX + 128 * m, PADX + 128 * m + 128)
                            chunks = [(slice(PADX + 128 * m - 64, PADX + 128 * m + 64), m),
                                      (slice(PADX + 128 * m + 64, PADX + 128 * m + 192), m + 1)]
                            mi = 1 if m == 0 else (2 if m == 15 else 0)
                        elif d == 4:
                            r, m = sbk, j
                            q0 = PADX + r + 512 * m
                            qsl = slice(q0, q0 + 509, 4)
                            k0 = PADX + r + 4 * (128 * m - 64)
                            chunks = [(slice(k0, k0 + 509, 4), r * 5 + m), (slice(k0 + 512, k0 + 512 + 509, 4), r * 5 + m + 1)]
                            mi = 1 if m == 0 else (2 if m == 3 else 0)
                        else:
                            r = 4 * sbk + j
                            qsl = slice(PADX + r, PADX + r + 2033, 16)
                            chunks = [(qsl, r)]
                            mi = 3
                        nch = len(chunks)
                        wd = nch * 128
                        for h in range(2):
                            hp_ = slice(64 * h, 64 * h + 64)
                            si = h + 2 * (cnt[0] % 2)
                            sbank, sbn = psb[si], "ps%d" % si
                            par = cnt[0] % 2
                            pTb, pTn = pT[h][:, par * 256:par * 256 + 256], "pT%d_%d" % (h, par)
                            for ci, (ks, vt) in enumerate(chunks):
                                mm(sbank[:, ci * 128:(ci + 1) * 128], Kr[hp_, ks], Qr[hp_, qsl], True, True, ["Kr", "Qr"], [sbn])
                            act(pTb[:, 0:wd], sbank[:, 0:wd], AF.Exp, [], [sbn, pTn], scale=0.125)
                            tt(V, pTb[:, 0:wd], pTb[:, 0:wd], msk[:, mi, 0:wd], ALU.mult, ["msk"], [pTn])
                            for ci, (ks, vt) in enumerate(chunks):
                                mm(psb[6][hp_, j * 128:(j + 1) * 128], Vt[:, vt, 64 * h:64 * h + 64], pTb[:, ci * 128:(ci + 1) * 128],
                                   ci == 0, ci == nch - 1, ["Vt", pTn], ["ps6"])
                            for ci, (ks, vt) in enumerate(chunks):
                                mm(psb[7][hp_, j * 128:(j + 1) * 128], onesb[:, 0:64], pTb[:, ci * 128:(ci + 1) * 128],
                                   ci == 0, ci == nch - 1, ["onesb", pTn], ["ps7"])
                        cnt[0] += 1
                    if d == 1:
                        vn, vd = accn[:, 512 * sbk:512 * sbk + 512], accd[:, 512 * sbk:512 * sbk + 512]
                        bn_, bd_ = psb[6][:, :], psb[7][:, :]
                    elif d == 4:
                        vn, vd = accn[:, sbk:S:4], accd[:, sbk:S:4]
                        bn_, bd_ = psb[6][:, :], psb[7][:, :]
                    else:
                        vn = accn.rearrange("p (i r) -> p r i", r=16)[:, 4 * sbk:4 * sbk + 4, :]
                        vd = accd.rearrange("p (i r) -> p r i", r=16)[:, 4 * sbk:4 * sbk + 4, :]
                        bn_ = psb[6][:, :].rearrange("p (r i) -> p r i", r=4)
                        bd_ = psb[7][:, :].rearrange("p (r i) -> p r i", r=4)
                    if g == 0:
                        cp(V, vn, bn_, [], ["ps6", "accn"])
                        cp(A, vd, bd_, [], ["ps7", "accd"])
                    else:
                        tt(V, vn, vn, bn_, ALU.add, [], ["ps6", "accn"])
                        tt(V, vd, vd, bd_, ALU.add, [], ["ps7", "accd"])
                if os.environ.get("ATT_STOP") == str(g + 1):
                    return
            oc = 14 + pp
            inproj(l, 64 + pp, lambda t4, bank, bn, oc=oc, pp=pp: act(oT[:, oc, t4 * 512:(t4 + 1) * 512], bank[:], AF.Silu,
                                                                        ["colsb"], [bn, "oT%d" % oc], bias=col(C_BIN + 64 + pp)))
            act(accd, accd, AF.Ln, [], ["accd"])
            act(accd, accd, AF.Exp, [], ["accd"], scale=-1.0)
            tt(V, accn, accn, accd, ALU.mult, ["accd"], ["accn"])
            tt(V, oT[:, oc, :], accn, oT[:, oc, :], ALU.mult, ["accn"], ["oT%d" % oc])

    C.attn = attn_phase

    def rwkv_phase(l):
        barrier()
        lw = abf(0, S)
        la = abf(1024, S)
        wup = abf(2048, 1024)
        aup = abf(2560, 1024)
        hbuf = af32(3072, 2050)
        tmpB = af32(3072, S)
        tmpf = af32(5124, S)
        rbf = abf(7172, S)
        kbf = abf(8196, S)
        vbf = abf(9220, S)
        kkbf = abf(10244, S)
        bonus = abf(11268, S)
        ytok = af32(12292, S).rearrange("p (c i) -> p c i", c=32)
        Vtok = abf(14340, S).rearrange("p (c i) -> p c i", c=32)
        reset = af32(15364, 512)
        STf = af32(15876, 64)
        STb = abf(15940, 64)
        Xs = abf(15972, 64)
        Us = abf(16004, 64)
        Wc = af32(16036, 8)
        totc = af32(16044, 8)
        stat = af32(16052, 128)
        ynb = rbf.rearrange("p (c i) -> p c i", c=32)
        tmpf3 = tmpf.rearrange("p (c i) -> p c i", c=32)
        sg = o8f32(0, 512)
        aa = o8f32(512, 512)
        Gc = o8f32(1024, 512)
        tmpG = o8f32(1536, 512)
        E = o8f32(2048, 512)
        bb = o8f32(2560, 512)
        kd = o8f32(3072, 512)
        AR2 = o8bf(2 * 3584, 1024).rearrange("p (c n) -> p c n", c=8)
        AR4 = o8bf(2 * 3584, 1024).rearrange("p (c a j) -> p c a j", c=8, a=2)
        BT = o8bf(2 * 4096, 512)
        KT = o8bf(2 * 4352, 512)
        BHT = o8bf(2 * 4608, 512)
        KHT = o8bf(2 * 4864, 512)
        BHtok = o8bf(2 * 5120, 512).rearrange("p (c j) -> p c j", c=8)
        KHtok = o8bf(2 * 5376, 512).rearrange("p (c j) -> p c j", c=8)
        G1s = o8bf(2 * 5632, 1024).rearrange("p (c n) -> p c n", c=8)
        G2s = o8bf(2 * 6144, 1024).rearrange("p (c n) -> p c n", c=8)
        QP = o8bf(2 * 6656, 1024).rearrange("p (c n) -> p c n", c=8)
        Pn = o8bf(2 * 7168, 512).rearrange("p (c n) -> p c n", c=8)
        m1 = [o8bf(2 * 7424, 512), o8bf(2 * 7680, 512)]
        m2 = [o8bf(2 * 7936, 256), o8bf(2 * 8064, 256)]
        t1f = E

        mfb = mixf[:, :].bitcast(BF16)
        Wc1 = af32(16180, 8)
        SETS = [
            (AR2, AR4, G1s, G2s, QP, BHtok, KHtok, Wc),
            (mfb[:, 0:1024].rearrange("p (c n) -> p c n", c=8), mfb[:, 0:1024].rearrange("p (c a j) -> p c a j", c=8, a=2),
             mfb[:, 1024:2048].rearrange("p (c n) -> p c n", c=8), mfb[:, 2048:3072].rearrange("p (c n) -> p c n", c=8),
             mfb[:, 3072:4096].rearrange("p (c n) -> p c n", c=8),
             xstage[:, 0:512].rearrange("p (c j) -> p c j", c=8), xstage[:, 512:1024].rearrange("p (c j) -> p c j", c=8), Wc1),
        ]

        STATE = [(STf, STb, Xs, Us), (af32(16188, 64), abf(16252, 64), abf(16284, 64), abf(16316, 64))]

        def c8v(ap):
            return ap.rearrange("p (c j) -> p c j", c=8)

        dma(G, wup, wup_d[l], [], ["wup"])
        dma(G, aup, aup_d[l], [], ["aup"])
        dma("sp", reset, rmask_d[0][:, 1280:1792], [], ["reset"])
        for z in range(2):
            dma(G, m1[z], rmask_d[z][:, 0:512], [], ["m1_%d" % z])
            dma(G, m2[z], rmask_d[z][:, 1024:1280], [], ["m2_%d" % z])
        memset(V, hbuf[:, 0:1], 0.0, ["hbuf"])
        memset(V, hbuf[:, 2049:2050], 0.0, ["hbuf"])

        def shifted(cg, dst, dstname):
            inproj(l, cg, lambda t4, bank, bn: act(hbuf[:, 1 + t4 * 512:1 + (t4 + 1) * 512], bank[:], AF.Identity,
                                                   ["colsb"], [bn, "hbuf"], bias=col(C_BIN + cg)))
            act(tmpf, hbuf[:, 1:2049], AF.Identity, ["hbuf", "c0col"], ["tmpf"], scale=c0col[:, cg:cg + 1])
            stt(tmpf, hbuf[:, 0:2048], col(C_MU0 + cg), tmpf, ALU.mult, ALU.add, ["hbuf", "colsb"], ["tmpf"])
            stt(dst, hbuf[:, 2:2050], col(C_MU1 + cg), tmpf, ALU.mult, ALU.add, ["hbuf", "colsb", "tmpf"], [dstname])

        def pairbank():
            return [nextbank(), nextbank()]

        shifted(24, tmpf, "tmpf")
        act(lw, tmpf, AF.Tanh, ["tmpf"], ["lw"])
        shifted(25, la, "la")

        for hp in range(8):
            shifted(hp, rbf, "rbf")
            shifted(8 + hp, kbf, "kbf")
            shifted(16 + hp, vbf, "vbf")
            inproj(l, 26 + hp, lambda t4, bank, bn, hp=hp: act(oT[:, hp, t4 * 512:(t4 + 1) * 512], bank[:], AF.Silu,
                                                               ["colsb"], [bn, "oT%d" % hp], bias=col(C_BIN + 26 + hp)))
            ts(V, tmpf, kbf, col(C_KK + hp), None, ALU.mult, None, ["kbf", "colsb"], ["tmpf"])
            act(tmpB, tmpf, AF.Square, ["tmpf"], ["hbuf"])
            for t4 in range(4):
                tsl = slice(t4 * 512, (t4 + 1) * 512)
                bank, bn = nextbank()
                mm(bank[:], bones[:], tmpB[:, tsl], True, True, ["bones", "hbuf"], [bn])
                act(tmpB[:, tsl], bank[:], AF.Ln, ["epsc"], [bn, "hbuf"], bias=epsc[:, 0:1])
                act(tmpB[:, tsl], tmpB[:, tsl], AF.Exp, [], ["hbuf"], scale=-0.5)
            tt(V, kkbf, tmpf, tmpB, ALU.mult, ["tmpf", "hbuf"], ["kkbf"])
            stt(tmpf, rbf, col(C_RK + hp), kbf, ALU.mult, ALU.mult, ["rbf", "kbf", "colsb"], ["tmpf"])
            for t4 in range(4):
                tsl = slice(t4 * 512, (t4 + 1) * 512)
                bank, bn = nextbank()
                mm(bank[:], bones[:], tmpf[:, tsl], True, True, ["bones", "tmpf"], [bn])
                tt(V, bonus[:, tsl], bank[:], vbf[:, tsl], ALU.mult, ["vbf"], [bn, "bonus"])
            for T in range(4):
                pb = pairbank()
                for h in range(2):
                    hs = slice(64 * h, 64 * h + 64)
                    bank, bn = pb[h]
                    for c8 in range(8):
                        c = T * 8 + c8
                        mm(bank[hs, c8 * 64:(c8 + 1) * 64], vbf[hs, c * 64:(c + 1) * 64], identb[hs, 64 * h:64 * h + 64],
                           True, True, ["vbf", "identb"], [bn])
                    cp(A, Vtok[hs, T * 8:(T + 1) * 8, :], c8v(bank[hs, :]), [], [bn, "Vtok"])

            items = [(0, T) for T in range(4)] + [(1, T) for T in range(3, -1, -1)]

            def produce(z, T, s):
                zs = slice(64 * z, 64 * z + 64)
                tsl = slice(T * 512, (T + 1) * 512)
                AR2_, AR4_, G1s_, G2s_, QP_, BHtok_, KHtok_, Wc_ = SETS[s]
                ss = str(s)
                ARn = "AR" + ss
                b1, b1n = nextbank()
                mm(b1[:], wup[zs, hp * 128:(hp + 1) * 128], lw[zs, tsl], True, True, ["wup", "lw"], [b1n])
                act(sg, b1[:], AF.Sigmoid, ["colsb"], [b1n, "sg"], bias=col(C_W0 + z * 8 + hp))
                b2, b2n = nextbank()
                mm(b2[:], aup[zs, hp * 128:(hp + 1) * 128], la[zs, tsl], True, True, ["aup", "la"], [b2n])
                act(aa, b2[:], AF.Sigmoid, ["colsb"], [b2n, "aa"], bias=col(C_A0 + z * 8 + hp))
                yield
                add(V, lambda e: e.tensor_tensor_scan(out=Gc, data0=reset, data1=sg, initial=0.0, op0=ALU.mult, op1=ALU.add),
                    reads=["reset", "sg"], writes=["Gc"])
                cp(V, totc, c8v(Gc)[:, :, 63], ["Gc"], ["totc"])
                totb = totc.unsqueeze(2).to_broadcast([128, 8, 64])
                if z == 1:
                    tt(V, tmpG, sg, Gc, ALU.subtract, ["sg", "Gc"], ["tmpG"])
                    tt(V, c8v(Gc), c8v(tmpG), totb, ALU.add, ["tmpG", "totc"], ["Gc"])
                yield
                tt(V, sg, Gc, sg, ALU.subtract, ["Gc"], ["sg"])
                tt(V, c8v(tmpG), c8v(Gc), totb, ALU.subtract, ["Gc", "totc"], ["tmpG"])
                act(E, Gc, AF.Exp, ["Gc"], ["E"], scale=-CDEC)
                act(sg, sg, AF.Exp, [], ["sg"], scale=-CDEC)
                act(Gc, Gc, AF.Exp, [], ["Gc"], scale=CDEC)
                act(tmpG, tmpG, AF.Exp, [], ["tmpG"], scale=CDEC)
                act(Wc_, totc, AF.Exp, ["totc"], ["Wc" + ss], scale=-CDEC)
                yield
                ts(V, kd, aa, -1.0, col(C_KA + hp), ALU.add, ALU.mult, ["aa", "colsb"], ["kd"])
                stt(kd, kd, 1.0, kbf[:, tsl], ALU.add, ALU.mult, ["kbf"], ["kd"])
                tt(V, bb, kkbf[:, tsl], aa, ALU.mult, ["kkbf", "aa"], ["bb"])
                yield
                tt(V, AR4_[:, :, 1, :], c8v(rbf[:, tsl]), c8v(E), ALU.mult, ["rbf", "E"], [ARn])
                stt(AR4_[:, :, 0, :], c8v(kkbf[:, tsl]), -1.0, c8v(sg), ALU.mult, ALU.mult, ["kkbf", "sg"], [ARn])
                yield
                tt(V, BT, bb, Gc, ALU.mult, ["bb", "Gc"], ["BT"])
                tt(V, KT, kd, Gc, ALU.mult, ["kd", "Gc"], ["KT"])
                tt(V, BHT, bb, tmpG, ALU.mult, ["bb", "tmpG"], ["BHT"])
                tt(V, KHT, kd, tmpG, ALU.mult, ["kd", "tmpG"], ["KHT"])
                yield
                for src, srcn, dst, dstn in ((BHT, "BHT", BHtok_, "BHtok" + ss), (KHT, "KHT", KHtok_, "KHtok" + ss)):
                    pb = pairbank()
                    for h in range(2):
                        hs = slice(64 * h, 64 * h + 64)
                        bank, bn = pb[h]
                        for c8 in range(8):
                            mm(bank[hs, c8 * 64:(c8 + 1) * 64], src[hs, c8 * 64:(c8 + 1) * 64], identb[hs, 64 * h:64 * h + 64],
                               True, True, [srcn, "identb"], [bn])
                        cp(A, dst[hs, :, :], c8v(bank[hs, :]), [], [bn, dstn + str(h)])
                    yield
                chains = []
                for hv in range(2):
                    for h in range(2):
                        ia, ib = {(0, 0): (0, 1), (0, 1): (2, 3), (1, 0): (4, 5), (1, 1): (6, 7)}[(hv, h)]
                        chains.append((hv, h, psb[ia], "ps%d" % ia, psb[ib], "ps%d" % ib))
                for hv, h, bA, bAn, bB, bBn in chains:
                    hs = slice(64 * h, 64 * h + 64)
                    for cl in range(4):
                        c8 = hv * 4 + cl
                        mm(bA[hs, cl * 128:(cl + 1) * 128], BT[hs, c8 * 64:(c8 + 1) * 64], AR2_[hs, c8, :], True, True,
                           ["BT", ARn], [bAn])
                    for cl in range(4):
                        c8 = hv * 4 + cl
                        mm(bB[hs, cl * 128:(cl + 1) * 128], KT[hs, c8 * 64:(c8 + 1) * 64], AR2_[hs, c8, :], True, True,
                           ["KT", ARn], [bBn])
                for hv, h, bA, bAn, bB, bBn in chains:
                    cs = slice(hv * 4, hv * 4 + 4)
                    hs = slice(64 * h, 64 * h + 64)
                    nm = "%s%d%d" % (ss, hv, h)
                    tt(V, G1s_[hs, cs, :], bA[hs, :].rearrange("p (c n) -> p c n", c=4),
                       m1[z][hs, :].rearrange("p (c n) -> p c n", c=4), ALU.mult, ["m1_%d" % z], [bAn, "G1s" + nm])
                    tt(V, G2s_[hs, cs, :], bB[hs, :].rearrange("p (c n) -> p c n", c=4),
                       m1[z][hs, :].rearrange("p (c n) -> p c n", c=4), ALU.mult, ["m1_%d" % z], [bBn, "G2s" + nm])
                for hv, h, bA, bAn, bB, bBn in chains:
                    hs = slice(64 * h, 64 * h + 64)
                    for cl in range(4):
                        c8 = hv * 4 + cl
                        mm(bA[hs, cl * 64:(cl + 1) * 64], AR4_[hs, c8, 0, :], BT[hs, c8 * 64:(c8 + 1) * 64], True, True,
                           ["BT", ARn], [bAn])
                for hv, h, bA, bAn, bB, bBn in chains:
                    cs = slice(hv * 4, hv * 4 + 4)
                    hs = slice(64 * h, 64 * h + 64)
                    nm = "%s%d%d" % (ss, hv, h)
                    pn = "Pn%d%d" % (hv, h)
                    tt(V, Pn[hs, cs, :], bA[hs, 0:256].rearrange("p (c n) -> p c n", c=4),
                       m2[z][hs, :].rearrange("p (c n) -> p c n", c=4), ALU.mult, ["m2_%d" % z], [bAn, pn])
                    cp(A, QP_[hs, cs, 0:64], eye2[hs, :].unsqueeze(1).to_broadcast([64, 4, 64]), ["eye2"], ["QP" + nm])
                    cp(A, QP_[hs, cs, 64:128], G1s_[hs, cs, 0:64], ["G1s" + nm], ["QP" + nm])
                for k in range(6):
                    last = (k == 5)
                    wA = 64 if last else 128
                    for hv, h, bA, bAn, bB, bBn in chains:
                        hs = slice(64 * h, 64 * h + 64)
                        nm = "%s%d%d" % (ss, hv, h)
                        pn = "Pn%d%d" % (hv, h)
                        for cl in range(4):
                            c8 = hv * 4 + cl
                            mm(bA[hs, cl * 128:cl * 128 + wA], Pn[hs, c8, :], QP_[hs, c8, 0:wA], True, True,
                               [pn, "QP" + nm], [bAn])
                        if not last:
                            for cl in range(4):
                                c8 = hv * 4 + cl
                                mm(bB[hs, cl * 64:(cl + 1) * 64], QP_[hs, c8, 64:128], Pn[hs, c8, :], True, True,
                                   [pn, "QP" + nm], [bBn])
                    for hv, h, bA, bAn, bB, bBn in chains:
                        cs = slice(hv * 4, hv * 4 + 4)
                        hs = slice(64 * h, 64 * h + 64)
                        nm = "%s%d%d" % (ss, hv, h)
                        pn = "Pn%d%d" % (hv, h)
                        bA3 = bA[hs, :].rearrange("p (c n) -> p c n", c=4)
                        tt(V, QP_[hs, cs, 0:64], QP_[hs, cs, 0:64], bA3[:, :, 0:64], ALU.add, [], [bAn, "QP" + nm])
                        if not last:
                            cp(A, QP_[hs, cs, 64:128], bA3[:, :, 64:128], [], [bAn, "QP" + nm])
                            cp(A, Pn[hs, cs, :], bB[hs, 0:256].rearrange("p (c n) -> p c n", c=4), [], [bBn, pn])
                yield

            def consume(z, T, s, first):
                AR2_, AR4_, G1s_, G2s_, QP_, BHtok_, KHtok_, Wc_ = SETS[s]
                STf_, STb_, Xs_, Us_ = STATE[z]
                ss = str(s)
                zn = str(z)
                ARn = "AR" + ss
                if first:
                    memset(V, STf_, 0.0, ["STf" + zn + "0", "STf" + zn + "1"])
                    memset(V, STb_, 0.0, ["STb" + zn + "0", "STb" + zn + "1"])
                cord = range(8) if z == 0 else range(7, -1, -1)
                for c8 in cord:
                    c = T * 8 + c8
                    H = []
                    for h in range(2):
                        H.append((slice(64 * h, 64 * h + 64), zn + str(h), "%s%d%d" % (ss, c8 // 4, h),
                                  psb[2 * z + h], "ps%d" % (2 * z + h), psb[4 + 2 * z + h], "ps%d" % (4 + 2 * z + h)))
                    for hs, hn, nm, cb, cbn, yb, ybn in H:
                        mm(cb[hs, 0:64], AR4_[hs, c8, 0, :], STb_[hs, :], True, False, [ARn, "STb" + hn], [cbn])
                        mm(cb[hs, 0:64], G2s_[hs, c8, 0:64], Vtok[hs, c, :], False, True, ["G2s" + nm, "Vtok"], [cbn])
                    yield
                    for hs, hn, nm, cb, cbn, yb, ybn in H:
                        cp(A, Xs_[hs, :], cb[hs, 0:64], [], [cbn, "Xs" + hn])
                    for hs, hn, nm, cb, cbn, yb, ybn in H:
                        mm(cb[hs, 64:128], QP_[hs, c8, 0:64], Xs_[hs, :], True, True, ["QP" + nm, "Xs" + hn], [cbn])
                    yield
                    for hs, hn, nm, cb, cbn, yb, ybn in H:
                        cp(V if z == 0 else A, Us_[hs, :], cb[hs, 64:128], [], [cbn, "Us" + hn])
                    for hs, hn, nm, cb, cbn, yb, ybn in H:
                        mm(cb[hs, 128:192], BHtok_[hs, c8, :], Us_[hs, :], True, False, ["BHtok" + ss + hn[1], "Us" + hn], [cbn])
                        mm(cb[hs, 128:192], KHtok_[hs, c8, :], Vtok[hs, c, :], False, True, ["KHtok" + ss + hn[1], "Vtok"], [cbn])
                    for hs, hn, nm, cb, cbn, yb, ybn in H:
                        mm(yb[hs, c8 * 64:(c8 + 1) * 64], AR4_[hs, c8, 1, :], STb_[hs, :], True, False, [ARn, "STb" + hn], [ybn])
                        mm(yb[hs, c8 * 64:(c8 + 1) * 64], G1s_[hs, c8, 64:128], Us_[hs, :], False, False,
                           ["G1s" + nm, "Us" + hn], [ybn])
                        mm(yb[hs, c8 * 64:(c8 + 1) * 64], G2s_[hs, c8, 64:128], Vtok[hs, c, :], False, True,
                           ["G2s" + nm, "Vtok"], [ybn])
                    yield
                    for hs, hn, nm, cb, cbn, yb, ybn in H:
                        stt(STb_[hs, :], STb_[hs, :], Wc_[hs, c8:c8 + 1], cb[hs, 128:192], ALU.mult, ALU.add,
                            ["Wc" + ss], [cbn, "STb" + hn])
                    yield
                for h in range(2):
                    hs = slice(64 * h, 64 * h + 64)
                    yb, ybn = psb[4 + 2 * z + h], "ps%d" % (4 + 2 * z + h)
                    tt(V, ytok[hs, T * 8:(T + 1) * 8, :], ytok[hs, T * 8:(T + 1) * 8, :], c8v(yb[hs, :]), ALU.add,
                       [], [ybn, "ytok%d" % T])
                yield

            for T_ in range(4):
                memset(V, ytok[:, T_ * 8:(T_ + 1) * 8, :], 0.0, ["ytok%d" % T_])
            for i in range(4):
                for _ in produce(0, i, 0):
                    pass
                for _ in produce(1, 3 - i, 1):
                    pass
                gens = [consume(0, i, 0, i == 0), consume(1, 3 - i, 1, i == 0)]
                while gens:
                    for g_ in list(gens):
                        try:
                            next(g_)
                        except StopIteration:
                            gens.remove(g_)
            add(V, lambda e: e.tensor_reduce(out=stat[:, 0:32], in_=ytok, axis=AX.X, op=ALU.add), reads=["ytok0", "ytok1", "ytok2", "ytok3"], writes=["st0"])
            ts(V, stat[:, 32:64], stat[:, 0:32], -1.0 / 64, None, ALU.mult, None, ["st0"], ["st1"])
            tt(V, ytok, ytok, stat[:, 32:64].unsqueeze(2).to_broadcast([128, 32, 64]), ALU.add, ["st1"], ["ytok0", "ytok1", "ytok2", "ytok3"])
            act(tmpf3, ytok, AF.Square, ["ytok0", "ytok1", "ytok2", "ytok3"], ["tmpf"])
            add(V, lambda e: e.tensor_reduce(out=stat[:, 64:96], in_=tmpf3, axis=AX.X, op=ALU.add), reads=["tmpf"], writes=["st2"])
            act(stat[:, 96:128], stat[:, 64:96], AF.Sqrt, ["st2"], ["st3"], bias=64e-5, scale=1.0 / 64)
            add(V, lambda e: e.reciprocal(out=stat[:, 96:128], in_=stat[:, 96:128]), reads=[], writes=["st3"])
            tt(V, ynb, ytok, stat[:, 96:128].unsqueeze(2).to_broadcast([128, 32, 64]), ALU.mult, ["ytok0", "ytok1", "ytok2", "ytok3", "st3"], ["rbf"])
            for T in range(4):
                tsl = slice(T * 512, (T + 1) * 512)
                pb = pairbank()
                for h in range(2):
                    hs = slice(64 * h, 64 * h + 64)
                    bank, bn = pb[h]
                    for c8 in range(8):
                        mm(bank[hs, c8 * 64:(c8 + 1) * 64], ynb[hs, T * 8 + c8, :], identb[hs, 64 * h:64 * h + 64], True, True,
                           ["rbf", "identb"], [bn])
                    act(t1f[hs, :], bank[hs, :], AF.Identity, ["colsb"], [bn, "t1f" + str(h)],
                        bias=colsb[hs, C_GB + hp:C_GB + hp + 1], scale=colsb[hs, C_GG + hp:C_GG + hp + 1])
                tt(V, t1f, t1f, bonus[:, tsl], ALU.add, ["bonus", "t1f0", "t1f1"], ["t1f0", "t1f1"])
                tt(V, oT[:, hp, tsl], t1f, oT[:, hp, tsl], ALU.mult, ["t1f0", "t1f1"], ["oT%d" % hp])

    C.rwkv = rwkv_phase


    dma("sp", colsb[:], cols_d[0], [], ["colsb"])
    xld = [af32(0, 1024), af32(1024, 1024)]
    for t16 in range(16):
        dma("sp", xld[t16 % 2], x_in[t16 * 128:(t16 + 1) * 128, :], [], ["xld%d" % (t16 % 2)])
        store_xT(xld[t16 % 2], "xld%d" % (t16 % 2), t16)
    import os
    STOP = int(os.environ.get("KSTOP", "9"))
    for l in range(depth if STOP > 1 else 0):
        if l > 0:
            dma("sp", colsb[:], cols_d[l], [], ["colsb"])
        dma(G, vrowb[:], vrow_d[l], [], ["vrowb"])
        tt(V, c0col[:], colsb[:, C_MU0:C_MU0 + 26], colsb[:, C_MU1:C_MU1 + 26], ALU.add, ["colsb"], ["c0col"])
        ts(V, c0col[:], c0col[:], -1.0, 1.0, ALU.mult, ALU.add, [], ["c0col"])
        if C.rwkv is not None and "A" in PH:
            C.rwkv(l)
        else:
            for c in range(8):
                memset(V, oT[:, c, :], 0.0, ["oT%d" % c])
        if "B" in PH:
            pool_phase(l)
        else:
            barrier()
            for c in range(8, 14):
                memset(V, oT[:, c, :], 0.0, ["oT%d" % c])
        if C.attn is not None and "C" in PH:
            C.attn(l)
        else:
            barrier()
            for c in range(14, 16):
                memset(V, oT[:, c, :], 0.0, ["oT%d" % c])
        if dbg is not None and l == depth - 1:
            dtmp = af32(0, S)
            for c in range(16):
                cp(V, dtmp, oT[:, c, :], ["oT%d" % c], ["dtmp"])
                dma("sp", dbg_out[:, c, :], dtmp, ["dtmp"], [])
        if STOP > 2:
            final_phase(l, l == depth - 1)
    P.emit(nc)
    st.close()
    return nc


PH = "ABC"


def EXTRA_PHASES(L):
    pass


def host_prep(inp):
    f = np.float32
    g = lambda k: np.asarray(inp[k], dtype=f)
    colv = lambda v: np.ascontiguousarray(v.reshape(-1, 128).T)
    cols = []
    for l in range(DEPTH):
        parts = [colv(g("b_in")[l]), colv(g("rwkv_mu")[l, 0]), colv(g("rwkv_mu")[l, 1]),
                 colv(g("rwkv_w0")[l, 0]), colv(g("rwkv_w0")[l, 1]), colv(g("rwkv_a0")[l, 0]), colv(g("rwkv_a0")[l, 1]),
                 colv(g("rwkv_k_k")[l]), colv(g("rwkv_k_a")[l]), colv(g("rwkv_r_k")[l].reshape(-1)),
                 colv(g("rwkv_gn_g")[l]), colv(g("rwkv_gn_b")[l]), colv(g("pool_b")[l]), colv(g("pool_scale")[l])]
        cols.append(np.concatenate(parts, axis=1))
    cols = np.stack(cols)
    assert cols.shape == (DEPTH, 128, NCOLS), cols.shape
    shared = {
        "w_in": g("w_in"), "cols": cols,
        "vrow": np.ascontiguousarray(g("b_in")[:, None, 7424:8192]),
        "w_up": np.ascontiguousarray(g("rwkv_w_up").reshape(DEPTH, 128, 1024)),
        "a_up": np.ascontiguousarray(g("rwkv_a_up").reshape(DEPTH, 128, 1024)),
        "pool_w": _blockdiag(g("pool_w")), "proj_a": g("proj_a"), "proj_b": g("proj_b"), "proj_c": g("proj_c"),
        "w_out": g("w_out"),
        "lnrow": np.ascontiguousarray(np.stack([g("ln_g"), g("ln_b")], axis=1)),
    }
    shared.update(host_consts())
    return shared


def _blockdiag(pw):
    out = np.zeros((DEPTH, 768, 768), np.float32)
    for gi in range(4):
        out[:, 192 * gi:192 * gi + 192, 192 * gi:192 * gi + 192] = pw[:, gi]
    return out


def host_consts():
    f = np.float32
    ident = np.eye(128, dtype=f)
    bones = np.zeros((128, 128), f)
    bones[:64, :64] = 1
    bones[64:, 64:] = 1
    perm = np.zeros((128, 128), f)
    for m in range(128):
        c = m % 64
        k = m + 32 if c < 32 else m - 32
        perm[k, m] = 1
    eye2 = np.concatenate([np.eye(64, dtype=f), np.eye(64, dtype=f)], 0)
    cst = np.concatenate([ident, bones, perm, eye2], axis=1)
    inv = np.power(f(10000.0), -np.arange(0, 64, 2, dtype=f) / f(64))
    ang = np.arange(S, dtype=f)[:, None] * inv[None, :]
    ang = np.concatenate([ang, ang], axis=-1).astype(f)
    cosT = np.cos(ang).T.astype(f)
    sinT = np.sin(ang).T.astype(f)
    sign = np.where(np.arange(64) < 32, -1.0, 1.0).astype(f)[:, None]
    rope = np.stack([np.concatenate([cosT, cosT], 0), np.concatenate([sinT * sign, sinT * sign], 0)]).astype(f)
    b = np.arange(128)[:, None]
    a = np.arange(128)[None, :]
    mA = (b >= a).astype(f)
    mB = (a >= b).astype(f)
    mAf = mA * (b >= 64)
    mBl = mB * (b < 64)
    m16 = (np.abs(a - b) <= 64).astype(f)
    amask = np.stack([np.concatenate([mA, mB, mA, mB], 1), np.concatenate([mAf, mB, mAf, mB], 1),
                      np.concatenate([mA, mBl, mA, mBl], 1), np.concatenate([m16, m16, m16, m16], 1)]).astype(f)
    s_ = np.arange(64)[:, None]
    t_ = np.arange(64)[None, :]
    rm = []
    for z in range(2):
        if z == 0:
            strict = (s_ < t_)
            incl = (s_ <= t_)
        else:
            strict = (s_ > t_)
            incl = (s_ >= t_)
        m1 = np.concatenate([strict, incl], 1).astype(f)
        m2 = strict.T.astype(f)
        reset = np.ones((64, 512), f)
        reset[:, ::64] = 0
        row = np.concatenate([np.tile(m1, (1, 4)), np.tile(m1, (1, 4)), np.tile(m2, (1, 4)), reset], 1)
        rm.append(np.concatenate([row, row], 0))
    rmask = np.stack(rm).astype(f)
    pedge = np.ones((128, 4, 16), f)
    for gi in range(4):
        h = 1 << gi
        for e in range(8):
            t = e
            cnt = min(t + h, S - 1) - max(t - h, 0) + 1
            pedge[:, gi, e] = (2 * h + 1) / cnt
            t = S - 8 + e
            cnt = min(t + h, S - 1) - max(t - h, 0) + 1
            pedge[:, gi, 8 + e] = (2 * h + 1) / cnt
    selc = np.zeros((128, 24), f)
    for c in range(6):
        for p in range(128):
            gi = (128 * c + p) // 192
            selc[p, c * 4 + gi] = 1.0 / (2 * (1 << gi) + 1)
    return {"selc": selc, "cst": cst, "rope": rope, "amask": amask, "rmask": rmask, "pedge": pedge}


_NC_CACHE = {}


def kernel(**inputs):
    shared = host_prep(inputs)
    x = np.asarray(inputs["x"], dtype=np.float32)
    if "nc" not in _NC_CACHE:
        _NC_CACHE["nc"] = build()
    nc = _NC_CACHE["nc"]
    in_maps = []
    for c in range(8):
        m = dict(shared)
        m["x"] = np.ascontiguousarray(x[c])
        in_maps.append(m)
    res = run_bass_kernel_spmd(nc, in_maps, core_ids=list(range(8)))
    return np.stack([np.asarray(r["y"], dtype=np.float32) for r in res.results], axis=0)
```

```python
import math
import os
import numpy as np
import ml_dtypes
import concourse.bass as bass
import concourse.mybir as mybir
from concourse.bass_utils import run_bass_kernel_spmd

F32 = mybir.dt.float32
BF16 = mybir.dt.bfloat16
AF = mybir.ActivationFunctionType
ALU = mybir.AluOpType
AX = mybir.AxisListType

ENGS = ("pe", "act", "dve", "pool", "sp")
DMA_POOL = 16


class _Buf:
    __slots__ = ("last_w", "readers", "dma_readers")

    def __init__(self):
        self.last_w = None
        self.readers = {}
        self.dma_readers = []


class _Op:
    __slots__ = ("eng", "idx", "gid", "fn", "deps", "is_dma", "signal", "sem", "val", "waits",
                 "know", "dma_n", "pre_wait")

    def __init__(self, eng, idx, gid, fn, is_dma):
        self.eng = eng
        self.idx = idx
        self.gid = gid
        self.fn = fn
        self.is_dma = is_dma
        self.deps = []
        self.signal = False
        self.sem = None
        self.val = None
        self.waits = []
        self.know = None
        self.dma_n = None
        self.pre_wait = None


class Prog:
    def __init__(self):
        self.ops = {e: [] for e in ENGS}
        self.all = []
        self.bufs = {}
        self.n_dma = {e: 0 for e in ENGS}

    def _buf(self, name):
        b = self.bufs.get(name)
        if b is None:
            b = _Buf()
            self.bufs[name] = b
        return b

    def add(self, eng, fn, reads=(), writes=(), dma=False):
        op = _Op(eng, len(self.ops[eng]), len(self.all), fn, dma)
        deps = {}
        for r in reads:
            b = self._buf(r)
            if b.last_w is not None:
                deps[b.last_w.gid] = b.last_w
        for w in writes:
            b = self._buf(w)
            if b.last_w is not None:
                deps[b.last_w.gid] = b.last_w
            for d in b.readers.values():
                deps[d.gid] = d
            for d in b.dma_readers:
                deps[d.gid] = d
        op.deps = [deps[k] for k in sorted(deps)]
        for r in reads:
            b = self._buf(r)
            if dma:
                b.dma_readers.append(op)
            else:
                b.readers[eng] = op
        for w in writes:
            b = self._buf(w)
            b.last_w = op
            b.readers = {}
            b.dma_readers = []
        if dma:
            op.dma_n = self.n_dma[eng]
            self.n_dma[eng] += 1
        self.ops[eng].append(op)
        self.all.append(op)
        return op

    def resolve(self):
        know = {e: {f: -1 for f in ENGS} for e in ENGS}
        know_dma = {e: set() for e in ENGS}
        sig_count = {e: 0 for e in ENGS}
        for op in self.all:
            E = op.eng
            kn = know[E]
            for d in op.deps:
                if d.is_dma:
                    if d.gid in know_dma[E]:
                        continue
                    know_dma[E].add(d.gid)
                    op.waits.append(d)
                    for f, v in d.know.items():
                        if v > kn[f]:
                            kn[f] = v
                    continue
                F = d.eng
                if F == E:
                    if E == "pe" or op.idx - d.idx > 2:
                        continue
                    if kn[F] >= d.idx:
                        continue
                elif kn[F] >= d.idx:
                    continue
                d.signal = True
                op.waits.append(d)
                kn[F] = max(kn[F], d.idx)
                for f, v in d.know.items():
                    if f != E and v > kn[f]:
                        kn[f] = v
            snap = dict(kn)
            if not op.is_dma:
                snap[E] = op.idx
            op.know = snap
        for e in ENGS:
            c = 0
            for op in self.ops[e]:
                if op.is_dma:
                    continue
                if op.signal:
                    c += 1
                    op.val = c

    def emit(self, nc):
        self.resolve()
        import contextlib
        with contextlib.ExitStack() as st:
            esem = {e: st.enter_context(nc.semaphore("s_" + e)) for e in ENGS}
            dsem = {e: [st.enter_context(nc.semaphore("d_%s%d" % (e, i))) for i in range(DMA_POOL)]
                    for e in ENGS if self.n_dma[e] > 0}
            block = st.enter_context(nc.Block())

            def wait_for(engine, d):
                if d.is_dma:
                    engine.wait_ge(dsem[d.eng][d.dma_n % DMA_POOL], 16 * (d.dma_n // DMA_POOL + 1))
                else:
                    engine.wait_ge(esem[d.eng], d.val)

            def run(engine, e):
                ops = self.ops[e]
                for op in ops:
                    for d in op.waits:
                        wait_for(engine, d)
                    if op.is_dma:
                        n = op.dma_n
                        if n >= DMA_POOL:
                            engine.wait_ge(dsem[e][n % DMA_POOL], 16 * (n // DMA_POOL))
                        ins = op.fn(engine)
                        ins.then_inc(dsem[e][n % DMA_POOL], 16)
                    else:
                        ins = op.fn(engine)
                        if op.signal:
                            ins.then_inc(esem[e], 1)
                nd = self.n_dma[e]
                for i in range(min(nd, DMA_POOL)):
                    n = nd - 1 - i
                    engine.wait_ge(dsem[e][n % DMA_POOL], 16 * (n // DMA_POOL + 1))

            @block.tensor
            def _(eng):
                run(eng, "pe")

            @block.scalar
            def _(eng):
                run(eng, "act")

            @block.vector
            def _(eng):
                run(eng, "dve")

            @block.gpsimd
            def _(eng):
                run(eng, "pool")

            @block.sync
            def _(eng):
                run(eng, "sp")


S = 2048
D = 1024
NIN = 11520
DEPTH = 4
PADX = 256
XW = S + 2 * PADX
ALPHA = (2 * DEPTH) ** 0.25
CDEC = math.exp(-0.5)
C_BIN = 0
C_MU0 = 90
C_MU1 = 116
C_W0 = 142
C_A0 = 158
C_KK = 174
C_KA = 182
C_RK = 190
C_GG = 198
C_GB = 206
C_PB = 214
C_PS = 220
NCOLS = 226


class Ctx:
    pass


def build(depth=DEPTH, dbg=None):
    nc = bass.Bass("TRN2", target_bir_lowering=False)
    P = Prog()
    dt_in = lambda name, shape: nc.dram_tensor(name, shape, F32, kind="ExternalInput").ap()
    x_in = dt_in("x", [S, D])
    w_in = dt_in("w_in", [DEPTH, D, NIN])
    cols_d = dt_in("cols", [DEPTH, 128, NCOLS])
    vrow_d = dt_in("vrow", [DEPTH, 1, 768])
    wup_d = dt_in("w_up", [DEPTH, 128, 1024])
    aup_d = dt_in("a_up", [DEPTH, 128, 1024])
    poolw_d = dt_in("pool_w", [DEPTH, 768, 768])
    proja_d = dt_in("proj_a", [DEPTH, 1024, 1024])
    projb_d = dt_in("proj_b", [DEPTH, 768, 1024])
    projc_d = dt_in("proj_c", [DEPTH, 256, 1024])
    wout_d = dt_in("w_out", [DEPTH, 1024, 1024])
    lnrow_d = dt_in("lnrow", [DEPTH, 2, 1024])
    cst_d = dt_in("cst", [128, 128 * 3 + 64])
    rope_d = dt_in("rope", [2, 128, S])
    amask_d = dt_in("amask", [4, 128, 512])
    rmask_d = dt_in("rmask", [2, 128, 512 + 512 + 256 + 512])
    pedge_d = dt_in("pedge", [128, 4, 16])
    y_out = nc.dram_tensor("y", [S, D], F32, kind="ExternalOutput").ap()
    xres = nc.dram_tensor("xres", [S, D], F32, kind="Internal").ap()
    dbg_out = None
    if dbg is not None:
        dbg_out = nc.dram_tensor("dbg", [128, 16, S], F32, kind="ExternalOutput").ap()

    import contextlib
    st = contextlib.ExitStack()
    sb = lambda name, shape, dt: st.enter_context(nc.sbuf_tensor(name, shape, dt))
    xT = sb("xT", [128, 8, XW], BF16)
    oT = sb("oT", [128, 16, S], BF16)
    wb = [sb("wb%d" % i, [128, 8, 512], BF16) for i in range(2)]
    ARENA = 16384
    arena = sb("arena", [128, ARENA], F32)
    colsb = sb("colsb", [128, NCOLS], F32)
    c0col = sb("c0col", [128, 26], F32)
    identb = sb("identb", [128, 128], BF16)
    permb = sb("permb", [128, 128], BF16)
    onesb = sb("onesb", [128, 128], BF16)
    eye2 = sb("eye2", [128, 64], BF16)
    bonesb = sb("bonesb", [128, 128], BF16)
    bones = sb("bones", [128, 128], F32)
    vrowb = sb("vrowb", [1, 768], BF16)
    selcol = sb("selcol", [128, 24], F32)
    mixf = sb("mixf", [128, S], F32)
    selc_d = dt_in("selc", [128, 24])
    psb = [st.enter_context(nc.psum_tensor("psb%d" % i, [128, 512], F32)) for i in range(8)]

    C = Ctx()
    C.bank_i = 0

    def nextbank():
        i = C.bank_i
        C.bank_i = (i + 1) % 4
        return psb[i], "ps%d" % i

    def af32(off, n):
        return arena[:, off:off + n]

    def abf(off, n):
        return arena[:, off:off + n // 2].bitcast(BF16)

    def o8f32(off, n):
        return oT[:, 8:16, :].rearrange("p a b -> p (a b)").bitcast(F32)[:, off:off + n]

    def o8bf(off, n):
        return oT[:, 8:16, :].rearrange("p a b -> p (a b)")[:, off:off + n]

    add = P.add
    V = "dve"
    A = "act"
    G = "pool"

    def mm(out, lhsT, rhs, start, stop, rd, wr):
        add("pe", lambda e: e.matmul(out, lhsT, rhs, start=start, stop=stop), reads=rd, writes=wr)

    def act(out, in_, func, rd, wr, bias=0.0, scale=1.0):
        add(A, lambda e: e.activation(out=out, in_=in_, func=func, bias=bias, scale=scale), reads=rd, writes=wr)

    def tt(eng, out, in0, in1, op, rd, wr):
        eng = V if eng == G else eng
        add(eng, lambda e: e.tensor_tensor(out=out, in0=in0, in1=in1, op=op), reads=rd, writes=wr)

    def ts(eng, out, in0, s1, s2, op0, op1, rd, wr):
        eng = V if eng == G else eng
        if s2 is None:
            add(eng, lambda e: e.tensor_scalar(out=out, in0=in0, scalar1=s1, scalar2=None, op0=op0), reads=rd, writes=wr)
        else:
            add(eng, lambda e: e.tensor_scalar(out=out, in0=in0, scalar1=s1, scalar2=s2, op0=op0, op1=op1), reads=rd, writes=wr)

    def stt(out, in0, scalar, in1, op0, op1, rd, wr):
        add(V, lambda e: e.scalar_tensor_tensor(out=out, in0=in0, scalar=scalar, in1=in1, op0=op0, op1=op1),
            reads=rd, writes=wr)

    def cp(eng, out, in_, rd, wr):
        eng = V if eng == G else eng
        if eng == A:
            add(eng, lambda e: e.copy(out=out, in_=in_), reads=rd, writes=wr)
        else:
            add(eng, lambda e: e.tensor_copy(out=out, in_=in_), reads=rd, writes=wr)

    def dma(q, out, in_, rd, wr):
        add(q, lambda e: e.dma_start(out=out, in_=in_), reads=rd, writes=wr, dma=True)

    def memset(eng, ap, val, wr):
        add(eng, lambda e: e.memset(ap, val), writes=wr)

    bscr = sb("bscr", [128, 8], F32)
    epsc = sb("epsc", [128, 1], F32)

    def barrier():
        names = [n for n in P.bufs.keys() if not n.startswith("ps")] + ["bscr"]
        mm(psb[7][:, 0:8], identb[:, 0:128], identb[:, 0:8], True, True, [], names + ["ps7"])
        act(bscr[:, 0:1], bscr[:, 1:2], AF.Copy, [], names)
        memset(V, bscr[:, 2:3], 0.0, names)
        dma("sp", bscr[0:1, 3:4], cst_d[0:1, 0:1], [], names)
        dma(G, bscr[0:1, 4:5], cst_d[0:1, 0:1], [], names)

    memset(V, bscr[:], 0.0, ["bscr"])
    memset(V, epsc[:], 1e-12, ["epsc"])
    dma(G, identb[:], cst_d[:, 0:128], [], ["identb"])
    dma("sp", bones[:], cst_d[:, 128:256], [], ["bones"])
    dma(G, permb[:], cst_d[:, 256:384], [], ["permb"])
    dma("sp", selcol[:], selc_d, [], ["selcol"])
    dma(G, eye2[:], cst_d[:, 384:448], [], ["eye2"])
    dma(G, bonesb[:], cst_d[:, 128:256], [], ["bonesb"])
    memset(V, onesb[:], 1.0, ["onesb"])
    memset(V, xT[:, :, 0:PADX], 0.0, ["xT"])
    memset(V, xT[:, :, PADX + S:XW], 0.0, ["xT"])

    C.wres = [None, None]
    C.wlast = 0

    def load_w(key, src3):
        for i in range(2):
            if C.wres[i] == key:
                C.wlast = i
                return wb[i], "wb%d" % i
        i = 1 - C.wlast
        C.wres[i] = key
        C.wlast = i
        kc, ncol = src3.shape[1], src3.shape[2]
        dma(G, wb[i][:, 0:kc, 0:ncol], src3, [], ["wb%d" % i])
        return wb[i], "wb%d" % i

    def win_src(l, col0, ncol):
        return w_in[l].rearrange("(k p) n -> p k n", p=128)[:, :, col0:col0 + ncol]

    def inproj(l, cg, evac, wkey=None):
        blk = cg // 4
        ncol = min(512, NIN - blk * 512)
        w, wn = load_w(("win", l, blk), win_src(l, blk * 512, ncol))
        c0 = (cg % 4) * 128
        for t4 in range(4):
            bank, bn = nextbank()
            for k in range(8):
                mm(bank[:], w[:, k, c0:c0 + 128], xT[:, k, PADX + t4 * 512:PADX + (t4 + 1) * 512],
                   k == 0, k == 7, ["xT", wn], [bn])
            evac(t4, bank, bn)

    def col(ci):
        return colsb[:, ci:ci + 1]

    xstage = sb("xstage", [128, 1024], BF16)

    def store_xT(src_f32, srcname, t16):
        cp(A, xstage[:], src_f32, [srcname], ["xstage"])
        bank, bn = nextbank()
        bb = bank[:].bitcast(BF16)
        for k in range(8):
            add("pe", lambda e, k=k: e.transpose(bb[:, k * 128:(k + 1) * 128], xstage[:, k * 128:(k + 1) * 128], identb[:]),
                reads=["xstage", "identb"], writes=[bn])
        cp(V, xT[:, :, PADX + t16 * 128:PADX + (t16 + 1) * 128], bb.rearrange("p (k t) -> p k t", k=8), [], [bn, "xT"])

    def final_phase(l, last):
        barrier()
        mergedT = abf(0, 8 * S).rearrange("p (k t) -> p k t", k=8)
        sig = abf(8192, 512)
        tmpf = af32(8448, 512)
        lng = af32(9216, 1024)
        lnb = af32(10240, 1024)
        xt_ = [af32(11264, 1024), af32(12288, 1024)]
        yt_ = [af32(13312, 1024), af32(14336, 1024)]
        stat = af32(15360, 8)
        dma("sp", lng, lnrow_d[l, 0:1, :].to_broadcast([128, 1024]), [], ["lng"])
        dma("sp", lnb, lnrow_d[l, 1:2, :].to_broadcast([128, 1024]), [], ["lnb"])
        if STOP == 4:
            return
        branches = [(proja_d, 8, 0, 66), (projb_d, 6, 8, 74), (projc_d, 2, 14, 82)]
        for bi, (pd, kc, o0, g0) in enumerate(branches):
            for eb in range(2):
                for ec in range(eb * 4, eb * 4 + 4):
                    for t4 in range(4):
                        tsl = slice(t4 * 512, (t4 + 1) * 512)
                        pw, pwn = load_w(("proj", l, bi, eb), pd[l].rearrange("(k p) n -> p k n", p=128)[:, :, eb * 512:(eb + 1) * 512])
                        b1, b1n = nextbank()
                        for k in range(kc):
                            mm(b1[:], pw[:, k, (ec % 4) * 128:(ec % 4 + 1) * 128], oT[:, o0 + k, tsl], k == 0, k == kc - 1,
                               ["oT%d" % (o0 + k), pwn], [b1n])
                        gw, gwn = load_w(("gate", l, bi, eb), win_src(l, (g0 + eb * 4) * 128, 512))
                        b2, b2n = nextbank()
                        for k in range(8):
                            mm(b2[:], gw[:, k, (ec % 4) * 128:(ec % 4 + 1) * 128], xT[:, k, PADX + t4 * 512:PADX + (t4 + 1) * 512],
                               k == 0, k == 7, ["xT", gwn], [b2n])
                        act(sig, b2[:], AF.Sigmoid, ["colsb"], [b2n, "sig"], bias=col(C_BIN + g0 + ec))
                        if bi == 0:
                            tt(V, mergedT[:, ec, tsl], b1[:], sig, ALU.mult, ["sig"], [b1n, "mg%d" % ec])
                        else:
                            tt(V, tmpf, b1[:], sig, ALU.mult, ["sig"], [b1n, "tmpf"])
                            tt(G, mergedT[:, ec, tsl], mergedT[:, ec, tsl], tmpf, ALU.add, ["tmpf"], ["mg%d" % ec])
        if STOP == 3:
            return
        wo = []
        for fh in range(2):
            wo.append(load_w(("wout", l, fh), wout_d[l].rearrange("(k p) n -> p k n", p=128)[:, :, fh * 512:(fh + 1) * 512]))
        xsrc = x_in if l == 0 else xres
        dst = y_out if last else xres
        def ld_x(t):
            dma("sp", xt_[t % 2], xsrc[t * 128:(t + 1) * 128, :], ["xres%d" % t] if l > 0 else [], ["xt%d" % (t % 2)])

        ld_x(0)
        for t16 in range(16):
            xt = xt_[t16 % 2]
            yt = yt_[t16 % 2]
            par = t16 % 2
            st_ = af32(15360 + 8 * par, 8)
            xn, yn = "xt%d" % par, "yt%d" % par
            sn = lambda k, par=par: "stat%d_%d" % (k, par)
            for fh in range(2):
                w, wn = wo[fh]
                bank, bn = nextbank()
                for k in range(8):
                    mm(bank[:], mergedT[:, k, t16 * 128:(t16 + 1) * 128], w[:, k, :], k == 0, k == 7,
                       ["mg%d" % k, wn], [bn])
                stt(yt[:, fh * 512:(fh + 1) * 512], xt[:, fh * 512:(fh + 1) * 512], ALPHA, bank[:], ALU.mult, ALU.add,
                    [xn], [bn, yn])
            add(V, lambda e, yt=yt, st_=st_: e.tensor_reduce(out=st_[:, 0:1], in_=yt, axis=AX.X, op=ALU.add), reads=[yn], writes=[sn(0)])
            ts(V, st_[:, 1:2], st_[:, 0:1], -1.0 / D, None, ALU.mult, None, [sn(0)], [sn(1)])
            ts(V, yt, yt, st_[:, 1:2], None, ALU.add, None, [sn(1)], [yn])
            add(A, lambda e, yt=yt, xt=xt, st_=st_: e.activation(out=xt, in_=yt, func=AF.Square, accum_out=st_[:, 2:3]),
                reads=[yn], writes=[xn, sn(2)])
            if t16 + 1 < 16:
                ld_x(t16 + 1)
            act(st_[:, 3:4], st_[:, 2:3], AF.Sqrt, [sn(2)], [sn(3)], bias=1e-5, scale=1.0 / D)
            add(V, lambda e, st_=st_: e.reciprocal(out=st_[:, 4:5], in_=st_[:, 3:4]), reads=[sn(3)], writes=[sn(4)])
            stt(yt, yt, st_[:, 4:5], lng, ALU.mult, ALU.mult, [sn(4), "lng"], [yn])
            tt(V, yt, yt, lnb, ALU.add, ["lnb"], [yn])
            dma("sp", dst[t16 * 128:(t16 + 1) * 128, :], yt, [yn], ["xres%d" % t16])
            if not last:
                store_xT(yt, yn, t16)

    def pool_phase(l):
        barrier()
        W = S + 32
        pbuf = af32(0, W)
        a_ = [af32(2080, W), af32(4160, W)]
        mixed = abf(6240, 6 * S).rearrange("p (c t) -> p c t", c=6)
        wgt = abf(12384, 6 * 768).rearrange("p (a b) -> p a b", a=6)
        sacc = af32(14688, 0) if False else None
        pe_t = af32(14688, 64).rearrange("p (g e) -> p g e", g=4)
        t1 = af32(14752, 512)
        memset(V, pbuf[:, 0:16], 0.0, ["pbuf"])
        memset(V, pbuf[:, W - 16:W], 0.0, ["pbuf"])
        dma("sp", pe_t, pedge_d, [], ["pe_t"])
        dma(G, wgt, poolw_d[l].rearrange("(k p) n -> p k n", p=128), [], ["wgt"])
        for c in range(6):
            inproj(l, 40 + c, lambda t4, bank, bn, c=c: act(oT[:, 8 + c, t4 * 512:(t4 + 1) * 512], bank[:], AF.Silu,
                                                             ["colsb"], [bn, "oT%d" % (8 + c)], bias=col(C_BIN + 40 + c)))
        for c in range(6):
            inproj(l, 34 + c, lambda t4, bank, bn, c=c: act(pbuf[:, 16 + t4 * 512:16 + (t4 + 1) * 512], bank[:], AF.Identity,
                                                             ["colsb"], [bn, "pbuf"], bias=col(C_BIN + 34 + c)))
            gs = sorted(set((2 * c + hf) // 3 for hf in range(2)))
            first = True
            for g in gs:
                h = 1 << g
                kk = g + 1
                src, srcn = pbuf, "pbuf"
                for j in range(kk):
                    sh = 1 << j
                    dstt = a_[j % 2]
                    n = W - (2 << j) + 1
                    tt(V, dstt[:, 0:n], src[:, 0:n], src[:, sh:sh + n], ALU.add, [srcn], ["a%d" % (j % 2)])
                    src, srcn = dstt, "a%d" % (j % 2)
                sfin = a_[kk % 2]
                sn = "a%d" % (kk % 2)
                tt(V, sfin[:, 0:S], src[:, 16 - h:16 - h + S], pbuf[:, 16 + h:16 + h + S], ALU.add, [srcn, "pbuf"], [sn])
                tt(V, sfin[:, 0:8], sfin[:, 0:8], pe_t[:, g, 0:8], ALU.mult, ["pe_t"], [sn])
                tt(V, sfin[:, S - 8:S], sfin[:, S - 8:S], pe_t[:, g, 8:16], ALU.mult, ["pe_t"], [sn])
                selw = selcol[:, c * 4 + g:c * 4 + g + 1]
                if first:
                    stt(mixf[:, :], sfin[:, 0:S], selw, pbuf[:, 16:16 + S], ALU.mult, ALU.subtract, [sn, "pbuf", "selcol"], ["mixf"])
                else:
                    stt(mixf[:, :], sfin[:, 0:S], selw, mixf[:, :], ALU.mult, ALU.add, [sn, "selcol"], ["mixf"])
                first = False
            cp(A, mixed[:, c, :], mixf[:, :], ["mixf"], ["mixed"])
        for oc in range(6):
            ics = [ic for ic in range(6) if any((2 * ic + a) // 3 == (2 * oc + b) // 3 for a in range(2) for b in range(2))]
            for t4 in range(4):
                tsl = slice(t4 * 512, (t4 + 1) * 512)
                bank, bn = nextbank()
                for n_, ic in enumerate(ics):
                    mm(bank[:], wgt[:, ic, oc * 128:(oc + 1) * 128], mixed[:, ic, tsl], n_ == 0, n_ == len(ics) - 1,
                       ["wgt", "mixed"], [bn])
                ts(V, t1, bank[:], col(C_PB + oc), col(C_PS + oc), ALU.add, ALU.mult, ["colsb"], [bn, "t1"])
                tt(V, oT[:, 8 + oc, tsl], t1, oT[:, 8 + oc, tsl], ALU.mult, ["t1"], ["oT%d" % (8 + oc)])

    C.attn = None
    C.rwkv = None
    def attn_phase(l):
        barrier()
        Qr = abf(0, XW)
        Kr = abf(1280, XW)
        qraw = abf(2560, S)
        ropec = af32(3584, S)
        ropes = af32(5632, S)
        t1 = af32(7680, 512)
        t2 = af32(8192, 512)
        accn = af32(8704, S)
        accd = af32(10752, S)
        Vt = abf(12800, 20 * 128).rearrange("p (a b) -> p a b", a=20)
        pT = [abf(14080, 512), abf(14336, 512)]
        msk = abf(14592, 4 * 512).rearrange("p (a b) -> p a b", a=4)
        dma("sp", ropec, rope_d[0], [], ["ropec"])
        dma("sp", ropes, rope_d[1], [], ["ropes"])
        for a_ in range(4):
            dma(G, msk[:, a_, :], amask_d[a_], [], ["msk"])
        for buf, nm in ((Qr, "Qr"), (Kr, "Kr")):
            memset(V, buf[:, 0:PADX], 0.0, [nm])
            memset(V, buf[:, PADX + S:XW], 0.0, [nm])
        cnt = [0]
        SUB = int(os.environ.get("ATT_SUB", "9"))
        if SUB == 1:
            return
        for pp in range(2):
            for g in range(3):
                d = (1, 4, 16)[g]
                for cg, dst, nm in ((46 + 2 * g + pp, Qr, "Qr"), (52 + 2 * g + pp, Kr, "Kr")):
                    inproj(l, cg, lambda t4, bank, bn, cg=cg: act(qraw[:, t4 * 512:(t4 + 1) * 512], bank[:], AF.Identity,
                                                                  ["colsb"], [bn, "qraw"], bias=col(C_BIN + cg)))
                    for t4 in range(4):
                        tsl = slice(t4 * 512, (t4 + 1) * 512)
                        bank, bn = nextbank()
                        mm(bank[:], permb[:], qraw[:, tsl], True, True, ["permb", "qraw"], [bn])
                        tt(V, t1, bank[:], ropes[:, tsl], ALU.mult, ["ropes"], [bn, "t1"])
                        tt(V, t2, qraw[:, tsl], ropec[:, tsl], ALU.mult, ["qraw", "ropec"], ["t2"])
                        tt(V, dst[:, PADX + t4 * 512:PADX + (t4 + 1) * 512], t1, t2, ALU.add, ["t1", "t2"], [nm])
                if SUB == 2:
                    return
                wv, wvn = load_w(("wv", l, g, pp), win_src(l, (58 + 2 * g + pp) * 128, 128))
                if d == 1:
                    tsls = [slice(PADX + 128 * m - 64, PADX + 128 * m + 64) for m in range(17)]
                elif d == 4:
                    tsls = []
                    for r in range(4):
                        for m in range(5):
                            s0 = PADX + r + 4 * (128 * m - 64)
                            tsls.append(slice(s0, s0 + 509, 4))
                else:
                    tsls = [slice(PADX + r, PADX + r + 2033, 16) for r in range(16)]
                for j0 in range(0, len(tsls), 4):
                    grp = tsls[j0:j0 + 4]
                    bank, bn = nextbank()
                    for j, sl in enumerate(grp):
                        for k in range(8):
                            mm(bank[:, j * 128:(j + 1) * 128], xT[:, k, sl], wv[:, k, 0:128], k == 0, False, ["xT", wvn], [bn])
                        mm(bank[:, j * 128:(j + 1) * 128], onesb[0:1, 0:128], vrowb[0:1, (2 * g + pp) * 128:(2 * g + pp + 1) * 128],
                           False, True, ["onesb", "vrowb"], [bn])
                    n = len(grp)
                    cp(A, Vt[:, j0:j0 + n, :], bank[:, 0:n * 128].rearrange("p (a b) -> p a b", a=n), [], [bn, "Vt"])
                if SUB == 3:
                    return
                for sbk in range(4):
                    for j in range(4):
                        if d == 1:
                            m = 4 * sbk + j
                            qsl = slice(PADX + 128 * m, PADX + 128 * m + 128)
                            chunks = [(slice(PADX + 128 * m - 64, PADX + 128 * m + 64), m),
                                      (slice(PADX + 128 * m + 64, PADX + 128 * m + 192), m + 1)]
                            mi = 1 if m == 0 else (2 if m == 15 else 0)
                        elif d == 4:
                            r, m = sbk, j
                            q0 = PADX + r + 512 * m
                            qsl = slice(q0, q0 + 509, 4)
                            k0 = PADX + r + 4 * (128 * m - 64)
                            chunks = [(slice(k0, k0 + 509, 4), r * 5 + m), (slice(k0 + 512, k0 + 512 + 509, 4), r * 5 + m + 1)]
                            mi = 1 if m == 0 else (2 if m == 3 else 0)
                        else:
                            r = 4 * sbk + j
                            qsl = slice(PADX + r, PADX + r + 2033, 16)
                            chunks = [(qsl, r)]
                            mi = 3
                        nch = len(chunks)
                        wd = nch * 128
                        for h in range(2):
                            hp_ = slice(64 * h, 64 * h + 64)
                            si = h + 2 * (cnt[0] % 2)
                            sbank, sbn = psb[si], "ps%d" % si
                            par = cnt[0] % 2
                            pTb, pTn = pT[h][:, par * 256:par * 256 + 256], "pT%d_%d" % (h, par)
                            for ci, (ks, vt) in enumerate(chunks):
                                mm(sbank[:, ci * 128:(ci + 1) * 128], Kr[hp_, ks], Qr[hp_, qsl], True, True, ["Kr", "Qr"], [sbn])
                            act(pTb[:, 0:wd], sbank[:, 0:wd], AF.Exp, [], [sbn, pTn], scale=0.125)
                            tt(V, pTb[:, 0:wd], pTb[:, 0:wd], msk[:, mi, 0:wd], ALU.mult, ["msk"], [pTn])
                            for ci, (ks, vt) in enumerate(chunks):
                                mm(psb[6][hp_, j * 128:(j + 1) * 128], Vt[:, vt, 64 * h:64 * h + 64], pTb[:, ci * 128:(ci + 1) * 128],
                                   ci == 0, ci == nch - 1, ["Vt", pTn], ["ps6"])
                            for ci, (ks, vt) in enumerate(chunks):
                                mm(psb[7][hp_, j * 128:(j + 1) * 128], onesb[:, 0:64], pTb[:, ci * 128:(ci + 1) * 128],
                                   ci == 0, ci == nch - 1, ["onesb", pTn], ["ps7"])
                        cnt[0] += 1
                    if d == 1:
                        vn, vd = accn[:, 512 * sbk:512 * sbk + 512], accd[:, 512 * sbk:512 * sbk + 512]
                        bn_, bd_ = psb[6][:, :], psb[7][:, :]
                    elif d == 4:
                        vn, vd = accn[:, sbk:S:4], accd[:, sbk:S:4]
                        bn_, bd_ = psb[6][:, :], psb[7][:, :]
                    else:
                        vn = accn.rearrange("p (i r) -> p r i", r=16)[:, 4 * sbk:4 * sbk + 4, :]
                        vd = accd.rearrange("p (i r) -> p r i", r=16)[:, 4 * sbk:4 * sbk + 4, :]
                        bn_ = psb[6][:, :].rearrange("p (r i) -> p r i", r=4)
                        bd_ = psb[7][:, :].rearrange("p (r i) -> p r i", r=4)
                    if g == 0:
                        cp(V, vn, bn_, [], ["ps6", "accn"])
                        cp(A, vd, bd_, [], ["ps7", "accd"])
                    else:
                        tt(V, vn, vn, bn_, ALU.add, [], ["ps6", "accn"])
                        tt(V, vd, vd, bd_, ALU.add, [], ["ps7", "accd"])
                if os.environ.get("ATT_STOP") == str(g + 1):
                    return
            oc = 14 + pp
            inproj(l, 64 + pp, lambda t4, bank, bn, oc=oc, pp=pp: act(oT[:, oc, t4 * 512:(t4 + 1) * 512], bank[:], AF.Silu,
                                                                        ["colsb"], [bn, "oT%d" % oc], bias=col(C_BIN + 64 + pp)))
            act(accd, accd, AF.Ln, [], ["accd"])
            act(accd, accd, AF.Exp, [], ["accd"], scale=-1.0)
            tt(V, accn, accn, accd, ALU.mult, ["accd"], ["accn"])
            tt(V, oT[:, oc, :], accn, oT[:, oc, :], ALU.mult, ["accn"], ["oT%d" % oc])

    C.attn = attn_phase

    def rwkv_phase(l):
        barrier()
        lw = abf(0, S)
        la = abf(1024, S)
        wup = abf(2048, 1024)
        aup = abf(2560, 1024)
        hbuf = af32(3072, 2050)
        tmpB = af32(3072, S)
        tmpf = af32(5124, S)
        rbf = abf(7172, S)
        kbf = abf(8196, S)
        vbf = abf(9220, S)
        kkbf = abf(10244, S)
        bonus = abf(11268, S)
        ytok = af32(12292, S).rearrange("p (c i) -> p c i", c=32)
        Vtok = abf(14340, S).rearrange("p (c i) -> p c i", c=32)
        reset = af32(15364, 512)
        STf = af32(15876, 64)
        STb = abf(15940, 64)
        Xs = abf(15972, 64)
        Us = abf(16004, 64)
        Wc = af32(16036, 8)
        totc = af32(16044, 8)
        stat = af32(16052, 128)
        ynb = rbf.rearrange("p (c i) -> p c i", c=32)
        tmpf3 = tmpf.rearrange("p (c i) -> p c i", c=32)
        sg = o8f32(0, 512)
        aa = o8f32(512, 512)
        Gc = o8f32(1024, 512)
        tmpG = o8f32(1536, 512)
        E = o8f32(2048, 512)
        bb = o8f32(2560, 512)
        kd = o8f32(3072, 512)
        AR2 = o8bf(2 * 3584, 1024).rearrange("p (c n) -> p c n", c=8)
        AR4 = o8bf(2 * 3584, 1024).rearrange("p (c a j) -> p c a j", c=8, a=2)
        BT = o8bf(2 * 4096, 512)
        KT = o8bf(2 * 4352, 512)
        BHT = o8bf(2 * 4608, 512)
        KHT = o8bf(2 * 4864, 512)
        BHtok = o8bf(2 * 5120, 512).rearrange("p (c j) -> p c j", c=8)
        KHtok = o8bf(2 * 5376, 512).rearrange("p (c j) -> p c j", c=8)
        G1s = o8bf(2 * 5632, 1024).rearrange("p (c n) -> p c n", c=8)
        G2s = o8bf(2 * 6144, 1024).rearrange("p (c n) -> p c n", c=8)
        QP = o8bf(2 * 6656, 1024).rearrange("p (c n) -> p c n", c=8)
        Pn = o8bf(2 * 7168, 512).rearrange("p (c n) -> p c n", c=8)
        m1 = [o8bf(2 * 7424, 512), o8bf(2 * 7680, 512)]
        m2 = [o8bf(2 * 7936, 256), o8bf(2 * 8064, 256)]
        t1f = E

        mfb = mixf[:, :].bitcast(BF16)
        Wc1 = af32(16180, 8)
        SETS = [
            (AR2, AR4, G1s, G2s, QP, BHtok, KHtok, Wc),
            (mfb[:, 0:1024].rearrange("p (c n) -> p c n", c=8), mfb[:, 0:1024].rearrange("p (c a j) -> p c a j", c=8, a=2),
             mfb[:, 1024:2048].rearrange("p (c n) -> p c n", c=8), mfb[:, 2048:3072].rearrange("p (c n) -> p c n", c=8),
             mfb[:, 3072:4096].rearrange("p (c n) -> p c n", c=8),
             xstage[:, 0:512].rearrange("p (c j) -> p c j", c=8), xstage[:, 512:1024].rearrange("p (c j) -> p c j", c=8), Wc1),
        ]

        STATE = [(STf, STb, Xs, Us), (af32(16188, 64), abf(16252, 64), abf(16284, 64), abf(16316, 64))]

        def c8v(ap):
            return ap.rearrange("p (c j) -> p c j", c=8)

        dma(G, wup, wup_d[l], [], ["wup"])
        dma(G, aup, aup_d[l], [], ["aup"])
        dma("sp", reset, rmask_d[0][:, 1280:1792], [], ["reset"])
        for z in range(2):
            dma(G, m1[z], rmask_d[z][:, 0:512], [], ["m1_%d" % z])
            dma(G, m2[z], rmask_d[z][:, 1024:1280], [], ["m2_%d" % z])
        memset(V, hbuf[:, 0:1], 0.0, ["hbuf"])
        memset(V, hbuf[:, 2049:2050], 0.0, ["hbuf"])

        def shifted(cg, dst, dstname):
            inproj(l, cg, lambda t4, bank, bn: act(hbuf[:, 1 + t4 * 512:1 + (t4 + 1) * 512], bank[:], AF.Identity,
                                                   ["colsb"], [bn, "hbuf"], bias=col(C_BIN + cg)))
            act(tmpf, hbuf[:, 1:2049], AF.Identity, ["hbuf", "c0col"], ["tmpf"], scale=c0col[:, cg:cg + 1])
            stt(tmpf, hbuf[:, 0:2048], col(C_MU0 + cg), tmpf, ALU.mult, ALU.add, ["hbuf", "colsb"], ["tmpf"])
            stt(dst, hbuf[:, 2:2050], col(C_MU1 + cg), tmpf, ALU.mult, ALU.add, ["hbuf", "colsb", "tmpf"], [dstname])

        def pairbank():
            return [nextbank(), nextbank()]

        shifted(24, tmpf, "tmpf")
        act(lw, tmpf, AF.Tanh, ["tmpf"], ["lw"])
        shifted(25, la, "la")

        for hp in range(8):
            shifted(hp, rbf, "rbf")
            shifted(8 + hp, kbf, "kbf")
            shifted(16 + hp, vbf, "vbf")
            inproj(l, 26 + hp, lambda t4, bank, bn, hp=hp: act(oT[:, hp, t4 * 512:(t4 + 1) * 512], bank[:], AF.Silu,
                                                               ["colsb"], [bn, "oT%d" % hp], bias=col(C_BIN + 26 + hp)))
            ts(V, tmpf, kbf, col(C_KK + hp), None, ALU.mult, None, ["kbf", "colsb"], ["tmpf"])
            act(bonus, tmpf, AF.Square, ["tmpf"], ["bonus"])
            for t4 in range(4):
                tsl = slice(t4 * 512, (t4 + 1) * 512)
                bank, bn = nextbank()
                mm(bank[:], bonesb[:], bonus[:, tsl], True, True, ["bonesb", "bonus"], [bn])
                act(tmpB[:, tsl], bank[:], AF.Ln, ["epsc"], [bn, "hbuf"], bias=epsc[:, 0:1])
                act(tmpB[:, tsl], tmpB[:, tsl], AF.Exp, [], ["hbuf"], scale=-0.5)
            tt(V, kkbf, tmpf, tmpB, ALU.mult, ["tmpf", "hbuf"], ["kkbf"])
            stt(bonus, rbf, col(C_RK + hp), kbf, ALU.mult, ALU.mult, ["rbf", "kbf", "colsb"], ["bonus"])
            for t4 in range(4):
                tsl = slice(t4 * 512, (t4 + 1) * 512)
                bank, bn = nextbank()
                mm(bank[:], bonesb[:], bonus[:, tsl], True, True, ["bonesb", "bonus"], [bn])
                tt(V, bonus[:, tsl], bank[:], vbf[:, tsl], ALU.mult, ["vbf"], [bn, "bonus"])
            for T in range(4):
                pb = pairbank()
                for h in range(2):
                    hs = slice(64 * h, 64 * h + 64)
                    bank, bn = pb[h]
                    for c8 in range(8):
                        c = T * 8 + c8
                        mm(bank[hs, c8 * 64:(c8 + 1) * 64], vbf[hs, c * 64:(c + 1) * 64], identb[hs, 64 * h:64 * h + 64],
                           True, True, ["vbf", "identb"], [bn])
                    cp(A, Vtok[hs, T * 8:(T + 1) * 8, :], c8v(bank[hs, :]), [], [bn, "Vtok"])

            items = [(0, T) for T in range(4)] + [(1, T) for T in range(3, -1, -1)]

            def produce(z, T, s):
                zs = slice(64 * z, 64 * z + 64)
                tsl = slice(T * 512, (T + 1) * 512)
                AR2_, AR4_, G1s_, G2s_, QP_, BHtok_, KHtok_, Wc_ = SETS[s]
                ss = str(s)
                ARn = "AR" + ss
                b1, b1n = nextbank()
                mm(b1[:], wup[zs, hp * 128:(hp + 1) * 128], lw[zs, tsl], True, True, ["wup", "lw"], [b1n])
                act(sg, b1[:], AF.Sigmoid, ["colsb"], [b1n, "sg"], bias=col(C_W0 + z * 8 + hp))
                b2, b2n = nextbank()
                mm(b2[:], aup[zs, hp * 128:(hp + 1) * 128], la[zs, tsl], True, True, ["aup", "la"], [b2n])
                act(aa, b2[:], AF.Sigmoid, ["colsb"], [b2n, "aa"], bias=col(C_A0 + z * 8 + hp))
                yield
                add(V, lambda e: e.tensor_tensor_scan(out=Gc, data0=reset, data1=sg, initial=0.0, op0=ALU.mult, op1=ALU.add),
                    reads=["reset", "sg"], writes=["Gc"])
                cp(V, totc, c8v(Gc)[:, :, 63], ["Gc"], ["totc"])
                totb = totc.unsqueeze(2).to_broadcast([128, 8, 64])
                if z == 1:
                    tt(V, tmpG, sg, Gc, ALU.subtract, ["sg", "Gc"], ["tmpG"])
                    tt(V, c8v(Gc), c8v(tmpG), totb, ALU.add, ["tmpG", "totc"], ["Gc"])
                yield
                tt(V, sg, Gc, sg, ALU.subtract, ["Gc"], ["sg"])
                tt(V, c8v(tmpG), c8v(Gc), totb, ALU.subtract, ["Gc", "totc"], ["tmpG"])
                act(E, Gc, AF.Exp, ["Gc"], ["E"], scale=-CDEC)
                act(sg, sg, AF.Exp, [], ["sg"], scale=-CDEC)
                act(Gc, Gc, AF.Exp, [], ["Gc"], scale=CDEC)
                act(tmpG, tmpG, AF.Exp, [], ["tmpG"], scale=CDEC)
                act(Wc_, totc, AF.Exp, ["totc"], ["Wc" + ss], scale=-CDEC)
                yield
                ts(V, kd, aa, -1.0, col(C_KA + hp), ALU.add, ALU.mult, ["aa", "colsb"], ["kd"])
                stt(kd, kd, 1.0, kbf[:, tsl], ALU.add, ALU.mult, ["kbf"], ["kd"])
                tt(V, bb, kkbf[:, tsl], aa, ALU.mult, ["kkbf", "aa"], ["bb"])
                yield
                tt(V, AR4_[:, :, 1, :], c8v(rbf[:, tsl]), c8v(E), ALU.mult, ["rbf", "E"], [ARn])
                stt(AR4_[:, :, 0, :], c8v(kkbf[:, tsl]), -1.0, c8v(sg), ALU.mult, ALU.mult, ["kkbf", "sg"], [ARn])
                yield
                tt(V, BT, bb, Gc, ALU.mult, ["bb", "Gc"], ["BT"])
                tt(V, KT, kd, Gc, ALU.mult, ["kd", "Gc"], ["KT"])
                tt(V, BHT, bb, tmpG, ALU.mult, ["bb", "tmpG"], ["BHT"])
                tt(V, KHT, kd, tmpG, ALU.mult, ["kd", "tmpG"], ["KHT"])
                yield
                for src, srcn, dst, dstn in ((BHT, "BHT", BHtok_, "BHtok" + ss), (KHT, "KHT", KHtok_, "KHtok" + ss)):
                    pb = pairbank()
                    for h in range(2):
                        hs = slice(64 * h, 64 * h + 64)
                        bank, bn = pb[h]
                        for c8 in range(8):
                            mm(bank[hs, c8 * 64:(c8 + 1) * 64], src[hs, c8 * 64:(c8 + 1) * 64], identb[hs, 64 * h:64 * h + 64],
                               True, True, [srcn, "identb"], [bn])
                        cp(A, dst[hs, :, :], c8v(bank[hs, :]), [], [bn, dstn + str(h)])
                    yield
                chains = []
                for hv in range(2):
                    for h in range(2):
                        ia, ib = {(0, 0): (0, 1), (0, 1): (2, 3), (1, 0): (4, 5), (1, 1): (6, 7)}[(hv, h)]
                        chains.append((hv, h, psb[ia], "ps%d" % ia, psb[ib], "ps%d" % ib))
                for hv, h, bA, bAn, bB, bBn in chains:
                    hs = slice(64 * h, 64 * h + 64)
                    for cl in range(4):
                        c8 = hv * 4 + cl
                        mm(bA[hs, cl * 128:(cl + 1) * 128], BT[hs, c8 * 64:(c8 + 1) * 64], AR2_[hs, c8, :], True, True,
                           ["BT", ARn], [bAn])
                    for cl in range(4):
                        c8 = hv * 4 + cl
                        mm(bB[hs, cl * 128:(cl + 1) * 128], KT[hs, c8 * 64:(c8 + 1) * 64], AR2_[hs, c8, :], True, True,
                           ["KT", ARn], [bBn])
                for hv, h, bA, bAn, bB, bBn in chains:
                    cs = slice(hv * 4, hv * 4 + 4)
                    hs = slice(64 * h, 64 * h + 64)
                    nm = "%s%d%d" % (ss, hv, h)
                    tt(V, G1s_[hs, cs, :], bA[hs, :].rearrange("p (c n) -> p c n", c=4),
                       m1[z][hs, :].rearrange("p (c n) -> p c n", c=4), ALU.mult, ["m1_%d" % z], [bAn, "G1s" + nm])
                    tt(V, G2s_[hs, cs, :], bB[hs, :].rearrange("p (c n) -> p c n", c=4),
                       m1[z][hs, :].rearrange("p (c n) -> p c n", c=4), ALU.mult, ["m1_%d" % z], [bBn, "G2s" + nm])
                for hv, h, bA, bAn, bB, bBn in chains:
                    hs = slice(64 * h, 64 * h + 64)
                    for cl in range(4):
                        c8 = hv * 4 + cl
                        mm(bA[hs, cl * 64:(cl + 1) * 64], AR4_[hs, c8, 0, :], BT[hs, c8 * 64:(c8 + 1) * 64], True, True,
                           ["BT", ARn], [bAn])
                for hv, h, bA, bAn, bB, bBn in chains:
                    cs = slice(hv * 4, hv * 4 + 4)
                    hs = slice(64 * h, 64 * h + 64)
                    nm = "%s%d%d" % (ss, hv, h)
                    pn = "Pn%d%d" % (hv, h)
                    tt(V, Pn[hs, cs, :], bA[hs, 0:256].rearrange("p (c n) -> p c n", c=4),
                       m2[z][hs, :].rearrange("p (c n) -> p c n", c=4), ALU.mult, ["m2_%d" % z], [bAn, pn])
                    cp(A, QP_[hs, cs, 0:64], eye2[hs, :].unsqueeze(1).to_broadcast([64, 4, 64]), ["eye2"], ["QP" + nm])
                    cp(A, QP_[hs, cs, 64:128], G1s_[hs, cs, 0:64], ["G1s" + nm], ["QP" + nm])
                for k in range(6):
                    last = (k == 5)
                    wA = 64 if last else 128
                    for hv, h, bA, bAn, bB, bBn in chains:
                        hs = slice(64 * h, 64 * h + 64)
                        nm = "%s%d%d" % (ss, hv, h)
                        pn = "Pn%d%d" % (hv, h)
                        for cl in range(4):
                            c8 = hv * 4 + cl
                            mm(bA[hs, cl * 128:cl * 128 + wA], Pn[hs, c8, :], QP_[hs, c8, 0:wA], True, True,
                               [pn, "QP" + nm], [bAn])
                        if not last:
                            for cl in range(4):
                                c8 = hv * 4 + cl
                                mm(bB[hs, cl * 64:(cl + 1) * 64], QP_[hs, c8, 64:128], Pn[hs, c8, :], True, True,
                                   [pn, "QP" + nm], [bBn])
                    for hv, h, bA, bAn, bB, bBn in chains:
                        cs = slice(hv * 4, hv * 4 + 4)
                        hs = slice(64 * h, 64 * h + 64)
                        nm = "%s%d%d" % (ss, hv, h)
                        pn = "Pn%d%d" % (hv, h)
                        bA3 = bA[hs, :].rearrange("p (c n) -> p c n", c=4)
                        tt(V, QP_[hs, cs, 0:64], QP_[hs, cs, 0:64], bA3[:, :, 0:64], ALU.add, [], [bAn, "QP" + nm])
                        if not last:
                            cp(A, QP_[hs, cs, 64:128], bA3[:, :, 64:128], [], [bAn, "QP" + nm])
                            cp(A, Pn[hs, cs, :], bB[hs, 0:256].rearrange("p (c n) -> p c n", c=4), [], [bBn, pn])
                yield

            def consume(z, T, s, first):
                AR2_, AR4_, G1s_, G2s_, QP_, BHtok_, KHtok_, Wc_ = SETS[s]
                STf_, STb_, Xs_, Us_ = STATE[z]
                ss = str(s)
                zn = str(z)
                ARn = "AR" + ss
                if first:
                    memset(V, STf_, 0.0, ["STf" + zn + "0", "STf" + zn + "1"])
                    memset(V, STb_, 0.0, ["STb" + zn + "0", "STb" + zn + "1"])
                cord = range(8) if z == 0 else range(7, -1, -1)
                for c8 in cord:
                    c = T * 8 + c8
                    H = []
                    for h in range(2):
                        H.append((slice(64 * h, 64 * h + 64), zn + str(h), "%s%d%d" % (ss, c8 // 4, h),
                                  psb[2 * z + h], "ps%d" % (2 * z + h), psb[4 + 2 * z + h], "ps%d" % (4 + 2 * z + h)))
                    for hs, hn, nm, cb, cbn, yb, ybn in H:
                        mm(cb[hs, 0:64], AR4_[hs, c8, 0, :], STb_[hs, :], True, False, [ARn, "STb" + hn], [cbn])
                        mm(cb[hs, 0:64], G2s_[hs, c8, 0:64], Vtok[hs, c, :], False, True, ["G2s" + nm, "Vtok"], [cbn])
                    yield
                    for hs, hn, nm, cb, cbn, yb, ybn in H:
                        cp(A, Xs_[hs, :], cb[hs, 0:64], [], [cbn, "Xs" + hn])
                    for hs, hn, nm, cb, cbn, yb, ybn in H:
                        mm(cb[hs, 64:128], QP_[hs, c8, 0:64], Xs_[hs, :], True, True, ["QP" + nm, "Xs" + hn], [cbn])
                    yield
                    for hs, hn, nm, cb, cbn, yb, ybn in H:
                        cp(V if z == 0 else A, Us_[hs, :], cb[hs, 64:128], [], [cbn, "Us" + hn])
                    for hs, hn, nm, cb, cbn, yb, ybn in H:
                        mm(cb[hs, 128:192], BHtok_[hs, c8, :], Us_[hs, :], True, False, ["BHtok" + ss + hn[1], "Us" + hn], [cbn])
                        mm(cb[hs, 128:192], KHtok_[hs, c8, :], Vtok[hs, c, :], False, True, ["KHtok" + ss + hn[1], "Vtok"], [cbn])
                    for hs, hn, nm, cb, cbn, yb, ybn in H:
                        mm(yb[hs, c8 * 64:(c8 + 1) * 64], AR4_[hs, c8, 1, :], STb_[hs, :], True, False, [ARn, "STb" + hn], [ybn])
                        mm(yb[hs, c8 * 64:(c8 + 1) * 64], G1s_[hs, c8, 64:128], Us_[hs, :], False, False,
                           ["G1s" + nm, "Us" + hn], [ybn])
                        mm(yb[hs, c8 * 64:(c8 + 1) * 64], G2s_[hs, c8, 64:128], Vtok[hs, c, :], False, True,
                           ["G2s" + nm, "Vtok"], [ybn])
                    yield
                    for hs, hn, nm, cb, cbn, yb, ybn in H:
                        stt(STb_[hs, :], STb_[hs, :], Wc_[hs, c8:c8 + 1], cb[hs, 128:192], ALU.mult, ALU.add,
                            ["Wc" + ss], [cbn, "STb" + hn])
                    yield
                for h in range(2):
                    hs = slice(64 * h, 64 * h + 64)
                    yb, ybn = psb[4 + 2 * z + h], "ps%d" % (4 + 2 * z + h)
                    tt(V, ytok[hs, T * 8:(T + 1) * 8, :], ytok[hs, T * 8:(T + 1) * 8, :], c8v(yb[hs, :]), ALU.add,
                       [], [ybn, "ytok%d" % T])
                yield

            for T_ in range(4):
                memset(V, ytok[:, T_ * 8:(T_ + 1) * 8, :], 0.0, ["ytok%d" % T_])
            for i in range(4):
                for _ in produce(0, i, 0):
                    pass
                for _ in produce(1, 3 - i, 1):
                    pass
                gens = [consume(0, i, 0, i == 0), consume(1, 3 - i, 1, i == 0)]
                while gens:
                    for g_ in list(gens):
                        try:
                            next(g_)
                        except StopIteration:
                            gens.remove(g_)
            add(V, lambda e: e.tensor_reduce(out=stat[:, 0:32], in_=ytok, axis=AX.X, op=ALU.add), reads=["ytok0", "ytok1", "ytok2", "ytok3"], writes=["st0"])
            ts(V, stat[:, 32:64], stat[:, 0:32], -1.0 / 64, None, ALU.mult, None, ["st0"], ["st1"])
            tt(V, ytok, ytok, stat[:, 32:64].unsqueeze(2).to_broadcast([128, 32, 64]), ALU.add, ["st1"], ["ytok0", "ytok1", "ytok2", "ytok3"])
            act(tmpf3, ytok, AF.Square, ["ytok0", "ytok1", "ytok2", "ytok3"], ["tmpf"])
            add(V, lambda e: e.tensor_reduce(out=stat[:, 64:96], in_=tmpf3, axis=AX.X, op=ALU.add), reads=["tmpf"], writes=["st2"])
            act(stat[:, 96:128], stat[:, 64:96], AF.Sqrt, ["st2"], ["st3"], bias=64e-5, scale=1.0 / 64)
            add(V, lambda e: e.reciprocal(out=stat[:, 96:128], in_=stat[:, 96:128]), reads=[], writes=["st3"])
            tt(V, ynb, ytok, stat[:, 96:128].unsqueeze(2).to_broadcast([128, 32, 64]), ALU.mult, ["ytok0", "ytok1", "ytok2", "ytok3", "st3"], ["rbf"])
            for T in range(4):
                tsl = slice(T * 512, (T + 1) * 512)
                pb = pairbank()
                for h in range(2):
                    hs = slice(64 * h, 64 * h + 64)
                    bank, bn = pb[h]
                    for c8 in range(8):
                        mm(bank[hs, c8 * 64:(c8 + 1) * 64], ynb[hs, T * 8 + c8, :], identb[hs, 64 * h:64 * h + 64], True, True,
                           ["rbf", "identb"], [bn])
                    act(t1f[hs, :], bank[hs, :], AF.Identity, ["colsb"], [bn, "t1f" + str(h)],
                        bias=colsb[hs, C_GB + hp:C_GB + hp + 1], scale=colsb[hs, C_GG + hp:C_GG + hp + 1])
                tt(V, t1f, t1f, bonus[:, tsl], ALU.add, ["bonus", "t1f0", "t1f1"], ["t1f0", "t1f1"])
                tt(V, oT[:, hp, tsl], t1f, oT[:, hp, tsl], ALU.mult, ["t1f0", "t1f1"], ["oT%d" % hp])

    C.rwkv = rwkv_phase


    dma("sp", colsb[:], cols_d[0], [], ["colsb"])
    xld = [af32(0, 1024), af32(1024, 1024)]
    for t16 in range(16):
        dma("sp", xld[t16 % 2], x_in[t16 * 128:(t16 + 1) * 128, :], [], ["xld%d" % (t16 % 2)])
        store_xT(xld[t16 % 2], "xld%d" % (t16 % 2), t16)
    import os
    STOP = int(os.environ.get("KSTOP", "9"))
    for l in range(depth if STOP > 1 else 0):
        if l > 0:
            dma("sp", colsb[:], cols_d[l], [], ["colsb"])
        dma(G, vrowb[:], vrow_d[l], [], ["vrowb"])
        tt(V, c0col[:], colsb[:, C_MU0:C_MU0 + 26], colsb[:, C_MU1:C_MU1 + 26], ALU.add, ["colsb"], ["c0col"])
        ts(V, c0col[:], c0col[:], -1.0, 1.0, ALU.mult, ALU.add, [], ["c0col"])
        if C.rwkv is not None and "A" in PH:
            C.rwkv(l)
        else:
            for c in range(8):
                memset(V, oT[:, c, :], 0.0, ["oT%d" % c])
        if "B" in PH:
            pool_phase(l)
        else:
            barrier()
            for c in range(8, 14):
                memset(V, oT[:, c, :], 0.0, ["oT%d" % c])
        if C.attn is not None and "C" in PH:
            C.attn(l)
        else:
            barrier()
            for c in range(14, 16):
                memset(V, oT[:, c, :], 0.0, ["oT%d" % c])
        if dbg is not None and l == depth - 1:
            dtmp = af32(0, S)
            for c in range(16):
                cp(V, dtmp, oT[:, c, :], ["oT%d" % c], ["dtmp"])
                dma("sp", dbg_out[:, c, :], dtmp, ["dtmp"], [])
        if STOP > 2:
            final_phase(l, l == depth - 1)
    P.emit(nc)
    st.close()
    return nc


PH = "ABC"


def EXTRA_PHASES(L):
    pass


def host_prep(inp):
    f = np.float32
    g = lambda k: np.asarray(inp[k], dtype=f)
    colv = lambda v: np.ascontiguousarray(v.reshape(-1, 128).T)
    cols = []
    for l in range(DEPTH):
        parts = [colv(g("b_in")[l]), colv(g("rwkv_mu")[l, 0]), colv(g("rwkv_mu")[l, 1]),
                 colv(g("rwkv_w0")[l, 0]), colv(g("rwkv_w0")[l, 1]), colv(g("rwkv_a0")[l, 0]), colv(g("rwkv_a0")[l, 1]),
                 colv(g("rwkv_k_k")[l]), colv(g("rwkv_k_a")[l]), colv(g("rwkv_r_k")[l].reshape(-1)),
                 colv(g("rwkv_gn_g")[l]), colv(g("rwkv_gn_b")[l]), colv(g("pool_b")[l]), colv(g("pool_scale")[l])]
        cols.append(np.concatenate(parts, axis=1))
    cols = np.stack(cols)
    assert cols.shape == (DEPTH, 128, NCOLS), cols.shape
    shared = {
        "w_in": g("w_in"), "cols": cols,
        "vrow": np.ascontiguousarray(g("b_in")[:, None, 7424:8192]),
        "w_up": np.ascontiguousarray(g("rwkv_w_up").reshape(DEPTH, 128, 1024)),
        "a_up": np.ascontiguousarray(g("rwkv_a_up").reshape(DEPTH, 128, 1024)),
        "pool_w": _blockdiag(g("pool_w")), "proj_a": g("proj_a"), "proj_b": g("proj_b"), "proj_c": g("proj_c"),
        "w_out": g("w_out"),
        "lnrow": np.ascontiguousarray(np.stack([g("ln_g"), g("ln_b")], axis=1)),
    }
    shared.update(host_consts())
    return shared


def _blockdiag(pw):
    out = np.zeros((DEPTH, 768, 768), np.float32)
    for gi in range(4):
        out[:, 192 * gi:192 * gi + 192, 192 * gi:192 * gi + 192] = pw[:, gi]
    return out


def host_consts():
    f = np.float32
    ident = np.eye(128, dtype=f)
    bones = np.zeros((128, 128), f)
    bones[:64, :64] = 1
    bones[64:, 64:] = 1
    perm = np.zeros((128, 128), f)
    for m in range(128):
        c = m % 64
        k = m + 32 if c < 32 else m - 32
        perm[k, m] = 1
    eye2 = np.concatenate([np.eye(64, dtype=f), np.eye(64, dtype=f)], 0)
    cst = np.concatenate([ident, bones, perm, eye2], axis=1)
    inv = np.power(f(10000.0), -np.arange(0, 64, 2, dtype=f) / f(64))
    ang = np.arange(S, dtype=f)[:, None] * inv[None, :]
    ang = np.concatenate([ang, ang], axis=-1).astype(f)
    cosT = np.cos(ang).T.astype(f)
    sinT = np.sin(ang).T.astype(f)
    sign = np.where(np.arange(64) < 32, -1.0, 1.0).astype(f)[:, None]
    rope = np.stack([np.concatenate([cosT, cosT], 0), np.concatenate([sinT * sign, sinT * sign], 0)]).astype(f)
    b = np.arange(128)[:, None]
    a = np.arange(128)[None, :]
    mA = (b >= a).astype(f)
    mB = (a >= b).astype(f)
    mAf = mA * (b >= 64)
    mBl = mB * (b < 64)
    m16 = (np.abs(a - b) <= 64).astype(f)
    amask = np.stack([np.concatenate([mA, mB, mA, mB], 1), np.concatenate([mAf, mB, mAf, mB], 1),
                      np.concatenate([mA, mBl, mA, mBl], 1), np.concatenate([m16, m16, m16, m16], 1)]).astype(f)
    s_ = np.arange(64)[:, None]
    t_ = np.arange(64)[None, :]
    rm = []
    for z in range(2):
        if z == 0:
            strict = (s_ < t_)
            incl = (s_ <= t_)
        else:
            strict = (s_ > t_)
            incl = (s_ >= t_)
        m1 = np.concatenate([strict, incl], 1).astype(f)
        m2 = strict.T.astype(f)
        reset = np.ones((64, 512), f)
        reset[:, ::64] = 0
        row = np.concatenate([np.tile(m1, (1, 4)), np.tile(m1, (1, 4)), np.tile(m2, (1, 4)), reset], 1)
        rm.append(np.concatenate([row, row], 0))
    rmask = np.stack(rm).astype(f)
    pedge = np.ones((128, 4, 16), f)
    for gi in range(4):
        h = 1 << gi
        for e in range(8):
            t = e
            cnt = min(t + h, S - 1) - max(t - h, 0) + 1
            pedge[:, gi, e] = (2 * h + 1) / cnt
            t = S - 8 + e
            cnt = min(t + h, S - 1) - max(t - h, 0) + 1
            pedge[:, gi, 8 + e] = (2 * h + 1) / cnt
    selc = np.zeros((128, 24), f)
    for c in range(6):
        for p in range(128):
            gi = (128 * c + p) // 192
            selc[p, c * 4 + gi] = 1.0 / (2 * (1 << gi) + 1)
    return {"selc": selc, "cst": cst, "rope": rope, "amask": amask, "rmask": rmask, "pedge": pedge}


_NC_CACHE = {}


def kernel(**inputs):
    shared = host_prep(inputs)
    x = np.asarray(inputs["x"], dtype=np.float32)
    if "nc" not in _NC_CACHE:
        _NC_CACHE["nc"] = build()
    nc = _NC_CACHE["nc"]
    in_maps = []
    for c in range(8):
        m = dict(shared)
        m["x"] = np.ascontiguousarray(x[c])
        in_maps.append(m)
    res = run_bass_kernel_spmd(nc, in_maps, core_ids=list(range(8)))
    return np.stack([np.asarray(r["y"], dtype=np.float32) for r in res.results], axis=0)
```
